# Optimizing a Trainium2 kernel written in Bass

```python
import jax
import jax.numpy as jnp
from jax import lax
import numpy as np

D_MODEL = 2048
BATCH = 2
SEQ = 8192
DEPTH = 4

N_META = 16
CHUNK = 64
NORM_EPS = 1e-6
NEG_BIG = -1e30
F_FLOOR = 1e-12
BRANCH_WIDTH = 1024
N_BRANCH = 3

A_HEADS = 8
A_DK = 128
A_DV = BRANCH_WIDTH // A_HEADS
B_HEADS = 4
B_DQK = 128
B_DV = BRANCH_WIDTH // B_HEADS
B_CONV = 4
C_HEADS = 16
C_DH = BRANCH_WIDTH // C_HEADS
C_DECAY_RANK = 64
C_ICLR_RANK = 64
C_LN_EPS = 64e-5

SPLITS = (
    A_HEADS * A_DK, A_HEADS * A_DK, BRANCH_WIDTH, BRANCH_WIDTH,
    B_HEADS * B_DQK, B_HEADS * B_DQK, BRANCH_WIDTH, BRANCH_WIDTH,
    B_HEADS, B_HEADS, BRANCH_WIDTH,
    BRANCH_WIDTH, BRANCH_WIDTH, BRANCH_WIDTH, C_DECAY_RANK, C_ICLR_RANK,
    BRANCH_WIDTH,
    D_MODEL, D_MODEL, D_MODEL,
)
N_IN = sum(SPLITS)
C_SHIFT_WIDTH = 3 * BRANCH_WIDTH + C_DECAY_RANK + C_ICLR_RANK

kernel_name = 'hybrid_hgrn2_mlstm_rwkv7_gated'


def rmsnorm(x, g):
    xf = x.astype(jnp.float32)
    y = xf * lax.rsqrt(jnp.mean(xf * xf, axis=-1, keepdims=True) + NORM_EPS)
    return (y * g.astype(jnp.float32)).astype(x.dtype)


def split_heads(a, n_heads):
    return a.reshape(a.shape[:-1] + (n_heads, a.shape[-1] // n_heads))


def head_rmsnorm(y, g):
    y = y * lax.rsqrt(jnp.mean(y * y, axis=-1, keepdims=True) + NORM_EPS)
    return y.reshape(y.shape[:2] + (-1,)) * g


def head_layernorm(y, g, eps):
    yc = y - jnp.mean(y, axis=-1, keepdims=True)
    y = yc * lax.rsqrt(jnp.mean(yc * yc, axis=-1, keepdims=True) + eps)
    return y.reshape(y.shape[:2] + (-1,)) * g


def split_cols(p):
    points = [int(v) for v in np.cumsum(SPLITS)[:-1]]
    return jnp.split(p, points, axis=-1)


def causal_conv(x, w):
    k = w.shape[0]
    return lax.conv_general_dilated(
        x, w[:, None, :].astype(x.dtype), window_strides=(1,), padding=[(k - 1, 0)],
        dimension_numbers=('NWC', 'WIO', 'NWC'), feature_group_count=x.shape[-1])


def causal_mask(length, strict=False):
    return jnp.tril(jnp.ones((length, length), dtype=bool), k=-1 if strict else 0)


def pair_decay(cum_t, cum_s, mask):
    m = mask[:, :, None]
    diff = cum_t[:, :, :, None, :] - cum_s[:, :, None, :, :]
    return jnp.where(m, jnp.exp(jnp.where(m, diff, 0.0)), 0.0)


def to_chunks(a):
    b, s, h, d = a.shape
    return a.reshape(b, s // CHUNK, CHUNK, h, d).transpose(1, 0, 3, 2, 4)


def run_chunked(step, state, seqs):
    meta = tuple(a[:, :N_META].transpose(0, 2, 1, 3) for a in seqs)
    real = tuple(to_chunks(a[:, N_META:]) for a in seqs)
    state, y_meta = step(state, meta)
    _, y_real = lax.scan(step, state, real)
    n_chunks, bsz, n_heads, length, dv = y_real.shape
    y_real = y_real.transpose(1, 0, 3, 2, 4).reshape(bsz, n_chunks * length, n_heads, dv)
    return jnp.concatenate([y_meta.transpose(0, 2, 1, 3), y_real], axis=1)


def hgrn2_chunk(s_mat, inp):
    q, k, log_f, v = inp
    length = q.shape[2]
    cg = jnp.cumsum(log_f, axis=2)
    att = jnp.einsum('bhtd,bhsd,bhtsd->bhts', q, k, pair_decay(cg, cg, causal_mask(length)))
    o = jnp.einsum('bhts,bhsv->bhtv', att, v) + jnp.einsum('bhtd,bhdv->bhtv', q * jnp.exp(cg), s_mat)
    tail = jnp.exp(cg[:, :, -1:] - cg)
    s_mat = jnp.exp(cg[:, :, -1])[..., None] * s_mat + jnp.einsum('bhsd,bhsv->bhdv', k * tail, v)
    return s_mat, o


def hgrn2_mixer(q_raw, f_raw, i_raw, z, lb, norm_g):
    f32 = jnp.float32
    f_raw = f_raw.astype(f32)
    q = split_heads(jax.nn.silu(q_raw.astype(f32)), A_HEADS) * A_DK ** -0.5
    k = (1.0 - lb) * jax.nn.sigmoid(-f_raw)
    log_f = jnp.log(jnp.maximum(lb + (1.0 - lb) * jax.nn.sigmoid(f_raw), F_FLOOR))
    v = split_heads(i_raw.astype(f32), A_HEADS)
    s0 = jnp.zeros((q.shape[0], A_HEADS, A_DK, A_DV), f32)
    o = run_chunked(hgrn2_chunk, s0, (q, split_heads(k, A_HEADS), split_heads(log_f, A_HEADS), v))
    return head_rmsnorm(o, norm_g) * jax.nn.silu(z.astype(f32))


def mlstm_chunk(state, inp):
    c_mat, n_vec, m_prev = state
    q, k, v, ig, lf = inp
    ig = ig[..., 0]
    length = q.shape[2]
    b = jnp.cumsum(lf[..., 0], axis=-1)
    log_w = b[:, :, :, None] - b[:, :, None, :] + ig[:, :, None, :]
    log_w = jnp.where(causal_mask(length), log_w, NEG_BIG)
    log_inter = b + m_prev[..., None]
    m_t = jnp.maximum(log_inter, jnp.max(log_w, axis=-1))
    scores = jnp.einsum('bhtd,bhsd->bhts', q, k) * jnp.exp(log_w - m_t[..., None])
    inter = jnp.exp(log_inter - m_t)
    num = jnp.einsum('bhts,bhsv->bhtv', scores, v) + inter[..., None] * jnp.einsum('bhtd,bhdv->bhtv', q, c_mat)
    den = jnp.sum(scores, axis=-1) + inter * jnp.einsum('bhtd,bhd->bht', q, n_vec)
    h = num / jnp.maximum(jnp.abs(den), jnp.exp(-m_t))[..., None]
    b_end = b[:, :, -1]
    log_s = b_end[..., None] - b + ig
    m_new = jnp.maximum(b_end + m_prev, jnp.max(log_s, axis=-1))
    w_s = jnp.exp(log_s - m_new[..., None])
    carry = jnp.exp(b_end + m_prev - m_new)
    c_mat = carry[..., None, None] * c_mat + jnp.einsum('bhs,bhsd,bhsv->bhdv', w_s, k, v)
    n_vec = carry[..., None] * n_vec + jnp.einsum('bhs,bhsd->bhd', w_s, k)
    return (c_mat, n_vec, m_new), h


def mlstm_mixer(q_raw, k_raw, v_raw, o_raw, ig_raw, fg_raw, z, conv_w, ig_b, fg_b, norm_g):
    f32 = jnp.float32
    qk = jax.nn.silu(causal_conv(jnp.concatenate([q_raw, k_raw], axis=-1).astype(f32), conv_w))
    q, k = jnp.split(qk, 2, axis=-1)
    q = split_heads(q, B_HEADS) * B_DQK ** -0.5
    k = split_heads(k, B_HEADS)
    v = split_heads(v_raw.astype(f32), B_HEADS)
    ig = (ig_raw.astype(f32) + ig_b)[..., None]
    lf = jax.nn.log_sigmoid(fg_raw.astype(f32) + fg_b)[..., None]
    bsz = q.shape[0]
    state = (jnp.zeros((bsz, B_HEADS, B_DQK, B_DV), f32), jnp.zeros((bsz, B_HEADS, B_DQK), f32),
             jnp.zeros((bsz, B_HEADS), f32))
    h = run_chunked(mlstm_chunk, state, (q, k, v, ig, lf))
    return (head_layernorm(h, norm_g, NORM_EPS) * jax.nn.sigmoid(o_raw.astype(f32))
            * jax.nn.silu(z.astype(f32)))


def rwkv7_chunk(s_mat, inp):
    r, k, v, lw, a_, b = inp
    length = r.shape[2]
    cw = jnp.cumsum(lw, axis=2)
    cw_prev = cw - lw
    d_strict = pair_decay(cw_prev, cw, causal_mask(length, strict=True))
    d_incl = pair_decay(cw, cw, causal_mask(length))
    l_ab = jnp.einsum('bhtd,bhsd,bhtsd->bhts', a_, b, d_strict)
    l_ak = jnp.einsum('bhtd,bhsd,bhtsd->bhts', a_, k, d_strict)
    rhs = (jnp.einsum('bhtd,bhdv->bhtv', a_ * jnp.exp(cw_prev), s_mat)
           + jnp.einsum('bhts,bhsv->bhtv', l_ak, v))
    u = lax.linalg.triangular_solve(jnp.eye(length, dtype=rhs.dtype) - l_ab, rhs,
                                    left_side=True, lower=True, unit_diagonal=True)
    r_b = jnp.einsum('bhtd,bhsd,bhtsd->bhts', r, b, d_incl)
    r_k = jnp.einsum('bhtd,bhsd,bhtsd->bhts', r, k, d_incl)
    y = (jnp.einsum('bhtd,bhdv->bhtv', r * jnp.exp(cw), s_mat)
         + jnp.einsum('bhts,bhsv->bhtv', r_b, u) + jnp.einsum('bhts,bhsv->bhtv', r_k, v))
    tail = jnp.exp(cw[:, :, -1:] - cw)
    s_mat = (jnp.exp(cw[:, :, -1])[..., None] * s_mat
             + jnp.einsum('bhsd,bhsv->bhdv', b * tail, u) + jnp.einsum('bhsd,bhsv->bhdv', k * tail, v))
    return s_mat, y


def rwkv7_mixer(c_r, c_k, c_v, c_wd, c_ad, z, mu, w0, w_up, a0, a_up, k_k, k_a, r_k, ln_g, ln_b):
    f32 = jnp.float32
    m = jnp.concatenate([c_r, c_k, c_v, c_wd, c_ad], axis=-1).astype(f32)
    prev = jnp.pad(m, ((0, 0), (1, 0), (0, 0)))[:, :-1]
    m = m + (prev - m) * mu
    w = BRANCH_WIDTH
    r, k, v, wd, ad = jnp.split(m, [w, 2 * w, 3 * w, 3 * w + C_DECAY_RANK], axis=-1)
    w_log = -jax.nn.softplus(-(w0 + jnp.tanh(wd) @ w_up)) - 0.5
    lw = -jnp.exp(w_log)
    a = jax.nn.sigmoid(a0 + ad @ a_up)
    kk = split_heads(k * k_k, C_HEADS)
    kk = kk / jnp.maximum(jnp.sqrt(jnp.sum(kk * kk, axis=-1, keepdims=True)), 1e-12)
    k = split_heads(k * (1.0 + (a - 1.0) * k_a), C_HEADS)
    r = split_heads(r, C_HEADS)
    v = split_heads(v, C_HEADS)
    s0 = jnp.zeros((r.shape[0], C_HEADS, C_DH, C_DH), f32)
    y = run_chunked(rwkv7_chunk, s0, (r, k, v, split_heads(lw, C_HEADS), -kk, kk * split_heads(a, C_HEADS)))
    y = head_layernorm(y, ln_g, C_LN_EPS) + ln_b
    bonus = jnp.sum(r * k * r_k.reshape(C_HEADS, C_DH), axis=-1, keepdims=True) * v
    return (y + bonus.reshape(y.shape)) * jax.nn.silu(z.astype(f32))


def setup_inputs(seed: int = 0) -> dict:
    key = jax.random.key(seed)
    ks = jax.random.split(key, 24)
    f32 = jnp.float32

    def nrm(k, shape, scale):
        return jax.random.normal(k, shape, f32) * scale

    return {
        'x': nrm(ks[0], (BATCH, SEQ, D_MODEL), 1.0),
        'meta_tokens': nrm(ks[1], (N_META, D_MODEL), 1.0),
        'norm_g': 1.0 + nrm(ks[2], (DEPTH, D_MODEL), 0.02),
        'w_in': nrm(ks[3], (DEPTH, D_MODEL, N_IN), D_MODEL ** -0.5),
        'hgrn_lb_logits': nrm(ks[4], (DEPTH, A_HEADS * A_DK), 0.5),
        'hgrn_norm_g': 1.0 + nrm(ks[5], (DEPTH, BRANCH_WIDTH), 0.02),
        'mlstm_conv': nrm(ks[6], (DEPTH, B_CONV, 2 * B_HEADS * B_DQK), B_CONV ** -0.5),
        'mlstm_ig_b': nrm(ks[7], (DEPTH, B_HEADS), 0.1),
        'mlstm_fg_b': jnp.linspace(3.0, 6.0, B_HEADS, dtype=f32)[None] + nrm(ks[8], (DEPTH, B_HEADS), 0.1),
        'mlstm_norm_g': 1.0 + nrm(ks[9], (DEPTH, BRANCH_WIDTH), 0.02),
        'rwkv_mu': jax.random.uniform(ks[10], (DEPTH, C_SHIFT_WIDTH), f32),
        'rwkv_w0': jnp.linspace(-6.0, -1.0, BRANCH_WIDTH, dtype=f32)[None] + nrm(ks[11], (DEPTH, BRANCH_WIDTH), 0.1),
        'rwkv_w_up': nrm(ks[12], (DEPTH, C_DECAY_RANK, BRANCH_WIDTH), 0.5 * C_DECAY_RANK ** -0.5),
        'rwkv_a0': nrm(ks[13], (DEPTH, BRANCH_WIDTH), 0.1),
        'rwkv_a_up': nrm(ks[14], (DEPTH, C_ICLR_RANK, BRANCH_WIDTH), 0.5 * C_ICLR_RANK ** -0.5),
        'rwkv_k_k': 0.85 + nrm(ks[15], (DEPTH, BRANCH_WIDTH), 0.02),
        'rwkv_k_a': 1.0 + nrm(ks[16], (DEPTH, BRANCH_WIDTH), 0.02),
        'rwkv_r_k': nrm(ks[17], (DEPTH, BRANCH_WIDTH), 0.1),
        'rwkv_ln_g': 1.0 + nrm(ks[18], (DEPTH, BRANCH_WIDTH), 0.02),
        'rwkv_ln_b': nrm(ks[19], (DEPTH, BRANCH_WIDTH), 0.02),
        'w_br': nrm(ks[20], (DEPTH, N_BRANCH, BRANCH_WIDTH, D_MODEL), BRANCH_WIDTH ** -0.5),
        'w_out': nrm(ks[21], (DEPTH, D_MODEL, D_MODEL), D_MODEL ** -0.5),
        'final_norm_g': 1.0 + nrm(ks[22], (D_MODEL,), 0.02),
    }


def reference(x, meta_tokens, norm_g, w_in, hgrn_lb_logits, hgrn_norm_g, mlstm_conv, mlstm_ig_b,
              mlstm_fg_b, mlstm_norm_g, rwkv_mu, rwkv_w0, rwkv_w_up, rwkv_a0, rwkv_a_up, rwkv_k_k,
              rwkv_k_a, rwkv_r_k, rwkv_ln_g, rwkv_ln_b, w_br, w_out, final_norm_g):
    dt = x.dtype
    bsz = x.shape[0]
    meta = jnp.broadcast_to(meta_tokens[None].astype(dt), (bsz, N_META, D_MODEL))
    h = jnp.concatenate([meta, x], axis=1)
    p = jax.nn.softmax(hgrn_lb_logits.astype(jnp.float32), axis=0)
    lower_bounds = jnp.cumsum(p, axis=0) - p[0]
    for l in range(DEPTH):
        xn = rmsnorm(h, norm_g[l])
        (a_q, a_f, a_i, a_z, b_q, b_k, b_v, b_o, b_ig, b_fg, b_z,
         c_r, c_k, c_v, c_wd, c_ad, c_z, g_a, g_b, g_c) = split_cols(xn @ w_in[l])
        y_a = hgrn2_mixer(a_q, a_f, a_i, a_z, lower_bounds[l], hgrn_norm_g[l]).astype(dt)
        y_b = mlstm_mixer(b_q, b_k, b_v, b_o, b_ig, b_fg, b_z, mlstm_conv[l], mlstm_ig_b[l],
                          mlstm_fg_b[l], mlstm_norm_g[l]).astype(dt)
        y_c = rwkv7_mixer(c_r, c_k, c_v, c_wd, c_ad, c_z, rwkv_mu[l], rwkv_w0[l], rwkv_w_up[l],
                          rwkv_a0[l], rwkv_a_up[l], rwkv_k_k[l], rwkv_k_a[l], rwkv_r_k[l],
                          rwkv_ln_g[l], rwkv_ln_b[l]).astype(dt)
        merged = (jax.nn.sigmoid(g_a) * (y_a @ w_br[l, 0])
                  + jax.nn.sigmoid(g_b) * (y_b @ w_br[l, 1])
                  + jax.nn.sigmoid(g_c) * (y_c @ w_br[l, 2]))
        h = h + merged @ w_out[l]
    return rmsnorm(h, final_norm_g)[:, N_META:]
```

```python
import ml_dtypes
from concourse.bass_utils import run_bass_kernel_spmd
import numpy as np
import concourse.bass as bass
import concourse.mybir as mybir
from contextlib import ExitStack

F32 = mybir.dt.float32
BF16 = mybir.dt.bfloat16
AF = mybir.ActivationFunctionType
ALU = mybir.AluOpType

COMPUTE = ("pe", "act", "dve", "pool")
DMAQ = ("sp", "poolq")
NDMASEM = 8


class Sched:
    def __init__(self, nc, strict_same=True):
        self.nc = nc
        self.ops = []
        self.last_w = {}
        self.readers = {}
        self.strict_same = strict_same
        self.pe_last = {}

    def op(self, eng, fn, reads=(), writes=(), dma=False, rowgrp=None):
        i = len(self.ops)
        ex = [k for k in reads if isinstance(k, str) and k[0] == "B" and len(k) <= 2]
        if ex:
            reads = [k for k in reads if k not in ex]
            writes = list(writes) + [k for k in ex if k not in writes]
        deps = set()
        for k in reads:
            w = self.last_w.get(k)
            if w is not None:
                deps.add(w)
        for k in writes:
            w = self.last_w.get(k)
            if w is not None:
                deps.add(w)
            for r in self.readers.get(k, {}).values():
                for x in r:
                    deps.add(x)
        deps.discard(i)
        forced = set()
        if eng == "pe":
            for k in writes:
                if isinstance(k, str) and k[0] == "B" and len(k) <= 2:
                    pl = self.pe_last.get(k)
                    if pl is not None and pl[1] != rowgrp:
                        forced.add(pl[0])
                    self.pe_last[k] = (i, rowgrp)
        deps |= forced
        self.ops.append(dict(eng=eng, fn=fn, deps=deps, dma=dma, forced=forced))
        for k in writes:
            self.last_w[k] = i
            self.readers[k] = {}
        for k in reads:
            d = self.readers.setdefault(k, {})
            if dma:
                d.setdefault(eng + "_dma", []).append(i)
            else:
                d[eng] = [i]
        return i

    def pe(self, fn, reads=(), writes=(), rowgrp=None):
        return self.op("pe", fn, reads, writes, rowgrp=rowgrp)

    def act(self, fn, reads=(), writes=()):
        return self.op("act", fn, reads, writes)

    def dve(self, fn, reads=(), writes=()):
        return self.op("dve", fn, reads, writes)

    def pool(self, fn, reads=(), writes=()):
        return self.op("pool", fn, reads, writes)

    def dma(self, q, fn, reads=(), writes=()):
        return self.op(q, fn, reads, writes, dma=True)

    def emit(self, final_wait_ops=()):
        nc = self.nc
        ops = self.ops
        streams = {"pe": [], "act": [], "dve": [], "pool": [], "sp": []}
        for i, o in enumerate(ops):
            streams[o["eng"]].append(i)
        need = [False] * len(ops)
        for i, o in enumerate(ops):
            for d in o["deps"]:
                od = ops[d]
                if od["dma"]:
                    need[d] = True
                elif od["eng"] != o["eng"]:
                    need[d] = True
                elif o["eng"] != "pe" and self.strict_same:
                    need[d] = True
                elif d in o["forced"]:
                    need[d] = True
        for d in final_wait_ops:
            need[d] = True
        with ExitStack() as es:
            csem = {e: es.enter_context(nc.semaphore("c_" + e)) for e in COMPUTE}
            dsem = {
                q: [es.enter_context(nc.semaphore("d_%s%d" % (q, k))) for k in range(NDMASEM)]
                for q in ("sp", "pool")
            }
            cnt = {e: 0 for e in COMPUTE}
            dcnt = {"sp": 0, "pool": 0}
            sig = [None] * len(ops)
            prev_same_sem = [None] * len(ops)
            for i, o in enumerate(ops):
                if o["dma"]:
                    q = o["eng"]
                    n = dcnt[q]
                    dcnt[q] += 1
                    s = dsem[q][n % NDMASEM]
                    sig[i] = (("d", q, n % NDMASEM), s, 16 * (n // NDMASEM + 1))
                    if n >= NDMASEM:
                        prev_same_sem[i] = (("d", q, n % NDMASEM), s, 16 * (n // NDMASEM))
                elif need[i]:
                    e = o["eng"]
                    cnt[e] += 1
                    sig[i] = (("c", e), csem[e], cnt[e])
            self.stats = dict(n_ops=len(ops), cnt=dict(cnt), dcnt=dict(dcnt),
                              per_eng={e: len(v) for e, v in streams.items()})
            blk = es.enter_context(nc.Block())

            def run_stream(ename, eobj):
                known = {}
                nwait = 0
                for i in streams[ename]:
                    o = ops[i]
                    waits = {}
                    if prev_same_sem[i] is not None:
                        k, s, v = prev_same_sem[i]
                        if known.get(k, 0) < v:
                            waits[k] = (s, v)
                    for d in o["deps"]:
                        od = ops[d]
                        if not od["dma"] and od["eng"] == ename and (ename == "pe" or not self.strict_same) and d not in o["forced"]:
                            continue
                        k, s, v = sig[d]
                        if known.get(k, 0) < v and (k not in waits or waits[k][1] < v):
                            waits[k] = (s, v)
                    for k, (s, v) in waits.items():
                        eobj.wait_ge(s, v)
                        known[k] = v
                        nwait += 1
                    ins = o["fn"](eobj)
                    if sig[i] is not None:
                        ins.then_inc(sig[i][1], 16 if o["dma"] else 1)
                if ename == "sp":
                    for d in final_wait_ops:
                        k, s, v = sig[d]
                        eobj.wait_ge(s, v)
                self.stats["waits_" + ename] = nwait

            blk.sync(lambda e: run_stream("sp", e))
            blk.tensor(lambda e: run_stream("pe", e))
            blk.scalar(lambda e: run_stream("act", e))
            blk.vector(lambda e: run_stream("dve", e))
            blk.gpsimd(lambda e: run_stream("pool", e))


import math, os
RSTOP = int(os.environ.get('RSTOP', '9'))

D = 2048
KC = 16
NMETA = 16
LCH = 64
C0 = math.exp(-0.5)
CT = dict(hq0=0, hq1=1, hf0=2, hf1=3, hi0=4, hi1=5, hz0=6, hz1=7,
          mq=8, mk=9, mv0=10, mv1=11, mo0=12, mo1=13, mz0=14, mz1=15,
          rr0=16, rr1=17, rk0=18, rk1=19, rv0=20, rv1=21, rz0=22, rz1=23, rwa=24)
NCOLA = 25 * 128 + 2
PA = {}
_n = 0
for nm, w in [("lbsel", 4), ("lbl0", 4), ("lbl1", 4), ("hg0", 1), ("hg1", 1), ("cwq", 4), ("cwk", 4), ("mg0", 1), ("mg1", 1),
              ("igb", 1), ("fgb", 1), ("eps", 1), ("lneps", 1), ("zero", 1),
              ("mur0", 1), ("mur1", 1), ("muk0", 1), ("muk1", 1), ("muv0", 1), ("muv1", 1), ("muwa", 1),
              ("w00", 1), ("w01", 1), ("a00", 1), ("a01", 1), ("kk0", 1), ("kk1", 1), ("ka0", 1), ("ka1", 1),
              ("rk0", 1), ("rk1", 1), ("lg0", 1), ("lg1", 1), ("lb0", 1), ("lb1", 1)]:
    PA[nm] = (_n, w)
    _n += w
NPA = _n


def host_consts():
    c = {}
    c["ident"] = np.eye(128, dtype=np.float32)
    c["ones"] = np.ones((128, 128), np.float32)
    bd = np.zeros((128, 128), np.float32)
    bd[:64, :64] = 1
    bd[64:, 64:] = 1
    c["bd"] = bd
    s = np.arange(64)[:, None]
    t = np.arange(64)[None, :]
    mI = (s <= t).astype(np.float32)
    mS = (s < t).astype(np.float32)
    c["maskI"] = mI
    c["mask2"] = np.stack([mS, mI], axis=1)
    c["maskL"] = (t < s).astype(np.float32)
    pm = np.zeros((64, 2, 128), np.float32)
    pm[:, 0, :64] = 1
    pm[:, 1, 64:] = 1
    c["padmask"] = pm
    return c


CONST_SHAPES = dict(ident=[128, 128], ones=[128, 128], bd=[128, 128], maskI=[64, 64], mask2=[64, 2, 64],
                    maskL=[64, 64], padmask=[64, 2, 128])


class Ctx:
    pass


def build_A(nc, S, es, NMS, TB, dram, mixers=("h", "m", "r"), layer=0, first=True, pref=""):
    def sb(name, shape, dt=F32):
        return es.enter_context(nc.sbuf_tensor("s_" + pref + name, shape, dt))

    def ps(name, shape, dt=F32):
        return es.enter_context(nc.psum_tensor("p_" + pref + name, shape, dt))

    TBM = TB
    NCH = TB // LCH
    cst = {}
    for nm, shp in CONST_SHAPES.items():
        dt = F32 if nm in ("maskI", "mask2", "maskL", "padmask") else BF16
        cst[nm] = sb("c_" + nm, shp, dt)
        S.dma("pool", lambda e, nm=nm: e.dma_start(out=cst[nm][:], in_=dram[nm]), writes=["c_" + nm])
    ident64f = sb("ident64f", [64, 64])
    S.dma("sp", lambda e: e.dma_start(out=ident64f[:], in_=dram["ident"][0:64, 0:64]), writes=["ident64f"])
    bdmaskf = sb("bdmaskf", [128, 128])
    S.dma("sp", lambda e: e.dma_start(out=bdmaskf[:], in_=dram["bd"]), writes=["bdmaskf"])
    onesf = sb("onesf", [128, TBM])
    S.pool(lambda e: e.memset(onesf[:], 1.0), writes=["onesf"])
    par = sb("par", [128, NPA])
    S.dma("sp", lambda e: e.dma_start(out=par[:], in_=dram["parA"]), writes=["par"])

    def P(nm, i=0):
        o, w = PA[nm]
        return par[:, o + i:o + i + 1]

    wA = sb("wA", [128, KC, NCOLA], BF16)
    wsrc = dram["wA"].rearrange("(kc p) n -> p kc n", p=128)
    for kc in range(KC):
        S.dma("pool", lambda e, kc=kc: e.dma_start(out=wA[:, kc, :], in_=wsrc[:, kc, :]), writes=[("wA", kc)])
    WAK = [("wA", kc) for kc in range(KC)]
    wud = sb("wud", [128, 256], BF16)
    S.dma("pool", lambda e: e.dma_start(out=wud[:], in_=dram["wud"]), writes=["wud"])

    B0 = ps("B0", [128, 512]); B1 = ps("B1", [128, 512]); Bt = ps("Bt", [128, 1024], BF16)
    B2 = ps("B2", [128, 512]); B3 = ps("B3", [128, 512]); B4 = ps("B4", [128, 512])
    B5 = ps("B5", [128, 512]); B6 = ps("B6", [128, 512])
    ppbuf = [(B0[:, 0:256], "B0"), (B1[:, 0:256], "B1")]
    ppi = [0]

    xn = [sb("xn%d" % i, [128, KC, TBM], BF16) for i in range(2)]
    xsrc = dram["xnT"].rearrange("(kc p) t -> p kc t", p=128)
    ydst = dram["yT"]

    st = Ctx()
    if "h" in mixers:
        st.hS = sb("hS", [128, 2, 128]); st.hSb = sb("hSb", [128, 2, 128], BF16)
        S.pool(lambda e: e.memset(st.hS[:], 0.0), writes=["hS"])
        S.pool(lambda e: e.memset(st.hSb[:], 0.0), writes=["hSb"])
        st.lb = sb("lb", [128, 2]); st.oml = sb("oml", [128, 2]); st.lbm1 = sb("lbm1", [128, 2])
        lbe = sb("lbe", [128, 2, 4]); lbs = sb("lbs", [128, 2]); lbr = sb("lbr", [128, 2])
        o0, _ = PA["lbl0"]
        S.act(lambda e: e.activation(out=lbe[:].rearrange("p a b -> p (a b)"), in_=par[:, o0:o0 + 8], func=AF.Exp), reads=["par"], writes=["lbe"])
        S.dve(lambda e: e.tensor_reduce(out=lbs[:], in_=lbe[:], axis=mybir.AxisListType.X, op=ALU.add), reads=["lbe"], writes=["lbs"])
        S.dve(lambda e: e.reciprocal(out=lbr[:], in_=lbs[:]), reads=["lbs"], writes=["lbr"])
        lbt = sb("lbt", [128, 2]); lbm = sb("lbm", [128, 2, 4])
        osel, _ = PA["lbsel"]
        S.dve(lambda e: e.tensor_tensor(out=lbm[:], in0=lbe[:], in1=par[:, osel:osel + 4].unsqueeze(1).to_broadcast([128, 2, 4]), op=ALU.mult), reads=["lbe", "par"], writes=["lbm"])
        S.dve(lambda e: e.tensor_reduce(out=lbt[:], in_=lbm[:], axis=mybir.AxisListType.X, op=ALU.add), reads=["lbm"], writes=["lbt"])
        S.dve(lambda e: e.tensor_tensor(out=st.lb[:], in0=lbt[:], in1=lbr[:], op=ALU.mult), reads=["lbt", "lbr"], writes=["lb"])
        S.dve(lambda e: e.tensor_scalar(out=st.oml[:], in0=st.lb[:], scalar1=-1.0, scalar2=1.0, op0=ALU.mult, op1=ALU.add), reads=["lb"], writes=["oml"])
        S.dve(lambda e: e.tensor_scalar_add(out=st.lbm1[:], in0=st.lb[:], scalar1=-1.0), reads=["lb"], writes=["lbm1"])
    if "m" in mixers:
        st.mC = sb("mC", [128, 257]); st.mCb = sb("mCb", [128, 257], BF16)
        S.pool(lambda e: e.memset(st.mC[:], 0.0), writes=["mC"])
        st.mxq = sb("mxq", [128, 3 + TBM]); st.mxk = sb("mxk", [128, 3 + TBM])
        S.pool(lambda e: e.memset(st.mxq[:, 0:3], 0.0), writes=["mqx"])
        S.pool(lambda e: e.memset(st.mxk[:, 0:3], 0.0), writes=["mkx"])
        st.mmin = sb("mmin", [1, 1])
        S.pool(lambda e: e.memset(st.mmin[:], 0.0), writes=["mmin"])
        st.mTT = sb("mTT", [64, 3 * 128 + 1], BF16)
        S.pool(lambda e: e.memset(st.mTT[:], 1.0), writes=["mTT"])
        st.onesrow = sb("onesrow", [1, 128])
        S.pool(lambda e: e.memset(st.onesrow[:], 1.0), writes=["onesrow"])
    if "r" in mixers:
        st.rS = sb("rS", [128, 2, 128]); st.rSb = sb("rSb", [128, 2, 128], BF16)
        S.pool(lambda e: e.memset(st.rS[:], 0.0), writes=["rS"])
        S.pool(lambda e: e.memset(st.rSb[:], 0.0), writes=["rSb"])
        st.rraw = {}
        for nm in ("rr0", "rr1", "rk0", "rk1", "rv0", "rv1", "rwa"):
            st.rraw[nm] = sb("raw_" + nm, [128, 1 + TBM])
            S.pool(lambda e, nm=nm: e.memset(st.rraw[nm][:, 0:1], 0.0), writes=["raw_" + nm])

    W = {}

    def wt(name, shape, dt=F32):
        if name not in W:
            W[name] = sb("w_" + name, shape, dt)
        return W[name]

    rr = [0]

    def ev(out, in_, reads, writes):
        rr[0] ^= 1
        if rr[0]:
            return S.dve(lambda e: e.tensor_copy(out=out, in_=in_), reads, writes)
        return S.act(lambda e: e.copy(out=out, in_=in_), reads, writes)

    def inproj(xt, xkey, col0, ncols, TBc):
        buf, key = ppbuf[ppi[0] % 2]
        ppi[0] += 1
        out = buf[0:ncols, 0:TBc]
        for kc in range(KC):
            S.pe(lambda e, kc=kc: e.matmul(out, wA[:, kc, col0:col0 + ncols], xt[:, kc, 0:TBc], start=(kc == 0), stop=(kc == KC - 1)),
                 reads=[xkey, ("wA", kc)], writes=[key])
        return out, key

    def macro(ms, t0, TBc, L):
        nch = TBc // L
        xt = xn[ms % 2]
        xkey = "xn%d" % (ms % 2)
        S.dma("sp", lambda e: e.dma_start(out=xt[:, :, 0:TBc], in_=xsrc[:, :, t0:t0 + TBc]), writes=[xkey])

        def c3(ap):
            return ap.rearrange("p (c j) -> p c j", j=L)

        if "h" in mixers:
            qa = wt("h_qa", [128, 2, TBM]); sig = wt("h_sig", [128, 2, TBM]); vb = wt("h_vb", [128, 2, TBM], BF16)
            sz = wt("h_sz", [128, 2, TBM])
            for hh in range(2):
                p_, k_ = inproj(xt, xkey, CT["hq%d" % hh] * 128, 128, TBc)
                S.act(lambda e, p_=p_, hh=hh: e.activation(out=qa[:, hh, 0:TBc], in_=p_, func=AF.Silu), reads=[k_], writes=[("h_qa", hh)])
                p_, k_ = inproj(xt, xkey, CT["hf%d" % hh] * 128, 128, TBc)
                S.act(lambda e, p_=p_, hh=hh: e.activation(out=sig[:, hh, 0:TBc], in_=p_, func=AF.Sigmoid), reads=[k_], writes=[("h_sig", hh)])
                p_, k_ = inproj(xt, xkey, CT["hi%d" % hh] * 128, 128, TBc)
                S.dve(lambda e, p_=p_, hh=hh: e.tensor_copy(out=vb[:, hh, 0:TBc], in_=p_), reads=[k_], writes=[("h_vb", hh)])
                p_, k_ = inproj(xt, xkey, CT["hz%d" % hh] * 128, 128, TBc)
                S.act(lambda e, p_=p_, hh=hh: e.activation(out=sz[:, hh, 0:TBc], in_=p_, func=AF.Silu), reads=[k_], writes=[("h_sz", hh)])
            kk = wt("h_k", [128, 2, TBM]); ft = wt("h_ft", [128, 2, TBM]); cg = wt("h_cg", [128, 2, TBM])
            d1 = wt("h_d1", [128, 2, TBM]); e1 = wt("h_e1", [128, 2, TBM]); eg = wt("h_eg", [128, 2, TBM])
            Qi = wt("h_Qi", [128, 2, TBM], BF16); Ki = wt("h_Ki", [128, 2, TBM], BF16)
            Qx = wt("h_Qx", [128, 2, TBM], BF16); Kt = wt("h_Kt", [128, 2, TBM], BF16)
            mid = L // 2
            for hh in range(2):
                S.dve(lambda e, hh=hh: e.tensor_scalar(out=kk[:, hh, 0:TBc], in0=sig[:, hh, 0:TBc], scalar1=-1.0, scalar2=st.lbm1[:, hh:hh + 1], op0=ALU.add, op1=ALU.mult),
                      reads=[("h_sig", hh), "lbm1"], writes=[("h_k", hh)])
                S.dve(lambda e, hh=hh: e.tensor_scalar(out=ft[:, hh, 0:TBc], in0=sig[:, hh, 0:TBc], scalar1=st.oml[:, hh:hh + 1], scalar2=st.lb[:, hh:hh + 1], op0=ALU.mult, op1=ALU.add),
                      reads=[("h_sig", hh), "oml", "lb"], writes=[("h_ft", hh)])
                S.dve(lambda e, hh=hh: e.tensor_scalar_max(out=ft[:, hh, 0:TBc], in0=ft[:, hh, 0:TBc], scalar1=1e-12), reads=[("h_ft", hh)], writes=[("h_ft", hh)])
                S.act(lambda e, hh=hh: e.activation(out=ft[:, hh, 0:TBc], in_=ft[:, hh, 0:TBc], func=AF.Ln), reads=[("h_ft", hh)], writes=[("h_ft", hh)])
                for c in range(nch):
                    S.dve(lambda e, hh=hh, c=c: e.tensor_tensor_scan(out=cg[:, hh, c * L:(c + 1) * L], data0=onesf[:, 0:L], data1=ft[:, hh, c * L:(c + 1) * L], initial=0.0, op0=ALU.mult, op1=ALU.add),
                          reads=[("h_ft", hh), "onesf"], writes=[("h_cg", hh)])
                cg3 = c3(cg[:, hh, 0:TBc])
                S.dve(lambda e, hh=hh, cg3=cg3: e.tensor_tensor(out=c3(d1[:, hh, 0:TBc]), in0=cg3, in1=cg3[:, :, mid:mid + 1].to_broadcast([128, nch, L]), op=ALU.subtract),
                      reads=[("h_cg", hh)], writes=[("h_d1", hh)])
                S.act(lambda e, hh=hh: e.activation(out=e1[:, hh, 0:TBc], in_=d1[:, hh, 0:TBc], func=AF.Exp), reads=[("h_d1", hh)], writes=[("h_e1", hh)])
                S.dve(lambda e, hh=hh: e.scalar_tensor_tensor(out=Qi[:, hh, 0:TBc], in0=qa[:, hh, 0:TBc], scalar=128.0 ** -0.5, in1=e1[:, hh, 0:TBc], op0=ALU.mult, op1=ALU.mult),
                      reads=[("h_qa", hh), ("h_e1", hh)], writes=[("h_Qi", hh)])
                S.act(lambda e, hh=hh: e.activation(out=e1[:, hh, 0:TBc], in_=d1[:, hh, 0:TBc], func=AF.Exp, scale=-1.0), reads=[("h_d1", hh), ("h_e1", hh)], writes=[("h_e1", hh)])
                S.dve(lambda e, hh=hh: e.tensor_tensor(out=Ki[:, hh, 0:TBc], in0=kk[:, hh, 0:TBc], in1=e1[:, hh, 0:TBc], op=ALU.mult),
                      reads=[("h_k", hh), ("h_e1", hh)], writes=[("h_Ki", hh)])
                S.act(lambda e, hh=hh: e.activation(out=eg[:, hh, 0:TBc], in_=cg[:, hh, 0:TBc], func=AF.Exp), reads=[("h_cg", hh)], writes=[("h_eg", hh)])
                S.dve(lambda e, hh=hh: e.scalar_tensor_tensor(out=Qx[:, hh, 0:TBc], in0=qa[:, hh, 0:TBc], scalar=128.0 ** -0.5, in1=eg[:, hh, 0:TBc], op0=ALU.mult, op1=ALU.mult),
                      reads=[("h_qa", hh), ("h_eg", hh)], writes=[("h_Qx", hh)])
                S.dve(lambda e, hh=hh, cg3=cg3: e.tensor_tensor(out=c3(d1[:, hh, 0:TBc]), in0=cg3[:, :, L - 1:L].to_broadcast([128, nch, L]), in1=cg3, op=ALU.subtract),
                      reads=[("h_cg", hh), ("h_d1", hh)], writes=[("h_d1", hh)])
                S.act(lambda e, hh=hh: e.activation(out=e1[:, hh, 0:TBc], in_=d1[:, hh, 0:TBc], func=AF.Exp), reads=[("h_d1", hh), ("h_e1", hh)], writes=[("h_e1", hh)])
                S.dve(lambda e, hh=hh: e.tensor_tensor(out=Kt[:, hh, 0:TBc], in0=kk[:, hh, 0:TBc], in1=e1[:, hh, 0:TBc], op=ALU.mult),
                      reads=[("h_k", hh), ("h_e1", hh)], writes=[("h_Kt", hh)])
            oall = wt("h_oall", [128, 2, TBM])
            hT = wt("h_T", [64, 4, 128], BF16); hatt = wt("h_att", [64, 2, 64], BF16)
            for c in range(nch):
                sl = slice(c * L, (c + 1) * L)
                h_trp = Bt[0:L, 0:512].rearrange("p (a b) -> p a b", b=128)
                for hh in range(2):
                    S.pe(lambda e, hh=hh, sl=sl: e.transpose(h_trp[:, hh, :], vb[:, hh, sl], cst["ident"][:]), reads=[("h_vb", hh), "c_ident"], writes=["Bt"])
                    S.pe(lambda e, hh=hh, sl=sl: e.transpose(h_trp[:, 2 + hh, :], Kt[:, hh, sl], cst["ident"][:]), reads=[("h_Kt", hh), "c_ident"], writes=["Bt"])
                ev(hT[0:L], h_trp, reads=["Bt"], writes=["h_T"])
                h_scp = B2[0:L, 0:128].rearrange("p (a b) -> p a b", b=64)
                for hh in range(2):
                    S.pe(lambda e, hh=hh, sl=sl: e.matmul(h_scp[:, hh, 0:L], Ki[:, hh, sl], Qi[:, hh, sl], start=True, stop=True), reads=[("h_Ki", hh), ("h_Qi", hh)], writes=["B2"])
                S.dve(lambda e: e.tensor_tensor(out=hatt[0:L, :, 0:L], in0=h_scp[:, :, 0:L], in1=cst["maskI"][0:L, 0:L].unsqueeze(1).to_broadcast([L, 2, L]), op=ALU.mult),
                      reads=["B2", "c_maskI"], writes=["h_att"])
                h_op = B5[:, 0:128].rearrange("p (a b) -> p a b", b=64)
                for hh in range(2):
                    S.pe(lambda e, hh=hh: e.matmul(h_op[:, hh, 0:L], hT[0:L, hh, :], hatt[0:L, hh, 0:L], start=True, stop=False), reads=["h_T", "h_att"], writes=["B5"])
                    S.pe(lambda e, hh=hh, sl=sl: e.matmul(h_op[:, hh, 0:L], st.hSb[:, hh, :], Qx[:, hh, sl], start=False, stop=True), reads=["hSb", ("h_Qx", hh)], writes=["B5"])
                ev(oall[:, :, sl], h_op[:, :, 0:L], reads=["B5"], writes=["h_oall"])
                h_up = B1[:, 0:256].rearrange("p (a b) -> p a b", b=128)
                for hh in range(2):
                    S.pe(lambda e, hh=hh: e.matmul(h_up[:, hh, :], hT[0:L, 2 + hh, :], hT[0:L, hh, :], start=True, stop=True), reads=["h_T"], writes=["B1"])
                for hh in range(2):
                    S.dve(lambda e, hh=hh, c=c: e.scalar_tensor_tensor(out=st.hS[:, hh, :], in0=st.hS[:, hh, :], scalar=eg[:, hh, c * L + L - 1:c * L + L], in1=h_up[:, hh, :], op0=ALU.mult, op1=ALU.add),
                          reads=["hS", ("h_eg", hh), "B1"], writes=["hS"])
                S.act(lambda e: e.copy(out=st.hSb[:], in_=st.hS[:]), reads=["hS"], writes=["hSb"])
            sq = wt("h_sq", [128, 2, TBM], BF16); rs = wt("h_rs", [128, 2, TBM]); yo = wt("h_yo", [128, 2, TBM], BF16)
            for hh in range(2):
                S.act(lambda e, hh=hh: e.activation(out=sq[:, hh, 0:TBc], in_=oall[:, hh, 0:TBc], func=AF.Square), reads=["h_oall"], writes=[("h_sq", hh)])
                ssp = B3[:, hh * 256:hh * 256 + TBc]
                S.pe(lambda e, hh=hh, ssp=ssp: e.matmul(ssp, cst["ones"][:], sq[:, hh, 0:TBc], start=True, stop=True), reads=[("h_sq", hh), "c_ones"], writes=["B3"])
                S.act(lambda e, hh=hh, ssp=ssp: e.activation(out=rs[:, hh, 0:TBc], in_=ssp, func=AF.Sqrt, bias=P("eps"), scale=1.0 / 128), reads=["B3", "par"], writes=[("h_rs", hh)])
                S.dve(lambda e, hh=hh: e.reciprocal(out=rs[:, hh, 0:TBc], in_=rs[:, hh, 0:TBc]), reads=[("h_rs", hh)], writes=[("h_rs", hh)])
                S.dve(lambda e, hh=hh: e.tensor_tensor(out=rs[:, hh, 0:TBc], in0=rs[:, hh, 0:TBc], in1=oall[:, hh, 0:TBc], op=ALU.mult), reads=[("h_rs", hh), "h_oall"], writes=[("h_rs", hh)])
                S.dve(lambda e, hh=hh: e.scalar_tensor_tensor(out=yo[:, hh, 0:TBc], in0=rs[:, hh, 0:TBc], scalar=P("hg%d" % hh), in1=sz[:, hh, 0:TBc], op0=ALU.mult, op1=ALU.mult),
                      reads=[("h_rs", hh), "par", ("h_sz", hh)], writes=[("h_yo", hh)])
                outs.append(S.dma("pool", lambda e, hh=hh: e.dma_start(out=ydst[hh * 128:(hh + 1) * 128, t0:t0 + TBc], in_=yo[:, hh, 0:TBc]), reads=[("h_yo", hh)]))

        if "m" in mixers:
            for nm, buf in (("mq", st.mxq), ("mk", st.mxk)):
                p_, k_ = inproj(xt, xkey, CT[nm] * 128, 128, TBc)
                ev(buf[:, 3:3 + TBc], p_, reads=[k_], writes=[nm + "x"])
            mvb = wt("m_vb", [128, 2, TBM], BF16); mso = wt("m_so", [128, 2, TBM]); msz = wt("m_sz", [128, 2, TBM])
            for i in range(2):
                p_, k_ = inproj(xt, xkey, CT["mv%d" % i] * 128, 128, TBc)
                S.dve(lambda e, p_=p_, i=i: e.tensor_copy(out=mvb[:, i, 0:TBc], in_=p_), reads=[k_], writes=[("m_vb", i)])
                p_, k_ = inproj(xt, xkey, CT["mo%d" % i] * 128, 128, TBc)
                S.act(lambda e, p_=p_, i=i: e.activation(out=mso[:, i, 0:TBc], in_=p_, func=AF.Sigmoid), reads=[k_], writes=[("m_so", i)])
                p_, k_ = inproj(xt, xkey, CT["mz%d" % i] * 128, 128, TBc)
                S.act(lambda e, p_=p_, i=i: e.activation(out=msz[:, i, 0:TBc], in_=p_, func=AF.Silu), reads=[k_], writes=[("m_sz", i)])
            rows = wt("m_rows", [1, 8, TBM])
            p_, k_ = inproj(xt, xkey, 25 * 128, 1, TBc)
            S.act(lambda e, p_=p_: e.activation(out=rows[:, 0, 0:TBc], in_=p_, func=AF.Identity, bias=par[0:1, PA["igb"][0]:PA["igb"][0] + 1]), reads=[k_, "par"], writes=[("m_rows", 0)])
            p_, k_ = inproj(xt, xkey, 25 * 128 + 1, 1, TBc)
            S.act(lambda e, p_=p_: e.activation(out=rows[:, 1, 0:TBc], in_=p_, func=AF.Sigmoid, bias=par[0:1, PA["fgb"][0]:PA["fgb"][0] + 1]), reads=[k_, "par"], writes=[("m_rows", 1)])
            S.act(lambda e: e.activation(out=rows[:, 1, 0:TBc], in_=rows[:, 1, 0:TBc], func=AF.Ln), reads=[("m_rows", 1)], writes=[("m_rows", 1)])
            S.dve(lambda e: e.tensor_tensor_scan(out=rows[:, 2, 0:TBc], data0=onesf[0:1, 0:TBc], data1=rows[:, 1, 0:TBc], initial=0.0, op0=ALU.mult, op1=ALU.add),
                  reads=[("m_rows", 1), "onesf"], writes=[("m_rows", 2)])
            S.dve(lambda e: e.tensor_tensor(out=rows[:, 3, 0:TBc], in0=rows[:, 0, 0:TBc], in1=rows[:, 2, 0:TBc], op=ALU.subtract), reads=[("m_rows", 0), ("m_rows", 2)], writes=[("m_rows", 3)])
            al0 = wt("m_al0", [1, 8])
            S.dve(lambda e: e.tensor_copy(out=al0[:, 0:1], in_=st.mmin[:]), reads=["mmin"], writes=["m_al0"])
            S.dve(lambda e: e.tensor_tensor_scan(out=rows[:, 4, 0:TBc], data0=onesf[0:1, 0:TBc], data1=rows[:, 3, 0:TBc], initial=st.mmin[:], op0=ALU.mult, op1=ALU.max),
                  reads=[("m_rows", 3), "onesf", "mmin"], writes=[("m_rows", 4)])
            Al3 = rows[:, 4, 0:TBc].rearrange("p (c j) -> p c j", j=L)
            al3 = rows[:, 3, 0:TBc].rearrange("p (c j) -> p c j", j=L)
            if nch > 1:
                S.dve(lambda e: e.tensor_copy(out=al0[:, 1:nch], in_=Al3[:, 0:nch - 1, L - 1]), reads=[("m_rows", 4), "m_al0"], writes=["m_al0"])
            cn_b = Al3[:, :, L - 1:L].to_broadcast([1, nch, L])
            S.dve(lambda e: e.tensor_tensor(out=rows[:, 5, 0:TBc].rearrange("p (c j) -> p c j", j=L), in0=al3, in1=cn_b, op=ALU.subtract), reads=[("m_rows", 3), ("m_rows", 4)], writes=[("m_rows", 5)])
            S.dve(lambda e: e.tensor_tensor(out=rows[:, 6, 0:TBc].rearrange("p (c j) -> p c j", j=L), in0=cn_b, in1=Al3, op=ALU.subtract), reads=[("m_rows", 4)], writes=[("m_rows", 6)])
            car = wt("m_car", [1, 8])
            S.dve(lambda e: e.tensor_tensor(out=car[:, 0:nch], in0=al0[:, 0:nch], in1=Al3[:, :, L - 1], op=ALU.subtract), reads=["m_al0", ("m_rows", 4)], writes=["m_car"])
            S.act(lambda e: e.activation(out=rows[:, 5:7, 0:TBc], in_=rows[:, 5:7, 0:TBc], func=AF.Exp), reads=[("m_rows", 5), ("m_rows", 6)], writes=[("m_rows", 5), ("m_rows", 6)])
            S.act(lambda e: e.activation(out=car[:, 0:nch], in_=car[:, 0:nch], func=AF.Exp), reads=["m_car"], writes=["m_car"])
            S.dve(lambda e: e.tensor_tensor(out=rows[:, 7, 0:TBc], in0=rows[:, 2, 0:TBc], in1=rows[:, 4, 0:TBc], op=ALU.add), reads=[("m_rows", 2), ("m_rows", 4)], writes=[("m_rows", 7)])
            S.dve(lambda e: e.tensor_copy(out=st.mmin[:], in_=rows[:, 7, TBc - 1:TBc]), reads=[("m_rows", 7), "mmin"], writes=["mmin"])
            S.act(lambda e: e.activation(out=rows[:, 7, 0:TBc], in_=rows[:, 7, 0:TBc], func=AF.Exp, scale=-1.0), reads=[("m_rows", 7)], writes=[("m_rows", 7)])
            bws = B3[:, 0:TBc]; bwt = B3[:, 256:256 + TBc]; bcar = B4[:, 0:nch]
            S.pe(lambda e: e.matmul(bws, st.onesrow[:], rows[:, 5, 0:TBc], start=True, stop=True), reads=["onesrow", ("m_rows", 5)], writes=["B3"])
            S.pe(lambda e: e.matmul(bwt, st.onesrow[:], rows[:, 6, 0:TBc], start=True, stop=True), reads=["onesrow", ("m_rows", 6)], writes=["B3"])
            S.pe(lambda e: e.matmul(bcar, st.onesrow[:], car[:, 0:nch], start=True, stop=True), reads=["onesrow", "m_car"], writes=["B4"])
            carb = wt("m_carb", [128, 8])
            S.act(lambda e: e.copy(out=carb[:, 0:nch], in_=bcar), reads=["B4"], writes=["m_carb"])
            qs = wt("m_qs", [128, TBM]); ks = wt("m_ks", [128, TBM]); kp = wt("m_kp", [128, TBM], BF16); qpp = wt("m_qpp", [128, TBM], BF16)
            for nm, buf, dst, cw in (("mq", st.mxq, qs, "cwq"), ("mk", st.mxk, ks, "cwk")):
                S.dve(lambda e, buf=buf, dst=dst, cw=cw: e.tensor_scalar_mul(out=dst[:, 0:TBc], in0=buf[:, 0:TBc], scalar1=P(cw, 0)), reads=[nm + "x", "par"], writes=["m_" + nm + "s"])
                for i in range(1, 4):
                    S.dve(lambda e, buf=buf, dst=dst, cw=cw, i=i: e.scalar_tensor_tensor(out=dst[:, 0:TBc], in0=buf[:, i:i + TBc], scalar=P(cw, i), in1=dst[:, 0:TBc], op0=ALU.mult, op1=ALU.add),
                          reads=[nm + "x", "par", "m_" + nm + "s"], writes=["m_" + nm + "s"])
                S.act(lambda e, dst=dst: e.activation(out=dst[:, 0:TBc], in_=dst[:, 0:TBc], func=AF.Silu), reads=["m_" + nm + "s"], writes=["m_" + nm + "s"])
                S.pool(lambda e, buf=buf: e.tensor_copy(out=buf[:, 0:3], in_=buf[:, TBc:TBc + 3]), reads=[nm + "x"], writes=[nm + "x"])
            S.dve(lambda e: e.tensor_tensor(out=kp[:, 0:TBc], in0=ks[:, 0:TBc], in1=bws, op=ALU.mult), reads=["m_mks", "B3"], writes=["m_kp"])
            S.dve(lambda e: e.scalar_tensor_tensor(out=qpp[:, 0:TBc], in0=qs[:, 0:TBc], scalar=128.0 ** -0.5, in1=bwt, op0=ALU.mult, op1=ALU.mult), reads=["m_mqs", "B3"], writes=["m_qpp"])
            numall = wt("m_num", [128, 2, TBM]); denr = wt("m_den", [1, TBM]); matt = wt("m_att", [64, 64], BF16)
            for c in range(nch):
                sl = slice(c * L, (c + 1) * L)
                m_trp = Bt[0:L, 0:384].rearrange("p (a b) -> p a b", b=128)
                S.pe(lambda e, sl=sl: e.transpose(m_trp[:, 0, :], kp[:, sl], cst["ident"][:]), reads=["m_kp", "c_ident"], writes=["Bt"])
                for i in range(2):
                    S.pe(lambda e, sl=sl, i=i: e.transpose(m_trp[:, 1 + i, :], mvb[:, i, sl], cst["ident"][:]), reads=[("m_vb", i), "c_ident"], writes=["Bt"])
                ev(st.mTT[0:L, 0:384], Bt[0:L, 0:384], reads=["Bt"], writes=["mTT"])
                S.dve(lambda e, c=c: e.tensor_scalar_mul(out=st.mC[:], in0=st.mC[:], scalar1=carb[:, c:c + 1]), reads=["mC", "m_carb"], writes=["mC"])
                S.act(lambda e: e.copy(out=st.mCb[:], in_=st.mC[:]), reads=["mC"], writes=["mCb"])
                m_scp = B2[0:L, 128:128 + L]
                S.pe(lambda e, sl=sl: e.matmul(m_scp, kp[:, sl], qpp[:, sl], start=True, stop=True), reads=["m_kp", "m_qpp"], writes=["B2"])
                S.dve(lambda e: e.tensor_tensor(out=matt[0:L, 0:L], in0=m_scp, in1=cst["maskI"][0:L, 0:L], op=ALU.mult), reads=["B2", "c_maskI"], writes=["m_att"])
                m_np = B5[:, 128:256].rearrange("p (a b) -> p a b", b=64)
                for i in range(2):
                    S.pe(lambda e, i=i: e.matmul(m_np[:, i, 0:L], st.mTT[0:L, 128 + i * 128:256 + i * 128], matt[0:L, 0:L], start=True, stop=False), reads=["mTT", "m_att"], writes=["B5"])
                    S.pe(lambda e, i=i, sl=sl: e.matmul(m_np[:, i, 0:L], st.mCb[:, i * 128:(i + 1) * 128], qpp[:, sl], start=False, stop=True), reads=["mCb", "m_qpp"], writes=["B5"])
                m_dp = B5[0:1, 256:256 + L]
                S.pe(lambda e: e.matmul(m_dp, st.mTT[0:L, 384:385], matt[0:L, 0:L], start=True, stop=False), reads=["mTT", "m_att"], writes=["B5"])
                S.pe(lambda e, sl=sl: e.matmul(m_dp, st.mCb[:, 256:257], qpp[:, sl], start=False, stop=True), reads=["mCb", "m_qpp"], writes=["B5"])
                ev(numall[:, :, sl], m_np[:, :, 0:L], reads=["B5"], writes=["m_num"])
                ev(denr[:, sl], m_dp, reads=["B5"], writes=["m_den"])
                cup = B1[:, 256:512]
                nup = B5[:, 448:449]
                S.pe(lambda e: e.matmul(cup, st.mTT[0:L, 0:128], st.mTT[0:L, 128:384], start=True, stop=True), reads=["mTT"], writes=["B1"])
                S.pe(lambda e: e.matmul(nup, st.mTT[0:L, 0:128], st.mTT[0:L, 384:385], start=True, stop=True), reads=["mTT"], writes=["B5"])
                S.dve(lambda e: e.tensor_tensor(out=st.mC[:, 0:256], in0=st.mC[:, 0:256], in1=cup, op=ALU.add), reads=["mC", "B1"], writes=["mC"])
                S.dve(lambda e: e.tensor_tensor(out=st.mC[:, 256:257], in0=st.mC[:, 256:257], in1=nup, op=ALU.add), reads=["mC", "B5"], writes=["mC"])
            S.act(lambda e: e.activation(out=denr[:, 0:TBc], in_=denr[:, 0:TBc], func=AF.Abs), reads=["m_den"], writes=["m_den"])
            S.dve(lambda e: e.tensor_tensor(out=denr[:, 0:TBc], in0=denr[:, 0:TBc], in1=rows[:, 7, 0:TBc], op=ALU.max), reads=["m_den", ("m_rows", 7)], writes=["m_den"])
            S.dve(lambda e: e.reciprocal(out=denr[:, 0:TBc], in_=denr[:, 0:TBc]), reads=["m_den"], writes=["m_den"])
            bdd = B4[:, 256:256 + TBc]
            S.pe(lambda e: e.matmul(bdd, st.onesrow[:], denr[:, 0:TBc], start=True, stop=True), reads=["onesrow", "m_den"], writes=["B4"])
            mh = wt("m_h", [128, 2, TBM]); mhb = wt("m_hb", [128, 2, TBM], BF16); msq = wt("m_sq", [128, 2, TBM], BF16)
            S.dve(lambda e: e.tensor_tensor(out=mh[:, :, 0:TBc], in0=numall[:, :, 0:TBc], in1=bdd.unsqueeze(1).to_broadcast([128, 2, TBc]), op=ALU.mult), reads=["m_num", "B4"], writes=["m_h"])
            S.act(lambda e: e.copy(out=mhb[:, :, 0:TBc], in_=mh[:, :, 0:TBc]), reads=["m_h"], writes=["m_hb"])
            S.act(lambda e: e.activation(out=msq[:, :, 0:TBc], in_=mh[:, :, 0:TBc], func=AF.Square), reads=["m_h"], writes=["m_sq"])
            sm = B3[:, 0:TBc]; sm2 = B3[:, 256:256 + TBc]
            for i in range(2):
                S.pe(lambda e, i=i: e.matmul(sm, cst["ones"][:], mhb[:, i, 0:TBc], start=(i == 0), stop=(i == 1)), reads=["m_hb", "c_ones"], writes=["B3"])
            for i in range(2):
                S.pe(lambda e, i=i: e.matmul(sm2, cst["ones"][:], msq[:, i, 0:TBc], start=(i == 0), stop=(i == 1)), reads=["m_sq", "c_ones"], writes=["B3"])
            mean = wt("m_mean", [128, TBM]); var = wt("m_var", [128, TBM]); myo = wt("m_yo", [128, 2, TBM], BF16)
            S.act(lambda e: e.mul(out=mean[:, 0:TBc], in_=sm, mul=1.0 / 256), reads=["B3"], writes=["m_mean"])
            S.dve(lambda e: e.tensor_tensor(out=var[:, 0:TBc], in0=mean[:, 0:TBc], in1=mean[:, 0:TBc], op=ALU.mult), reads=["m_mean"], writes=["m_var"])
            S.dve(lambda e: e.scalar_tensor_tensor(out=var[:, 0:TBc], in0=sm2, scalar=1.0 / 256, in1=var[:, 0:TBc], op0=ALU.mult, op1=ALU.subtract), reads=["B3", "m_var"], writes=["m_var"])
            S.act(lambda e: e.activation(out=var[:, 0:TBc], in_=var[:, 0:TBc], func=AF.Sqrt, bias=P("eps")), reads=["m_var", "par"], writes=["m_var"])
            S.dve(lambda e: e.reciprocal(out=var[:, 0:TBc], in_=var[:, 0:TBc]), reads=["m_var"], writes=["m_var"])
            S.dve(lambda e: e.tensor_tensor(out=mh[:, :, 0:TBc], in0=mh[:, :, 0:TBc], in1=mean[:, 0:TBc].unsqueeze(1).to_broadcast([128, 2, TBc]), op=ALU.subtract), reads=["m_h", "m_mean"], writes=["m_h"])
            S.dve(lambda e: e.tensor_tensor(out=mh[:, :, 0:TBc], in0=mh[:, :, 0:TBc], in1=var[:, 0:TBc].unsqueeze(1).to_broadcast([128, 2, TBc]), op=ALU.mult), reads=["m_h", "m_var"], writes=["m_h"])
            for i in range(2):
                S.dve(lambda e, i=i: e.scalar_tensor_tensor(out=mh[:, i, 0:TBc], in0=mh[:, i, 0:TBc], scalar=P("mg%d" % i), in1=mso[:, i, 0:TBc], op0=ALU.mult, op1=ALU.mult), reads=["m_h", "par", ("m_so", i)], writes=["m_h"])
            S.dve(lambda e: e.tensor_tensor(out=myo[:, :, 0:TBc], in0=mh[:, :, 0:TBc], in1=msz[:, :, 0:TBc], op=ALU.mult), reads=["m_h", ("m_sz", 0), ("m_sz", 1)], writes=["m_yo"])
            for i in range(2):
                outs.append(S.dma("pool", lambda e, i=i: e.dma_start(out=ydst[256 + i * 128:256 + (i + 1) * 128, t0:t0 + TBc], in_=myo[:, i, 0:TBc]), reads=["m_yo"]))

        if "r" in mixers:
            for nm in ("rr0", "rr1", "rk0", "rk1", "rv0", "rv1", "rwa"):
                p_, k_ = inproj(xt, xkey, CT[nm] * 128, 128, TBc)
                ev(st.rraw[nm][:, 1:1 + TBc], p_, reads=[k_], writes=["raw_" + nm])
            rsz = wt("r_sz", [128, 2, TBM])
            for p in range(2):
                p_, k_ = inproj(xt, xkey, CT["rz%d" % p] * 128, 128, TBc)
                S.act(lambda e, p_=p_, p=p: e.activation(out=rsz[:, p, 0:TBc], in_=p_, func=AF.Silu), reads=[k_], writes=[("r_sz", p)])
            if RSTOP <= -3:
                return
            lm = {}
            for nm, mu in (("rr0", "mur0"), ("rr1", "mur1"), ("rk0", "muk0"), ("rk1", "muk1"), ("rv0", "muv0"), ("rv1", "muv1"), ("rwa", "muwa")):
                raw = st.rraw[nm]
                m = wt("r_m_" + nm, [128, TBM])
                lm[nm] = m
                S.dve(lambda e, raw=raw, m=m: e.tensor_tensor(out=m[:, 0:TBc], in0=raw[:, 0:TBc], in1=raw[:, 1:1 + TBc], op=ALU.subtract), reads=["raw_" + nm], writes=["r_m_" + nm])
                S.dve(lambda e, raw=raw, m=m, mu=mu: e.scalar_tensor_tensor(out=m[:, 0:TBc], in0=m[:, 0:TBc], scalar=P(mu), in1=raw[:, 1:1 + TBc], op0=ALU.mult, op1=ALU.add),
                      reads=["raw_" + nm, "r_m_" + nm, "par"], writes=["r_m_" + nm])
                S.pool(lambda e, raw=raw: e.tensor_copy(out=raw[:, 0:1], in_=raw[:, TBc:TBc + 1]), reads=["raw_" + nm], writes=["raw_" + nm])
            wab = wt("r_wab", [128, TBM], BF16)
            S.act(lambda e: e.activation(out=wab[0:64, 0:TBc], in_=lm["rwa"][0:64, 0:TBc], func=AF.Tanh), reads=["r_m_rwa"], writes=["r_wab0"])
            S.act(lambda e: e.copy(out=wab[64:128, 0:TBc], in_=lm["rwa"][64:128, 0:TBc]), reads=["r_m_rwa"], writes=["r_wab1"])
            if RSTOP <= -2:
                return
            sgw = wt("r_sgw", [128, 2, TBM]); av = wt("r_a", [128, 2, TBM]); kkn = wt("r_kkn", [128, 2, TBM]); kf = wt("r_kf", [128, 2, TBM])
            bv = wt("r_bv", [128, 2, TBM]); cs = wt("r_cs", [128, 2, TBM]); tmp = wt("r_tmp", [128, 2, TBM]); tmpb = wt("r_tmpb", [128, 2, TBM], BF16)
            ecw = wt("r_ecw", [128, 2, TBM]); ex = wt("r_ex", [128, 2, TBM])
            AR = wt("r_AR", [128, 2, max(NCH, 1), 2, 64], BF16)
            Bd = wt("r_Bd", [128, 2, TBM], BF16); Kd = wt("r_Kd", [128, 2, TBM], BF16)
            Btl = wt("r_Btl", [128, 2, TBM], BF16); Ktl = wt("r_Ktl", [128, 2, TBM], BF16)
            rvb = wt("r_vb", [128, 2, TBM], BF16); bon = wt("r_bon", [128, 2, TBM])
            for p in range(2):
                rm = lm["rr%d" % p]; km = lm["rk%d" % p]; vm = lm["rv%d" % p]
                RK = ["r_m_rr%d" % p, "r_m_rk%d" % p, "r_m_rv%d" % p]
                wp = B3[:, 0:TBc]; ap_ = B3[:, 256:256 + TBc]
                S.pe(lambda e, p=p, wp=wp: e.matmul(wp, wud[0:64, p * 128:(p + 1) * 128], wab[0:64, 0:TBc], start=True, stop=True), reads=["wud", "r_wab0"], writes=["B3"])
                S.pe(lambda e, p=p, ap_=ap_: e.matmul(ap_, wud[64:128, p * 128:(p + 1) * 128], wab[64:128, 0:TBc], start=True, stop=True), reads=["wud", "r_wab1"], writes=["B3"], rowgrp=1)
                S.act(lambda e, p=p, wp=wp: e.activation(out=sgw[:, p, 0:TBc], in_=wp, func=AF.Sigmoid, bias=P("w0%d" % p)), reads=["B3", "par"], writes=[("r_sgw", p)])
                S.act(lambda e, p=p, ap_=ap_: e.activation(out=av[:, p, 0:TBc], in_=ap_, func=AF.Sigmoid, bias=P("a0%d" % p)), reads=["B3", "par"], writes=[("r_a", p)])
                S.dve(lambda e, p=p, km=km: e.tensor_scalar_mul(out=kkn[:, p, 0:TBc], in0=km[:, 0:TBc], scalar1=P("kk%d" % p)), reads=[RK[1], "par"], writes=[("r_kkn", p)])
                S.act(lambda e, p=p: e.activation(out=tmpb[:, p, 0:TBc], in_=kkn[:, p, 0:TBc], func=AF.Square), reads=[("r_kkn", p)], writes=[("r_tmpb", p)])
                ssk = B4[:, 0:TBc]
                S.pe(lambda e, p=p, ssk=ssk: e.matmul(ssk, cst["bd"][:], tmpb[:, p, 0:TBc], start=True, stop=True), reads=[("r_tmpb", p), "c_bd"], writes=["B4"])
                S.act(lambda e, p=p, ssk=ssk: e.activation(out=tmp[:, p, 0:TBc], in_=ssk, func=AF.Sqrt), reads=["B4"], writes=[("r_tmp", p)])
                S.dve(lambda e, p=p: e.tensor_scalar_max(out=tmp[:, p, 0:TBc], in0=tmp[:, p, 0:TBc], scalar1=1e-12), reads=[("r_tmp", p)], writes=[("r_tmp", p)])
                S.dve(lambda e, p=p: e.reciprocal(out=tmp[:, p, 0:TBc], in_=tmp[:, p, 0:TBc]), reads=[("r_tmp", p)], writes=[("r_tmp", p)])
                S.dve(lambda e, p=p: e.tensor_tensor(out=kkn[:, p, 0:TBc], in0=kkn[:, p, 0:TBc], in1=tmp[:, p, 0:TBc], op=ALU.mult), reads=[("r_kkn", p), ("r_tmp", p)], writes=[("r_kkn", p)])
                S.dve(lambda e, p=p: e.tensor_scalar(out=kf[:, p, 0:TBc], in0=av[:, p, 0:TBc], scalar1=-1.0, scalar2=P("ka%d" % p), op0=ALU.add, op1=ALU.mult), reads=[("r_a", p), "par"], writes=[("r_kf", p)])
                S.dve(lambda e, p=p, km=km: e.scalar_tensor_tensor(out=kf[:, p, 0:TBc], in0=kf[:, p, 0:TBc], scalar=1.0, in1=km[:, 0:TBc], op0=ALU.add, op1=ALU.mult), reads=[("r_kf", p), RK[1]], writes=[("r_kf", p)])
                S.dve(lambda e, p=p: e.tensor_tensor(out=bv[:, p, 0:TBc], in0=kkn[:, p, 0:TBc], in1=av[:, p, 0:TBc], op=ALU.mult), reads=[("r_kkn", p), ("r_a", p)], writes=[("r_bv", p)])
                for c in range(nch):
                    S.dve(lambda e, p=p, c=c: e.tensor_tensor_scan(out=cs[:, p, c * L:(c + 1) * L], data0=onesf[:, 0:L], data1=sgw[:, p, c * L:(c + 1) * L], initial=0.0, op0=ALU.mult, op1=ALU.add),
                          reads=[("r_sgw", p), "onesf"], writes=[("r_cs", p)])
                cs3 = c3(cs[:, p, 0:TBc])
                ARp = AR[:, p, 0:nch, :, 0:L]
                S.act(lambda e, p=p: e.activation(out=ecw[:, p, 0:TBc], in_=cs[:, p, 0:TBc], func=AF.Exp, scale=-C0), reads=[("r_cs", p)], writes=[("r_ecw", p)])
                S.dve(lambda e, p=p, rm=rm, ARp=ARp: e.tensor_tensor(out=ARp[:, :, 1, :], in0=c3(rm[:, 0:TBc]), in1=c3(ecw[:, p, 0:TBc]), op=ALU.mult), reads=[RK[0], ("r_ecw", p)], writes=[("r_AR", p)])
                S.act(lambda e, p=p: e.activation(out=ex[:, p, 0:TBc], in_=cs[:, p, 0:TBc], func=AF.Exp, scale=C0), reads=[("r_cs", p)], writes=[("r_ex", p)])
                S.dve(lambda e, p=p: e.tensor_tensor(out=Kd[:, p, 0:TBc], in0=kf[:, p, 0:TBc], in1=ex[:, p, 0:TBc], op=ALU.mult), reads=[("r_kf", p), ("r_ex", p)], writes=[("r_Kd", p)])
                S.dve(lambda e, p=p: e.tensor_tensor(out=Bd[:, p, 0:TBc], in0=bv[:, p, 0:TBc], in1=ex[:, p, 0:TBc], op=ALU.mult), reads=[("r_bv", p), ("r_ex", p)], writes=[("r_Bd", p)])
                S.dve(lambda e, p=p: e.tensor_tensor(out=tmp[:, p, 0:TBc], in0=cs[:, p, 0:TBc], in1=sgw[:, p, 0:TBc], op=ALU.subtract), reads=[("r_cs", p), ("r_sgw", p), ("r_tmp", p)], writes=[("r_tmp", p)])
                S.act(lambda e, p=p: e.activation(out=ex[:, p, 0:TBc], in_=tmp[:, p, 0:TBc], func=AF.Exp, scale=-C0), reads=[("r_tmp", p), ("r_ex", p)], writes=[("r_ex", p)])
                S.dve(lambda e, p=p, ARp=ARp: e.scalar_tensor_tensor(out=ARp[:, :, 0, :], in0=c3(kkn[:, p, 0:TBc]), scalar=-1.0, in1=c3(ex[:, p, 0:TBc]), op0=ALU.mult, op1=ALU.mult), reads=[("r_kkn", p), ("r_ex", p)], writes=[("r_AR", p)])
                S.dve(lambda e, p=p, cs3=cs3: e.tensor_tensor(out=c3(tmp[:, p, 0:TBc]), in0=cs3[:, :, L - 1:L].to_broadcast([128, nch, L]), in1=cs3, op=ALU.subtract), reads=[("r_cs", p), ("r_tmp", p)], writes=[("r_tmp", p)])
                S.act(lambda e, p=p: e.activation(out=ex[:, p, 0:TBc], in_=tmp[:, p, 0:TBc], func=AF.Exp, scale=-C0), reads=[("r_tmp", p), ("r_ex", p)], writes=[("r_ex", p)])
                S.dve(lambda e, p=p: e.tensor_tensor(out=Btl[:, p, 0:TBc], in0=bv[:, p, 0:TBc], in1=ex[:, p, 0:TBc], op=ALU.mult), reads=[("r_bv", p), ("r_ex", p)], writes=[("r_Btl", p)])
                S.dve(lambda e, p=p: e.tensor_tensor(out=Ktl[:, p, 0:TBc], in0=kf[:, p, 0:TBc], in1=ex[:, p, 0:TBc], op=ALU.mult), reads=[("r_kf", p), ("r_ex", p)], writes=[("r_Ktl", p)])
                S.act(lambda e, p=p, vm=vm: e.copy(out=rvb[:, p, 0:TBc], in_=vm[:, 0:TBc]), reads=[RK[2]], writes=[("r_vb", p)])
                S.dve(lambda e, p=p, rm=rm: e.scalar_tensor_tensor(out=tmpb[:, p, 0:TBc], in0=rm[:, 0:TBc], scalar=P("rk%d" % p), in1=kf[:, p, 0:TBc], op0=ALU.mult, op1=ALU.mult), reads=[RK[0], "par", ("r_kf", p), ("r_tmpb", p)], writes=[("r_tmpb", p)])
                bsp = B4[:, 256:256 + TBc]
                S.pe(lambda e, p=p, bsp=bsp: e.matmul(bsp, cst["bd"][:], tmpb[:, p, 0:TBc], start=True, stop=True), reads=[("r_tmpb", p), "c_bd"], writes=["B4"])
                S.dve(lambda e, p=p, vm=vm, bsp=bsp: e.tensor_tensor(out=bon[:, p, 0:TBc], in0=vm[:, 0:TBc], in1=bsp, op=ALU.mult), reads=[RK[2], "B4"], writes=[("r_bon", p)])
            yall = wt("r_yall", [128, 2, TBM])
            T3 = wt("r_T3", [64, 2, 3, 128], BF16)
            scBm = wt("r_scBm", [64, 4, 2, 64], BF16); scKm = wt("r_scKm", [64, 4, 2, 64], BF16); labm = wt("r_labm", [64, 4, 64], BF16)
            TTf = wt("r_TTf", [64, 4, 64]); TTb = wt("r_TTb", [64, 4, 64], BF16)
            X = [wt("r_X%d" % i, [64, 4, 2, 64], BF16) for i in range(2)]
            Q = [wt("r_Q%d" % i, [64, 4, 64], BF16) for i in range(2)]
            rhsb = wt("r_rhsb", [64, 4, 64], BF16); ub = wt("r_ub", [64, 4, 64], BF16)
            upad = wt("r_upad", [64, 2, 2, 128], BF16); vpad = wt("r_vpad", [64, 2, 2, 128], BF16)
            tmpS = wt("r_tmpS", [128, 2, 128])
            nlev = int(round(math.log2(L)))
            for c in range(nch if RSTOP >= 1 else 0):
                sl = slice(c * L, (c + 1) * L)
                trp = Bt[0:L, 0:768].rearrange("p (a b c) -> p a b c", a=2, b=3)
                for p in range(2):
                    S.pe(lambda e, p=p, sl=sl: e.transpose(trp[:, p, 0, :], rvb[:, p, sl], cst["ident"][:]), reads=[("r_vb", p), "c_ident"], writes=["Bt"])
                    S.pe(lambda e, p=p, sl=sl: e.transpose(trp[:, p, 1, :], Btl[:, p, sl], cst["ident"][:]), reads=[("r_Btl", p), "c_ident"], writes=["Bt"])
                    S.pe(lambda e, p=p, sl=sl: e.transpose(trp[:, p, 2, :], Ktl[:, p, sl], cst["ident"][:]), reads=[("r_Ktl", p), "c_ident"], writes=["Bt"])
                ev(T3[0:L], trp, reads=["Bt"], writes=["r_T3"])
                S.dve(lambda e: e.tensor_tensor(out=vpad[0:L], in0=T3[0:L, :, 0, :].unsqueeze(2).to_broadcast([L, 2, 2, 128]), in1=cst["padmask"][0:L].unsqueeze(1).to_broadcast([L, 2, 2, 128]), op=ALU.mult),
                      reads=["r_T3", "c_padmask"], writes=["r_vpad"])
                scB = B3[0:L, :].rearrange("p (h a j) -> p h a j", h=4, a=2)
                scK = B4[0:L, :].rearrange("p (h a j) -> p h a j", h=4, a=2)
                lab = B2[0:L, 256:512].rearrange("p (h j) -> p h j", h=4)
                for h in (0, 2, 1, 3):
                    p, hh = divmod(h, 2)
                    b0 = hh * 64
                    rg = 1 if hh == 1 else None
                    arh = AR[b0:b0 + 64, p, c, :, 0:L]
                    S.pe(lambda e, h=h, p=p, b0=b0, arh=arh, sl=sl: e.matmul(scB[:, h, :, 0:L], Bd[b0:b0 + 64, p, sl], arh, start=True, stop=True), reads=[("r_Bd", p), ("r_AR", p)], writes=["B3"], rowgrp=rg)
                    S.pe(lambda e, h=h, p=p, b0=b0, arh=arh, sl=sl: e.matmul(scK[:, h, :, 0:L], Kd[b0:b0 + 64, p, sl], arh, start=True, stop=True), reads=[("r_Kd", p), ("r_AR", p)], writes=["B4"], rowgrp=rg)
                    S.pe(lambda e, h=h, p=p, b0=b0, arh=arh, sl=sl: e.matmul(lab[:, h, 0:L], arh[:, 0, :], Bd[b0:b0 + 64, p, sl], start=True, stop=True), reads=[("r_Bd", p), ("r_AR", p)], writes=["B2"], rowgrp=rg)
                m2 = cst["mask2"][0:L, :, 0:L].unsqueeze(1).to_broadcast([L, 4, 2, L])
                S.dve(lambda e: e.tensor_tensor(out=scBm[0:L, :, :, 0:L], in0=scB[:, :, :, 0:L], in1=m2, op=ALU.mult), reads=["B3", "c_mask2"], writes=["r_scBm"])
                S.dve(lambda e: e.tensor_tensor(out=scKm[0:L, :, :, 0:L], in0=scK[:, :, :, 0:L], in1=m2, op=ALU.mult), reads=["B4", "c_mask2"], writes=["r_scKm"])
                S.dve(lambda e: e.tensor_tensor(out=labm[0:L, :, 0:L], in0=lab[:, :, 0:L], in1=cst["maskL"][0:L, 0:L].unsqueeze(1).to_broadcast([L, 4, L]), op=ALU.mult), reads=["B2", "c_maskL"], writes=["r_labm"])
                if RSTOP < 2:
                    continue
                S.dve(lambda e: e.tensor_tensor(out=TTf[0:L, :, 0:L], in0=scBm[0:L, :, 0, 0:L], in1=ident64f[0:L, 0:L].unsqueeze(1).to_broadcast([L, 4, L]), op=ALU.add), reads=["r_scBm", "ident64f"], writes=["r_TTf"])
                S.act(lambda e: e.copy(out=X[0][0:L, :, 0, 0:L], in_=TTf[0:L, :, 0:L]), reads=["r_TTf"], writes=["r_X0a"])
                PPp = B6[0:L, :].rearrange("p (h a j) -> p h a j", h=4, a=2)
                QQp = B1[0:L, 0:256].rearrange("p (h j) -> p h j", h=4)
                for h in range(4):
                    S.pe(lambda e, h=h: e.matmul(PPp[:, h, 1, 0:L], labm[0:L, h, 0:L], scBm[0:L, h, 0, 0:L], start=True, stop=True), reads=["r_labm", "r_scBm"], writes=["B6"])
                    S.pe(lambda e, h=h: e.matmul(QQp[:, h, 0:L], scBm[0:L, h, 0, 0:L], labm[0:L, h, 0:L], start=True, stop=True), reads=["r_labm", "r_scBm"], writes=["B1"])
                S.dve(lambda e: e.tensor_copy(out=X[0][0:L, :, 1, 0:L], in_=PPp[:, :, 1, 0:L]), reads=["B6"], writes=["r_X0b"])
                S.act(lambda e: e.copy(out=Q[0][0:L, :, 0:L], in_=QQp[:, :, 0:L]), reads=["B1"], writes=["r_Q0"])
                cur = 0
                for lev in range(1, nlev):
                    last = lev == nlev - 1
                    Xc, Qc = X[cur], Q[cur]
                    xk = ["r_X%da" % cur, "r_X%db" % cur]
                    qk = "r_Q%d" % cur
                    nxt = cur ^ 1
                    for h in range(4):
                        if last:
                            S.pe(lambda e, h=h, Xc=Xc, Qc=Qc: e.matmul(PPp[:, h, 0, 0:L], Qc[0:L, h, 0:L], Xc[0:L, h, 0, 0:L], start=True, stop=True), reads=[qk] + xk, writes=["B6"])
                        else:
                            S.pe(lambda e, h=h, Xc=Xc, Qc=Qc: e.matmul(PPp[:, h, :, 0:L], Qc[0:L, h, 0:L], Xc[0:L, h, :, 0:L], start=True, stop=True), reads=[qk] + xk, writes=["B6"])
                            S.pe(lambda e, h=h, Xc=Xc, Qc=Qc: e.matmul(QQp[:, h, 0:L], Xc[0:L, h, 1, 0:L], Qc[0:L, h, 0:L], start=True, stop=True), reads=[qk] + xk, writes=["B1"])
                    S.dve(lambda e: e.tensor_tensor(out=TTf[0:L, :, 0:L], in0=TTf[0:L, :, 0:L], in1=PPp[:, :, 0, 0:L], op=ALU.add), reads=["r_TTf", "B6"], writes=["r_TTf"])
                    if last:
                        S.act(lambda e: e.copy(out=TTb[0:L, :, 0:L], in_=TTf[0:L, :, 0:L]), reads=["r_TTf"], writes=["r_TTb"])
                    else:
                        S.act(lambda e, nxt=nxt: e.copy(out=X[nxt][0:L, :, 0, 0:L], in_=TTf[0:L, :, 0:L]), reads=["r_TTf"], writes=["r_X%da" % nxt])
                        S.dve(lambda e, nxt=nxt: e.tensor_copy(out=X[nxt][0:L, :, 1, 0:L], in_=PPp[:, :, 1, 0:L]), reads=["B6"], writes=["r_X%db" % nxt])
                        S.act(lambda e, nxt=nxt: e.copy(out=Q[nxt][0:L, :, 0:L], in_=QQp[:, :, 0:L]), reads=["B1"], writes=["r_Q%d" % nxt])
                    cur = nxt
                if RSTOP < 3:
                    continue
                rhp = B6[0:L, 0:256].rearrange("p (h j) -> p h j", h=4)
                for h in range(4):
                    p, hh = divmod(h, 2)
                    S.pe(lambda e, h=h, p=p, hh=hh, c=c: e.matmul(rhp[:, h, :], AR[:, p, c, 0, 0:L], st.rSb[:, p, hh * 64:(hh + 1) * 64], start=True, stop=False), reads=[("r_AR", p), "rSb"], writes=["B6"])
                    S.pe(lambda e, h=h, p=p, hh=hh: e.matmul(rhp[:, h, :], scKm[0:L, h, 0, 0:L], T3[0:L, p, 0, hh * 64:(hh + 1) * 64], start=False, stop=True), reads=["r_scKm", "r_T3"], writes=["B6"])
                S.act(lambda e: e.copy(out=rhsb[0:L], in_=rhp), reads=["B6"], writes=["r_rhsb"])
                up_ = B6[0:L, 256:512].rearrange("p (h j) -> p h j", h=4)
                for h in range(4):
                    S.pe(lambda e, h=h: e.matmul(up_[:, h, :], TTb[0:L, h, 0:L], rhsb[0:L, h, :], start=True, stop=True), reads=["r_TTb", "r_rhsb"], writes=["B6"])
                S.act(lambda e: e.copy(out=ub[0:L], in_=up_), reads=["B6"], writes=["r_ub"])
                S.dve(lambda e: e.tensor_tensor(out=upad[0:L], in0=B6[0:L, 256:512].rearrange("p (a k) -> p a k", a=2).unsqueeze(2).to_broadcast([L, 2, 2, 128]), in1=cst["padmask"][0:L].unsqueeze(1).to_broadcast([L, 2, 2, 128]), op=ALU.mult),
                      reads=["B6", "c_padmask"], writes=["r_upad"])
                yp = B5[:, 320:448].rearrange("p (a j) -> p a j", a=2)
                for p in range(2):
                    S.pe(lambda e, p=p, c=c: e.matmul(yp[:, p, 0:L], st.rSb[:, p, :], AR[:, p, c, 1, 0:L], start=True, stop=False), reads=["rSb", ("r_AR", p)], writes=["B5"])
                    for hh in range(2):
                        h = p * 2 + hh
                        S.pe(lambda e, p=p, hh=hh, h=h: e.matmul(yp[:, p, 0:L], upad[0:L, p, hh, :], scBm[0:L, h, 1, 0:L], start=False, stop=False), reads=["r_upad", "r_scBm"], writes=["B5"])
                        S.pe(lambda e, p=p, hh=hh, h=h: e.matmul(yp[:, p, 0:L], vpad[0:L, p, hh, :], scKm[0:L, h, 1, 0:L], start=False, stop=(hh == 1)), reads=["r_vpad", "r_scKm"], writes=["B5"])
                ev(yall[:, :, sl], yp[:, :, 0:L], reads=["B5"], writes=["r_yall"])
                sup = B2[:, 0:256].rearrange("p (a j) -> p a j", a=2)
                for p in range(2):
                    S.pe(lambda e, p=p: e.matmul(sup[:, p, :], T3[0:L, p, 1, :], ub[0:L, 2 * p:2 * p + 2, :], start=True, stop=False), reads=["r_T3", "r_ub"], writes=["B2"])
                    S.pe(lambda e, p=p: e.matmul(sup[:, p, :], T3[0:L, p, 2, :], T3[0:L, p, 0, :], start=False, stop=True), reads=["r_T3"], writes=["B2"])
                S.dve(lambda e: e.tensor_tensor(out=tmpS[:], in0=sup, in1=bdmaskf[:].unsqueeze(1).to_broadcast([128, 2, 128]), op=ALU.mult), reads=["B2", "bdmaskf"], writes=["r_tmpS"])
                for p in range(2):
                    S.dve(lambda e, p=p, c=c: e.scalar_tensor_tensor(out=st.rS[:, p, :], in0=st.rS[:, p, :], scalar=ecw[:, p, c * L + L - 1:c * L + L], in1=tmpS[:, p, :], op0=ALU.mult, op1=ALU.add),
                          reads=["rS", ("r_ecw", p), "r_tmpS"], writes=["rS"])
                S.act(lambda e: e.copy(out=st.rSb[:], in_=st.rS[:]), reads=["rS"], writes=["rSb"])
            if RSTOP <= -1:
                return
            ryb = wt("r_yb", [128, 2, TBM], BF16); rsq = wt("r_sq", [128, 2, TBM], BF16); ryo = wt("r_yo", [128, 2, TBM], BF16)
            rmean = wt("r_mean", [128, TBM]); rvar = wt("r_var", [128, TBM])
            for p in range(2):
                S.act(lambda e, p=p: e.copy(out=ryb[:, p, 0:TBc], in_=yall[:, p, 0:TBc]), reads=["r_yall"], writes=[("r_yb", p)])
                S.act(lambda e, p=p: e.activation(out=rsq[:, p, 0:TBc], in_=yall[:, p, 0:TBc], func=AF.Square), reads=["r_yall"], writes=[("r_sq", p)])
                sm = B3[:, 0:TBc]; sm2 = B3[:, 256:256 + TBc]
                S.pe(lambda e, p=p, sm=sm: e.matmul(sm, cst["bd"][:], ryb[:, p, 0:TBc], start=True, stop=True), reads=[("r_yb", p), "c_bd"], writes=["B3"])
                S.pe(lambda e, p=p, sm2=sm2: e.matmul(sm2, cst["bd"][:], rsq[:, p, 0:TBc], start=True, stop=True), reads=[("r_sq", p), "c_bd"], writes=["B3"])
                S.act(lambda e, sm=sm: e.mul(out=rmean[:, 0:TBc], in_=sm, mul=1.0 / 64), reads=["B3"], writes=["r_mean"])
                S.dve(lambda e: e.tensor_tensor(out=rvar[:, 0:TBc], in0=rmean[:, 0:TBc], in1=rmean[:, 0:TBc], op=ALU.mult), reads=["r_mean"], writes=["r_var"])
                S.dve(lambda e, sm2=sm2: e.scalar_tensor_tensor(out=rvar[:, 0:TBc], in0=sm2, scalar=1.0 / 64, in1=rvar[:, 0:TBc], op0=ALU.mult, op1=ALU.subtract), reads=["B3", "r_var"], writes=["r_var"])
                S.act(lambda e: e.activation(out=rvar[:, 0:TBc], in_=rvar[:, 0:TBc], func=AF.Sqrt, bias=P("lneps")), reads=["r_var", "par"], writes=["r_var"])
                S.dve(lambda e: e.reciprocal(out=rvar[:, 0:TBc], in_=rvar[:, 0:TBc]), reads=["r_var"], writes=["r_var"])
                S.dve(lambda e, p=p: e.tensor_tensor(out=yall[:, p, 0:TBc], in0=yall[:, p, 0:TBc], in1=rmean[:, 0:TBc], op=ALU.subtract), reads=["r_yall", "r_mean"], writes=["r_yall"])
                S.dve(lambda e, p=p: e.tensor_tensor(out=yall[:, p, 0:TBc], in0=yall[:, p, 0:TBc], in1=rvar[:, 0:TBc], op=ALU.mult), reads=["r_yall", "r_var"], writes=["r_yall"])
                S.dve(lambda e, p=p: e.tensor_scalar(out=yall[:, p, 0:TBc], in0=yall[:, p, 0:TBc], scalar1=P("lg%d" % p), scalar2=P("lb%d" % p), op0=ALU.mult, op1=ALU.add), reads=["r_yall", "par"], writes=["r_yall"])
                S.dve(lambda e, p=p: e.tensor_tensor(out=yall[:, p, 0:TBc], in0=yall[:, p, 0:TBc], in1=bon[:, p, 0:TBc], op=ALU.add), reads=["r_yall", ("r_bon", p)], writes=["r_yall"])
                S.dve(lambda e, p=p: e.tensor_tensor(out=ryo[:, p, 0:TBc], in0=yall[:, p, 0:TBc], in1=rsz[:, p, 0:TBc], op=ALU.mult), reads=["r_yall", ("r_sz", p)], writes=[("r_yo", p)])
                outs.append(S.dma("pool", lambda e, p=p: e.dma_start(out=ydst[512 + p * 128:512 + (p + 1) * 128, t0:t0 + TBc], in_=ryo[:, p, 0:TBc]), reads=[("r_yo", p)]))

    outs = []
    macro(0, 0, NMETA, NMETA)
    for ms in range(NMS):
        macro(ms + 1, NMETA + ms * TB, TB, LCH)
    return outs


def build_B(nc, S, es, TOK, dram, proj=True, last=False, pref="b"):
    def sb(name, shape, dt=F32):
        return es.enter_context(nc.sbuf_tensor("s_" + pref + name, shape, dt))

    def ps(name, shape, dt=F32):
        return es.enter_context(nc.psum_tensor("p_" + pref + name, shape, dt))

    NT = 512
    tiles = [(0, NMETA)] + [(NMETA + i * NT, NT) for i in range((TOK - NMETA) // NT)]
    Bk = [ps("B%d" % i, [128, 512]) for i in range(8)]
    onesf = sb("onesf", [128, 128])
    S.pool(lambda e: e.memset(onesf[:], 1.0), writes=["onesf"])
    gn = sb("gn", [128, 16])
    S.dma("sp", lambda e: e.dma_start(out=gn[:], in_=dram["gnext"]), writes=["gn"])
    epsb = sb("epsb", [128, 1])
    S.pool(lambda e: e.memset(epsb[:], 1e-6), writes=["epsb"])
    h = sb("h", [128, 16, NT]); sq = sb("sq", [128, NT]); rstd = sb("rstd", [128, NT])
    xo = sb("xo", [128, 16, NT], BF16)
    xof = sb("xof", [128, 16, NT], F32)
    hsrc = dram["hT"].rearrange("(kc p) t -> p kc t", p=128)
    if proj:
        hdst = dram["ho"].rearrange("(kc p) t -> p kc t", p=128)
    xdst = dram["xo"].rearrange("(kc p) t -> p kc t", p=128)
    xfdst = dram["xof"].rearrange("(kc p) t -> p kc t", p=128)
    outs = []
    if proj:
        xn = sb("xn", [128, 16, NT], BF16); y = sb("y", [128, 24, NT], BF16); mg = sb("mg", [128, 16, NT], BF16)
        xsrc = dram["xnT"].rearrange("(kc p) t -> p kc t", p=128)
        ysrc = dram["yT"].rearrange("(kc p) t -> p kc t", p=128)
        wgs = dram["wg"].rearrange("(kc p) n -> p kc n", p=128)
        wbs = dram["wbr"].rearrange("(b kc p) n -> p b kc n", p=128, b=3)
        wos = dram["wo"].rearrange("(kc p) n -> p kc n", p=128)
        GW = 128
        wgt = [sb("wg%d" % i, [128, 3, 16, GW], BF16) for i in range(2)]
        wbt = [sb("wb%d" % i, [128, 3, 8, GW], BF16) for i in range(2)]
        wot = [sb("wo%d" % i, [128, 16, GW], BF16) for i in range(2)]
        sg = sb("sg", [128, 3, NT]); tt = sb("tt", [128, 3, NT]); msum = sb("msum", [128, NT])
    wcount = [0]
    for (t0, N) in tiles:
        S.dma("sp", lambda e, t0=t0, N=N: e.dma_start(out=h[:, :, 0:N], in_=hsrc[:, :, t0:t0 + N]), writes=["h"])
        if proj:
            S.dma("sp", lambda e, t0=t0, N=N: e.dma_start(out=xn[:, :, 0:N], in_=xsrc[:, :, t0:t0 + N]), writes=["xn"])
            S.dma("sp", lambda e, t0=t0, N=N: e.dma_start(out=y[:, :, 0:N], in_=ysrc[:, :, t0:t0 + N]), writes=["y"])
            for g in range(2048 // GW):
                wi = wcount[0] % 2
                wcount[0] += 1
                for br in range(3):
                    S.dma("pool", lambda e, g=g, br=br, wi=wi: e.dma_start(out=wgt[wi][:, br], in_=wgs[:, :, br * 2048 + g * GW:br * 2048 + (g + 1) * GW]), writes=["wg%d" % wi])
                    S.dma("pool", lambda e, g=g, br=br, wi=wi: e.dma_start(out=wbt[wi][:, br], in_=wbs[:, br, :, g * GW:(g + 1) * GW]), writes=["wb%d" % wi])
                for d in range(GW // 128):
                    db = g * (GW // 128) + d
                    for br in range(3):
                        gp = Bk[br][:, 0:N]
                        for kc in range(16):
                            S.pe(lambda e, br=br, kc=kc, wi=wi, d=d, gp=gp, N=N: e.matmul(gp, wgt[wi][:, br, kc, d * 128:(d + 1) * 128], xn[:, kc, 0:N], start=(kc == 0), stop=(kc == 15)),
                                 reads=["wg%d" % wi, "xn"], writes=["B%d" % br])
                        S.act(lambda e, br=br, gp=gp, N=N: e.activation(out=sg[:, br, 0:N], in_=gp, func=AF.Sigmoid), reads=["B%d" % br], writes=[("sg", br)])
                        pp = Bk[3 + br][:, 0:N]
                        for kc in range(8):
                            S.pe(lambda e, br=br, kc=kc, wi=wi, d=d, pp=pp, N=N: e.matmul(pp, wbt[wi][:, br, kc, d * 128:(d + 1) * 128], y[:, br * 8 + kc, 0:N], start=(kc == 0), stop=(kc == 7)),
                                 reads=["wb%d" % wi, "y"], writes=["B%d" % (3 + br)])
                        S.dve(lambda e, br=br, pp=pp, N=N: e.tensor_tensor(out=tt[:, br, 0:N], in0=sg[:, br, 0:N], in1=pp, op=ALU.mult), reads=[("sg", br), "B%d" % (3 + br)], writes=[("tt", br)])
                    S.pool(lambda e, N=N: e.tensor_tensor(out=msum[:, 0:N], in0=tt[:, 0, 0:N], in1=tt[:, 1, 0:N], op=ALU.add), reads=[("tt", 0), ("tt", 1)], writes=["msum"])
                    S.pool(lambda e, N=N, db=db: e.tensor_tensor(out=mg[:, db, 0:N], in0=msum[:, 0:N], in1=tt[:, 2, 0:N], op=ALU.add), reads=["msum", ("tt", 2)], writes=[("mg", db)])
            for g in range(2048 // GW):
                wi = g % 2
                S.dma("pool", lambda e, g=g, wi=wi: e.dma_start(out=wot[wi][:], in_=wos[:, :, g * GW:(g + 1) * GW]), writes=["wo%d" % wi])
                for d in range(GW // 128):
                    ob = g * (GW // 128) + d
                    op_ = Bk[6][:, 0:N]
                    for kc in range(16):
                        S.pe(lambda e, kc=kc, wi=wi, d=d, op_=op_, N=N: e.matmul(op_, wot[wi][:, kc, d * 128:(d + 1) * 128], mg[:, kc, 0:N], start=(kc == 0), stop=(kc == 15)),
                             reads=["wo%d" % wi] + [("mg", k) for k in range(16)] if kc == 0 else ["wo%d" % wi], writes=["B6"])
                    S.dve(lambda e, ob=ob, op_=op_, N=N: e.tensor_tensor(out=h[:, ob, 0:N], in0=h[:, ob, 0:N], in1=op_, op=ALU.add), reads=["h", "B6"], writes=["h"])
            outs.append(S.dma("sp", lambda e, t0=t0, N=N: e.dma_start(out=hdst[:, :, t0:t0 + N], in_=h[:, :, 0:N]), reads=["h"]))
        ssp = Bk[7][:, 0:N]
        for ob in range(16):
            S.act(lambda e, ob=ob, N=N: e.activation(out=sq[:, 0:N], in_=h[:, ob, 0:N], func=AF.Square), reads=["h"], writes=["sq"])
            S.pe(lambda e, ob=ob, ssp=ssp, N=N: e.matmul(ssp, onesf[:], sq[:, 0:N], start=(ob == 0), stop=(ob == 15)), reads=["sq", "onesf"], writes=["B7"])
        S.act(lambda e, ssp=ssp, N=N: e.activation(out=rstd[:, 0:N], in_=ssp, func=AF.Sqrt, bias=epsb[:], scale=1.0 / 2048), reads=["B7", "epsb"], writes=["rstd"])
        S.dve(lambda e, N=N: e.reciprocal(out=rstd[:, 0:N], in_=rstd[:, 0:N]), reads=["rstd"], writes=["rstd"])
        for ob in range(16):
            eng = S.dve
            eng(lambda e, ob=ob, N=N: e.scalar_tensor_tensor(out=xof[:, ob, 0:N], in0=h[:, ob, 0:N], scalar=gn[:, ob:ob + 1], in1=rstd[:, 0:N], op0=ALU.mult, op1=ALU.mult),
                reads=["h", "gn", "rstd"], writes=["xof"])
            S.act(lambda e, ob=ob, N=N: e.copy(out=xo[:, ob, 0:N], in_=xof[:, ob, 0:N]), reads=["xof"], writes=["xo"])
        outs.append(S.dma("sp", lambda e, t0=t0, N=N: e.dma_start(out=xdst[:, :, t0:t0 + N], in_=xo[:, :, 0:N]), reads=["xo"]))
        outs.append(S.dma("sp", lambda e, t0=t0, N=N: e.dma_start(out=xfdst[:, :, t0:t0 + N], in_=xof[:, :, 0:N]), reads=["xof"]))
    return outs


SPL = [1024, 1024, 1024, 1024, 512, 512, 1024, 1024, 4, 4, 1024, 1024, 1024, 1024, 64, 64, 1024, 2048, 2048, 2048]
NAMES = ["a_q", "a_f", "a_i", "a_z", "b_q", "b_k", "b_v", "b_o", "b_ig", "b_fg", "b_z", "c_r", "c_k", "c_v", "c_wd", "c_ad", "c_z", "g_a", "g_b", "g_c"]
OFF = {}
_o = 0
for n_, w_ in zip(NAMES, SPL):
    OFF[n_] = _o
    _o += w_
NIN = _o


def colsA(j):
    cols = np.zeros(NCOLA, np.int64)

    def put(tile, start):
        cols[CT[tile] * 128:(CT[tile] + 1) * 128] = np.arange(start, start + 128)

    for hh in range(2):
        h = 2 * j + hh
        put("hq%d" % hh, OFF["a_q"] + h * 128)
        put("hf%d" % hh, OFF["a_f"] + h * 128)
        put("hi%d" % hh, OFF["a_i"] + h * 128)
        put("hz%d" % hh, OFF["a_z"] + h * 128)
    put("mq", OFF["b_q"] + j * 128)
    put("mk", OFF["b_k"] + j * 128)
    for i in range(2):
        put("mv%d" % i, OFF["b_v"] + j * 256 + i * 128)
        put("mo%d" % i, OFF["b_o"] + j * 256 + i * 128)
        put("mz%d" % i, OFF["b_z"] + j * 256 + i * 128)
    for p in range(2):
        c0 = j * 256 + p * 128
        put("rr%d" % p, OFF["c_r"] + c0)
        put("rk%d" % p, OFF["c_k"] + c0)
        put("rv%d" % p, OFF["c_v"] + c0)
        put("rz%d" % p, OFF["c_z"] + c0)
    cols[CT["rwa"] * 128:CT["rwa"] * 128 + 64] = np.arange(OFF["c_wd"], OFF["c_wd"] + 64)
    cols[CT["rwa"] * 128 + 64:CT["rwa"] * 128 + 128] = np.arange(OFF["c_ad"], OFF["c_ad"] + 64)
    cols[25 * 128] = OFF["b_ig"] + j
    cols[25 * 128 + 1] = OFF["b_fg"] + j
    return cols


def pack_A(inp, l, j):
    f = np.float32
    wA = np.ascontiguousarray(np.asarray(inp["w_in"][l])[:, colsA(j)], dtype=f)
    par = np.zeros((128, NPA), f)

    def put(nm, v):
        o, w = PA[nm]
        v = np.asarray(v, f)
        if v.ndim == 0:
            par[:, o:o + w] = v
        else:
            par[:, o:o + w] = v.reshape(128, w)

    lbl = np.asarray(inp["hgrn_lb_logits"])
    par[:, PA["lbsel"][0]:PA["lbsel"][0] + 4] = np.array([1.0 if 1 <= i <= l else 0.0 for i in range(4)], f)[None, :]
    for hh in range(2):
        ch = slice((2 * j + hh) * 128, (2 * j + hh + 1) * 128)
        put("lbl%d" % hh, lbl[:, ch].T)
        put("hg%d" % hh, np.asarray(inp["hgrn_norm_g"][l])[ch])
    cw = np.asarray(inp["mlstm_conv"][l])
    put("cwq", cw[:, j * 128:(j + 1) * 128].T)
    put("cwk", cw[:, 512 + j * 128:512 + (j + 1) * 128].T)
    for i in range(2):
        put("mg%d" % i, np.asarray(inp["mlstm_norm_g"][l])[j * 256 + i * 128:j * 256 + (i + 1) * 128])
    put("igb", float(np.asarray(inp["mlstm_ig_b"])[l, j]))
    put("fgb", float(np.asarray(inp["mlstm_fg_b"])[l, j]))
    put("eps", 1e-6)
    put("lneps", 64e-5)
    put("zero", 0.0)
    mu = np.asarray(inp["rwkv_mu"][l])
    for p in range(2):
        c = slice(j * 256 + p * 128, j * 256 + (p + 1) * 128)
        put("mur%d" % p, mu[0:1024][c])
        put("muk%d" % p, mu[1024:2048][c])
        put("muv%d" % p, mu[2048:3072][c])
        put("w0%d" % p, np.asarray(inp["rwkv_w0"][l])[c])
        put("a0%d" % p, np.asarray(inp["rwkv_a0"][l])[c])
        put("kk%d" % p, np.asarray(inp["rwkv_k_k"][l])[c])
        put("ka%d" % p, np.asarray(inp["rwkv_k_a"][l])[c])
        put("rk%d" % p, np.asarray(inp["rwkv_r_k"][l])[c])
        put("lg%d" % p, np.asarray(inp["rwkv_ln_g"][l])[c])
        put("lb%d" % p, np.asarray(inp["rwkv_ln_b"][l])[c])
    put("muwa", mu[3072:3200])
    wud = np.zeros((128, 256), f)
    wud[0:64] = np.asarray(inp["rwkv_w_up"][l])[:, j * 256:(j + 1) * 256]
    wud[64:128] = np.asarray(inp["rwkv_a_up"][l])[:, j * 256:(j + 1) * 256]
    return dict(wA=wA, parA=par, wud=wud)


TB_A = 128
SEQ = 8192
TSEQ = NMETA + SEQ
TOKC = NMETA + 2048
NMS_A = SEQ // TB_A


def _prog_A():
    nc = bass.Bass("TRN2", target_bir_lowering=False)
    d = {}
    d["xnT"] = nc.dram_tensor("xnT", [2048, TSEQ], BF16, kind="ExternalInput").ap()
    d["wA"] = nc.dram_tensor("wA", [2048, NCOLA], F32, kind="ExternalInput").ap()
    d["parA"] = nc.dram_tensor("parA", [128, NPA], F32, kind="ExternalInput").ap()
    d["wud"] = nc.dram_tensor("wud", [128, 256], F32, kind="ExternalInput").ap()
    for nm, shp in CONST_SHAPES.items():
        d[nm] = nc.dram_tensor(nm, shp, F32, kind="ExternalInput").ap()
    d["yT"] = nc.dram_tensor("yT", [768, TSEQ], BF16, kind="ExternalOutput").ap()
    S = Sched(nc)
    with ExitStack() as es:
        outs = build_A(nc, S, es, NMS_A, TB_A, d)
        S.emit(final_wait_ops=outs)
    return nc


def _prog_B(proj):
    nc = bass.Bass("TRN2", target_bir_lowering=False)
    d = {}
    d["hT"] = nc.dram_tensor("hT", [2048, TOKC], F32, kind="ExternalInput").ap()
    d["gnext"] = nc.dram_tensor("gnext", [128, 16], F32, kind="ExternalInput").ap()
    if proj:
        d["xnT"] = nc.dram_tensor("xnT", [2048, TOKC], BF16, kind="ExternalInput").ap()
        d["yT"] = nc.dram_tensor("yT", [3072, TOKC], BF16, kind="ExternalInput").ap()
        d["wg"] = nc.dram_tensor("wg", [2048, 6144], F32, kind="ExternalInput").ap()
        d["wbr"] = nc.dram_tensor("wbr", [3072, 2048], F32, kind="ExternalInput").ap()
        d["wo"] = nc.dram_tensor("wo", [2048, 2048], F32, kind="ExternalInput").ap()
        d["ho"] = nc.dram_tensor("ho", [2048, TOKC], F32, kind="ExternalOutput").ap()
    d["xo"] = nc.dram_tensor("xo", [2048, TOKC], BF16, kind="ExternalOutput").ap()
    d["xof"] = nc.dram_tensor("xof", [2048, TOKC], F32, kind="ExternalOutput").ap()
    S = Sched(nc)
    with ExitStack() as es:
        outs = build_B(nc, S, es, TOKC, d, proj=proj)
        S.emit(final_wait_ops=outs)
    return nc


def kernel(**inp):
    f = np.float32
    x = np.asarray(inp["x"], f)
    meta = np.asarray(inp["meta_tokens"], f)
    cores = [(b, j) for b in range(2) for j in range(4)]
    ids = list(range(8))

    def gn(g):
        return np.ascontiguousarray(np.asarray(g, f).reshape(16, 128).T)

    hT = [np.ascontiguousarray(np.concatenate([meta, x[b, j * 2048:(j + 1) * 2048]], axis=0).T) for (b, j) in cores]
    pN = _prog_B(False)
    res = run_bass_kernel_spmd(pN, [{"hT": hT[c], "gnext": gn(inp["norm_g"][0])} for c in range(8)], core_ids=ids)
    xo = [r["xo"] for r in res.results]
    xof = None
    pA = _prog_A()
    pB = _prog_B(True)
    cs = host_consts()
    for l in range(4):
        xfull = []
        for b in range(2):
            parts = [xo[b * 4][:, :NMETA]] + [xo[b * 4 + j][:, NMETA:] for j in range(4)]
            xfull.append(np.ascontiguousarray(np.concatenate(parts, axis=1)))
        maps = []
        for c, (b, j) in enumerate(cores):
            m = pack_A(inp, l, j)
            m["xnT"] = xfull[b]
            m.update(cs)
            maps.append(m)
        res = run_bass_kernel_spmd(pA, maps, core_ids=ids)
        yA = [r["yT"] for r in res.results]
        w_in_l = np.asarray(inp["w_in"][l])
        wg = np.ascontiguousarray(w_in_l[:, OFF["g_a"]:OFF["g_a"] + 6144], dtype=f)
        wbr = np.ascontiguousarray(np.asarray(inp["w_br"][l], f).reshape(3072, 2048))
        wo = np.ascontiguousarray(np.asarray(inp["w_out"][l], f))
        gnx = gn(inp["norm_g"][l + 1] if l < 3 else inp["final_norm_g"])
        maps = []
        for c, (b, j) in enumerate(cores):
            yfull = np.empty((3072, TOKC), yA[0].dtype)
            for jj in range(4):
                src = yA[b * 4 + jj]
                for br in range(3):
                    dst = yfull[br * 1024 + jj * 256: br * 1024 + (jj + 1) * 256]
                    dst[:, :NMETA] = src[br * 256:(br + 1) * 256, :NMETA]
                    dst[:, NMETA:] = src[br * 256:(br + 1) * 256, NMETA + j * 2048: NMETA + (j + 1) * 2048]
            maps.append({"hT": hT[c], "gnext": gnx, "xnT": xo[c], "yT": yfull, "wg": wg, "wbr": wbr, "wo": wo})
        res = run_bass_kernel_spmd(pB, maps, core_ids=ids)
        hT = [r["ho"] for r in res.results]
        xo = [r["xo"] for r in res.results]
        xof = [r["xof"] for r in res.results]
    out = np.empty((2, SEQ, 2048), f)
    for c, (b, j) in enumerate(cores):
        out[b, j * 2048:(j + 1) * 2048] = np.asarray(xof[c], f)[:, NMETA:].T
    return out
```

```python
import ml_dtypes
from concourse.bass_utils import run_bass_kernel_spmd
import numpy as np
import concourse.bass as bass
import concourse.mybir as mybir
from contextlib import ExitStack

F32 = mybir.dt.float32
BF16 = mybir.dt.bfloat16
AF = mybir.ActivationFunctionType
ALU = mybir.AluOpType

COMPUTE = ("pe", "act", "dve", "pool")
DMAQ = ("sp", "poolq")
NDMASEM = 8


class Sched:
    def __init__(self, nc, strict_same=True):
        self.nc = nc
        self.ops = []
        self.last_w = {}
        self.readers = {}
        self.strict_same = strict_same
        self.pe_last = {}
        self.bar = set()
        self.epoch = 0
        self.last_eng = {}
        self.dma_since = []

    def op(self, eng, fn, reads=(), writes=(), dma=False, rowgrp=None, extra=(), cc=False):
        i = len(self.ops)
        ex = [k for k in reads if isinstance(k, str) and k[0] == "B" and len(k) <= 2]
        if ex:
            reads = [k for k in reads if k not in ex]
            writes = list(writes) + [k for k in ex if k not in writes]
        deps = set()
        for k in reads:
            w = self.last_w.get(k)
            if w is not None:
                deps.add(w)
        for k in writes:
            w = self.last_w.get(k)
            if w is not None:
                deps.add(w)
            for r in self.readers.get(k, {}).values():
                for x in r:
                    deps.add(x)
        deps.discard(i)
        forced = set()
        if eng == "pe":
            for k in writes:
                if isinstance(k, str) and k[0] == "B" and len(k) <= 2:
                    pl = self.pe_last.get(k)
                    if pl is not None and pl[1] != rowgrp:
                        forced.add(pl[0])
                    self.pe_last[k] = (i, rowgrp)
        deps |= forced
        deps |= set(extra)
        deps |= self.bar
        self.ops.append(dict(eng=eng, fn=fn, deps=deps, dma=dma, forced=forced, cc=cc, epoch=self.epoch))
        self.last_eng[eng] = i
        if dma or cc:
            self.dma_since.append(i)
        for k in writes:
            self.last_w[k] = i
            self.readers[k] = {}
        for k in reads:
            d = self.readers.setdefault(k, {})
            if dma:
                d.setdefault(eng + "_dma", []).append(i)
            else:
                d[eng] = [i]
        return i

    def barrier(self):
        self.bar = set(self.last_eng.values()) | set(self.dma_since)
        self.dma_since = []
        self.epoch += 1
        self.pe_last = {}

    def cc(self, fn, reads=(), writes=(), extra=()):
        return self.op("pool", fn, reads, writes, cc=True, extra=extra)

    def pe(self, fn, reads=(), writes=(), rowgrp=None):
        return self.op("pe", fn, reads, writes, rowgrp=rowgrp)

    def act(self, fn, reads=(), writes=()):
        return self.op("act", fn, reads, writes)

    def dve(self, fn, reads=(), writes=()):
        return self.op("dve", fn, reads, writes)

    def pool(self, fn, reads=(), writes=()):
        return self.op("pool", fn, reads, writes)

    def dma(self, q, fn, reads=(), writes=()):
        return self.op(q, fn, reads, writes, dma=True)

    def emit(self, final_wait_ops=()):
        nc = self.nc
        ops = self.ops
        streams = {"pe": [], "act": [], "dve": [], "pool": [], "sp": []}
        for i, o in enumerate(ops):
            streams[o["eng"]].append(i)
        need = [False] * len(ops)
        for i, o in enumerate(ops):
            for d in o["deps"]:
                od = ops[d]
                if od["dma"] or od["cc"]:
                    need[d] = True
                elif od["eng"] != o["eng"]:
                    need[d] = True
                elif o["eng"] != "pe" and self.strict_same:
                    need[d] = True
                elif d in o["forced"]:
                    need[d] = True
        for d in final_wait_ops:
            need[d] = True
        with ExitStack() as es:
            nep = self.epoch + 1
            csem = {(e, ep): es.enter_context(nc.semaphore("c_%s%d" % (e, ep))) for e in COMPUTE for ep in range(nep)}
            ccsem = es.enter_context(nc.semaphore("ccsem"))
            cccnt = 0
            dsem = {
                q: [es.enter_context(nc.semaphore("d_%s%d" % (q, k))) for k in range(NDMASEM)]
                for q in ("sp", "pool")
            }
            cnt = {(e, ep): 0 for e in COMPUTE for ep in range(nep)}
            dcnt = {"sp": 0, "pool": 0}
            sig = [None] * len(ops)
            prev_same_sem = [None] * len(ops)
            for i, o in enumerate(ops):
                if o["dma"]:
                    q = o["eng"]
                    n = dcnt[q]
                    dcnt[q] += 1
                    s = dsem[q][n % NDMASEM]
                    sig[i] = (("d", q, n % NDMASEM), s, 16 * (n // NDMASEM + 1))
                    if n >= NDMASEM:
                        prev_same_sem[i] = (("d", q, n % NDMASEM), s, 16 * (n // NDMASEM))
                elif o["cc"]:
                    cccnt += 1
                    sig[i] = (("cc",), ccsem, cccnt)
                elif need[i]:
                    e = (o["eng"], o["epoch"])
                    cnt[e] += 1
                    sig[i] = (("c", e), csem[e], cnt[e])
            self.stats = dict(n_ops=len(ops), cnt={str(k): v for k, v in cnt.items()}, dcnt=dict(dcnt),
                              per_eng={e: len(v) for e, v in streams.items()})
            blk = es.enter_context(nc.Block())

            def run_stream(ename, eobj):
                known = {}
                nwait = 0
                for i in streams[ename]:
                    o = ops[i]
                    waits = {}
                    if prev_same_sem[i] is not None:
                        k, s, v = prev_same_sem[i]
                        if known.get(k, 0) < v:
                            waits[k] = (s, v)
                    for d in o["deps"]:
                        od = ops[d]
                        if not od["dma"] and not od["cc"] and od["eng"] == ename and (ename == "pe" or not self.strict_same) and d not in o["forced"]:
                            continue
                        if sig[d] is None:
                            continue
                        k, s, v = sig[d]
                        if known.get(k, 0) < v and (k not in waits or waits[k][1] < v):
                            waits[k] = (s, v)
                    for k, (s, v) in waits.items():
                        eobj.wait_ge(s, v)
                        known[k] = v
                        nwait += 1
                    ins = o["fn"](eobj)
                    if sig[i] is not None:
                        ins.then_inc(sig[i][1], 16 if o["dma"] else 1)
                if ename == "sp":
                    for d in final_wait_ops:
                        k, s, v = sig[d]
                        eobj.wait_ge(s, v)
                self.stats["waits_" + ename] = nwait

            blk.sync(lambda e: run_stream("sp", e))
            blk.tensor(lambda e: run_stream("pe", e))
            blk.scalar(lambda e: run_stream("act", e))
            blk.vector(lambda e: run_stream("dve", e))
            blk.gpsimd(lambda e: run_stream("pool", e))


import math, os
RSTOP = int(os.environ.get('RSTOP', '9'))

D = 2048
KC = 16
NMETA = 16
LCH = 64
C0 = math.exp(-0.5)
CT = dict(hq0=0, hq1=1, hf0=2, hf1=3, hi0=4, hi1=5, hz0=6, hz1=7,
          mq=8, mk=9, mv0=10, mv1=11, mo0=12, mo1=13, mz0=14, mz1=15,
          rr0=16, rr1=17, rk0=18, rk1=19, rv0=20, rv1=21, rz0=22, rz1=23, rwa=24)
NCOLA = 25 * 128 + 2
PA = {}
_n = 0
for nm, w in [("lbsel", 4), ("lbl0", 4), ("lbl1", 4), ("hg0", 1), ("hg1", 1), ("cwq", 4), ("cwk", 4), ("mg0", 1), ("mg1", 1),
              ("igb", 1), ("fgb", 1), ("eps", 1), ("lneps", 1), ("zero", 1),
              ("mur0", 1), ("mur1", 1), ("muk0", 1), ("muk1", 1), ("muv0", 1), ("muv1", 1), ("muwa", 1),
              ("w00", 1), ("w01", 1), ("a00", 1), ("a01", 1), ("kk0", 1), ("kk1", 1), ("ka0", 1), ("ka1", 1),
              ("rk0", 1), ("rk1", 1), ("lg0", 1), ("lg1", 1), ("lb0", 1), ("lb1", 1)]:
    PA[nm] = (_n, w)
    _n += w
NPA = _n


def host_consts():
    c = {}
    c["ident"] = np.eye(128, dtype=np.float32)
    c["ones"] = np.ones((128, 128), np.float32)
    bd = np.zeros((128, 128), np.float32)
    bd[:64, :64] = 1
    bd[64:, 64:] = 1
    c["bd"] = bd
    s = np.arange(64)[:, None]
    t = np.arange(64)[None, :]
    mI = (s <= t).astype(np.float32)
    mS = (s < t).astype(np.float32)
    c["maskI"] = mI
    c["mask2"] = np.stack([mS, mI], axis=1)
    c["maskL"] = (t < s).astype(np.float32)
    pm = np.zeros((64, 2, 128), np.float32)
    pm[:, 0, :64] = 1
    pm[:, 1, 64:] = 1
    c["padmask"] = pm
    return c


CONST_SHAPES = dict(ident=[128, 128], ones=[128, 128], bd=[128, 128], maskI=[64, 64], mask2=[64, 2, 64],
                    maskL=[64, 64], padmask=[64, 2, 128])


class Ctx:
    pass


def build_A(nc, S, es, NMS, TB, dram, mixers=("h", "m", "r"), layer=0, first=True, pref=""):
    def sb(name, shape, dt=F32):
        return es.enter_context(nc.sbuf_tensor("s_" + pref + name, shape, dt))

    def ps(name, shape, dt=F32):
        return es.enter_context(nc.psum_tensor("p_" + pref + name, shape, dt))

    TBM = TB
    NCH = TB // LCH
    cst = {}
    for nm, shp in CONST_SHAPES.items():
        dt = F32 if nm in ("maskI", "mask2", "maskL", "padmask") else BF16
        cst[nm] = sb("c_" + nm, shp, dt)
        S.dma("pool", lambda e, nm=nm: e.dma_start(out=cst[nm][:], in_=dram[nm]), writes=["c_" + nm])
    ident64f = sb("ident64f", [64, 64])
    S.dma("sp", lambda e: e.dma_start(out=ident64f[:], in_=dram["ident"][0:64, 0:64]), writes=["ident64f"])
    bdmaskf = sb("bdmaskf", [128, 128])
    S.dma("sp", lambda e: e.dma_start(out=bdmaskf[:], in_=dram["bd"]), writes=["bdmaskf"])
    onesf = sb("onesf", [128, TBM])
    S.pool(lambda e: e.memset(onesf[:], 1.0), writes=["onesf"])
    par = sb("par", [128, NPA])
    S.dma("sp", lambda e: e.dma_start(out=par[:], in_=dram["parA"]), writes=["par"])

    def P(nm, i=0):
        o, w = PA[nm]
        return par[:, o + i:o + i + 1]

    wA = sb("wA", [128, KC, NCOLA], BF16)
    wsrc = dram["wA"].rearrange("(kc p) n -> p kc n", p=128)
    for kc in range(KC):
        S.dma("pool", lambda e, kc=kc: e.dma_start(out=wA[:, kc, :], in_=wsrc[:, kc, :]), writes=[("wA", kc)])
    WAK = [("wA", kc) for kc in range(KC)]
    wud = sb("wud", [128, 256], BF16)
    S.dma("pool", lambda e: e.dma_start(out=wud[:], in_=dram["wud"]), writes=["wud"])

    B0 = ps("B0", [128, 512]); B1 = ps("B1", [128, 512]); Bt = ps("Bt", [128, 1024], BF16)
    B2 = ps("B2", [128, 512]); B3 = ps("B3", [128, 512]); B4 = ps("B4", [128, 512])
    B5 = ps("B5", [128, 512]); B6 = ps("B6", [128, 512])
    ppbuf = [(B0[:, 0:256], "B0"), (B1[:, 0:256], "B1")]
    ppi = [0]

    xn = [sb("xn%d" % i, [128, KC, TBM], BF16) for i in range(2)]
    if "xn_src" in dram:
        xn_src = dram["xn_src"]
        y_dst = dram["y_dst"]
    else:
        xsrc = dram["xnT"].rearrange("(kc p) t -> p kc t", p=128)
        ydst = dram["yT"]

        def xn_src(t0, TBc):
            return xsrc[:, :, t0:t0 + TBc], "d_xin"

        def y_dst(r0, t0, TBc):
            return ydst[r0:r0 + 128, t0:t0 + TBc], ("d_yl", r0, t0)

    st = Ctx()
    if "h" in mixers:
        st.hS = sb("hS", [128, 2, 128]); st.hSb = sb("hSb", [128, 2, 128], BF16)
        S.pool(lambda e: e.memset(st.hS[:], 0.0), writes=["hS"])
        S.pool(lambda e: e.memset(st.hSb[:], 0.0), writes=["hSb"])
        st.lb = sb("lb", [128, 2]); st.oml = sb("oml", [128, 2]); st.lbm1 = sb("lbm1", [128, 2])
        lbe = sb("lbe", [128, 2, 4]); lbs = sb("lbs", [128, 2]); lbr = sb("lbr", [128, 2])
        o0, _ = PA["lbl0"]
        S.act(lambda e: e.activation(out=lbe[:].rearrange("p a b -> p (a b)"), in_=par[:, o0:o0 + 8], func=AF.Exp), reads=["par"], writes=["lbe"])
        S.dve(lambda e: e.tensor_reduce(out=lbs[:], in_=lbe[:], axis=mybir.AxisListType.X, op=ALU.add), reads=["lbe"], writes=["lbs"])
        S.dve(lambda e: e.reciprocal(out=lbr[:], in_=lbs[:]), reads=["lbs"], writes=["lbr"])
        lbt = sb("lbt", [128, 2]); lbm = sb("lbm", [128, 2, 4])
        osel, _ = PA["lbsel"]
        S.dve(lambda e: e.tensor_tensor(out=lbm[:], in0=lbe[:], in1=par[:, osel:osel + 4].unsqueeze(1).to_broadcast([128, 2, 4]), op=ALU.mult), reads=["lbe", "par"], writes=["lbm"])
        S.dve(lambda e: e.tensor_reduce(out=lbt[:], in_=lbm[:], axis=mybir.AxisListType.X, op=ALU.add), reads=["lbm"], writes=["lbt"])
        S.dve(lambda e: e.tensor_tensor(out=st.lb[:], in0=lbt[:], in1=lbr[:], op=ALU.mult), reads=["lbt", "lbr"], writes=["lb"])
        S.dve(lambda e: e.tensor_scalar(out=st.oml[:], in0=st.lb[:], scalar1=-1.0, scalar2=1.0, op0=ALU.mult, op1=ALU.add), reads=["lb"], writes=["oml"])
        S.dve(lambda e: e.tensor_scalar_add(out=st.lbm1[:], in0=st.lb[:], scalar1=-1.0), reads=["lb"], writes=["lbm1"])
    if "m" in mixers:
        st.mC = sb("mC", [128, 257]); st.mCb = sb("mCb", [128, 257], BF16)
        S.pool(lambda e: e.memset(st.mC[:], 0.0), writes=["mC"])
        st.mxq = sb("mxq", [128, 3 + TBM]); st.mxk = sb("mxk", [128, 3 + TBM])
        S.pool(lambda e: e.memset(st.mxq[:, 0:3], 0.0), writes=["mqx"])
        S.pool(lambda e: e.memset(st.mxk[:, 0:3], 0.0), writes=["mkx"])
        st.mmin = sb("mmin", [1, 1])
        S.pool(lambda e: e.memset(st.mmin[:], 0.0), writes=["mmin"])
        st.mTT = sb("mTT", [64, 3 * 128 + 1], BF16)
        S.pool(lambda e: e.memset(st.mTT[:], 1.0), writes=["mTT"])
        st.onesrow = sb("onesrow", [1, 128])
        S.pool(lambda e: e.memset(st.onesrow[:], 1.0), writes=["onesrow"])
    if "r" in mixers:
        st.rS = sb("rS", [128, 2, 128]); st.rSb = sb("rSb", [128, 2, 128], BF16)
        S.pool(lambda e: e.memset(st.rS[:], 0.0), writes=["rS"])
        S.pool(lambda e: e.memset(st.rSb[:], 0.0), writes=["rSb"])
        st.rraw = {}
        for nm in ("rr0", "rr1", "rk0", "rk1", "rv0", "rv1", "rwa"):
            st.rraw[nm] = sb("raw_" + nm, [128, 1 + TBM])
            S.pool(lambda e, nm=nm: e.memset(st.rraw[nm][:, 0:1], 0.0), writes=["raw_" + nm])

    W = {}

    def wt(name, shape, dt=F32):
        if name not in W:
            W[name] = sb("w_" + name, shape, dt)
        return W[name]

    rr = [0]

    def ev(out, in_, reads, writes):
        rr[0] ^= 1
        if rr[0]:
            return S.dve(lambda e: e.tensor_copy(out=out, in_=in_), reads, writes)
        return S.act(lambda e: e.copy(out=out, in_=in_), reads, writes)

    def inproj(xt, xkey, col0, ncols, TBc):
        buf, key = ppbuf[ppi[0] % 2]
        ppi[0] += 1
        out = buf[0:ncols, 0:TBc]
        for kc in range(KC):
            S.pe(lambda e, kc=kc: e.matmul(out, wA[:, kc, col0:col0 + ncols], xt[:, kc, 0:TBc], start=(kc == 0), stop=(kc == KC - 1)),
                 reads=[xkey, ("wA", kc)], writes=[key])
        return out, key

    def macro(ms, t0, TBc, L):
        nch = TBc // L
        xt = xn[ms % 2]
        xkey = "xn%d" % (ms % 2)
        xin_, xink_ = xn_src(t0, TBc)
        S.dma("sp", lambda e: e.dma_start(out=xt[:, :, 0:TBc], in_=xin_), reads=[xink_], writes=[xkey])

        def c3(ap):
            return ap.rearrange("p (c j) -> p c j", j=L)

        if "h" in mixers:
            qa = wt("h_qa", [128, 2, TBM]); sig = wt("h_sig", [128, 2, TBM]); vb = wt("h_vb", [128, 2, TBM], BF16)
            sz = wt("h_sz", [128, 2, TBM])
            for hh in range(2):
                p_, k_ = inproj(xt, xkey, CT["hq%d" % hh] * 128, 128, TBc)
                S.act(lambda e, p_=p_, hh=hh: e.activation(out=qa[:, hh, 0:TBc], in_=p_, func=AF.Silu), reads=[k_], writes=[("h_qa", hh)])
                p_, k_ = inproj(xt, xkey, CT["hf%d" % hh] * 128, 128, TBc)
                S.act(lambda e, p_=p_, hh=hh: e.activation(out=sig[:, hh, 0:TBc], in_=p_, func=AF.Sigmoid), reads=[k_], writes=[("h_sig", hh)])
                p_, k_ = inproj(xt, xkey, CT["hi%d" % hh] * 128, 128, TBc)
                S.dve(lambda e, p_=p_, hh=hh: e.tensor_copy(out=vb[:, hh, 0:TBc], in_=p_), reads=[k_], writes=[("h_vb", hh)])
                p_, k_ = inproj(xt, xkey, CT["hz%d" % hh] * 128, 128, TBc)
                S.act(lambda e, p_=p_, hh=hh: e.activation(out=sz[:, hh, 0:TBc], in_=p_, func=AF.Silu), reads=[k_], writes=[("h_sz", hh)])
            kk = wt("h_k", [128, 2, TBM]); ft = wt("h_ft", [128, 2, TBM]); cg = wt("h_cg", [128, 2, TBM])
            d1 = wt("h_d1", [128, 2, TBM]); e1 = wt("h_e1", [128, 2, TBM]); eg = wt("h_eg", [128, 2, TBM])
            Qi = wt("h_Qi", [128, 2, TBM], BF16); Ki = wt("h_Ki", [128, 2, TBM], BF16)
            Qx = wt("h_Qx", [128, 2, TBM], BF16); Kt = wt("h_Kt", [128, 2, TBM], BF16)
            mid = L // 2
            for hh in range(2):
                S.dve(lambda e, hh=hh: e.tensor_scalar(out=kk[:, hh, 0:TBc], in0=sig[:, hh, 0:TBc], scalar1=-1.0, scalar2=st.lbm1[:, hh:hh + 1], op0=ALU.add, op1=ALU.mult),
                      reads=[("h_sig", hh), "lbm1"], writes=[("h_k", hh)])
                S.dve(lambda e, hh=hh: e.tensor_scalar(out=ft[:, hh, 0:TBc], in0=sig[:, hh, 0:TBc], scalar1=st.oml[:, hh:hh + 1], scalar2=st.lb[:, hh:hh + 1], op0=ALU.mult, op1=ALU.add),
                      reads=[("h_sig", hh), "oml", "lb"], writes=[("h_ft", hh)])
                S.dve(lambda e, hh=hh: e.tensor_scalar_max(out=ft[:, hh, 0:TBc], in0=ft[:, hh, 0:TBc], scalar1=1e-12), reads=[("h_ft", hh)], writes=[("h_ft", hh)])
                S.act(lambda e, hh=hh: e.activation(out=ft[:, hh, 0:TBc], in_=ft[:, hh, 0:TBc], func=AF.Ln), reads=[("h_ft", hh)], writes=[("h_ft", hh)])
                for c in range(nch):
                    S.dve(lambda e, hh=hh, c=c: e.tensor_tensor_scan(out=cg[:, hh, c * L:(c + 1) * L], data0=onesf[:, 0:L], data1=ft[:, hh, c * L:(c + 1) * L], initial=0.0, op0=ALU.mult, op1=ALU.add),
                          reads=[("h_ft", hh), "onesf"], writes=[("h_cg", hh)])
                cg3 = c3(cg[:, hh, 0:TBc])
                S.dve(lambda e, hh=hh, cg3=cg3: e.tensor_tensor(out=c3(d1[:, hh, 0:TBc]), in0=cg3, in1=cg3[:, :, mid:mid + 1].to_broadcast([128, nch, L]), op=ALU.subtract),
                      reads=[("h_cg", hh)], writes=[("h_d1", hh)])
                S.act(lambda e, hh=hh: e.activation(out=e1[:, hh, 0:TBc], in_=d1[:, hh, 0:TBc], func=AF.Exp), reads=[("h_d1", hh)], writes=[("h_e1", hh)])
                S.dve(lambda e, hh=hh: e.scalar_tensor_tensor(out=Qi[:, hh, 0:TBc], in0=qa[:, hh, 0:TBc], scalar=128.0 ** -0.5, in1=e1[:, hh, 0:TBc], op0=ALU.mult, op1=ALU.mult),
                      reads=[("h_qa", hh), ("h_e1", hh)], writes=[("h_Qi", hh)])
                S.act(lambda e, hh=hh: e.activation(out=e1[:, hh, 0:TBc], in_=d1[:, hh, 0:TBc], func=AF.Exp, scale=-1.0), reads=[("h_d1", hh), ("h_e1", hh)], writes=[("h_e1", hh)])
                S.dve(lambda e, hh=hh: e.tensor_tensor(out=Ki[:, hh, 0:TBc], in0=kk[:, hh, 0:TBc], in1=e1[:, hh, 0:TBc], op=ALU.mult),
                      reads=[("h_k", hh), ("h_e1", hh)], writes=[("h_Ki", hh)])
                S.act(lambda e, hh=hh: e.activation(out=eg[:, hh, 0:TBc], in_=cg[:, hh, 0:TBc], func=AF.Exp), reads=[("h_cg", hh)], writes=[("h_eg", hh)])
                S.dve(lambda e, hh=hh: e.scalar_tensor_tensor(out=Qx[:, hh, 0:TBc], in0=qa[:, hh, 0:TBc], scalar=128.0 ** -0.5, in1=eg[:, hh, 0:TBc], op0=ALU.mult, op1=ALU.mult),
                      reads=[("h_qa", hh), ("h_eg", hh)], writes=[("h_Qx", hh)])
                S.dve(lambda e, hh=hh, cg3=cg3: e.tensor_tensor(out=c3(d1[:, hh, 0:TBc]), in0=cg3[:, :, L - 1:L].to_broadcast([128, nch, L]), in1=cg3, op=ALU.subtract),
                      reads=[("h_cg", hh), ("h_d1", hh)], writes=[("h_d1", hh)])
                S.act(lambda e, hh=hh: e.activation(out=e1[:, hh, 0:TBc], in_=d1[:, hh, 0:TBc], func=AF.Exp), reads=[("h_d1", hh), ("h_e1", hh)], writes=[("h_e1", hh)])
                S.dve(lambda e, hh=hh: e.tensor_tensor(out=Kt[:, hh, 0:TBc], in0=kk[:, hh, 0:TBc], in1=e1[:, hh, 0:TBc], op=ALU.mult),
                      reads=[("h_k", hh), ("h_e1", hh)], writes=[("h_Kt", hh)])
            oall = wt("h_oall", [128, 2, TBM])
            hT = wt("h_T", [64, 4, 128], BF16); hatt = wt("h_att", [64, 2, 64], BF16)
            for c in range(nch):
                sl = slice(c * L, (c + 1) * L)
                h_trp = Bt[0:L, 0:512].rearrange("p (a b) -> p a b", b=128)
                for hh in range(2):
                    S.pe(lambda e, hh=hh, sl=sl: e.transpose(h_trp[:, hh, :], vb[:, hh, sl], cst["ident"][:]), reads=[("h_vb", hh), "c_ident"], writes=["Bt"])
                    S.pe(lambda e, hh=hh, sl=sl: e.transpose(h_trp[:, 2 + hh, :], Kt[:, hh, sl], cst["ident"][:]), reads=[("h_Kt", hh), "c_ident"], writes=["Bt"])
                ev(hT[0:L], h_trp, reads=["Bt"], writes=["h_T"])
                h_scp = B2[0:L, 0:128].rearrange("p (a b) -> p a b", b=64)
                for hh in range(2):
                    S.pe(lambda e, hh=hh, sl=sl: e.matmul(h_scp[:, hh, 0:L], Ki[:, hh, sl], Qi[:, hh, sl], start=True, stop=True), reads=[("h_Ki", hh), ("h_Qi", hh)], writes=["B2"])
                S.dve(lambda e: e.tensor_tensor(out=hatt[0:L, :, 0:L], in0=h_scp[:, :, 0:L], in1=cst["maskI"][0:L, 0:L].unsqueeze(1).to_broadcast([L, 2, L]), op=ALU.mult),
                      reads=["B2", "c_maskI"], writes=["h_att"])
                h_op = B5[:, 0:128].rearrange("p (a b) -> p a b", b=64)
                for hh in range(2):
                    S.pe(lambda e, hh=hh: e.matmul(h_op[:, hh, 0:L], hT[0:L, hh, :], hatt[0:L, hh, 0:L], start=True, stop=False), reads=["h_T", "h_att"], writes=["B5"])
                    S.pe(lambda e, hh=hh, sl=sl: e.matmul(h_op[:, hh, 0:L], st.hSb[:, hh, :], Qx[:, hh, sl], start=False, stop=True), reads=["hSb", ("h_Qx", hh)], writes=["B5"])
                ev(oall[:, :, sl], h_op[:, :, 0:L], reads=["B5"], writes=["h_oall"])
                h_up = B1[:, 0:256].rearrange("p (a b) -> p a b", b=128)
                for hh in range(2):
                    S.pe(lambda e, hh=hh: e.matmul(h_up[:, hh, :], hT[0:L, 2 + hh, :], hT[0:L, hh, :], start=True, stop=True), reads=["h_T"], writes=["B1"])
                for hh in range(2):
                    S.dve(lambda e, hh=hh, c=c: e.scalar_tensor_tensor(out=st.hS[:, hh, :], in0=st.hS[:, hh, :], scalar=eg[:, hh, c * L + L - 1:c * L + L], in1=h_up[:, hh, :], op0=ALU.mult, op1=ALU.add),
                          reads=["hS", ("h_eg", hh), "B1"], writes=["hS"])
                S.act(lambda e: e.copy(out=st.hSb[:], in_=st.hS[:]), reads=["hS"], writes=["hSb"])
            sq = wt("h_sq", [128, 2, TBM], BF16); rs = wt("h_rs", [128, 2, TBM]); yo = wt("h_yo", [128, 2, TBM], BF16)
            for hh in range(2):
                S.act(lambda e, hh=hh: e.activation(out=sq[:, hh, 0:TBc], in_=oall[:, hh, 0:TBc], func=AF.Square), reads=["h_oall"], writes=[("h_sq", hh)])
                ssp = B3[:, hh * 256:hh * 256 + TBc]
                S.pe(lambda e, hh=hh, ssp=ssp: e.matmul(ssp, cst["ones"][:], sq[:, hh, 0:TBc], start=True, stop=True), reads=[("h_sq", hh), "c_ones"], writes=["B3"])
                S.act(lambda e, hh=hh, ssp=ssp: e.activation(out=rs[:, hh, 0:TBc], in_=ssp, func=AF.Sqrt, bias=P("eps"), scale=1.0 / 128), reads=["B3", "par"], writes=[("h_rs", hh)])
                S.dve(lambda e, hh=hh: e.reciprocal(out=rs[:, hh, 0:TBc], in_=rs[:, hh, 0:TBc]), reads=[("h_rs", hh)], writes=[("h_rs", hh)])
                S.dve(lambda e, hh=hh: e.tensor_tensor(out=rs[:, hh, 0:TBc], in0=rs[:, hh, 0:TBc], in1=oall[:, hh, 0:TBc], op=ALU.mult), reads=[("h_rs", hh), "h_oall"], writes=[("h_rs", hh)])
                S.dve(lambda e, hh=hh: e.scalar_tensor_tensor(out=yo[:, hh, 0:TBc], in0=rs[:, hh, 0:TBc], scalar=P("hg%d" % hh), in1=sz[:, hh, 0:TBc], op0=ALU.mult, op1=ALU.mult),
                      reads=[("h_rs", hh), "par", ("h_sz", hh)], writes=[("h_yo", hh)])
                yd_, ydk_ = y_dst(hh * 128, t0, TBc)
                outs.append(S.dma("pool", lambda e, hh=hh, yd_=yd_: e.dma_start(out=yd_, in_=yo[:, hh, 0:TBc]), reads=[("h_yo", hh)], writes=[ydk_]))

        if "m" in mixers:
            for nm, buf in (("mq", st.mxq), ("mk", st.mxk)):
                p_, k_ = inproj(xt, xkey, CT[nm] * 128, 128, TBc)
                ev(buf[:, 3:3 + TBc], p_, reads=[k_], writes=[nm + "x"])
            mvb = wt("m_vb", [128, 2, TBM], BF16); mso = wt("m_so", [128, 2, TBM]); msz = wt("m_sz", [128, 2, TBM])
            for i in range(2):
                p_, k_ = inproj(xt, xkey, CT["mv%d" % i] * 128, 128, TBc)
                S.dve(lambda e, p_=p_, i=i: e.tensor_copy(out=mvb[:, i, 0:TBc], in_=p_), reads=[k_], writes=[("m_vb", i)])
                p_, k_ = inproj(xt, xkey, CT["mo%d" % i] * 128, 128, TBc)
                S.act(lambda e, p_=p_, i=i: e.activation(out=mso[:, i, 0:TBc], in_=p_, func=AF.Sigmoid), reads=[k_], writes=[("m_so", i)])
                p_, k_ = inproj(xt, xkey, CT["mz%d" % i] * 128, 128, TBc)
                S.act(lambda e, p_=p_, i=i: e.activation(out=msz[:, i, 0:TBc], in_=p_, func=AF.Silu), reads=[k_], writes=[("m_sz", i)])
            rows = wt("m_rows", [1, 8, TBM])
            p_, k_ = inproj(xt, xkey, 25 * 128, 1, TBc)
            S.act(lambda e, p_=p_: e.activation(out=rows[:, 0, 0:TBc], in_=p_, func=AF.Identity, bias=par[0:1, PA["igb"][0]:PA["igb"][0] + 1]), reads=[k_, "par"], writes=[("m_rows", 0)])
            p_, k_ = inproj(xt, xkey, 25 * 128 + 1, 1, TBc)
            S.act(lambda e, p_=p_: e.activation(out=rows[:, 1, 0:TBc], in_=p_, func=AF.Sigmoid, bias=par[0:1, PA["fgb"][0]:PA["fgb"][0] + 1]), reads=[k_, "par"], writes=[("m_rows", 1)])
            S.act(lambda e: e.activation(out=rows[:, 1, 0:TBc], in_=rows[:, 1, 0:TBc], func=AF.Ln), reads=[("m_rows", 1)], writes=[("m_rows", 1)])
            S.dve(lambda e: e.tensor_tensor_scan(out=rows[:, 2, 0:TBc], data0=onesf[0:1, 0:TBc], data1=rows[:, 1, 0:TBc], initial=0.0, op0=ALU.mult, op1=ALU.add),
                  reads=[("m_rows", 1), "onesf"], writes=[("m_rows", 2)])
            S.dve(lambda e: e.tensor_tensor(out=rows[:, 3, 0:TBc], in0=rows[:, 0, 0:TBc], in1=rows[:, 2, 0:TBc], op=ALU.subtract), reads=[("m_rows", 0), ("m_rows", 2)], writes=[("m_rows", 3)])
            al0 = wt("m_al0", [1, 8])
            S.dve(lambda e: e.tensor_copy(out=al0[:, 0:1], in_=st.mmin[:]), reads=["mmin"], writes=["m_al0"])
            S.dve(lambda e: e.tensor_tensor_scan(out=rows[:, 4, 0:TBc], data0=onesf[0:1, 0:TBc], data1=rows[:, 3, 0:TBc], initial=st.mmin[:], op0=ALU.mult, op1=ALU.max),
                  reads=[("m_rows", 3), "onesf", "mmin"], writes=[("m_rows", 4)])
            Al3 = rows[:, 4, 0:TBc].rearrange("p (c j) -> p c j", j=L)
            al3 = rows[:, 3, 0:TBc].rearrange("p (c j) -> p c j", j=L)
            if nch > 1:
                S.dve(lambda e: e.tensor_copy(out=al0[:, 1:nch], in_=Al3[:, 0:nch - 1, L - 1]), reads=[("m_rows", 4), "m_al0"], writes=["m_al0"])
            cn_b = Al3[:, :, L - 1:L].to_broadcast([1, nch, L])
            S.dve(lambda e: e.tensor_tensor(out=rows[:, 5, 0:TBc].rearrange("p (c j) -> p c j", j=L), in0=al3, in1=cn_b, op=ALU.subtract), reads=[("m_rows", 3), ("m_rows", 4)], writes=[("m_rows", 5)])
            S.dve(lambda e: e.tensor_tensor(out=rows[:, 6, 0:TBc].rearrange("p (c j) -> p c j", j=L), in0=cn_b, in1=Al3, op=ALU.subtract), reads=[("m_rows", 4)], writes=[("m_rows", 6)])
            car = wt("m_car", [1, 8])
            S.dve(lambda e: e.tensor_tensor(out=car[:, 0:nch], in0=al0[:, 0:nch], in1=Al3[:, :, L - 1], op=ALU.subtract), reads=["m_al0", ("m_rows", 4)], writes=["m_car"])
            S.act(lambda e: e.activation(out=rows[:, 5:7, 0:TBc], in_=rows[:, 5:7, 0:TBc], func=AF.Exp), reads=[("m_rows", 5), ("m_rows", 6)], writes=[("m_rows", 5), ("m_rows", 6)])
            S.act(lambda e: e.activation(out=car[:, 0:nch], in_=car[:, 0:nch], func=AF.Exp), reads=["m_car"], writes=["m_car"])
            S.dve(lambda e: e.tensor_tensor(out=rows[:, 7, 0:TBc], in0=rows[:, 2, 0:TBc], in1=rows[:, 4, 0:TBc], op=ALU.add), reads=[("m_rows", 2), ("m_rows", 4)], writes=[("m_rows", 7)])
            S.dve(lambda e: e.tensor_copy(out=st.mmin[:], in_=rows[:, 7, TBc - 1:TBc]), reads=[("m_rows", 7), "mmin"], writes=["mmin"])
            S.act(lambda e: e.activation(out=rows[:, 7, 0:TBc], in_=rows[:, 7, 0:TBc], func=AF.Exp, scale=-1.0), reads=[("m_rows", 7)], writes=[("m_rows", 7)])
            bws = B3[:, 0:TBc]; bwt = B3[:, 256:256 + TBc]; bcar = B4[:, 0:nch]
            S.pe(lambda e: e.matmul(bws, st.onesrow[:], rows[:, 5, 0:TBc], start=True, stop=True), reads=["onesrow", ("m_rows", 5)], writes=["B3"])
            S.pe(lambda e: e.matmul(bwt, st.onesrow[:], rows[:, 6, 0:TBc], start=True, stop=True), reads=["onesrow", ("m_rows", 6)], writes=["B3"])
            S.pe(lambda e: e.matmul(bcar, st.onesrow[:], car[:, 0:nch], start=True, stop=True), reads=["onesrow", "m_car"], writes=["B4"])
            carb = wt("m_carb", [128, 8])
            S.act(lambda e: e.copy(out=carb[:, 0:nch], in_=bcar), reads=["B4"], writes=["m_carb"])
            qs = wt("m_qs", [128, TBM]); ks = wt("m_ks", [128, TBM]); kp = wt("m_kp", [128, TBM], BF16); qpp = wt("m_qpp", [128, TBM], BF16)
            for nm, buf, dst, cw in (("mq", st.mxq, qs, "cwq"), ("mk", st.mxk, ks, "cwk")):
                S.dve(lambda e, buf=buf, dst=dst, cw=cw: e.tensor_scalar_mul(out=dst[:, 0:TBc], in0=buf[:, 0:TBc], scalar1=P(cw, 0)), reads=[nm + "x", "par"], writes=["m_" + nm + "s"])
                for i in range(1, 4):
                    S.dve(lambda e, buf=buf, dst=dst, cw=cw, i=i: e.scalar_tensor_tensor(out=dst[:, 0:TBc], in0=buf[:, i:i + TBc], scalar=P(cw, i), in1=dst[:, 0:TBc], op0=ALU.mult, op1=ALU.add),
                          reads=[nm + "x", "par", "m_" + nm + "s"], writes=["m_" + nm + "s"])
                S.act(lambda e, dst=dst: e.activation(out=dst[:, 0:TBc], in_=dst[:, 0:TBc], func=AF.Silu), reads=["m_" + nm + "s"], writes=["m_" + nm + "s"])
                S.pool(lambda e, buf=buf: e.tensor_copy(out=buf[:, 0:3], in_=buf[:, TBc:TBc + 3]), reads=[nm + "x"], writes=[nm + "x"])
            S.dve(lambda e: e.tensor_tensor(out=kp[:, 0:TBc], in0=ks[:, 0:TBc], in1=bws, op=ALU.mult), reads=["m_mks", "B3"], writes=["m_kp"])
            S.dve(lambda e: e.scalar_tensor_tensor(out=qpp[:, 0:TBc], in0=qs[:, 0:TBc], scalar=128.0 ** -0.5, in1=bwt, op0=ALU.mult, op1=ALU.mult), reads=["m_mqs", "B3"], writes=["m_qpp"])
            numall = wt("m_num", [128, 2, TBM]); denr = wt("m_den", [1, TBM]); matt = wt("m_att", [64, 64], BF16)
            for c in range(nch):
                sl = slice(c * L, (c + 1) * L)
                m_trp = Bt[0:L, 0:384].rearrange("p (a b) -> p a b", b=128)
                S.pe(lambda e, sl=sl: e.transpose(m_trp[:, 0, :], kp[:, sl], cst["ident"][:]), reads=["m_kp", "c_ident"], writes=["Bt"])
                for i in range(2):
                    S.pe(lambda e, sl=sl, i=i: e.transpose(m_trp[:, 1 + i, :], mvb[:, i, sl], cst["ident"][:]), reads=[("m_vb", i), "c_ident"], writes=["Bt"])
                ev(st.mTT[0:L, 0:384], Bt[0:L, 0:384], reads=["Bt"], writes=["mTT"])
                S.dve(lambda e, c=c: e.tensor_scalar_mul(out=st.mC[:], in0=st.mC[:], scalar1=carb[:, c:c + 1]), reads=["mC", "m_carb"], writes=["mC"])
                S.act(lambda e: e.copy(out=st.mCb[:], in_=st.mC[:]), reads=["mC"], writes=["mCb"])
                m_scp = B2[0:L, 128:128 + L]
                S.pe(lambda e, sl=sl: e.matmul(m_scp, kp[:, sl], qpp[:, sl], start=True, stop=True), reads=["m_kp", "m_qpp"], writes=["B2"])
                S.dve(lambda e: e.tensor_tensor(out=matt[0:L, 0:L], in0=m_scp, in1=cst["maskI"][0:L, 0:L], op=ALU.mult), reads=["B2", "c_maskI"], writes=["m_att"])
                m_np = B5[:, 128:256].rearrange("p (a b) -> p a b", b=64)
                for i in range(2):
                    S.pe(lambda e, i=i: e.matmul(m_np[:, i, 0:L], st.mTT[0:L, 128 + i * 128:256 + i * 128], matt[0:L, 0:L], start=True, stop=False), reads=["mTT", "m_att"], writes=["B5"])
                    S.pe(lambda e, i=i, sl=sl: e.matmul(m_np[:, i, 0:L], st.mCb[:, i * 128:(i + 1) * 128], qpp[:, sl], start=False, stop=True), reads=["mCb", "m_qpp"], writes=["B5"])
                m_dp = B5[0:1, 256:256 + L]
                S.pe(lambda e: e.matmul(m_dp, st.mTT[0:L, 384:385], matt[0:L, 0:L], start=True, stop=False), reads=["mTT", "m_att"], writes=["B5"])
                S.pe(lambda e, sl=sl: e.matmul(m_dp, st.mCb[:, 256:257], qpp[:, sl], start=False, stop=True), reads=["mCb", "m_qpp"], writes=["B5"])
                ev(numall[:, :, sl], m_np[:, :, 0:L], reads=["B5"], writes=["m_num"])
                ev(denr[:, sl], m_dp, reads=["B5"], writes=["m_den"])
                cup = B1[:, 256:512]
                nup = B5[:, 448:449]
                S.pe(lambda e: e.matmul(cup, st.mTT[0:L, 0:128], st.mTT[0:L, 128:384], start=True, stop=True), reads=["mTT"], writes=["B1"])
                S.pe(lambda e: e.matmul(nup, st.mTT[0:L, 0:128], st.mTT[0:L, 384:385], start=True, stop=True), reads=["mTT"], writes=["B5"])
                S.dve(lambda e: e.tensor_tensor(out=st.mC[:, 0:256], in0=st.mC[:, 0:256], in1=cup, op=ALU.add), reads=["mC", "B1"], writes=["mC"])
                S.dve(lambda e: e.tensor_tensor(out=st.mC[:, 256:257], in0=st.mC[:, 256:257], in1=nup, op=ALU.add), reads=["mC", "B5"], writes=["mC"])
            S.act(lambda e: e.activation(out=denr[:, 0:TBc], in_=denr[:, 0:TBc], func=AF.Abs), reads=["m_den"], writes=["m_den"])
            S.dve(lambda e: e.tensor_tensor(out=denr[:, 0:TBc], in0=denr[:, 0:TBc], in1=rows[:, 7, 0:TBc], op=ALU.max), reads=["m_den", ("m_rows", 7)], writes=["m_den"])
            S.dve(lambda e: e.reciprocal(out=denr[:, 0:TBc], in_=denr[:, 0:TBc]), reads=["m_den"], writes=["m_den"])
            bdd = B4[:, 256:256 + TBc]
            S.pe(lambda e: e.matmul(bdd, st.onesrow[:], denr[:, 0:TBc], start=True, stop=True), reads=["onesrow", "m_den"], writes=["B4"])
            mh = wt("m_h", [128, 2, TBM]); mhb = wt("m_hb", [128, 2, TBM], BF16); msq = wt("m_sq", [128, 2, TBM], BF16)
            S.dve(lambda e: e.tensor_tensor(out=mh[:, :, 0:TBc], in0=numall[:, :, 0:TBc], in1=bdd.unsqueeze(1).to_broadcast([128, 2, TBc]), op=ALU.mult), reads=["m_num", "B4"], writes=["m_h"])
            S.act(lambda e: e.copy(out=mhb[:, :, 0:TBc], in_=mh[:, :, 0:TBc]), reads=["m_h"], writes=["m_hb"])
            S.act(lambda e: e.activation(out=msq[:, :, 0:TBc], in_=mh[:, :, 0:TBc], func=AF.Square), reads=["m_h"], writes=["m_sq"])
            sm = B3[:, 0:TBc]; sm2 = B3[:, 256:256 + TBc]
            for i in range(2):
                S.pe(lambda e, i=i: e.matmul(sm, cst["ones"][:], mhb[:, i, 0:TBc], start=(i == 0), stop=(i == 1)), reads=["m_hb", "c_ones"], writes=["B3"])
            for i in range(2):
                S.pe(lambda e, i=i: e.matmul(sm2, cst["ones"][:], msq[:, i, 0:TBc], start=(i == 0), stop=(i == 1)), reads=["m_sq", "c_ones"], writes=["B3"])
            mean = wt("m_mean", [128, TBM]); var = wt("m_var", [128, TBM]); myo = wt("m_yo", [128, 2, TBM], BF16)
            S.act(lambda e: e.mul(out=mean[:, 0:TBc], in_=sm, mul=1.0 / 256), reads=["B3"], writes=["m_mean"])
            S.dve(lambda e: e.tensor_tensor(out=var[:, 0:TBc], in0=mean[:, 0:TBc], in1=mean[:, 0:TBc], op=ALU.mult), reads=["m_mean"], writes=["m_var"])
            S.dve(lambda e: e.scalar_tensor_tensor(out=var[:, 0:TBc], in0=sm2, scalar=1.0 / 256, in1=var[:, 0:TBc], op0=ALU.mult, op1=ALU.subtract), reads=["B3", "m_var"], writes=["m_var"])
            S.act(lambda e: e.activation(out=var[:, 0:TBc], in_=var[:, 0:TBc], func=AF.Sqrt, bias=P("eps")), reads=["m_var", "par"], writes=["m_var"])
            S.dve(lambda e: e.reciprocal(out=var[:, 0:TBc], in_=var[:, 0:TBc]), reads=["m_var"], writes=["m_var"])
            S.dve(lambda e: e.tensor_tensor(out=mh[:, :, 0:TBc], in0=mh[:, :, 0:TBc], in1=mean[:, 0:TBc].unsqueeze(1).to_broadcast([128, 2, TBc]), op=ALU.subtract), reads=["m_h", "m_mean"], writes=["m_h"])
            S.dve(lambda e: e.tensor_tensor(out=mh[:, :, 0:TBc], in0=mh[:, :, 0:TBc], in1=var[:, 0:TBc].unsqueeze(1).to_broadcast([128, 2, TBc]), op=ALU.mult), reads=["m_h", "m_var"], writes=["m_h"])
            for i in range(2):
                S.dve(lambda e, i=i: e.scalar_tensor_tensor(out=mh[:, i, 0:TBc], in0=mh[:, i, 0:TBc], scalar=P("mg%d" % i), in1=mso[:, i, 0:TBc], op0=ALU.mult, op1=ALU.mult), reads=["m_h", "par", ("m_so", i)], writes=["m_h"])
            S.dve(lambda e: e.tensor_tensor(out=myo[:, :, 0:TBc], in0=mh[:, :, 0:TBc], in1=msz[:, :, 0:TBc], op=ALU.mult), reads=["m_h", ("m_sz", 0), ("m_sz", 1)], writes=["m_yo"])
            for i in range(2):
                yd_, ydk_ = y_dst(256 + i * 128, t0, TBc)
                outs.append(S.dma("pool", lambda e, i=i, yd_=yd_: e.dma_start(out=yd_, in_=myo[:, i, 0:TBc]), reads=["m_yo"], writes=[ydk_]))

        if "r" in mixers:
            for nm in ("rr0", "rr1", "rk0", "rk1", "rv0", "rv1", "rwa"):
                p_, k_ = inproj(xt, xkey, CT[nm] * 128, 128, TBc)
                ev(st.rraw[nm][:, 1:1 + TBc], p_, reads=[k_], writes=["raw_" + nm])
            rsz = wt("r_sz", [128, 2, TBM])
            for p in range(2):
                p_, k_ = inproj(xt, xkey, CT["rz%d" % p] * 128, 128, TBc)
                S.act(lambda e, p_=p_, p=p: e.activation(out=rsz[:, p, 0:TBc], in_=p_, func=AF.Silu), reads=[k_], writes=[("r_sz", p)])
            if RSTOP <= -3:
                return
            lm = {}
            for nm, mu in (("rr0", "mur0"), ("rr1", "mur1"), ("rk0", "muk0"), ("rk1", "muk1"), ("rv0", "muv0"), ("rv1", "muv1"), ("rwa", "muwa")):
                raw = st.rraw[nm]
                m = wt("r_m_" + nm, [128, TBM])
                lm[nm] = m
                S.dve(lambda e, raw=raw, m=m: e.tensor_tensor(out=m[:, 0:TBc], in0=raw[:, 0:TBc], in1=raw[:, 1:1 + TBc], op=ALU.subtract), reads=["raw_" + nm], writes=["r_m_" + nm])
                S.dve(lambda e, raw=raw, m=m, mu=mu: e.scalar_tensor_tensor(out=m[:, 0:TBc], in0=m[:, 0:TBc], scalar=P(mu), in1=raw[:, 1:1 + TBc], op0=ALU.mult, op1=ALU.add),
                      reads=["raw_" + nm, "r_m_" + nm, "par"], writes=["r_m_" + nm])
                S.pool(lambda e, raw=raw: e.tensor_copy(out=raw[:, 0:1], in_=raw[:, TBc:TBc + 1]), reads=["raw_" + nm], writes=["raw_" + nm])
            wab = wt("r_wab", [128, TBM], BF16)
            S.act(lambda e: e.activation(out=wab[0:64, 0:TBc], in_=lm["rwa"][0:64, 0:TBc], func=AF.Tanh), reads=["r_m_rwa"], writes=["r_wab0"])
            S.act(lambda e: e.copy(out=wab[64:128, 0:TBc], in_=lm["rwa"][64:128, 0:TBc]), reads=["r_m_rwa"], writes=["r_wab1"])
            if RSTOP <= -2:
                return
            sgw = wt("r_sgw", [128, 2, TBM]); av = wt("r_a", [128, 2, TBM]); kkn = wt("r_kkn", [128, 2, TBM]); kf = wt("r_kf", [128, 2, TBM])
            bv = wt("r_bv", [128, 2, TBM]); cs = wt("r_cs", [128, 2, TBM]); tmp = wt("r_tmp", [128, 2, TBM]); tmpb = wt("r_tmpb", [128, 2, TBM], BF16)
            ecw = wt("r_ecw", [128, 2, TBM]); ex = wt("r_ex", [128, 2, TBM])
            AR = wt("r_AR", [128, 2, max(NCH, 1), 2, 64], BF16)
            Bd = wt("r_Bd", [128, 2, TBM], BF16); Kd = wt("r_Kd", [128, 2, TBM], BF16)
            Btl = wt("r_Btl", [128, 2, TBM], BF16); Ktl = wt("r_Ktl", [128, 2, TBM], BF16)
            rvb = wt("r_vb", [128, 2, TBM], BF16); bon = wt("r_bon", [128, 2, TBM])
            for p in range(2):
                rm = lm["rr%d" % p]; km = lm["rk%d" % p]; vm = lm["rv%d" % p]
                RK = ["r_m_rr%d" % p, "r_m_rk%d" % p, "r_m_rv%d" % p]
                wp = B3[:, 0:TBc]; ap_ = B3[:, 256:256 + TBc]
                S.pe(lambda e, p=p, wp=wp: e.matmul(wp, wud[0:64, p * 128:(p + 1) * 128], wab[0:64, 0:TBc], start=True, stop=True), reads=["wud", "r_wab0"], writes=["B3"])
                S.pe(lambda e, p=p, ap_=ap_: e.matmul(ap_, wud[64:128, p * 128:(p + 1) * 128], wab[64:128, 0:TBc], start=True, stop=True), reads=["wud", "r_wab1"], writes=["B3"], rowgrp=1)
                S.act(lambda e, p=p, wp=wp: e.activation(out=sgw[:, p, 0:TBc], in_=wp, func=AF.Sigmoid, bias=P("w0%d" % p)), reads=["B3", "par"], writes=[("r_sgw", p)])
                S.act(lambda e, p=p, ap_=ap_: e.activation(out=av[:, p, 0:TBc], in_=ap_, func=AF.Sigmoid, bias=P("a0%d" % p)), reads=["B3", "par"], writes=[("r_a", p)])
                S.dve(lambda e, p=p, km=km: e.tensor_scalar_mul(out=kkn[:, p, 0:TBc], in0=km[:, 0:TBc], scalar1=P("kk%d" % p)), reads=[RK[1], "par"], writes=[("r_kkn", p)])
                S.act(lambda e, p=p: e.activation(out=tmpb[:, p, 0:TBc], in_=kkn[:, p, 0:TBc], func=AF.Square), reads=[("r_kkn", p)], writes=[("r_tmpb", p)])
                ssk = B4[:, 0:TBc]
                S.pe(lambda e, p=p, ssk=ssk: e.matmul(ssk, cst["bd"][:], tmpb[:, p, 0:TBc], start=True, stop=True), reads=[("r_tmpb", p), "c_bd"], writes=["B4"])
                S.act(lambda e, p=p, ssk=ssk: e.activation(out=tmp[:, p, 0:TBc], in_=ssk, func=AF.Sqrt), reads=["B4"], writes=[("r_tmp", p)])
                S.dve(lambda e, p=p: e.tensor_scalar_max(out=tmp[:, p, 0:TBc], in0=tmp[:, p, 0:TBc], scalar1=1e-12), reads=[("r_tmp", p)], writes=[("r_tmp", p)])
                S.dve(lambda e, p=p: e.reciprocal(out=tmp[:, p, 0:TBc], in_=tmp[:, p, 0:TBc]), reads=[("r_tmp", p)], writes=[("r_tmp", p)])
                S.dve(lambda e, p=p: e.tensor_tensor(out=kkn[:, p, 0:TBc], in0=kkn[:, p, 0:TBc], in1=tmp[:, p, 0:TBc], op=ALU.mult), reads=[("r_kkn", p), ("r_tmp", p)], writes=[("r_kkn", p)])
                S.dve(lambda e, p=p: e.tensor_scalar(out=kf[:, p, 0:TBc], in0=av[:, p, 0:TBc], scalar1=-1.0, scalar2=P("ka%d" % p), op0=ALU.add, op1=ALU.mult), reads=[("r_a", p), "par"], writes=[("r_kf", p)])
                S.dve(lambda e, p=p, km=km: e.scalar_tensor_tensor(out=kf[:, p, 0:TBc], in0=kf[:, p, 0:TBc], scalar=1.0, in1=km[:, 0:TBc], op0=ALU.add, op1=ALU.mult), reads=[("r_kf", p), RK[1]], writes=[("r_kf", p)])
                S.dve(lambda e, p=p: e.tensor_tensor(out=bv[:, p, 0:TBc], in0=kkn[:, p, 0:TBc], in1=av[:, p, 0:TBc], op=ALU.mult), reads=[("r_kkn", p), ("r_a", p)], writes=[("r_bv", p)])
                for c in range(nch):
                    S.dve(lambda e, p=p, c=c: e.tensor_tensor_scan(out=cs[:, p, c * L:(c + 1) * L], data0=onesf[:, 0:L], data1=sgw[:, p, c * L:(c + 1) * L], initial=0.0, op0=ALU.mult, op1=ALU.add),
                          reads=[("r_sgw", p), "onesf"], writes=[("r_cs", p)])
                cs3 = c3(cs[:, p, 0:TBc])
                ARp = AR[:, p, 0:nch, :, 0:L]
                S.act(lambda e, p=p: e.activation(out=ecw[:, p, 0:TBc], in_=cs[:, p, 0:TBc], func=AF.Exp, scale=-C0), reads=[("r_cs", p)], writes=[("r_ecw", p)])
                S.dve(lambda e, p=p, rm=rm, ARp=ARp: e.tensor_tensor(out=ARp[:, :, 1, :], in0=c3(rm[:, 0:TBc]), in1=c3(ecw[:, p, 0:TBc]), op=ALU.mult), reads=[RK[0], ("r_ecw", p)], writes=[("r_AR", p)])
                S.act(lambda e, p=p: e.activation(out=ex[:, p, 0:TBc], in_=cs[:, p, 0:TBc], func=AF.Exp, scale=C0), reads=[("r_cs", p)], writes=[("r_ex", p)])
                S.dve(lambda e, p=p: e.tensor_tensor(out=Kd[:, p, 0:TBc], in0=kf[:, p, 0:TBc], in1=ex[:, p, 0:TBc], op=ALU.mult), reads=[("r_kf", p), ("r_ex", p)], writes=[("r_Kd", p)])
                S.dve(lambda e, p=p: e.tensor_tensor(out=Bd[:, p, 0:TBc], in0=bv[:, p, 0:TBc], in1=ex[:, p, 0:TBc], op=ALU.mult), reads=[("r_bv", p), ("r_ex", p)], writes=[("r_Bd", p)])
                S.dve(lambda e, p=p: e.tensor_tensor(out=tmp[:, p, 0:TBc], in0=cs[:, p, 0:TBc], in1=sgw[:, p, 0:TBc], op=ALU.subtract), reads=[("r_cs", p), ("r_sgw", p), ("r_tmp", p)], writes=[("r_tmp", p)])
                S.act(lambda e, p=p: e.activation(out=ex[:, p, 0:TBc], in_=tmp[:, p, 0:TBc], func=AF.Exp, scale=-C0), reads=[("r_tmp", p), ("r_ex", p)], writes=[("r_ex", p)])
                S.dve(lambda e, p=p, ARp=ARp: e.scalar_tensor_tensor(out=ARp[:, :, 0, :], in0=c3(kkn[:, p, 0:TBc]), scalar=-1.0, in1=c3(ex[:, p, 0:TBc]), op0=ALU.mult, op1=ALU.mult), reads=[("r_kkn", p), ("r_ex", p)], writes=[("r_AR", p)])
                S.dve(lambda e, p=p, cs3=cs3: e.tensor_tensor(out=c3(tmp[:, p, 0:TBc]), in0=cs3[:, :, L - 1:L].to_broadcast([128, nch, L]), in1=cs3, op=ALU.subtract), reads=[("r_cs", p), ("r_tmp", p)], writes=[("r_tmp", p)])
                S.act(lambda e, p=p: e.activation(out=ex[:, p, 0:TBc], in_=tmp[:, p, 0:TBc], func=AF.Exp, scale=-C0), reads=[("r_tmp", p), ("r_ex", p)], writes=[("r_ex", p)])
                S.dve(lambda e, p=p: e.tensor_tensor(out=Btl[:, p, 0:TBc], in0=bv[:, p, 0:TBc], in1=ex[:, p, 0:TBc], op=ALU.mult), reads=[("r_bv", p), ("r_ex", p)], writes=[("r_Btl", p)])
                S.dve(lambda e, p=p: e.tensor_tensor(out=Ktl[:, p, 0:TBc], in0=kf[:, p, 0:TBc], in1=ex[:, p, 0:TBc], op=ALU.mult), reads=[("r_kf", p), ("r_ex", p)], writes=[("r_Ktl", p)])
                S.act(lambda e, p=p, vm=vm: e.copy(out=rvb[:, p, 0:TBc], in_=vm[:, 0:TBc]), reads=[RK[2]], writes=[("r_vb", p)])
                S.dve(lambda e, p=p, rm=rm: e.scalar_tensor_tensor(out=tmpb[:, p, 0:TBc], in0=rm[:, 0:TBc], scalar=P("rk%d" % p), in1=kf[:, p, 0:TBc], op0=ALU.mult, op1=ALU.mult), reads=[RK[0], "par", ("r_kf", p), ("r_tmpb", p)], writes=[("r_tmpb", p)])
                bsp = B4[:, 256:256 + TBc]
                S.pe(lambda e, p=p, bsp=bsp: e.matmul(bsp, cst["bd"][:], tmpb[:, p, 0:TBc], start=True, stop=True), reads=[("r_tmpb", p), "c_bd"], writes=["B4"])
                S.dve(lambda e, p=p, vm=vm, bsp=bsp: e.tensor_tensor(out=bon[:, p, 0:TBc], in0=vm[:, 0:TBc], in1=bsp, op=ALU.mult), reads=[RK[2], "B4"], writes=[("r_bon", p)])
            yall = wt("r_yall", [128, 2, TBM])
            T3 = wt("r_T3", [64, 2, 3, 128], BF16)
            scBm = wt("r_scBm", [64, 4, 2, 64], BF16); scKm = wt("r_scKm", [64, 4, 2, 64], BF16); labm = wt("r_labm", [64, 4, 64], BF16)
            TTf = wt("r_TTf", [64, 4, 64]); TTb = wt("r_TTb", [64, 4, 64], BF16)
            X = [wt("r_X%d" % i, [64, 4, 2, 64], BF16) for i in range(2)]
            Q = [wt("r_Q%d" % i, [64, 4, 64], BF16) for i in range(2)]
            rhsb = wt("r_rhsb", [64, 4, 64], BF16); ub = wt("r_ub", [64, 4, 64], BF16)
            upad = wt("r_upad", [64, 2, 2, 128], BF16); vpad = wt("r_vpad", [64, 2, 2, 128], BF16)
            tmpS = wt("r_tmpS", [128, 2, 128])
            nlev = int(round(math.log2(L)))
            for c in range(nch if RSTOP >= 1 else 0):
                sl = slice(c * L, (c + 1) * L)
                trp = Bt[0:L, 0:768].rearrange("p (a b c) -> p a b c", a=2, b=3)
                for p in range(2):
                    S.pe(lambda e, p=p, sl=sl: e.transpose(trp[:, p, 0, :], rvb[:, p, sl], cst["ident"][:]), reads=[("r_vb", p), "c_ident"], writes=["Bt"])
                    S.pe(lambda e, p=p, sl=sl: e.transpose(trp[:, p, 1, :], Btl[:, p, sl], cst["ident"][:]), reads=[("r_Btl", p), "c_ident"], writes=["Bt"])
                    S.pe(lambda e, p=p, sl=sl: e.transpose(trp[:, p, 2, :], Ktl[:, p, sl], cst["ident"][:]), reads=[("r_Ktl", p), "c_ident"], writes=["Bt"])
                ev(T3[0:L], trp, reads=["Bt"], writes=["r_T3"])
                S.dve(lambda e: e.tensor_tensor(out=vpad[0:L], in0=T3[0:L, :, 0, :].unsqueeze(2).to_broadcast([L, 2, 2, 128]), in1=cst["padmask"][0:L].unsqueeze(1).to_broadcast([L, 2, 2, 128]), op=ALU.mult),
                      reads=["r_T3", "c_padmask"], writes=["r_vpad"])
                scB = B3[0:L, :].rearrange("p (h a j) -> p h a j", h=4, a=2)
                scK = B4[0:L, :].rearrange("p (h a j) -> p h a j", h=4, a=2)
                lab = B2[0:L, 256:512].rearrange("p (h j) -> p h j", h=4)
                for h in (0, 2, 1, 3):
                    p, hh = divmod(h, 2)
                    b0 = hh * 64
                    rg = 1 if hh == 1 else None
                    arh = AR[b0:b0 + 64, p, c, :, 0:L]
                    S.pe(lambda e, h=h, p=p, b0=b0, arh=arh, sl=sl: e.matmul(scB[:, h, :, 0:L], Bd[b0:b0 + 64, p, sl], arh, start=True, stop=True), reads=[("r_Bd", p), ("r_AR", p)], writes=["B3"], rowgrp=rg)
                    S.pe(lambda e, h=h, p=p, b0=b0, arh=arh, sl=sl: e.matmul(scK[:, h, :, 0:L], Kd[b0:b0 + 64, p, sl], arh, start=True, stop=True), reads=[("r_Kd", p), ("r_AR", p)], writes=["B4"], rowgrp=rg)
                    S.pe(lambda e, h=h, p=p, b0=b0, arh=arh, sl=sl: e.matmul(lab[:, h, 0:L], arh[:, 0, :], Bd[b0:b0 + 64, p, sl], start=True, stop=True), reads=[("r_Bd", p), ("r_AR", p)], writes=["B2"], rowgrp=rg)
                m2 = cst["mask2"][0:L, :, 0:L].unsqueeze(1).to_broadcast([L, 4, 2, L])
                S.dve(lambda e: e.tensor_tensor(out=scBm[0:L, :, :, 0:L], in0=scB[:, :, :, 0:L], in1=m2, op=ALU.mult), reads=["B3", "c_mask2"], writes=["r_scBm"])
                S.dve(lambda e: e.tensor_tensor(out=scKm[0:L, :, :, 0:L], in0=scK[:, :, :, 0:L], in1=m2, op=ALU.mult), reads=["B4", "c_mask2"], writes=["r_scKm"])
                S.dve(lambda e: e.tensor_tensor(out=labm[0:L, :, 0:L], in0=lab[:, :, 0:L], in1=cst["maskL"][0:L, 0:L].unsqueeze(1).to_broadcast([L, 4, L]), op=ALU.mult), reads=["B2", "c_maskL"], writes=["r_labm"])
                if RSTOP < 2:
                    continue
                S.dve(lambda e: e.tensor_tensor(out=TTf[0:L, :, 0:L], in0=scBm[0:L, :, 0, 0:L], in1=ident64f[0:L, 0:L].unsqueeze(1).to_broadcast([L, 4, L]), op=ALU.add), reads=["r_scBm", "ident64f"], writes=["r_TTf"])
                S.act(lambda e: e.copy(out=X[0][0:L, :, 0, 0:L], in_=TTf[0:L, :, 0:L]), reads=["r_TTf"], writes=["r_X0a"])
                PPp = B6[0:L, :].rearrange("p (h a j) -> p h a j", h=4, a=2)
                QQp = B1[0:L, 0:256].rearrange("p (h j) -> p h j", h=4)
                for h in range(4):
                    S.pe(lambda e, h=h: e.matmul(PPp[:, h, 1, 0:L], labm[0:L, h, 0:L], scBm[0:L, h, 0, 0:L], start=True, stop=True), reads=["r_labm", "r_scBm"], writes=["B6"])
                    S.pe(lambda e, h=h: e.matmul(QQp[:, h, 0:L], scBm[0:L, h, 0, 0:L], labm[0:L, h, 0:L], start=True, stop=True), reads=["r_labm", "r_scBm"], writes=["B1"])
                S.dve(lambda e: e.tensor_copy(out=X[0][0:L, :, 1, 0:L], in_=PPp[:, :, 1, 0:L]), reads=["B6"], writes=["r_X0b"])
                S.act(lambda e: e.copy(out=Q[0][0:L, :, 0:L], in_=QQp[:, :, 0:L]), reads=["B1"], writes=["r_Q0"])
                cur = 0
                for lev in range(1, nlev):
                    last = lev == nlev - 1
                    Xc, Qc = X[cur], Q[cur]
                    xk = ["r_X%da" % cur, "r_X%db" % cur]
                    qk = "r_Q%d" % cur
                    nxt = cur ^ 1
                    for h in range(4):
                        if last:
                            S.pe(lambda e, h=h, Xc=Xc, Qc=Qc: e.matmul(PPp[:, h, 0, 0:L], Qc[0:L, h, 0:L], Xc[0:L, h, 0, 0:L], start=True, stop=True), reads=[qk] + xk, writes=["B6"])
                        else:
                            S.pe(lambda e, h=h, Xc=Xc, Qc=Qc: e.matmul(PPp[:, h, :, 0:L], Qc[0:L, h, 0:L], Xc[0:L, h, :, 0:L], start=True, stop=True), reads=[qk] + xk, writes=["B6"])
                            S.pe(lambda e, h=h, Xc=Xc, Qc=Qc: e.matmul(QQp[:, h, 0:L], Xc[0:L, h, 1, 0:L], Qc[0:L, h, 0:L], start=True, stop=True), reads=[qk] + xk, writes=["B1"])
                    S.dve(lambda e: e.tensor_tensor(out=TTf[0:L, :, 0:L], in0=TTf[0:L, :, 0:L], in1=PPp[:, :, 0, 0:L], op=ALU.add), reads=["r_TTf", "B6"], writes=["r_TTf"])
                    if last:
                        S.act(lambda e: e.copy(out=TTb[0:L, :, 0:L], in_=TTf[0:L, :, 0:L]), reads=["r_TTf"], writes=["r_TTb"])
                    else:
                        S.act(lambda e, nxt=nxt: e.copy(out=X[nxt][0:L, :, 0, 0:L], in_=TTf[0:L, :, 0:L]), reads=["r_TTf"], writes=["r_X%da" % nxt])
                        S.dve(lambda e, nxt=nxt: e.tensor_copy(out=X[nxt][0:L, :, 1, 0:L], in_=PPp[:, :, 1, 0:L]), reads=["B6"], writes=["r_X%db" % nxt])
                        S.act(lambda e, nxt=nxt: e.copy(out=Q[nxt][0:L, :, 0:L], in_=QQp[:, :, 0:L]), reads=["B1"], writes=["r_Q%d" % nxt])
                    cur = nxt
                if RSTOP < 3:
                    continue
                rhp = B6[0:L, 0:256].rearrange("p (h j) -> p h j", h=4)
                for h in range(4):
                    p, hh = divmod(h, 2)
                    S.pe(lambda e, h=h, p=p, hh=hh, c=c: e.matmul(rhp[:, h, :], AR[:, p, c, 0, 0:L], st.rSb[:, p, hh * 64:(hh + 1) * 64], start=True, stop=False), reads=[("r_AR", p), "rSb"], writes=["B6"])
                    S.pe(lambda e, h=h, p=p, hh=hh: e.matmul(rhp[:, h, :], scKm[0:L, h, 0, 0:L], T3[0:L, p, 0, hh * 64:(hh + 1) * 64], start=False, stop=True), reads=["r_scKm", "r_T3"], writes=["B6"])
                S.act(lambda e: e.copy(out=rhsb[0:L], in_=rhp), reads=["B6"], writes=["r_rhsb"])
                up_ = B6[0:L, 256:512].rearrange("p (h j) -> p h j", h=4)
                for h in range(4):
                    S.pe(lambda e, h=h: e.matmul(up_[:, h, :], TTb[0:L, h, 0:L], rhsb[0:L, h, :], start=True, stop=True), reads=["r_TTb", "r_rhsb"], writes=["B6"])
                S.act(lambda e: e.copy(out=ub[0:L], in_=up_), reads=["B6"], writes=["r_ub"])
                S.dve(lambda e: e.tensor_tensor(out=upad[0:L], in0=B6[0:L, 256:512].rearrange("p (a k) -> p a k", a=2).unsqueeze(2).to_broadcast([L, 2, 2, 128]), in1=cst["padmask"][0:L].unsqueeze(1).to_broadcast([L, 2, 2, 128]), op=ALU.mult),
                      reads=["B6", "c_padmask"], writes=["r_upad"])
                yp = B5[:, 320:448].rearrange("p (a j) -> p a j", a=2)
                for p in range(2):
                    S.pe(lambda e, p=p, c=c: e.matmul(yp[:, p, 0:L], st.rSb[:, p, :], AR[:, p, c, 1, 0:L], start=True, stop=False), reads=["rSb", ("r_AR", p)], writes=["B5"])
                    for hh in range(2):
                        h = p * 2 + hh
                        S.pe(lambda e, p=p, hh=hh, h=h: e.matmul(yp[:, p, 0:L], upad[0:L, p, hh, :], scBm[0:L, h, 1, 0:L], start=False, stop=False), reads=["r_upad", "r_scBm"], writes=["B5"])
                        S.pe(lambda e, p=p, hh=hh, h=h: e.matmul(yp[:, p, 0:L], vpad[0:L, p, hh, :], scKm[0:L, h, 1, 0:L], start=False, stop=(hh == 1)), reads=["r_vpad", "r_scKm"], writes=["B5"])
                ev(yall[:, :, sl], yp[:, :, 0:L], reads=["B5"], writes=["r_yall"])
                sup = B2[:, 0:256].rearrange("p (a j) -> p a j", a=2)
                for p in range(2):
                    S.pe(lambda e, p=p: e.matmul(sup[:, p, :], T3[0:L, p, 1, :], ub[0:L, 2 * p:2 * p + 2, :], start=True, stop=False), reads=["r_T3", "r_ub"], writes=["B2"])
                    S.pe(lambda e, p=p: e.matmul(sup[:, p, :], T3[0:L, p, 2, :], T3[0:L, p, 0, :], start=False, stop=True), reads=["r_T3"], writes=["B2"])
                S.dve(lambda e: e.tensor_tensor(out=tmpS[:], in0=sup, in1=bdmaskf[:].unsqueeze(1).to_broadcast([128, 2, 128]), op=ALU.mult), reads=["B2", "bdmaskf"], writes=["r_tmpS"])
                for p in range(2):
                    S.dve(lambda e, p=p, c=c: e.scalar_tensor_tensor(out=st.rS[:, p, :], in0=st.rS[:, p, :], scalar=ecw[:, p, c * L + L - 1:c * L + L], in1=tmpS[:, p, :], op0=ALU.mult, op1=ALU.add),
                          reads=["rS", ("r_ecw", p), "r_tmpS"], writes=["rS"])
                S.act(lambda e: e.copy(out=st.rSb[:], in_=st.rS[:]), reads=["rS"], writes=["rSb"])
            if RSTOP <= -1:
                return
            ryb = wt("r_yb", [128, 2, TBM], BF16); rsq = wt("r_sq", [128, 2, TBM], BF16); ryo = wt("r_yo", [128, 2, TBM], BF16)
            rmean = wt("r_mean", [128, TBM]); rvar = wt("r_var", [128, TBM])
            for p in range(2):
                S.act(lambda e, p=p: e.copy(out=ryb[:, p, 0:TBc], in_=yall[:, p, 0:TBc]), reads=["r_yall"], writes=[("r_yb", p)])
                S.act(lambda e, p=p: e.activation(out=rsq[:, p, 0:TBc], in_=yall[:, p, 0:TBc], func=AF.Square), reads=["r_yall"], writes=[("r_sq", p)])
                sm = B3[:, 0:TBc]; sm2 = B3[:, 256:256 + TBc]
                S.pe(lambda e, p=p, sm=sm: e.matmul(sm, cst["bd"][:], ryb[:, p, 0:TBc], start=True, stop=True), reads=[("r_yb", p), "c_bd"], writes=["B3"])
                S.pe(lambda e, p=p, sm2=sm2: e.matmul(sm2, cst["bd"][:], rsq[:, p, 0:TBc], start=True, stop=True), reads=[("r_sq", p), "c_bd"], writes=["B3"])
                S.act(lambda e, sm=sm: e.mul(out=rmean[:, 0:TBc], in_=sm, mul=1.0 / 64), reads=["B3"], writes=["r_mean"])
                S.dve(lambda e: e.tensor_tensor(out=rvar[:, 0:TBc], in0=rmean[:, 0:TBc], in1=rmean[:, 0:TBc], op=ALU.mult), reads=["r_mean"], writes=["r_var"])
                S.dve(lambda e, sm2=sm2: e.scalar_tensor_tensor(out=rvar[:, 0:TBc], in0=sm2, scalar=1.0 / 64, in1=rvar[:, 0:TBc], op0=ALU.mult, op1=ALU.subtract), reads=["B3", "r_var"], writes=["r_var"])
                S.act(lambda e: e.activation(out=rvar[:, 0:TBc], in_=rvar[:, 0:TBc], func=AF.Sqrt, bias=P("lneps")), reads=["r_var", "par"], writes=["r_var"])
                S.dve(lambda e: e.reciprocal(out=rvar[:, 0:TBc], in_=rvar[:, 0:TBc]), reads=["r_var"], writes=["r_var"])
                S.dve(lambda e, p=p: e.tensor_tensor(out=yall[:, p, 0:TBc], in0=yall[:, p, 0:TBc], in1=rmean[:, 0:TBc], op=ALU.subtract), reads=["r_yall", "r_mean"], writes=["r_yall"])
                S.dve(lambda e, p=p: e.tensor_tensor(out=yall[:, p, 0:TBc], in0=yall[:, p, 0:TBc], in1=rvar[:, 0:TBc], op=ALU.mult), reads=["r_yall", "r_var"], writes=["r_yall"])
                S.dve(lambda e, p=p: e.tensor_scalar(out=yall[:, p, 0:TBc], in0=yall[:, p, 0:TBc], scalar1=P("lg%d" % p), scalar2=P("lb%d" % p), op0=ALU.mult, op1=ALU.add), reads=["r_yall", "par"], writes=["r_yall"])
                S.dve(lambda e, p=p: e.tensor_tensor(out=yall[:, p, 0:TBc], in0=yall[:, p, 0:TBc], in1=bon[:, p, 0:TBc], op=ALU.add), reads=["r_yall", ("r_bon", p)], writes=["r_yall"])
                S.dve(lambda e, p=p: e.tensor_tensor(out=ryo[:, p, 0:TBc], in0=yall[:, p, 0:TBc], in1=rsz[:, p, 0:TBc], op=ALU.mult), reads=["r_yall", ("r_sz", p)], writes=[("r_yo", p)])
                yd_, ydk_ = y_dst(512 + p * 128, t0, TBc)
                outs.append(S.dma("pool", lambda e, p=p, yd_=yd_: e.dma_start(out=yd_, in_=ryo[:, p, 0:TBc]), reads=[("r_yo", p)], writes=[ydk_]))

    outs = []
    macro(0, 0, NMETA, NMETA)
    for ms in range(NMS):
        macro(ms + 1, NMETA + ms * TB, TB, LCH)
    return outs


def build_B(nc, S, es, TOK, dram, proj=True, last=False, pref="b", skip_meta=False):
    def sb(name, shape, dt=F32):
        return es.enter_context(nc.sbuf_tensor("s_" + pref + name, shape, dt))

    def ps(name, shape, dt=F32):
        return es.enter_context(nc.psum_tensor("p_" + pref + name, shape, dt))

    NT = min(512, TOK - NMETA)
    tiles = [(0, NMETA)] + [(NMETA + i * NT, NT) for i in range((TOK - NMETA) // NT)]
    if skip_meta:
        tiles = tiles[1:]
    if "h_src" not in dram:
        hsrc = dram["hT"].rearrange("(kc p) t -> p kc t", p=128)
        dram = dict(dram)
        dram["h_src"] = lambda t0, N: (hsrc[:, :, t0:t0 + N], ("d_hin", t0))
        if proj:
            hdst = dram["ho"].rearrange("(kc p) t -> p kc t", p=128)
            xsrc = dram["xnT"].rearrange("(kc p) t -> p kc t", p=128)
            ysrc = dram["yT"].rearrange("(kc p) t -> p kc t", p=128)
            dram["h_dst"] = lambda t0, N: (hdst[:, :, t0:t0 + N], ("d_ho", t0))
            dram["xn_srcB"] = lambda t0, N: [(xsrc[:, :, t0:t0 + N], "d_xinB", 0, N)]

            def y_load(y, cand, t0, N):
                S.dma("sp", lambda e: e.dma_start(out=y[:, :, 0:N], in_=ysrc[:, :, t0:t0 + N]), writes=["y"])
            dram["y_load"] = y_load
        xdst = dram["xo"].rearrange("(kc p) t -> p kc t", p=128)
        dram["x_dst"] = lambda t0, N: [(xdst[:, :, t0:t0 + N], ("d_xo", t0), 0, N)]
        if "xof" in dram:
            xfd = dram["xof"].rearrange("(kc p) t -> p kc t", p=128)
            dram["out_dst"] = lambda ob, t0, N: (xfd[:, ob, t0:t0 + N], ("d_xof", ob, t0))
    Bk = [ps("B%d" % i, [128, 512]) for i in range(8)]
    onesf = sb("onesf", [128, 128])
    S.pool(lambda e: e.memset(onesf[:], 1.0), writes=["onesf"])
    gn = sb("gn", [128, 16])
    S.dma("sp", lambda e: e.dma_start(out=gn[:], in_=dram["gnext"]), writes=["gn"])
    epsb = sb("epsb", [128, 1])
    S.pool(lambda e: e.memset(epsb[:], 1e-6), writes=["epsb"])
    h = sb("h", [128, 16, NT]); sq = sb("sq", [128, NT]); rstd = sb("rstd", [128, NT])
    xo = sb("xo", [128, 16, NT], BF16)
    want_f32 = "out_dst" in dram
    if want_f32:
        otmp = [sb("otmp%d" % i, [128, NT]) for i in range(2)]
    outs = []
    if proj:
        xn = sb("xn", [128, 16, NT], BF16); y = sb("y", [128, 24, NT], BF16); mg = sb("mg", [128, 16, NT], BF16)
        cand = sb("cand", [128, 24, NT], BF16) if dram.get("need_cand") else None
        wgs = dram["wg"].rearrange("(kc p) n -> p kc n", p=128)
        wbs = dram["wbr"].rearrange("(b kc p) n -> p b kc n", p=128, b=3)
        wos = dram["wo"].rearrange("(kc p) n -> p kc n", p=128)
        GW = 128
        wgt = [sb("wg%d" % i, [128, 3, 16, GW], BF16) for i in range(2)]
        wbt = [sb("wb%d" % i, [128, 3, 8, GW], BF16) for i in range(2)]
        wot = [sb("wo%d" % i, [128, 16, GW], BF16) for i in range(2)]
        sg = sb("sg", [128, 3, NT]); tt = sb("tt", [128, 3, NT]); msum = sb("msum", [128, NT])
    wcount = [0]
    yidx = dram.get("yidx", lambda br, kc: br * 8 + kc)
    for (t0, N) in tiles:
        hin, hk = dram["h_src"](t0, N)
        S.dma("sp", lambda e, hin=hin, N=N: e.dma_start(out=h[:, :, 0:N], in_=hin), reads=[hk], writes=["h"])
        if proj:
            for (xin, xk, lo, hi) in dram["xn_srcB"](t0, N):
                S.dma("sp", lambda e, xin=xin, lo=lo, hi=hi: e.dma_start(out=xn[:, :, lo:hi], in_=xin), reads=[xk], writes=["xn"])
            dram["y_load"](y, cand, t0, N)
            for g in range(2048 // GW):
                wi = wcount[0] % 2
                wcount[0] += 1
                for br in range(3):
                    S.dma("pool", lambda e, g=g, br=br, wi=wi: e.dma_start(out=wgt[wi][:, br], in_=wgs[:, :, br * 2048 + g * GW:br * 2048 + (g + 1) * GW]), writes=["wg%d" % wi])
                    S.dma("pool", lambda e, g=g, br=br, wi=wi: e.dma_start(out=wbt[wi][:, br], in_=wbs[:, br, :, g * GW:(g + 1) * GW]), writes=["wb%d" % wi])
                for d in range(GW // 128):
                    db = g * (GW // 128) + d
                    for br in range(3):
                        gp = Bk[br][:, 0:N]
                        for kc in range(16):
                            S.pe(lambda e, br=br, kc=kc, wi=wi, d=d, gp=gp, N=N: e.matmul(gp, wgt[wi][:, br, kc, d * 128:(d + 1) * 128], xn[:, kc, 0:N], start=(kc == 0), stop=(kc == 15)),
                                 reads=["wg%d" % wi, "xn"], writes=["B%d" % br])
                        S.act(lambda e, br=br, gp=gp, N=N: e.activation(out=sg[:, br, 0:N], in_=gp, func=AF.Sigmoid), reads=["B%d" % br], writes=[("sg", br)])
                        pp = Bk[3 + br][:, 0:N]
                        for kc in range(8):
                            S.pe(lambda e, br=br, kc=kc, wi=wi, d=d, pp=pp, N=N: e.matmul(pp, wbt[wi][:, br, kc, d * 128:(d + 1) * 128], y[:, yidx(br, kc), 0:N], start=(kc == 0), stop=(kc == 7)),
                                 reads=["wb%d" % wi, "y"], writes=["B%d" % (3 + br)])
                        S.dve(lambda e, br=br, pp=pp, N=N: e.tensor_tensor(out=tt[:, br, 0:N], in0=sg[:, br, 0:N], in1=pp, op=ALU.mult), reads=[("sg", br), "B%d" % (3 + br)], writes=[("tt", br)])
                    S.pool(lambda e, N=N: e.tensor_tensor(out=msum[:, 0:N], in0=tt[:, 0, 0:N], in1=tt[:, 1, 0:N], op=ALU.add), reads=[("tt", 0), ("tt", 1)], writes=["msum"])
                    S.pool(lambda e, N=N, db=db: e.tensor_tensor(out=mg[:, db, 0:N], in0=msum[:, 0:N], in1=tt[:, 2, 0:N], op=ALU.add), reads=["msum", ("tt", 2)], writes=[("mg", db)])
            for g in range(2048 // GW):
                wi = g % 2
                S.dma("pool", lambda e, g=g, wi=wi: e.dma_start(out=wot[wi][:], in_=wos[:, :, g * GW:(g + 1) * GW]), writes=["wo%d" % wi])
                for d in range(GW // 128):
                    ob = g * (GW // 128) + d
                    op_ = Bk[6][:, 0:N]
                    for kc in range(16):
                        S.pe(lambda e, kc=kc, wi=wi, d=d, op_=op_, N=N: e.matmul(op_, wot[wi][:, kc, d * 128:(d + 1) * 128], mg[:, kc, 0:N], start=(kc == 0), stop=(kc == 15)),
                             reads=["wo%d" % wi] + [("mg", k) for k in range(16)] if kc == 0 else ["wo%d" % wi], writes=["B6"])
                    S.dve(lambda e, ob=ob, op_=op_, N=N: e.tensor_tensor(out=h[:, ob, 0:N], in0=h[:, ob, 0:N], in1=op_, op=ALU.add), reads=["h", "B6"], writes=["h"])
            if "h_dst" in dram:
                hd, hdk = dram["h_dst"](t0, N)
                outs.append(S.dma("sp", lambda e, hd=hd, N=N: e.dma_start(out=hd, in_=h[:, :, 0:N]), reads=["h"], writes=[hdk]))
        ssp = Bk[7][:, 0:N]
        for ob in range(16):
            S.act(lambda e, ob=ob, N=N: e.activation(out=sq[:, 0:N], in_=h[:, ob, 0:N], func=AF.Square), reads=["h"], writes=["sq"])
            S.pe(lambda e, ob=ob, ssp=ssp, N=N: e.matmul(ssp, onesf[:], sq[:, 0:N], start=(ob == 0), stop=(ob == 15)), reads=["sq", "onesf"], writes=["B7"])
        S.act(lambda e, ssp=ssp, N=N: e.activation(out=rstd[:, 0:N], in_=ssp, func=AF.Sqrt, bias=epsb[:], scale=1.0 / 2048), reads=["B7", "epsb"], writes=["rstd"])
        S.dve(lambda e, N=N: e.reciprocal(out=rstd[:, 0:N], in_=rstd[:, 0:N]), reads=["rstd"], writes=["rstd"])
        for ob in range(16):
            if want_f32:
                ot = otmp[ob % 2]
                otk = "otmp%d" % (ob % 2)
                S.dve(lambda e, ob=ob, N=N, ot=ot: e.scalar_tensor_tensor(out=ot[:, 0:N], in0=h[:, ob, 0:N], scalar=gn[:, ob:ob + 1], in1=rstd[:, 0:N], op0=ALU.mult, op1=ALU.mult),
                      reads=["h", "gn", "rstd"], writes=[otk])
                if "x_dst" in dram:
                    S.act(lambda e, ob=ob, N=N, ot=ot: e.copy(out=xo[:, ob, 0:N], in_=ot[:, 0:N]), reads=[otk], writes=["xo"])
                od, odk = dram["out_dst"](ob, t0, N)
                outs.append(S.dma("sp", lambda e, od=od, ot=ot, N=N: e.dma_start(out=od, in_=ot[:, 0:N]), reads=[otk], writes=[odk]))
            else:
                S.dve(lambda e, ob=ob, N=N: e.scalar_tensor_tensor(out=xo[:, ob, 0:N], in0=h[:, ob, 0:N], scalar=gn[:, ob:ob + 1], in1=rstd[:, 0:N], op0=ALU.mult, op1=ALU.mult),
                      reads=["h", "gn", "rstd"], writes=["xo"])
        if "x_dst" in dram:
            for (xd, xdk, lo, hi) in dram["x_dst"](t0, N):
                outs.append(S.dma("sp", lambda e, xd=xd, lo=lo, hi=hi: e.dma_start(out=xd, in_=xo[:, :, lo:hi]), reads=["xo"], writes=[xdk]))
    return outs


def prog_fused(NMS, TB):
    SEQr = NMS * TB
    Q = SEQr // 4
    TOKC_ = NMETA + Q
    CHX = min(256, Q)
    CHY = min(512, Q)
    NCX = Q // CHX
    NCY = SEQr // CHY
    nc = bass.Bass("TRN2", target_bir_lowering=False)
    ext = {}
    ext["hT"] = nc.dram_tensor("hT", [2048, TOKC_], F32, kind="ExternalInput").ap()
    ext["sel"] = nc.dram_tensor("sel", [128, 4], F32, kind="ExternalInput").ap()
    for nm, shp in CONST_SHAPES.items():
        ext[nm] = nc.dram_tensor(nm, shp, F32, kind="ExternalInput").ap()
    for l in range(5):
        ext["gn%d" % l] = nc.dram_tensor("gn%d" % l, [128, 16], F32, kind="ExternalInput").ap()
    for l in range(4):
        ext["wA%d" % l] = nc.dram_tensor("wA%d" % l, [2048, NCOLA], F32, kind="ExternalInput").ap()
        ext["parA%d" % l] = nc.dram_tensor("parA%d" % l, [128, NPA], F32, kind="ExternalInput").ap()
        ext["wud%d" % l] = nc.dram_tensor("wud%d" % l, [128, 256], F32, kind="ExternalInput").ap()
        ext["wg%d" % l] = nc.dram_tensor("wg%d" % l, [2048, 6144], F32, kind="ExternalInput").ap()
        ext["wbr%d" % l] = nc.dram_tensor("wbr%d" % l, [3072, 2048], F32, kind="ExternalInput").ap()
        ext["wo%d" % l] = nc.dram_tensor("wo%d" % l, [2048, 2048], F32, kind="ExternalInput").ap()
    out = nc.dram_tensor("out", [2048, Q], F32, kind="ExternalOutput").ap()
    hloc = nc.dram_tensor("hloc", [2048, TOKC_], F32).ap()
    xm = nc.dram_tensor("xm", [2048, NMETA], BF16).ap()
    xr_t = [nc.dram_tensor("xr%d" % k, [2048, CHX], BF16) for k in range(NCX)]
    xg_t = [nc.dram_tensor("xg%d" % k, [4 * 2048, CHX], BF16) for k in range(NCX)]
    ylm_t = nc.dram_tensor("ylm", [768, NMETA], BF16)
    ygm_t = nc.dram_tensor("ygm", [4 * 768, NMETA], BF16)
    yl_t = [nc.dram_tensor("yl%d" % k, [768, CHY], BF16) for k in range(NCY)]
    yg_t = [nc.dram_tensor("yg%d" % k, [4 * 768, CHY], BF16) for k in range(NCY)]
    groups = [[0, 1, 2, 3], [4, 5, 6, 7]]
    S = Sched(nc)
    hT3 = ext["hT"].rearrange("(kc p) t -> p kc t", p=128)
    hl3 = hloc.rearrange("(kc p) t -> p kc t", p=128)
    xm3 = xm.rearrange("(kc p) t -> p kc t", p=128)
    xr3 = [t.ap().rearrange("(kc p) t -> p kc t", p=128) for t in xr_t]
    xg4 = [t.ap().rearrange("(q kc p) t -> p q kc t", q=4, p=128) for t in xg_t]
    ygm4 = ygm_t.ap().rearrange("(jj c p) t -> p jj c t", jj=4, c=6, p=128)
    yg4 = [t.ap().rearrange("(jj c p) t -> p jj c t", jj=4, c=6, p=128) for t in yg_t]
    out3 = out.rearrange("(kc p) t -> p kc t", p=128)

    def xloc(t0, N):
        if t0 == 0:
            return [(xm3[:, :, 0:N], "d_xm", 0, N)]
        r0 = t0 - NMETA
        res = []
        for k in range(r0 // CHX, (r0 + N) // CHX):
            res.append((xr3[k][:, :, :], ("d_xr", k), k * CHX - r0, (k + 1) * CHX - r0))
        return res

    with ExitStack() as top:
        selt = top.enter_context(nc.sbuf_tensor("s_sel", [128, 4], F32))
        S.dma("sp", lambda e: e.dma_start(out=selt[:], in_=ext["sel"]), writes=["sel"])

        def y_load(y, cand, t0, N):
            if t0 == 0:
                for jj in range(4):
                    S.dma("sp", lambda e, jj=jj: e.dma_start(out=y[:, jj * 6:(jj + 1) * 6, 0:N], in_=ygm4[:, jj, :, 0:N]), reads=["d_ygm"], writes=["y"])
                return
            for q in range(4):
                kk = (q * Q + (t0 - NMETA)) // CHY
                for jj in range(4):
                    S.dma("sp", lambda e, jj=jj, kk=kk: e.dma_start(out=cand[:, jj * 6:(jj + 1) * 6, 0:N], in_=yg4[kk][:, jj, :, 0:N]), reads=[("d_yg", kk)], writes=["cand"])
                if q == 0:
                    S.dve(lambda e: e.tensor_scalar_mul(out=y[:, :, 0:N], in0=cand[:, :, 0:N], scalar1=selt[:, 0:1]), reads=["cand", "sel"], writes=["y"])
                else:
                    S.dve(lambda e, q=q: e.scalar_tensor_tensor(out=y[:, :, 0:N], in0=cand[:, :, 0:N], scalar=selt[:, q:q + 1], in1=y[:, :, 0:N], op0=ALU.mult, op1=ALU.add),
                          reads=["cand", "sel", "y"], writes=["y"])

        def gather_all(pairs, outs):
            last = None
            for (src_t, dst_t, wk) in pairs:
                last = S.cc(lambda e, src_t=src_t, dst_t=dst_t: e.collective_compute("AllGather", ALU.bypass, replica_groups=groups, ins=[src_t.ap().opt()], outs=[dst_t.ap().opt()]),
                            writes=[wk], extra=outs)
            S.barrier()
            return last

        xpairs = [(xr_t[k], xg_t[k], ("d_xg", k)) for k in range(NCX)]
        ypairs = [(ylm_t, ygm_t, "d_ygm")] + [(yl_t[k], yg_t[k], ("d_yg", k)) for k in range(NCY)]
        dN = dict(gnext=ext["gn0"], h_src=lambda t0, N: (hT3[:, :, t0:t0 + N], ("d_h", t0)), x_dst=xloc)
        with ExitStack() as es:
            outs = build_B(nc, S, es, TOKC_, dN, proj=False, pref="n0")
        g0 = gather_all(xpairs, outs)
        finals = None
        FSTOP = int(os.environ.get("FSTOP", "99"))
        if FSTOP == 0:
            S.emit(final_wait_ops=[g0])
            return nc, S
        for l in range(4):
            dA = dict(wA=ext["wA%d" % l], parA=ext["parA%d" % l], wud=ext["wud%d" % l])
            for nm in CONST_SHAPES:
                dA[nm] = ext[nm]

            def xn_src(t0, TBc):
                if t0 == 0:
                    return xm3[:, :, 0:TBc], "d_xm"
                q, off = divmod(t0 - NMETA, Q)
                k, c = divmod(off, CHX)
                return xg4[k][:, q, :, c:c + TBc], ("d_xg", k)

            def y_dst(r0, t0, TBc):
                if t0 == 0:
                    return ylm_t.ap()[r0:r0 + 128, 0:TBc], ("d_yl", r0, t0)
                k, c = divmod(t0 - NMETA, CHY)
                return yl_t[k].ap()[r0:r0 + 128, c:c + TBc], ("d_yl", r0, t0)

            dA["xn_src"] = xn_src
            dA["y_dst"] = y_dst
            with ExitStack() as es:
                outs = build_A(nc, S, es, NMS, TB, dA, pref="a%d" % l)
            g1 = gather_all(ypairs, outs)
            if FSTOP == 1:
                S.emit(final_wait_ops=[g1])
                return nc, S
            last = l == 3
            hsrc3 = hT3 if l == 0 else hl3
            dB = dict(gnext=ext["gn%d" % (l + 1)], wg=ext["wg%d" % l], wbr=ext["wbr%d" % l], wo=ext["wo%d" % l],
                      h_src=lambda t0, N, hsrc3=hsrc3: (hsrc3[:, :, t0:t0 + N], ("d_h", t0)),
                      xn_srcB=xloc, y_load=y_load, need_cand=True, yidx=lambda br, kc: (kc // 2) * 6 + br * 2 + (kc % 2))
            if not last:
                dB["h_dst"] = lambda t0, N: (hl3[:, :, t0:t0 + N], ("d_h", t0))
                dB["x_dst"] = xloc
            else:
                dB["out_dst"] = lambda ob, t0, N: (out3[:, ob, t0 - NMETA:t0 - NMETA + N], ("d_out", ob, t0))
            with ExitStack() as es:
                outs = build_B(nc, S, es, TOKC_, dB, proj=True, last=last, skip_meta=last, pref="b%d" % l)
            if not last:
                gather_all(xpairs, outs)
            else:
                finals = outs
        S.emit(final_wait_ops=finals)
    return nc, S


SPL = [1024, 1024, 1024, 1024, 512, 512, 1024, 1024, 4, 4, 1024, 1024, 1024, 1024, 64, 64, 1024, 2048, 2048, 2048]
NAMES = ["a_q", "a_f", "a_i", "a_z", "b_q", "b_k", "b_v", "b_o", "b_ig", "b_fg", "b_z", "c_r", "c_k", "c_v", "c_wd", "c_ad", "c_z", "g_a", "g_b", "g_c"]
OFF = {}
_o = 0
for n_, w_ in zip(NAMES, SPL):
    OFF[n_] = _o
    _o += w_
NIN = _o


def colsA(j):
    cols = np.zeros(NCOLA, np.int64)

    def put(tile, start):
        cols[CT[tile] * 128:(CT[tile] + 1) * 128] = np.arange(start, start + 128)

    for hh in range(2):
        h = 2 * j + hh
        put("hq%d" % hh, OFF["a_q"] + h * 128)
        put("hf%d" % hh, OFF["a_f"] + h * 128)
        put("hi%d" % hh, OFF["a_i"] + h * 128)
        put("hz%d" % hh, OFF["a_z"] + h * 128)
    put("mq", OFF["b_q"] + j * 128)
    put("mk", OFF["b_k"] + j * 128)
    for i in range(2):
        put("mv%d" % i, OFF["b_v"] + j * 256 + i * 128)
        put("mo%d" % i, OFF["b_o"] + j * 256 + i * 128)
        put("mz%d" % i, OFF["b_z"] + j * 256 + i * 128)
    for p in range(2):
        c0 = j * 256 + p * 128
        put("rr%d" % p, OFF["c_r"] + c0)
        put("rk%d" % p, OFF["c_k"] + c0)
        put("rv%d" % p, OFF["c_v"] + c0)
        put("rz%d" % p, OFF["c_z"] + c0)
    cols[CT["rwa"] * 128:CT["rwa"] * 128 + 64] = np.arange(OFF["c_wd"], OFF["c_wd"] + 64)
    cols[CT["rwa"] * 128 + 64:CT["rwa"] * 128 + 128] = np.arange(OFF["c_ad"], OFF["c_ad"] + 64)
    cols[25 * 128] = OFF["b_ig"] + j
    cols[25 * 128 + 1] = OFF["b_fg"] + j
    return cols


def pack_A(inp, l, j):
    f = np.float32
    wA = np.ascontiguousarray(np.asarray(inp["w_in"][l])[:, colsA(j)], dtype=f)
    par = np.zeros((128, NPA), f)

    def put(nm, v):
        o, w = PA[nm]
        v = np.asarray(v, f)
        if v.ndim == 0:
            par[:, o:o + w] = v
        else:
            par[:, o:o + w] = v.reshape(128, w)

    lbl = np.asarray(inp["hgrn_lb_logits"])
    par[:, PA["lbsel"][0]:PA["lbsel"][0] + 4] = np.array([1.0 if 1 <= i <= l else 0.0 for i in range(4)], f)[None, :]
    for hh in range(2):
        ch = slice((2 * j + hh) * 128, (2 * j + hh + 1) * 128)
        put("lbl%d" % hh, lbl[:, ch].T)
        put("hg%d" % hh, np.asarray(inp["hgrn_norm_g"][l])[ch])
    cw = np.asarray(inp["mlstm_conv"][l])
    put("cwq", cw[:, j * 128:(j + 1) * 128].T)
    put("cwk", cw[:, 512 + j * 128:512 + (j + 1) * 128].T)
    for i in range(2):
        put("mg%d" % i, np.asarray(inp["mlstm_norm_g"][l])[j * 256 + i * 128:j * 256 + (i + 1) * 128])
    put("igb", float(np.asarray(inp["mlstm_ig_b"])[l, j]))
    put("fgb", float(np.asarray(inp["mlstm_fg_b"])[l, j]))
    put("eps", 1e-6)
    put("lneps", 64e-5)
    put("zero", 0.0)
    mu = np.asarray(inp["rwkv_mu"][l])
    for p in range(2):
        c = slice(j * 256 + p * 128, j * 256 + (p + 1) * 128)
        put("mur%d" % p, mu[0:1024][c])
        put("muk%d" % p, mu[1024:2048][c])
        put("muv%d" % p, mu[2048:3072][c])
        put("w0%d" % p, np.asarray(inp["rwkv_w0"][l])[c])
        put("a0%d" % p, np.asarray(inp["rwkv_a0"][l])[c])
        put("kk%d" % p, np.asarray(inp["rwkv_k_k"][l])[c])
        put("ka%d" % p, np.asarray(inp["rwkv_k_a"][l])[c])
        put("rk%d" % p, np.asarray(inp["rwkv_r_k"][l])[c])
        put("lg%d" % p, np.asarray(inp["rwkv_ln_g"][l])[c])
        put("lb%d" % p, np.asarray(inp["rwkv_ln_b"][l])[c])
    put("muwa", mu[3072:3200])
    wud = np.zeros((128, 256), f)
    wud[0:64] = np.asarray(inp["rwkv_w_up"][l])[:, j * 256:(j + 1) * 256]
    wud[64:128] = np.asarray(inp["rwkv_a_up"][l])[:, j * 256:(j + 1) * 256]
    return dict(wA=wA, parA=par, wud=wud)


TB_A = 128
SEQ = 8192
NMS_A = SEQ // TB_A


def kernel(**inp):
    f = np.float32
    x = np.asarray(inp["x"], f)
    meta = np.asarray(inp["meta_tokens"], f)
    Q = SEQ // 4
    nc, _S = prog_fused(NMS_A, TB_A)
    cs = host_consts()

    def gn(g):
        return np.ascontiguousarray(np.asarray(g, f).reshape(16, 128).T)

    gns = [gn(inp["norm_g"][l]) for l in range(4)] + [gn(inp["final_norm_g"])]
    shared = {}
    for l in range(4):
        w_in_l = np.asarray(inp["w_in"][l])
        shared["wg%d" % l] = np.ascontiguousarray(w_in_l[:, OFF["g_a"]:OFF["g_a"] + 6144], dtype=f)
        shared["wbr%d" % l] = np.ascontiguousarray(np.asarray(inp["w_br"][l], f).reshape(3072, 2048))
        shared["wo%d" % l] = np.ascontiguousarray(np.asarray(inp["w_out"][l], f))
    packs = {}
    for j in range(4):
        for l in range(4):
            packs[(l, j)] = pack_A(inp, l, j)
    maps = []
    for c in range(8):
        b, j = divmod(c, 4)
        m = {"hT": np.ascontiguousarray(np.concatenate([meta, x[b, j * Q:(j + 1) * Q]], axis=0).T)}
        sel = np.zeros((128, 4), f)
        sel[:, j] = 1.0
        m["sel"] = sel
        m.update(cs)
        for l in range(5):
            m["gn%d" % l] = gns[l]
        for l in range(4):
            pa = packs[(l, j)]
            m["wA%d" % l] = pa["wA"]
            m["parA%d" % l] = pa["parA"]
            m["wud%d" % l] = pa["wud"]
        m.update(shared)
        maps.append(m)
    res = run_bass_kernel_spmd(nc, maps, core_ids=list(range(8)))
    out = np.empty((2, SEQ, 2048), f)
    for c in range(8):
        b, j = divmod(c, 4)
        out[b, j * Q:(j + 1) * Q] = np.asarray(res.results[c]["out"], f).T
    return out
```

```python
import ml_dtypes
from concourse.bass_utils import run_bass_kernel_spmd
import numpy as np
import concourse.bass as bass
import concourse.mybir as mybir
from contextlib import ExitStack

F32 = mybir.dt.float32
BF16 = mybir.dt.bfloat16
AF = mybir.ActivationFunctionType
ALU = mybir.AluOpType

COMPUTE = ("pe", "act", "dve", "pool")
DMAQ = ("sp", "poolq")
NDMASEM = 8
import os as _os
FUSEWAIT = _os.environ.get('FUSEWAIT', '1') == '1'


class Sched:
    def __init__(self, nc, strict_same=True):
        self.nc = nc
        self.ops = []
        self.last_w = {}
        self.readers = {}
        self.strict_same = strict_same
        self.pe_last = {}
        self.keymap = {}
        self.bar = set()
        self.epoch = 0
        self.last_eng = {}
        self.dma_since = []

    def op(self, eng, fn, reads=(), writes=(), dma=False, rowgrp=None, extra=(), cc=False):
        i = len(self.ops)
        if self.keymap:
            km = self.keymap

            def _mk(k):
                if isinstance(k, tuple) and k and k[0] in km:
                    return (km[k[0]],) + tuple(k[1:])
                if isinstance(k, str) and k in km:
                    return km[k]
                return k
            reads = [_mk(k) for k in reads]
            writes = [_mk(k) for k in writes]
        ex = [k for k in reads if isinstance(k, str) and k[0] == "B" and len(k) <= 2]
        if ex:
            reads = [k for k in reads if k not in ex]
            writes = list(writes) + [k for k in ex if k not in writes]
        deps = set()
        for k in reads:
            w = self.last_w.get(k)
            if w is not None:
                deps.add(w)
        for k in writes:
            w = self.last_w.get(k)
            if w is not None:
                deps.add(w)
            for r in self.readers.get(k, {}).values():
                for x in r:
                    deps.add(x)
        deps.discard(i)
        forced = set()
        if eng == "pe":
            for k in writes:
                if isinstance(k, str) and k[0] == "B" and len(k) <= 2:
                    pl = self.pe_last.get(k)
                    if pl is not None and pl[1] != rowgrp:
                        forced.add(pl[0])
                    self.pe_last[k] = (i, rowgrp)
        deps |= forced
        deps |= set(extra)
        deps |= self.bar
        self.ops.append(dict(eng=eng, fn=fn, deps=deps, dma=dma, forced=forced, cc=cc, epoch=self.epoch))
        self.last_eng[eng] = i
        if dma or cc:
            self.dma_since.append(i)
        for k in writes:
            self.last_w[k] = i
            self.readers[k] = {}
        for k in reads:
            d = self.readers.setdefault(k, {})
            if dma:
                d.setdefault(eng + "_dma", []).append(i)
            else:
                d[eng] = [i]
        return i

    def barrier(self):
        self.bar = set(self.last_eng.values()) | set(self.dma_since)
        self.dma_since = []
        self.epoch += 1
        self.pe_last = {}

    def cc(self, fn, reads=(), writes=(), extra=()):
        return self.op("pool", fn, reads, writes, cc=True, extra=extra)

    def pe(self, fn, reads=(), writes=(), rowgrp=None):
        return self.op("pe", fn, reads, writes, rowgrp=rowgrp)

    def act(self, fn, reads=(), writes=()):
        return self.op("act", fn, reads, writes)

    def dve(self, fn, reads=(), writes=()):
        return self.op("dve", fn, reads, writes)

    def pool(self, fn, reads=(), writes=()):
        return self.op("pool", fn, reads, writes)

    def dma(self, q, fn, reads=(), writes=()):
        return self.op(q, fn, reads, writes, dma=True)

    def emit(self, final_wait_ops=()):
        nc = self.nc
        ops = self.ops
        streams = {"pe": [], "act": [], "dve": [], "pool": [], "sp": []}
        for i, o in enumerate(ops):
            streams[o["eng"]].append(i)
        need = [False] * len(ops)
        for i, o in enumerate(ops):
            for d in o["deps"]:
                od = ops[d]
                if od["dma"] or od["cc"]:
                    need[d] = True
                elif od["eng"] != o["eng"]:
                    need[d] = True
                elif o["eng"] != "pe" and self.strict_same:
                    need[d] = True
                elif d in o["forced"]:
                    need[d] = True
        for d in final_wait_ops:
            need[d] = True
        with ExitStack() as es:
            nep = self.epoch + 1
            csem = {(e, ep): es.enter_context(nc.semaphore("c_%s%d" % (e, ep))) for e in COMPUTE for ep in range(nep)}
            ccsem = es.enter_context(nc.semaphore("ccsem"))
            cccnt = 0
            dsem = {
                q: [es.enter_context(nc.semaphore("d_%s%d" % (q, k))) for k in range(NDMASEM)]
                for q in ("sp", "pool")
            }
            cnt = {(e, ep): 0 for e in COMPUTE for ep in range(nep)}
            dcnt = {"sp": 0, "pool": 0}
            sig = [None] * len(ops)
            prev_same_sem = [None] * len(ops)
            for i, o in enumerate(ops):
                if o["dma"]:
                    q = o["eng"]
                    n = dcnt[q]
                    dcnt[q] += 1
                    s = dsem[q][n % NDMASEM]
                    sig[i] = (("d", q, n % NDMASEM), s, 16 * (n // NDMASEM + 1))
                    if n >= NDMASEM:
                        prev_same_sem[i] = (("d", q, n % NDMASEM), s, 16 * (n // NDMASEM))
                elif o["cc"]:
                    cccnt += 1
                    sig[i] = (("cc",), ccsem, cccnt)
                elif need[i]:
                    e = (o["eng"], o["epoch"])
                    cnt[e] += 1
                    sig[i] = (("c", e), csem[e], cnt[e])
            self.stats = dict(n_ops=len(ops), cnt={str(k): v for k, v in cnt.items()}, dcnt=dict(dcnt),
                              per_eng={e: len(v) for e, v in streams.items()})
            blk = es.enter_context(nc.Block())

            def run_stream(ename, eobj):
                known = {}
                nwait = 0
                for i in streams[ename]:
                    o = ops[i]
                    waits = {}
                    if prev_same_sem[i] is not None:
                        k, s, v = prev_same_sem[i]
                        if known.get(k, 0) < v:
                            waits[k] = (s, v)
                    for d in o["deps"]:
                        od = ops[d]
                        if not od["dma"] and not od["cc"] and od["eng"] == ename and (ename == "pe" or not self.strict_same) and d not in o["forced"]:
                            continue
                        if sig[d] is None:
                            continue
                        k, s, v = sig[d]
                        if known.get(k, 0) < v and (k not in waits or waits[k][1] < v):
                            waits[k] = (s, v)
                    wl = list(waits.items())
                    fuse = None
                    if FUSEWAIT and wl and not o["cc"]:
                        fuse = wl.pop()
                    for k, (s, v) in wl:
                        eobj.wait_ge(s, v)
                        known[k] = v
                        nwait += 1
                    ins = o["fn"](eobj)
                    if fuse is not None:
                        k, (s, v) = fuse
                        ins._wait_ge(s, v)
                        known[k] = v
                    if sig[i] is not None:
                        ins.then_inc(sig[i][1], 16 if o["dma"] else 1)
                if ename == "sp":
                    for d in final_wait_ops:
                        k, s, v = sig[d]
                        eobj.wait_ge(s, v)
                self.stats["waits_" + ename] = nwait

            blk.sync(lambda e: run_stream("sp", e))
            blk.tensor(lambda e: run_stream("pe", e))
            blk.scalar(lambda e: run_stream("act", e))
            blk.vector(lambda e: run_stream("dve", e))
            blk.gpsimd(lambda e: run_stream("pool", e))


import math, os
RSTOP = int(os.environ.get('RSTOP', '9'))

D = 2048
KC = 16
NMETA = 16
LCH = 64
C0 = math.exp(-0.5)
CT = dict(hq0=0, hq1=1, hf0=2, hf1=3, hi0=4, hi1=5, hz0=6, hz1=7,
          mq=8, mk=9, mv0=10, mv1=11, mo0=12, mo1=13, mz0=14, mz1=15,
          rr0=16, rr1=17, rk0=18, rk1=19, rv0=20, rv1=21, rz0=22, rz1=23, rwa=24)
NCOLA = 25 * 128 + 2
PA = {}
_n = 0
for nm, w in [("lbsel", 4), ("lbl0", 4), ("lbl1", 4), ("hg0", 1), ("hg1", 1), ("cwq", 4), ("cwk", 4), ("mg0", 1), ("mg1", 1),
              ("igb", 1), ("fgb", 1), ("eps", 1), ("lneps", 1), ("zero", 1),
              ("mur0", 1), ("mur1", 1), ("muk0", 1), ("muk1", 1), ("muv0", 1), ("muv1", 1), ("muwa", 1),
              ("w00", 1), ("w01", 1), ("a00", 1), ("a01", 1), ("kk0", 1), ("kk1", 1), ("ka0", 1), ("ka1", 1),
              ("rk0", 1), ("rk1", 1), ("lg0", 1), ("lg1", 1), ("lb0", 1), ("lb1", 1)]:
    PA[nm] = (_n, w)
    _n += w
NPA = _n


def host_consts():
    c = {}
    c["ident"] = np.eye(128, dtype=np.float32)
    c["ones"] = np.ones((128, 128), np.float32)
    bd = np.zeros((128, 128), np.float32)
    bd[:64, :64] = 1
    bd[64:, 64:] = 1
    c["bd"] = bd
    s = np.arange(64)[:, None]
    t = np.arange(64)[None, :]
    mI = (s <= t).astype(np.float32)
    mS = (s < t).astype(np.float32)
    c["maskI"] = mI
    c["mask2"] = np.stack([mS, mI], axis=1)
    c["maskL"] = (t < s).astype(np.float32)
    pm = np.zeros((64, 2, 128), np.float32)
    pm[:, 0, :64] = 1
    pm[:, 1, 64:] = 1
    c["padmask"] = pm
    return c


CONST_SHAPES = dict(ident=[128, 128], ones=[128, 128], bd=[128, 128], maskI=[64, 64], mask2=[64, 2, 64],
                    maskL=[64, 64], padmask=[64, 2, 128])


class Ctx:
    pass


def build_A(nc, S, es, NMS, TB, dram, mixers=("h", "m", "r"), layer=0, first=True, pref=""):
    def sb(name, shape, dt=F32):
        return es.enter_context(nc.sbuf_tensor("s_" + pref + name, shape, dt))

    def ps(name, shape, dt=F32):
        return es.enter_context(nc.psum_tensor("p_" + pref + name, shape, dt))

    TBM = TB
    NCH = TB // LCH
    cst = {}
    for nm, shp in CONST_SHAPES.items():
        dt = F32 if nm in ("maskI", "mask2", "maskL", "padmask") else BF16
        cst[nm] = sb("c_" + nm, shp, dt)
        S.dma("pool", lambda e, nm=nm: e.dma_start(out=cst[nm][:], in_=dram[nm]), writes=["c_" + nm])
    ident64f = sb("ident64f", [64, 64])
    S.dma("sp", lambda e: e.dma_start(out=ident64f[:], in_=dram["ident"][0:64, 0:64]), writes=["ident64f"])
    bdmaskf = sb("bdmaskf", [128, 128])
    S.dma("sp", lambda e: e.dma_start(out=bdmaskf[:], in_=dram["bd"]), writes=["bdmaskf"])
    onesf = sb("onesf", [128, TBM])
    S.pool(lambda e: e.memset(onesf[:], 1.0), writes=["onesf"])
    par = sb("par", [128, NPA])
    S.dma("sp", lambda e: e.dma_start(out=par[:], in_=dram["parA"]), writes=["par"])

    def P(nm, i=0):
        o, w = PA[nm]
        return par[:, o + i:o + i + 1]

    wA = sb("wA", [128, KC, NCOLA], BF16)
    wsrc = dram["wA"].rearrange("(kc p) n -> p kc n", p=128)
    for kc in range(KC):
        S.dma("pool", lambda e, kc=kc: e.dma_start(out=wA[:, kc, :], in_=wsrc[:, kc, :]), writes=[("wA", kc)])
    WAK = [("wA", kc) for kc in range(KC)]
    wud = sb("wud", [128, 256], BF16)
    S.dma("pool", lambda e: e.dma_start(out=wud[:], in_=dram["wud"]), writes=["wud"])

    B0 = ps("B0", [128, 512]); B1 = ps("B1", [128, 512]); Bt = ps("Bt", [128, 1024], BF16)
    B2 = ps("B2", [128, 512]); B3 = ps("B3", [128, 512]); B4 = ps("B4", [128, 512])
    B5 = ps("B5", [128, 512]); B6 = ps("B6", [128, 512])
    ppbuf = [(B0[:, 0:256], "B0"), (B1[:, 0:256], "B1")]
    ppi = [0]

    NXB = 1 if TB >= 256 else 2
    xn = [sb("xn%d" % i, [128, KC, TBM], BF16) for i in range(NXB)]
    if "xn_src" in dram:
        xn_src = dram["xn_src"]
        y_dst = dram["y_dst"]
    else:
        xsrc = dram["xnT"].rearrange("(kc p) t -> p kc t", p=128)
        ydst = dram["yT"]

        def xn_src(t0, TBc):
            return xsrc[:, :, t0:t0 + TBc], "d_xin"

        def y_dst(r0, t0, TBc):
            return ydst[r0:r0 + 128, t0:t0 + TBc], ("d_yl", r0, t0)

    st = Ctx()
    if "h" in mixers:
        st.hS = sb("hS", [128, 2, 128]); st.hSb = sb("hSb", [128, 2, 128], BF16)
        S.pool(lambda e: e.memset(st.hS[:], 0.0), writes=["hS"])
        S.pool(lambda e: e.memset(st.hSb[:], 0.0), writes=["hSb"])
        st.lb = sb("lb", [128, 2]); st.oml = sb("oml", [128, 2]); st.lbm1 = sb("lbm1", [128, 2])
        lbe = sb("lbe", [128, 2, 4]); lbs = sb("lbs", [128, 2]); lbr = sb("lbr", [128, 2])
        o0, _ = PA["lbl0"]
        S.act(lambda e: e.activation(out=lbe[:].rearrange("p a b -> p (a b)"), in_=par[:, o0:o0 + 8], func=AF.Exp), reads=["par"], writes=["lbe"])
        S.dve(lambda e: e.tensor_reduce(out=lbs[:], in_=lbe[:], axis=mybir.AxisListType.X, op=ALU.add), reads=["lbe"], writes=["lbs"])
        S.dve(lambda e: e.reciprocal(out=lbr[:], in_=lbs[:]), reads=["lbs"], writes=["lbr"])
        lbt = sb("lbt", [128, 2]); lbm = sb("lbm", [128, 2, 4])
        osel, _ = PA["lbsel"]
        S.dve(lambda e: e.tensor_tensor(out=lbm[:], in0=lbe[:], in1=par[:, osel:osel + 4].unsqueeze(1).to_broadcast([128, 2, 4]), op=ALU.mult), reads=["lbe", "par"], writes=["lbm"])
        S.dve(lambda e: e.tensor_reduce(out=lbt[:], in_=lbm[:], axis=mybir.AxisListType.X, op=ALU.add), reads=["lbm"], writes=["lbt"])
        S.dve(lambda e: e.tensor_tensor(out=st.lb[:], in0=lbt[:], in1=lbr[:], op=ALU.mult), reads=["lbt", "lbr"], writes=["lb"])
        S.dve(lambda e: e.tensor_scalar(out=st.oml[:], in0=st.lb[:], scalar1=-1.0, scalar2=1.0, op0=ALU.mult, op1=ALU.add), reads=["lb"], writes=["oml"])
        S.dve(lambda e: e.tensor_scalar_add(out=st.lbm1[:], in0=st.lb[:], scalar1=-1.0), reads=["lb"], writes=["lbm1"])
    if "m" in mixers:
        st.mC = sb("mC", [128, 257]); st.mCb = sb("mCb", [128, 257], BF16)
        S.pool(lambda e: e.memset(st.mC[:], 0.0), writes=["mC"])
        st.mxq = sb("mxq", [128, 3 + TBM]); st.mxk = sb("mxk", [128, 3 + TBM])
        S.pool(lambda e: e.memset(st.mxq[:, 0:3], 0.0), writes=["mqx"])
        S.pool(lambda e: e.memset(st.mxk[:, 0:3], 0.0), writes=["mkx"])
        st.mmin = sb("mmin", [1, 1])
        S.pool(lambda e: e.memset(st.mmin[:], 0.0), writes=["mmin"])
        st.mTT = sb("mTT", [64, 3 * 128 + 1], BF16)
        S.pool(lambda e: e.memset(st.mTT[:], 1.0), writes=["mTT"])
        st.onesrow = sb("onesrow", [1, 128])
        S.pool(lambda e: e.memset(st.onesrow[:], 1.0), writes=["onesrow"])
    if "r" in mixers:
        st.rS = sb("rS", [128, 2, 128]); st.rSb = sb("rSb", [128, 2, 128], BF16)
        S.pool(lambda e: e.memset(st.rS[:], 0.0), writes=["rS"])
        S.pool(lambda e: e.memset(st.rSb[:], 0.0), writes=["rSb"])
        st.rraw = {}
        for nm in ("rr0", "rr1", "rk0", "rk1", "rv0", "rv1", "rwa"):
            st.rraw[nm] = sb("raw_" + nm, [128, 1 + TBM])
            S.pool(lambda e, nm=nm: e.memset(st.rraw[nm][:, 0:1], 0.0), writes=["raw_" + nm])

    W = {}

    ALIAS = {}
    if TB >= 256:
        AL = [("h_d1", "h_ft", "h_ft"), ("h_rs", "h_e1", "h_e1"), ("h_sq", "h_Qi", "h_Qi"),
              ("r_sgw", "h_qa", "h_qa"), ("r_a", "h_sig", "h_sig"), ("r_kkn", "h_sz", "h_sz"), ("r_kf", "h_k", "h_k"),
              ("r_bv", "h_ft", "h_ft"), ("r_cs", "h_cg", "h_cg"), ("r_tmp", "h_e1", "h_e1"), ("r_ecw", "h_eg", "h_eg"),
              ("r_yall", "h_oall", "h_oall"),
              ("r_tmpb", "h_vb", "h_vb"), ("r_Bd", "h_Qi", "h_Qi"), ("r_Kd", "h_Ki", "h_Ki"), ("r_Btl", "h_Qx", "h_Qx"),
              ("r_Ktl", "h_Kt", "h_Kt"), ("r_vb", "h_yo", "h_yo"),
              ("r_yb", "h_Qi", "h_Qi"), ("r_sq", "h_Ki", "h_Ki"), ("r_yo", "h_Qx", "h_Qx"),
              ("m_mean", "m_qs", "m_mqs"), ("m_var", "m_ks", "m_mks"), ("r_mean", "r_m_rr0", "r_m_rr0"), ("r_var", "r_m_rr1", "r_m_rr1")]
        for a, t_, k_ in AL:
            ALIAS[a] = t_
            S.keymap[a] = k_

    def wt(name, shape, dt=F32):
        if name in ALIAS:
            return W[ALIAS[name]]
        if name not in W:
            W[name] = sb("w_" + name, shape, dt)
        return W[name]

    rr = [0]

    def ev(out, in_, reads, writes):
        rr[0] ^= 1
        if rr[0]:
            return S.dve(lambda e: e.tensor_copy(out=out, in_=in_), reads, writes)
        return S.act(lambda e: e.copy(out=out, in_=in_), reads, writes)

    def inproj(xt, xkey, col0, ncols, TBc):
        buf, key = ppbuf[ppi[0] % 2]
        ppi[0] += 1
        out = buf[0:ncols, 0:TBc]
        for kc in range(KC):
            S.pe(lambda e, kc=kc: e.matmul(out, wA[:, kc, col0:col0 + ncols], xt[:, kc, 0:TBc], start=(kc == 0), stop=(kc == KC - 1)),
                 reads=[xkey, ("wA", kc)], writes=[key])
        return out, key

    def macro(ms, t0, TBc, L):
        nch = TBc // L
        xt = xn[ms % NXB]
        xkey = "xn%d" % (ms % NXB)
        xin_, xink_ = xn_src(t0, TBc)
        S.dma("sp", lambda e: e.dma_start(out=xt[:, :, 0:TBc], in_=xin_), reads=[xink_], writes=[xkey])

        def c3(ap):
            return ap.rearrange("p (c j) -> p c j", j=L)

        if "h" in mixers:
            qa = wt("h_qa", [128, 2, TBM]); sig = wt("h_sig", [128, 2, TBM]); vb = wt("h_vb", [128, 2, TBM], BF16)
            sz = wt("h_sz", [128, 2, TBM])
            for hh in range(2):
                p_, k_ = inproj(xt, xkey, CT["hq%d" % hh] * 128, 128, TBc)
                S.act(lambda e, p_=p_, hh=hh: e.activation(out=qa[:, hh, 0:TBc], in_=p_, func=AF.Silu), reads=[k_], writes=[("h_qa", hh)])
                p_, k_ = inproj(xt, xkey, CT["hf%d" % hh] * 128, 128, TBc)
                S.act(lambda e, p_=p_, hh=hh: e.activation(out=sig[:, hh, 0:TBc], in_=p_, func=AF.Sigmoid), reads=[k_], writes=[("h_sig", hh)])
                p_, k_ = inproj(xt, xkey, CT["hi%d" % hh] * 128, 128, TBc)
                S.dve(lambda e, p_=p_, hh=hh: e.tensor_copy(out=vb[:, hh, 0:TBc], in_=p_), reads=[k_], writes=[("h_vb", hh)])
                p_, k_ = inproj(xt, xkey, CT["hz%d" % hh] * 128, 128, TBc)
                S.act(lambda e, p_=p_, hh=hh: e.activation(out=sz[:, hh, 0:TBc], in_=p_, func=AF.Silu), reads=[k_], writes=[("h_sz", hh)])
            kk = wt("h_k", [128, 2, TBM]); ft = wt("h_ft", [128, 2, TBM]); cg = wt("h_cg", [128, 2, TBM])
            d1 = wt("h_d1", [128, 2, TBM]); e1 = wt("h_e1", [128, 2, TBM]); eg = wt("h_eg", [128, 2, TBM])
            Qi = wt("h_Qi", [128, 2, TBM], BF16); Ki = wt("h_Ki", [128, 2, TBM], BF16)
            Qx = wt("h_Qx", [128, 2, TBM], BF16); Kt = wt("h_Kt", [128, 2, TBM], BF16)
            mid = L // 2
            for hh in range(2):
                S.dve(lambda e, hh=hh: e.tensor_scalar(out=kk[:, hh, 0:TBc], in0=sig[:, hh, 0:TBc], scalar1=-1.0, scalar2=st.lbm1[:, hh:hh + 1], op0=ALU.add, op1=ALU.mult),
                      reads=[("h_sig", hh), "lbm1"], writes=[("h_k", hh)])
                S.dve(lambda e, hh=hh: e.tensor_scalar(out=ft[:, hh, 0:TBc], in0=sig[:, hh, 0:TBc], scalar1=st.oml[:, hh:hh + 1], scalar2=st.lb[:, hh:hh + 1], op0=ALU.mult, op1=ALU.add),
                      reads=[("h_sig", hh), "oml", "lb"], writes=[("h_ft", hh)])
                S.dve(lambda e, hh=hh: e.tensor_scalar_max(out=ft[:, hh, 0:TBc], in0=ft[:, hh, 0:TBc], scalar1=1e-12), reads=[("h_ft", hh)], writes=[("h_ft", hh)])
                S.act(lambda e, hh=hh: e.activation(out=ft[:, hh, 0:TBc], in_=ft[:, hh, 0:TBc], func=AF.Ln), reads=[("h_ft", hh)], writes=[("h_ft", hh)])
                for c in range(nch):
                    S.dve(lambda e, hh=hh, c=c: e.tensor_tensor_scan(out=cg[:, hh, c * L:(c + 1) * L], data0=onesf[:, 0:L], data1=ft[:, hh, c * L:(c + 1) * L], initial=0.0, op0=ALU.mult, op1=ALU.add),
                          reads=[("h_ft", hh), "onesf"], writes=[("h_cg", hh)])
                cg3 = c3(cg[:, hh, 0:TBc])
                S.dve(lambda e, hh=hh, cg3=cg3: e.tensor_tensor(out=c3(d1[:, hh, 0:TBc]), in0=cg3, in1=cg3[:, :, mid:mid + 1].to_broadcast([128, nch, L]), op=ALU.subtract),
                      reads=[("h_cg", hh)], writes=[("h_d1", hh)])
                S.act(lambda e, hh=hh: e.activation(out=e1[:, hh, 0:TBc], in_=d1[:, hh, 0:TBc], func=AF.Exp), reads=[("h_d1", hh)], writes=[("h_e1", hh)])
                S.dve(lambda e, hh=hh: e.scalar_tensor_tensor(out=Qi[:, hh, 0:TBc], in0=qa[:, hh, 0:TBc], scalar=128.0 ** -0.5, in1=e1[:, hh, 0:TBc], op0=ALU.mult, op1=ALU.mult),
                      reads=[("h_qa", hh), ("h_e1", hh)], writes=[("h_Qi", hh)])
                S.act(lambda e, hh=hh: e.activation(out=e1[:, hh, 0:TBc], in_=d1[:, hh, 0:TBc], func=AF.Exp, scale=-1.0), reads=[("h_d1", hh), ("h_e1", hh)], writes=[("h_e1", hh)])
                S.dve(lambda e, hh=hh: e.tensor_tensor(out=Ki[:, hh, 0:TBc], in0=kk[:, hh, 0:TBc], in1=e1[:, hh, 0:TBc], op=ALU.mult),
                      reads=[("h_k", hh), ("h_e1", hh)], writes=[("h_Ki", hh)])
                S.act(lambda e, hh=hh: e.activation(out=eg[:, hh, 0:TBc], in_=cg[:, hh, 0:TBc], func=AF.Exp), reads=[("h_cg", hh)], writes=[("h_eg", hh)])
                S.dve(lambda e, hh=hh: e.scalar_tensor_tensor(out=Qx[:, hh, 0:TBc], in0=qa[:, hh, 0:TBc], scalar=128.0 ** -0.5, in1=eg[:, hh, 0:TBc], op0=ALU.mult, op1=ALU.mult),
                      reads=[("h_qa", hh), ("h_eg", hh)], writes=[("h_Qx", hh)])
                S.dve(lambda e, hh=hh, cg3=cg3: e.tensor_tensor(out=c3(d1[:, hh, 0:TBc]), in0=cg3[:, :, L - 1:L].to_broadcast([128, nch, L]), in1=cg3, op=ALU.subtract),
                      reads=[("h_cg", hh), ("h_d1", hh)], writes=[("h_d1", hh)])
                S.act(lambda e, hh=hh: e.activation(out=e1[:, hh, 0:TBc], in_=d1[:, hh, 0:TBc], func=AF.Exp), reads=[("h_d1", hh), ("h_e1", hh)], writes=[("h_e1", hh)])
                S.dve(lambda e, hh=hh: e.tensor_tensor(out=Kt[:, hh, 0:TBc], in0=kk[:, hh, 0:TBc], in1=e1[:, hh, 0:TBc], op=ALU.mult),
                      reads=[("h_k", hh), ("h_e1", hh)], writes=[("h_Kt", hh)])
            oall = wt("h_oall", [128, 2, TBM])
            hT = wt("h_T", [64, 4, 128], BF16); hatt = wt("h_att", [64, 2, 64], BF16)
            for c in range(nch):
                sl = slice(c * L, (c + 1) * L)
                h_trp = Bt[0:L, 0:512].rearrange("p (a b) -> p a b", b=128)
                for hh in range(2):
                    S.pe(lambda e, hh=hh, sl=sl: e.transpose(h_trp[:, hh, :], vb[:, hh, sl], cst["ident"][:]), reads=[("h_vb", hh), "c_ident"], writes=["Bt"])
                    S.pe(lambda e, hh=hh, sl=sl: e.transpose(h_trp[:, 2 + hh, :], Kt[:, hh, sl], cst["ident"][:]), reads=[("h_Kt", hh), "c_ident"], writes=["Bt"])
                ev(hT[0:L], h_trp, reads=["Bt"], writes=["h_T"])
                h_scp = B2[0:L, 0:128].rearrange("p (a b) -> p a b", b=64)
                for hh in range(2):
                    S.pe(lambda e, hh=hh, sl=sl: e.matmul(h_scp[:, hh, 0:L], Ki[:, hh, sl], Qi[:, hh, sl], start=True, stop=True), reads=[("h_Ki", hh), ("h_Qi", hh)], writes=["B2"])
                S.dve(lambda e: e.tensor_tensor(out=hatt[0:L, :, 0:L], in0=h_scp[:, :, 0:L], in1=cst["maskI"][0:L, 0:L].unsqueeze(1).to_broadcast([L, 2, L]), op=ALU.mult),
                      reads=["B2", "c_maskI"], writes=["h_att"])
                h_op = B5[:, 0:128].rearrange("p (a b) -> p a b", b=64)
                for hh in range(2):
                    S.pe(lambda e, hh=hh: e.matmul(h_op[:, hh, 0:L], hT[0:L, hh, :], hatt[0:L, hh, 0:L], start=True, stop=False), reads=["h_T", "h_att"], writes=["B5"])
                    S.pe(lambda e, hh=hh, sl=sl: e.matmul(h_op[:, hh, 0:L], st.hSb[:, hh, :], Qx[:, hh, sl], start=False, stop=True), reads=["hSb", ("h_Qx", hh)], writes=["B5"])
                ev(oall[:, :, sl], h_op[:, :, 0:L], reads=["B5"], writes=["h_oall"])
                h_up = B1[:, 0:256].rearrange("p (a b) -> p a b", b=128)
                for hh in range(2):
                    S.pe(lambda e, hh=hh: e.matmul(h_up[:, hh, :], hT[0:L, 2 + hh, :], hT[0:L, hh, :], start=True, stop=True), reads=["h_T"], writes=["B1"])
                for hh in range(2):
                    S.dve(lambda e, hh=hh, c=c: e.scalar_tensor_tensor(out=st.hS[:, hh, :], in0=st.hS[:, hh, :], scalar=eg[:, hh, c * L + L - 1:c * L + L], in1=h_up[:, hh, :], op0=ALU.mult, op1=ALU.add),
                          reads=["hS", ("h_eg", hh), "B1"], writes=["hS"])
                S.act(lambda e: e.copy(out=st.hSb[:], in_=st.hS[:]), reads=["hS"], writes=["hSb"])
            sq = wt("h_sq", [128, 2, TBM], BF16); rs = wt("h_rs", [128, 2, TBM]); yo = wt("h_yo", [128, 2, TBM], BF16)
            for hh in range(2):
                S.act(lambda e, hh=hh: e.activation(out=sq[:, hh, 0:TBc], in_=oall[:, hh, 0:TBc], func=AF.Square), reads=["h_oall"], writes=[("h_sq", hh)])
                ssp = B3[:, hh * 256:hh * 256 + TBc]
                S.pe(lambda e, hh=hh, ssp=ssp: e.matmul(ssp, cst["ones"][:], sq[:, hh, 0:TBc], start=True, stop=True), reads=[("h_sq", hh), "c_ones"], writes=["B3"])
                S.act(lambda e, hh=hh, ssp=ssp: e.activation(out=rs[:, hh, 0:TBc], in_=ssp, func=AF.Sqrt, bias=P("eps"), scale=1.0 / 128), reads=["B3", "par"], writes=[("h_rs", hh)])
                S.dve(lambda e, hh=hh: e.reciprocal(out=rs[:, hh, 0:TBc], in_=rs[:, hh, 0:TBc]), reads=[("h_rs", hh)], writes=[("h_rs", hh)])
                S.dve(lambda e, hh=hh: e.tensor_tensor(out=rs[:, hh, 0:TBc], in0=rs[:, hh, 0:TBc], in1=oall[:, hh, 0:TBc], op=ALU.mult), reads=[("h_rs", hh), "h_oall"], writes=[("h_rs", hh)])
                S.dve(lambda e, hh=hh: e.scalar_tensor_tensor(out=yo[:, hh, 0:TBc], in0=rs[:, hh, 0:TBc], scalar=P("hg%d" % hh), in1=sz[:, hh, 0:TBc], op0=ALU.mult, op1=ALU.mult),
                      reads=[("h_rs", hh), "par", ("h_sz", hh)], writes=[("h_yo", hh)])
                yd_, ydk_ = y_dst(hh * 128, t0, TBc)
                outs.append(S.dma("pool", lambda e, hh=hh, yd_=yd_: e.dma_start(out=yd_, in_=yo[:, hh, 0:TBc]), reads=[("h_yo", hh)], writes=[ydk_]))

        if "m" in mixers:
            for nm, buf in (("mq", st.mxq), ("mk", st.mxk)):
                p_, k_ = inproj(xt, xkey, CT[nm] * 128, 128, TBc)
                ev(buf[:, 3:3 + TBc], p_, reads=[k_], writes=[nm + "x"])
            mvb = wt("m_vb", [128, 2, TBM], BF16); mso = wt("m_so", [128, 2, TBM]); msz = wt("m_sz", [128, 2, TBM])
            for i in range(2):
                p_, k_ = inproj(xt, xkey, CT["mv%d" % i] * 128, 128, TBc)
                S.dve(lambda e, p_=p_, i=i: e.tensor_copy(out=mvb[:, i, 0:TBc], in_=p_), reads=[k_], writes=[("m_vb", i)])
                p_, k_ = inproj(xt, xkey, CT["mo%d" % i] * 128, 128, TBc)
                S.act(lambda e, p_=p_, i=i: e.activation(out=mso[:, i, 0:TBc], in_=p_, func=AF.Sigmoid), reads=[k_], writes=[("m_so", i)])
                p_, k_ = inproj(xt, xkey, CT["mz%d" % i] * 128, 128, TBc)
                S.act(lambda e, p_=p_, i=i: e.activation(out=msz[:, i, 0:TBc], in_=p_, func=AF.Silu), reads=[k_], writes=[("m_sz", i)])
            rows = wt("m_rows", [1, 8, TBM])
            p_, k_ = inproj(xt, xkey, 25 * 128, 1, TBc)
            S.act(lambda e, p_=p_: e.activation(out=rows[:, 0, 0:TBc], in_=p_, func=AF.Identity, bias=par[0:1, PA["igb"][0]:PA["igb"][0] + 1]), reads=[k_, "par"], writes=[("m_rows", 0)])
            p_, k_ = inproj(xt, xkey, 25 * 128 + 1, 1, TBc)
            S.act(lambda e, p_=p_: e.activation(out=rows[:, 1, 0:TBc], in_=p_, func=AF.Sigmoid, bias=par[0:1, PA["fgb"][0]:PA["fgb"][0] + 1]), reads=[k_, "par"], writes=[("m_rows", 1)])
            S.act(lambda e: e.activation(out=rows[:, 1, 0:TBc], in_=rows[:, 1, 0:TBc], func=AF.Ln), reads=[("m_rows", 1)], writes=[("m_rows", 1)])
            S.dve(lambda e: e.tensor_tensor_scan(out=rows[:, 2, 0:TBc], data0=onesf[0:1, 0:TBc], data1=rows[:, 1, 0:TBc], initial=0.0, op0=ALU.mult, op1=ALU.add),
                  reads=[("m_rows", 1), "onesf"], writes=[("m_rows", 2)])
            S.dve(lambda e: e.tensor_tensor(out=rows[:, 3, 0:TBc], in0=rows[:, 0, 0:TBc], in1=rows[:, 2, 0:TBc], op=ALU.subtract), reads=[("m_rows", 0), ("m_rows", 2)], writes=[("m_rows", 3)])
            al0 = wt("m_al0", [1, 8])
            S.dve(lambda e: e.tensor_copy(out=al0[:, 0:1], in_=st.mmin[:]), reads=["mmin"], writes=["m_al0"])
            S.dve(lambda e: e.tensor_tensor_scan(out=rows[:, 4, 0:TBc], data0=onesf[0:1, 0:TBc], data1=rows[:, 3, 0:TBc], initial=st.mmin[:], op0=ALU.mult, op1=ALU.max),
                  reads=[("m_rows", 3), "onesf", "mmin"], writes=[("m_rows", 4)])
            Al3 = rows[:, 4, 0:TBc].rearrange("p (c j) -> p c j", j=L)
            al3 = rows[:, 3, 0:TBc].rearrange("p (c j) -> p c j", j=L)
            if nch > 1:
                S.dve(lambda e: e.tensor_copy(out=al0[:, 1:nch], in_=Al3[:, 0:nch - 1, L - 1]), reads=[("m_rows", 4), "m_al0"], writes=["m_al0"])
            cn_b = Al3[:, :, L - 1:L].to_broadcast([1, nch, L])
            S.dve(lambda e: e.tensor_tensor(out=rows[:, 5, 0:TBc].rearrange("p (c j) -> p c j", j=L), in0=al3, in1=cn_b, op=ALU.subtract), reads=[("m_rows", 3), ("m_rows", 4)], writes=[("m_rows", 5)])
            S.dve(lambda e: e.tensor_tensor(out=rows[:, 6, 0:TBc].rearrange("p (c j) -> p c j", j=L), in0=cn_b, in1=Al3, op=ALU.subtract), reads=[("m_rows", 4)], writes=[("m_rows", 6)])
            car = wt("m_car", [1, 8])
            S.dve(lambda e: e.tensor_tensor(out=car[:, 0:nch], in0=al0[:, 0:nch], in1=Al3[:, :, L - 1], op=ALU.subtract), reads=["m_al0", ("m_rows", 4)], writes=["m_car"])
            S.act(lambda e: e.activation(out=rows[:, 5:7, 0:TBc], in_=rows[:, 5:7, 0:TBc], func=AF.Exp), reads=[("m_rows", 5), ("m_rows", 6)], writes=[("m_rows", 5), ("m_rows", 6)])
            S.act(lambda e: e.activation(out=car[:, 0:nch], in_=car[:, 0:nch], func=AF.Exp), reads=["m_car"], writes=["m_car"])
            S.dve(lambda e: e.tensor_tensor(out=rows[:, 7, 0:TBc], in0=rows[:, 2, 0:TBc], in1=rows[:, 4, 0:TBc], op=ALU.add), reads=[("m_rows", 2), ("m_rows", 4)], writes=[("m_rows", 7)])
            S.dve(lambda e: e.tensor_copy(out=st.mmin[:], in_=rows[:, 7, TBc - 1:TBc]), reads=[("m_rows", 7), "mmin"], writes=["mmin"])
            S.act(lambda e: e.activation(out=rows[:, 7, 0:TBc], in_=rows[:, 7, 0:TBc], func=AF.Exp, scale=-1.0), reads=[("m_rows", 7)], writes=[("m_rows", 7)])
            bws = B3[:, 0:TBc]; bwt = B3[:, 256:256 + TBc]; bcar = B4[:, 0:nch]
            S.pe(lambda e: e.matmul(bws, st.onesrow[:], rows[:, 5, 0:TBc], start=True, stop=True), reads=["onesrow", ("m_rows", 5)], writes=["B3"])
            S.pe(lambda e: e.matmul(bwt, st.onesrow[:], rows[:, 6, 0:TBc], start=True, stop=True), reads=["onesrow", ("m_rows", 6)], writes=["B3"])
            S.pe(lambda e: e.matmul(bcar, st.onesrow[:], car[:, 0:nch], start=True, stop=True), reads=["onesrow", "m_car"], writes=["B4"])
            carb = wt("m_carb", [128, 8])
            S.act(lambda e: e.copy(out=carb[:, 0:nch], in_=bcar), reads=["B4"], writes=["m_carb"])
            qs = wt("m_qs", [128, TBM]); ks = wt("m_ks", [128, TBM]); kp = wt("m_kp", [128, TBM], BF16); qpp = wt("m_qpp", [128, TBM], BF16)
            for nm, buf, dst, cw in (("mq", st.mxq, qs, "cwq"), ("mk", st.mxk, ks, "cwk")):
                S.dve(lambda e, buf=buf, dst=dst, cw=cw: e.tensor_scalar_mul(out=dst[:, 0:TBc], in0=buf[:, 0:TBc], scalar1=P(cw, 0)), reads=[nm + "x", "par"], writes=["m_" + nm + "s"])
                for i in range(1, 4):
                    S.dve(lambda e, buf=buf, dst=dst, cw=cw, i=i: e.scalar_tensor_tensor(out=dst[:, 0:TBc], in0=buf[:, i:i + TBc], scalar=P(cw, i), in1=dst[:, 0:TBc], op0=ALU.mult, op1=ALU.add),
                          reads=[nm + "x", "par", "m_" + nm + "s"], writes=["m_" + nm + "s"])
                S.act(lambda e, dst=dst: e.activation(out=dst[:, 0:TBc], in_=dst[:, 0:TBc], func=AF.Silu), reads=["m_" + nm + "s"], writes=["m_" + nm + "s"])
                S.pool(lambda e, buf=buf: e.tensor_copy(out=buf[:, 0:3], in_=buf[:, TBc:TBc + 3]), reads=[nm + "x"], writes=[nm + "x"])
            S.dve(lambda e: e.tensor_tensor(out=kp[:, 0:TBc], in0=ks[:, 0:TBc], in1=bws, op=ALU.mult), reads=["m_mks", "B3"], writes=["m_kp"])
            S.dve(lambda e: e.scalar_tensor_tensor(out=qpp[:, 0:TBc], in0=qs[:, 0:TBc], scalar=128.0 ** -0.5, in1=bwt, op0=ALU.mult, op1=ALU.mult), reads=["m_mqs", "B3"], writes=["m_qpp"])
            numall = wt("m_num", [128, 2, TBM]); denr = wt("m_den", [1, TBM]); matt = wt("m_att", [64, 64], BF16)
            for c in range(nch):
                sl = slice(c * L, (c + 1) * L)
                m_trp = Bt[0:L, 0:384].rearrange("p (a b) -> p a b", b=128)
                S.pe(lambda e, sl=sl: e.transpose(m_trp[:, 0, :], kp[:, sl], cst["ident"][:]), reads=["m_kp", "c_ident"], writes=["Bt"])
                for i in range(2):
                    S.pe(lambda e, sl=sl, i=i: e.transpose(m_trp[:, 1 + i, :], mvb[:, i, sl], cst["ident"][:]), reads=[("m_vb", i), "c_ident"], writes=["Bt"])
                ev(st.mTT[0:L, 0:384], Bt[0:L, 0:384], reads=["Bt"], writes=["mTT"])
                S.dve(lambda e, c=c: e.tensor_scalar_mul(out=st.mC[:], in0=st.mC[:], scalar1=carb[:, c:c + 1]), reads=["mC", "m_carb"], writes=["mC"])
                S.act(lambda e: e.copy(out=st.mCb[:], in_=st.mC[:]), reads=["mC"], writes=["mCb"])
                m_scp = B2[0:L, 128:128 + L]
                S.pe(lambda e, sl=sl: e.matmul(m_scp, kp[:, sl], qpp[:, sl], start=True, stop=True), reads=["m_kp", "m_qpp"], writes=["B2"])
                S.dve(lambda e: e.tensor_tensor(out=matt[0:L, 0:L], in0=m_scp, in1=cst["maskI"][0:L, 0:L], op=ALU.mult), reads=["B2", "c_maskI"], writes=["m_att"])
                m_np = B5[:, 128:256].rearrange("p (a b) -> p a b", b=64)
                for i in range(2):
                    S.pe(lambda e, i=i: e.matmul(m_np[:, i, 0:L], st.mTT[0:L, 128 + i * 128:256 + i * 128], matt[0:L, 0:L], start=True, stop=False), reads=["mTT", "m_att"], writes=["B5"])
                    S.pe(lambda e, i=i, sl=sl: e.matmul(m_np[:, i, 0:L], st.mCb[:, i * 128:(i + 1) * 128], qpp[:, sl], start=False, stop=True), reads=["mCb", "m_qpp"], writes=["B5"])
                m_dp = B5[0:1, 256:256 + L]
                S.pe(lambda e: e.matmul(m_dp, st.mTT[0:L, 384:385], matt[0:L, 0:L], start=True, stop=False), reads=["mTT", "m_att"], writes=["B5"])
                S.pe(lambda e, sl=sl: e.matmul(m_dp, st.mCb[:, 256:257], qpp[:, sl], start=False, stop=True), reads=["mCb", "m_qpp"], writes=["B5"])
                ev(numall[:, :, sl], m_np[:, :, 0:L], reads=["B5"], writes=["m_num"])
                ev(denr[:, sl], m_dp, reads=["B5"], writes=["m_den"])
                cup = B1[:, 256:512]
                nup = B5[:, 448:449]
                S.pe(lambda e: e.matmul(cup, st.mTT[0:L, 0:128], st.mTT[0:L, 128:384], start=True, stop=True), reads=["mTT"], writes=["B1"])
                S.pe(lambda e: e.matmul(nup, st.mTT[0:L, 0:128], st.mTT[0:L, 384:385], start=True, stop=True), reads=["mTT"], writes=["B5"])
                S.dve(lambda e: e.tensor_tensor(out=st.mC[:, 0:256], in0=st.mC[:, 0:256], in1=cup, op=ALU.add), reads=["mC", "B1"], writes=["mC"])
                S.dve(lambda e: e.tensor_tensor(out=st.mC[:, 256:257], in0=st.mC[:, 256:257], in1=nup, op=ALU.add), reads=["mC", "B5"], writes=["mC"])
            S.act(lambda e: e.activation(out=denr[:, 0:TBc], in_=denr[:, 0:TBc], func=AF.Abs), reads=["m_den"], writes=["m_den"])
            S.dve(lambda e: e.tensor_tensor(out=denr[:, 0:TBc], in0=denr[:, 0:TBc], in1=rows[:, 7, 0:TBc], op=ALU.max), reads=["m_den", ("m_rows", 7)], writes=["m_den"])
            S.dve(lambda e: e.reciprocal(out=denr[:, 0:TBc], in_=denr[:, 0:TBc]), reads=["m_den"], writes=["m_den"])
            bdd = B4[:, 256:256 + TBc]
            S.pe(lambda e: e.matmul(bdd, st.onesrow[:], denr[:, 0:TBc], start=True, stop=True), reads=["onesrow", "m_den"], writes=["B4"])
            mh = wt("m_h", [128, 2, TBM]); mhb = wt("m_hb", [128, 2, TBM], BF16); msq = wt("m_sq", [128, 2, TBM], BF16)
            S.dve(lambda e: e.tensor_tensor(out=mh[:, :, 0:TBc], in0=numall[:, :, 0:TBc], in1=bdd.unsqueeze(1).to_broadcast([128, 2, TBc]), op=ALU.mult), reads=["m_num", "B4"], writes=["m_h"])
            S.act(lambda e: e.copy(out=mhb[:, :, 0:TBc], in_=mh[:, :, 0:TBc]), reads=["m_h"], writes=["m_hb"])
            S.act(lambda e: e.activation(out=msq[:, :, 0:TBc], in_=mh[:, :, 0:TBc], func=AF.Square), reads=["m_h"], writes=["m_sq"])
            sm = B3[:, 0:TBc]; sm2 = B3[:, 256:256 + TBc]
            for i in range(2):
                S.pe(lambda e, i=i: e.matmul(sm, cst["ones"][:], mhb[:, i, 0:TBc], start=(i == 0), stop=(i == 1)), reads=["m_hb", "c_ones"], writes=["B3"])
            for i in range(2):
                S.pe(lambda e, i=i: e.matmul(sm2, cst["ones"][:], msq[:, i, 0:TBc], start=(i == 0), stop=(i == 1)), reads=["m_sq", "c_ones"], writes=["B3"])
            mean = wt("m_mean", [128, TBM]); var = wt("m_var", [128, TBM]); myo = wt("m_yo", [128, 2, TBM], BF16)
            S.act(lambda e: e.mul(out=mean[:, 0:TBc], in_=sm, mul=1.0 / 256), reads=["B3"], writes=["m_mean"])
            S.dve(lambda e: e.tensor_tensor(out=var[:, 0:TBc], in0=mean[:, 0:TBc], in1=mean[:, 0:TBc], op=ALU.mult), reads=["m_mean"], writes=["m_var"])
            S.dve(lambda e: e.scalar_tensor_tensor(out=var[:, 0:TBc], in0=sm2, scalar=1.0 / 256, in1=var[:, 0:TBc], op0=ALU.mult, op1=ALU.subtract), reads=["B3", "m_var"], writes=["m_var"])
            S.act(lambda e: e.activation(out=var[:, 0:TBc], in_=var[:, 0:TBc], func=AF.Sqrt, bias=P("eps")), reads=["m_var", "par"], writes=["m_var"])
            S.dve(lambda e: e.reciprocal(out=var[:, 0:TBc], in_=var[:, 0:TBc]), reads=["m_var"], writes=["m_var"])
            S.dve(lambda e: e.tensor_tensor(out=mh[:, :, 0:TBc], in0=mh[:, :, 0:TBc], in1=mean[:, 0:TBc].unsqueeze(1).to_broadcast([128, 2, TBc]), op=ALU.subtract), reads=["m_h", "m_mean"], writes=["m_h"])
            S.dve(lambda e: e.tensor_tensor(out=mh[:, :, 0:TBc], in0=mh[:, :, 0:TBc], in1=var[:, 0:TBc].unsqueeze(1).to_broadcast([128, 2, TBc]), op=ALU.mult), reads=["m_h", "m_var"], writes=["m_h"])
            for i in range(2):
                S.dve(lambda e, i=i: e.scalar_tensor_tensor(out=mh[:, i, 0:TBc], in0=mh[:, i, 0:TBc], scalar=P("mg%d" % i), in1=mso[:, i, 0:TBc], op0=ALU.mult, op1=ALU.mult), reads=["m_h", "par", ("m_so", i)], writes=["m_h"])
            S.dve(lambda e: e.tensor_tensor(out=myo[:, :, 0:TBc], in0=mh[:, :, 0:TBc], in1=msz[:, :, 0:TBc], op=ALU.mult), reads=["m_h", ("m_sz", 0), ("m_sz", 1)], writes=["m_yo"])
            for i in range(2):
                yd_, ydk_ = y_dst(256 + i * 128, t0, TBc)
                outs.append(S.dma("pool", lambda e, i=i, yd_=yd_: e.dma_start(out=yd_, in_=myo[:, i, 0:TBc]), reads=["m_yo"], writes=[ydk_]))

        if "r" in mixers:
            for nm in ("rr0", "rr1", "rk0", "rk1", "rv0", "rv1", "rwa"):
                p_, k_ = inproj(xt, xkey, CT[nm] * 128, 128, TBc)
                ev(st.rraw[nm][:, 1:1 + TBc], p_, reads=[k_], writes=["raw_" + nm])
            rsz = wt("r_sz", [128, 2, TBM])
            for p in range(2):
                p_, k_ = inproj(xt, xkey, CT["rz%d" % p] * 128, 128, TBc)
                S.act(lambda e, p_=p_, p=p: e.activation(out=rsz[:, p, 0:TBc], in_=p_, func=AF.Silu), reads=[k_], writes=[("r_sz", p)])
            if RSTOP <= -3:
                return
            lm = {}
            for nm, mu in (("rr0", "mur0"), ("rr1", "mur1"), ("rk0", "muk0"), ("rk1", "muk1"), ("rv0", "muv0"), ("rv1", "muv1"), ("rwa", "muwa")):
                raw = st.rraw[nm]
                m = wt("r_m_" + nm, [128, TBM])
                lm[nm] = m
                S.dve(lambda e, raw=raw, m=m: e.tensor_tensor(out=m[:, 0:TBc], in0=raw[:, 0:TBc], in1=raw[:, 1:1 + TBc], op=ALU.subtract), reads=["raw_" + nm], writes=["r_m_" + nm])
                S.dve(lambda e, raw=raw, m=m, mu=mu: e.scalar_tensor_tensor(out=m[:, 0:TBc], in0=m[:, 0:TBc], scalar=P(mu), in1=raw[:, 1:1 + TBc], op0=ALU.mult, op1=ALU.add),
                      reads=["raw_" + nm, "r_m_" + nm, "par"], writes=["r_m_" + nm])
                S.pool(lambda e, raw=raw: e.tensor_copy(out=raw[:, 0:1], in_=raw[:, TBc:TBc + 1]), reads=["raw_" + nm], writes=["raw_" + nm])
            wab = wt("r_wab", [128, TBM], BF16)
            S.act(lambda e: e.activation(out=wab[0:64, 0:TBc], in_=lm["rwa"][0:64, 0:TBc], func=AF.Tanh), reads=["r_m_rwa"], writes=["r_wab0"])
            S.act(lambda e: e.copy(out=wab[64:128, 0:TBc], in_=lm["rwa"][64:128, 0:TBc]), reads=["r_m_rwa"], writes=["r_wab1"])
            if RSTOP <= -2:
                return
            sgw = wt("r_sgw", [128, 2, TBM]); av = wt("r_a", [128, 2, TBM]); kkn = wt("r_kkn", [128, 2, TBM]); kf = wt("r_kf", [128, 2, TBM])
            bv = wt("r_bv", [128, 2, TBM]); cs = wt("r_cs", [128, 2, TBM]); tmp = wt("r_tmp", [128, 2, TBM]); tmpb = wt("r_tmpb", [128, 2, TBM], BF16)
            ecw = wt("r_ecw", [128, 2, TBM]); ex = wt("r_ex", [128, 2, TBM])
            AR = wt("r_AR", [128, 2, max(NCH, 1), 2, 64], BF16)
            Bd = wt("r_Bd", [128, 2, TBM], BF16); Kd = wt("r_Kd", [128, 2, TBM], BF16)
            Btl = wt("r_Btl", [128, 2, TBM], BF16); Ktl = wt("r_Ktl", [128, 2, TBM], BF16)
            rvb = wt("r_vb", [128, 2, TBM], BF16); bon = wt("r_bon", [128, 2, TBM])
            for p in range(2):
                rm = lm["rr%d" % p]; km = lm["rk%d" % p]; vm = lm["rv%d" % p]
                RK = ["r_m_rr%d" % p, "r_m_rk%d" % p, "r_m_rv%d" % p]
                wp = B3[:, 0:TBc]; ap_ = B3[:, 256:256 + TBc]
                S.pe(lambda e, p=p, wp=wp: e.matmul(wp, wud[0:64, p * 128:(p + 1) * 128], wab[0:64, 0:TBc], start=True, stop=True), reads=["wud", "r_wab0"], writes=["B3"])
                S.pe(lambda e, p=p, ap_=ap_: e.matmul(ap_, wud[64:128, p * 128:(p + 1) * 128], wab[64:128, 0:TBc], start=True, stop=True), reads=["wud", "r_wab1"], writes=["B3"], rowgrp=1)
                S.act(lambda e, p=p, wp=wp: e.activation(out=sgw[:, p, 0:TBc], in_=wp, func=AF.Sigmoid, bias=P("w0%d" % p)), reads=["B3", "par"], writes=[("r_sgw", p)])
                S.act(lambda e, p=p, ap_=ap_: e.activation(out=av[:, p, 0:TBc], in_=ap_, func=AF.Sigmoid, bias=P("a0%d" % p)), reads=["B3", "par"], writes=[("r_a", p)])
                S.dve(lambda e, p=p, km=km: e.tensor_scalar_mul(out=kkn[:, p, 0:TBc], in0=km[:, 0:TBc], scalar1=P("kk%d" % p)), reads=[RK[1], "par"], writes=[("r_kkn", p)])
                S.act(lambda e, p=p: e.activation(out=tmpb[:, p, 0:TBc], in_=kkn[:, p, 0:TBc], func=AF.Square), reads=[("r_kkn", p)], writes=[("r_tmpb", p)])
                ssk = B4[:, 0:TBc]
                S.pe(lambda e, p=p, ssk=ssk: e.matmul(ssk, cst["bd"][:], tmpb[:, p, 0:TBc], start=True, stop=True), reads=[("r_tmpb", p), "c_bd"], writes=["B4"])
                S.act(lambda e, p=p, ssk=ssk: e.activation(out=tmp[:, p, 0:TBc], in_=ssk, func=AF.Sqrt), reads=["B4"], writes=[("r_tmp", p)])
                S.dve(lambda e, p=p: e.tensor_scalar_max(out=tmp[:, p, 0:TBc], in0=tmp[:, p, 0:TBc], scalar1=1e-12), reads=[("r_tmp", p)], writes=[("r_tmp", p)])
                S.dve(lambda e, p=p: e.reciprocal(out=tmp[:, p, 0:TBc], in_=tmp[:, p, 0:TBc]), reads=[("r_tmp", p)], writes=[("r_tmp", p)])
                S.dve(lambda e, p=p: e.tensor_tensor(out=kkn[:, p, 0:TBc], in0=kkn[:, p, 0:TBc], in1=tmp[:, p, 0:TBc], op=ALU.mult), reads=[("r_kkn", p), ("r_tmp", p)], writes=[("r_kkn", p)])
                S.dve(lambda e, p=p: e.tensor_scalar(out=kf[:, p, 0:TBc], in0=av[:, p, 0:TBc], scalar1=-1.0, scalar2=P("ka%d" % p), op0=ALU.add, op1=ALU.mult), reads=[("r_a", p), "par"], writes=[("r_kf", p)])
                S.dve(lambda e, p=p, km=km: e.scalar_tensor_tensor(out=kf[:, p, 0:TBc], in0=kf[:, p, 0:TBc], scalar=1.0, in1=km[:, 0:TBc], op0=ALU.add, op1=ALU.mult), reads=[("r_kf", p), RK[1]], writes=[("r_kf", p)])
                S.dve(lambda e, p=p: e.tensor_tensor(out=bv[:, p, 0:TBc], in0=kkn[:, p, 0:TBc], in1=av[:, p, 0:TBc], op=ALU.mult), reads=[("r_kkn", p), ("r_a", p)], writes=[("r_bv", p)])
                for c in range(nch):
                    S.dve(lambda e, p=p, c=c: e.tensor_tensor_scan(out=cs[:, p, c * L:(c + 1) * L], data0=onesf[:, 0:L], data1=sgw[:, p, c * L:(c + 1) * L], initial=0.0, op0=ALU.mult, op1=ALU.add),
                          reads=[("r_sgw", p), "onesf"], writes=[("r_cs", p)])
                cs3 = c3(cs[:, p, 0:TBc])
                ARp = AR[:, p, 0:nch, :, 0:L]
                S.act(lambda e, p=p: e.activation(out=ecw[:, p, 0:TBc], in_=cs[:, p, 0:TBc], func=AF.Exp, scale=-C0), reads=[("r_cs", p)], writes=[("r_ecw", p)])
                S.dve(lambda e, p=p, rm=rm, ARp=ARp: e.tensor_tensor(out=ARp[:, :, 1, :], in0=c3(rm[:, 0:TBc]), in1=c3(ecw[:, p, 0:TBc]), op=ALU.mult), reads=[RK[0], ("r_ecw", p)], writes=[("r_AR", p)])
                S.act(lambda e, p=p: e.activation(out=ex[:, p, 0:TBc], in_=cs[:, p, 0:TBc], func=AF.Exp, scale=C0), reads=[("r_cs", p)], writes=[("r_ex", p)])
                S.dve(lambda e, p=p: e.tensor_tensor(out=Kd[:, p, 0:TBc], in0=kf[:, p, 0:TBc], in1=ex[:, p, 0:TBc], op=ALU.mult), reads=[("r_kf", p), ("r_ex", p)], writes=[("r_Kd", p)])
                S.dve(lambda e, p=p: e.tensor_tensor(out=Bd[:, p, 0:TBc], in0=bv[:, p, 0:TBc], in1=ex[:, p, 0:TBc], op=ALU.mult), reads=[("r_bv", p), ("r_ex", p)], writes=[("r_Bd", p)])
                S.dve(lambda e, p=p: e.tensor_tensor(out=tmp[:, p, 0:TBc], in0=cs[:, p, 0:TBc], in1=sgw[:, p, 0:TBc], op=ALU.subtract), reads=[("r_cs", p), ("r_sgw", p), ("r_tmp", p)], writes=[("r_tmp", p)])
                S.act(lambda e, p=p: e.activation(out=ex[:, p, 0:TBc], in_=tmp[:, p, 0:TBc], func=AF.Exp, scale=-C0), reads=[("r_tmp", p), ("r_ex", p)], writes=[("r_ex", p)])
                S.dve(lambda e, p=p, ARp=ARp: e.scalar_tensor_tensor(out=ARp[:, :, 0, :], in0=c3(kkn[:, p, 0:TBc]), scalar=-1.0, in1=c3(ex[:, p, 0:TBc]), op0=ALU.mult, op1=ALU.mult), reads=[("r_kkn", p), ("r_ex", p)], writes=[("r_AR", p)])
                S.dve(lambda e, p=p, cs3=cs3: e.tensor_tensor(out=c3(tmp[:, p, 0:TBc]), in0=cs3[:, :, L - 1:L].to_broadcast([128, nch, L]), in1=cs3, op=ALU.subtract), reads=[("r_cs", p), ("r_tmp", p)], writes=[("r_tmp", p)])
                S.act(lambda e, p=p: e.activation(out=ex[:, p, 0:TBc], in_=tmp[:, p, 0:TBc], func=AF.Exp, scale=-C0), reads=[("r_tmp", p), ("r_ex", p)], writes=[("r_ex", p)])
                S.dve(lambda e, p=p: e.tensor_tensor(out=Btl[:, p, 0:TBc], in0=bv[:, p, 0:TBc], in1=ex[:, p, 0:TBc], op=ALU.mult), reads=[("r_bv", p), ("r_ex", p)], writes=[("r_Btl", p)])
                S.dve(lambda e, p=p: e.tensor_tensor(out=Ktl[:, p, 0:TBc], in0=kf[:, p, 0:TBc], in1=ex[:, p, 0:TBc], op=ALU.mult), reads=[("r_kf", p), ("r_ex", p)], writes=[("r_Ktl", p)])
                S.act(lambda e, p=p, vm=vm: e.copy(out=rvb[:, p, 0:TBc], in_=vm[:, 0:TBc]), reads=[RK[2]], writes=[("r_vb", p)])
                S.dve(lambda e, p=p, rm=rm: e.scalar_tensor_tensor(out=tmpb[:, p, 0:TBc], in0=rm[:, 0:TBc], scalar=P("rk%d" % p), in1=kf[:, p, 0:TBc], op0=ALU.mult, op1=ALU.mult), reads=[RK[0], "par", ("r_kf", p), ("r_tmpb", p)], writes=[("r_tmpb", p)])
                bsp = B4[:, 256:256 + TBc]
                S.pe(lambda e, p=p, bsp=bsp: e.matmul(bsp, cst["bd"][:], tmpb[:, p, 0:TBc], start=True, stop=True), reads=[("r_tmpb", p), "c_bd"], writes=["B4"])
                S.dve(lambda e, p=p, vm=vm, bsp=bsp: e.tensor_tensor(out=bon[:, p, 0:TBc], in0=vm[:, 0:TBc], in1=bsp, op=ALU.mult), reads=[RK[2], "B4"], writes=[("r_bon", p)])
            yall = wt("r_yall", [128, 2, TBM])
            T3 = wt("r_T3", [64, 2, 3, 128], BF16)
            scBm = wt("r_scBm", [64, 4, 2, 64], BF16); scKm = wt("r_scKm", [64, 4, 2, 64], BF16); labm = wt("r_labm", [64, 4, 64], BF16)
            TTf = wt("r_TTf", [64, 4, 64]); TTb = wt("r_TTb", [64, 4, 64], BF16)
            X = [wt("r_X%d" % i, [64, 4, 2, 64], BF16) for i in range(2)]
            Q = [wt("r_Q%d" % i, [64, 4, 64], BF16) for i in range(2)]
            rhsb = wt("r_rhsb", [64, 4, 64], BF16); ub = wt("r_ub", [64, 4, 64], BF16)
            upad = wt("r_upad", [64, 2, 2, 128], BF16); vpad = wt("r_vpad", [64, 2, 2, 128], BF16)
            tmpS = wt("r_tmpS", [128, 2, 128])
            nlev = int(round(math.log2(L)))
            for c in range(nch if RSTOP >= 1 else 0):
                sl = slice(c * L, (c + 1) * L)
                trp = Bt[0:L, 0:768].rearrange("p (a b c) -> p a b c", a=2, b=3)
                for p in range(2):
                    S.pe(lambda e, p=p, sl=sl: e.transpose(trp[:, p, 0, :], rvb[:, p, sl], cst["ident"][:]), reads=[("r_vb", p), "c_ident"], writes=["Bt"])
                    S.pe(lambda e, p=p, sl=sl: e.transpose(trp[:, p, 1, :], Btl[:, p, sl], cst["ident"][:]), reads=[("r_Btl", p), "c_ident"], writes=["Bt"])
                    S.pe(lambda e, p=p, sl=sl: e.transpose(trp[:, p, 2, :], Ktl[:, p, sl], cst["ident"][:]), reads=[("r_Ktl", p), "c_ident"], writes=["Bt"])
                ev(T3[0:L], trp, reads=["Bt"], writes=["r_T3"])
                S.dve(lambda e: e.tensor_tensor(out=vpad[0:L], in0=T3[0:L, :, 0, :].unsqueeze(2).to_broadcast([L, 2, 2, 128]), in1=cst["padmask"][0:L].unsqueeze(1).to_broadcast([L, 2, 2, 128]), op=ALU.mult),
                      reads=["r_T3", "c_padmask"], writes=["r_vpad"])
                scB = B3[0:L, :].rearrange("p (h a j) -> p h a j", h=4, a=2)
                scK = B4[0:L, :].rearrange("p (h a j) -> p h a j", h=4, a=2)
                lab = B2[0:L, 256:512].rearrange("p (h j) -> p h j", h=4)
                for h in (0, 2, 1, 3):
                    p, hh = divmod(h, 2)
                    b0 = hh * 64
                    rg = 1 if hh == 1 else None
                    arh = AR[b0:b0 + 64, p, c, :, 0:L]
                    S.pe(lambda e, h=h, p=p, b0=b0, arh=arh, sl=sl: e.matmul(scB[:, h, :, 0:L], Bd[b0:b0 + 64, p, sl], arh, start=True, stop=True), reads=[("r_Bd", p), ("r_AR", p)], writes=["B3"], rowgrp=rg)
                    S.pe(lambda e, h=h, p=p, b0=b0, arh=arh, sl=sl: e.matmul(scK[:, h, :, 0:L], Kd[b0:b0 + 64, p, sl], arh, start=True, stop=True), reads=[("r_Kd", p), ("r_AR", p)], writes=["B4"], rowgrp=rg)
                    S.pe(lambda e, h=h, p=p, b0=b0, arh=arh, sl=sl: e.matmul(lab[:, h, 0:L], arh[:, 0, :], Bd[b0:b0 + 64, p, sl], start=True, stop=True), reads=[("r_Bd", p), ("r_AR", p)], writes=["B2"], rowgrp=rg)
                m2 = cst["mask2"][0:L, :, 0:L].unsqueeze(1).to_broadcast([L, 4, 2, L])
                S.dve(lambda e: e.tensor_tensor(out=scBm[0:L, :, :, 0:L], in0=scB[:, :, :, 0:L], in1=m2, op=ALU.mult), reads=["B3", "c_mask2"], writes=["r_scBm"])
                S.dve(lambda e: e.tensor_tensor(out=scKm[0:L, :, :, 0:L], in0=scK[:, :, :, 0:L], in1=m2, op=ALU.mult), reads=["B4", "c_mask2"], writes=["r_scKm"])
                S.dve(lambda e: e.tensor_tensor(out=labm[0:L, :, 0:L], in0=lab[:, :, 0:L], in1=cst["maskL"][0:L, 0:L].unsqueeze(1).to_broadcast([L, 4, L]), op=ALU.mult), reads=["B2", "c_maskL"], writes=["r_labm"])
                if RSTOP < 2:
                    continue
                S.dve(lambda e: e.tensor_tensor(out=TTf[0:L, :, 0:L], in0=scBm[0:L, :, 0, 0:L], in1=ident64f[0:L, 0:L].unsqueeze(1).to_broadcast([L, 4, L]), op=ALU.add), reads=["r_scBm", "ident64f"], writes=["r_TTf"])
                S.act(lambda e: e.copy(out=X[0][0:L, :, 0, 0:L], in_=TTf[0:L, :, 0:L]), reads=["r_TTf"], writes=["r_X0a"])
                PPp = B6[0:L, :].rearrange("p (h a j) -> p h a j", h=4, a=2)
                QQp = B1[0:L, 0:256].rearrange("p (h j) -> p h j", h=4)
                for h in range(4):
                    S.pe(lambda e, h=h: e.matmul(PPp[:, h, 1, 0:L], labm[0:L, h, 0:L], scBm[0:L, h, 0, 0:L], start=True, stop=True), reads=["r_labm", "r_scBm"], writes=["B6"])
                    S.pe(lambda e, h=h: e.matmul(QQp[:, h, 0:L], scBm[0:L, h, 0, 0:L], labm[0:L, h, 0:L], start=True, stop=True), reads=["r_labm", "r_scBm"], writes=["B1"])
                S.dve(lambda e: e.tensor_copy(out=X[0][0:L, :, 1, 0:L], in_=PPp[:, :, 1, 0:L]), reads=["B6"], writes=["r_X0b"])
                S.act(lambda e: e.copy(out=Q[0][0:L, :, 0:L], in_=QQp[:, :, 0:L]), reads=["B1"], writes=["r_Q0"])
                cur = 0
                for lev in range(1, nlev):
                    last = lev == nlev - 1
                    Xc, Qc = X[cur], Q[cur]
                    xk = ["r_X%da" % cur, "r_X%db" % cur]
                    qk = "r_Q%d" % cur
                    nxt = cur ^ 1
                    for h in range(4):
                        if last:
                            S.pe(lambda e, h=h, Xc=Xc, Qc=Qc: e.matmul(PPp[:, h, 0, 0:L], Qc[0:L, h, 0:L], Xc[0:L, h, 0, 0:L], start=True, stop=True), reads=[qk] + xk, writes=["B6"])
                        else:
                            S.pe(lambda e, h=h, Xc=Xc, Qc=Qc: e.matmul(PPp[:, h, :, 0:L], Qc[0:L, h, 0:L], Xc[0:L, h, :, 0:L], start=True, stop=True), reads=[qk] + xk, writes=["B6"])
                            S.pe(lambda e, h=h, Xc=Xc, Qc=Qc: e.matmul(QQp[:, h, 0:L], Xc[0:L, h, 1, 0:L], Qc[0:L, h, 0:L], start=True, stop=True), reads=[qk] + xk, writes=["B1"])
                    S.dve(lambda e: e.tensor_tensor(out=TTf[0:L, :, 0:L], in0=TTf[0:L, :, 0:L], in1=PPp[:, :, 0, 0:L], op=ALU.add), reads=["r_TTf", "B6"], writes=["r_TTf"])
                    if last:
                        S.act(lambda e: e.copy(out=TTb[0:L, :, 0:L], in_=TTf[0:L, :, 0:L]), reads=["r_TTf"], writes=["r_TTb"])
                    else:
                        S.act(lambda e, nxt=nxt: e.copy(out=X[nxt][0:L, :, 0, 0:L], in_=TTf[0:L, :, 0:L]), reads=["r_TTf"], writes=["r_X%da" % nxt])
                        S.dve(lambda e, nxt=nxt: e.tensor_copy(out=X[nxt][0:L, :, 1, 0:L], in_=PPp[:, :, 1, 0:L]), reads=["B6"], writes=["r_X%db" % nxt])
                        S.act(lambda e, nxt=nxt: e.copy(out=Q[nxt][0:L, :, 0:L], in_=QQp[:, :, 0:L]), reads=["B1"], writes=["r_Q%d" % nxt])
                    cur = nxt
                if RSTOP < 3:
                    continue
                rhp = B6[0:L, 0:256].rearrange("p (h j) -> p h j", h=4)
                for h in range(4):
                    p, hh = divmod(h, 2)
                    S.pe(lambda e, h=h, p=p, hh=hh, c=c: e.matmul(rhp[:, h, :], AR[:, p, c, 0, 0:L], st.rSb[:, p, hh * 64:(hh + 1) * 64], start=True, stop=False), reads=[("r_AR", p), "rSb"], writes=["B6"])
                    S.pe(lambda e, h=h, p=p, hh=hh: e.matmul(rhp[:, h, :], scKm[0:L, h, 0, 0:L], T3[0:L, p, 0, hh * 64:(hh + 1) * 64], start=False, stop=True), reads=["r_scKm", "r_T3"], writes=["B6"])
                S.act(lambda e: e.copy(out=rhsb[0:L], in_=rhp), reads=["B6"], writes=["r_rhsb"])
                up_ = B6[0:L, 256:512].rearrange("p (h j) -> p h j", h=4)
                for h in range(4):
                    S.pe(lambda e, h=h: e.matmul(up_[:, h, :], TTb[0:L, h, 0:L], rhsb[0:L, h, :], start=True, stop=True), reads=["r_TTb", "r_rhsb"], writes=["B6"])
                S.act(lambda e: e.copy(out=ub[0:L], in_=up_), reads=["B6"], writes=["r_ub"])
                S.dve(lambda e: e.tensor_tensor(out=upad[0:L], in0=B6[0:L, 256:512].rearrange("p (a k) -> p a k", a=2).unsqueeze(2).to_broadcast([L, 2, 2, 128]), in1=cst["padmask"][0:L].unsqueeze(1).to_broadcast([L, 2, 2, 128]), op=ALU.mult),
                      reads=["B6", "c_padmask"], writes=["r_upad"])
                yp = B5[:, 320:448].rearrange("p (a j) -> p a j", a=2)
                for p in range(2):
                    S.pe(lambda e, p=p, c=c: e.matmul(yp[:, p, 0:L], st.rSb[:, p, :], AR[:, p, c, 1, 0:L], start=True, stop=False), reads=["rSb", ("r_AR", p)], writes=["B5"])
                    for hh in range(2):
                        h = p * 2 + hh
                        S.pe(lambda e, p=p, hh=hh, h=h: e.matmul(yp[:, p, 0:L], upad[0:L, p, hh, :], scBm[0:L, h, 1, 0:L], start=False, stop=False), reads=["r_upad", "r_scBm"], writes=["B5"])
                        S.pe(lambda e, p=p, hh=hh, h=h: e.matmul(yp[:, p, 0:L], vpad[0:L, p, hh, :], scKm[0:L, h, 1, 0:L], start=False, stop=(hh == 1)), reads=["r_vpad", "r_scKm"], writes=["B5"])
                ev(yall[:, :, sl], yp[:, :, 0:L], reads=["B5"], writes=["r_yall"])
                sup = B2[:, 0:256].rearrange("p (a j) -> p a j", a=2)
                for p in range(2):
                    S.pe(lambda e, p=p: e.matmul(sup[:, p, :], T3[0:L, p, 1, :], ub[0:L, 2 * p:2 * p + 2, :], start=True, stop=False), reads=["r_T3", "r_ub"], writes=["B2"])
                    S.pe(lambda e, p=p: e.matmul(sup[:, p, :], T3[0:L, p, 2, :], T3[0:L, p, 0, :], start=False, stop=True), reads=["r_T3"], writes=["B2"])
                S.dve(lambda e: e.tensor_tensor(out=tmpS[:], in0=sup, in1=bdmaskf[:].unsqueeze(1).to_broadcast([128, 2, 128]), op=ALU.mult), reads=["B2", "bdmaskf"], writes=["r_tmpS"])
                for p in range(2):
                    S.dve(lambda e, p=p, c=c: e.scalar_tensor_tensor(out=st.rS[:, p, :], in0=st.rS[:, p, :], scalar=ecw[:, p, c * L + L - 1:c * L + L], in1=tmpS[:, p, :], op0=ALU.mult, op1=ALU.add),
                          reads=["rS", ("r_ecw", p), "r_tmpS"], writes=["rS"])
                S.act(lambda e: e.copy(out=st.rSb[:], in_=st.rS[:]), reads=["rS"], writes=["rSb"])
            if RSTOP <= -1:
                return
            ryb = wt("r_yb", [128, 2, TBM], BF16); rsq = wt("r_sq", [128, 2, TBM], BF16); ryo = wt("r_yo", [128, 2, TBM], BF16)
            rmean = wt("r_mean", [128, TBM]); rvar = wt("r_var", [128, TBM])
            for p in range(2):
                S.act(lambda e, p=p: e.copy(out=ryb[:, p, 0:TBc], in_=yall[:, p, 0:TBc]), reads=["r_yall"], writes=[("r_yb", p)])
                S.act(lambda e, p=p: e.activation(out=rsq[:, p, 0:TBc], in_=yall[:, p, 0:TBc], func=AF.Square), reads=["r_yall"], writes=[("r_sq", p)])
                sm = B3[:, 0:TBc]; sm2 = B3[:, 256:256 + TBc]
                S.pe(lambda e, p=p, sm=sm: e.matmul(sm, cst["bd"][:], ryb[:, p, 0:TBc], start=True, stop=True), reads=[("r_yb", p), "c_bd"], writes=["B3"])
                S.pe(lambda e, p=p, sm2=sm2: e.matmul(sm2, cst["bd"][:], rsq[:, p, 0:TBc], start=True, stop=True), reads=[("r_sq", p), "c_bd"], writes=["B3"])
                S.act(lambda e, sm=sm: e.mul(out=rmean[:, 0:TBc], in_=sm, mul=1.0 / 64), reads=["B3"], writes=["r_mean"])
                S.dve(lambda e: e.tensor_tensor(out=rvar[:, 0:TBc], in0=rmean[:, 0:TBc], in1=rmean[:, 0:TBc], op=ALU.mult), reads=["r_mean"], writes=["r_var"])
                S.dve(lambda e, sm2=sm2: e.scalar_tensor_tensor(out=rvar[:, 0:TBc], in0=sm2, scalar=1.0 / 64, in1=rvar[:, 0:TBc], op0=ALU.mult, op1=ALU.subtract), reads=["B3", "r_var"], writes=["r_var"])
                S.act(lambda e: e.activation(out=rvar[:, 0:TBc], in_=rvar[:, 0:TBc], func=AF.Sqrt, bias=P("lneps")), reads=["r_var", "par"], writes=["r_var"])
                S.dve(lambda e: e.reciprocal(out=rvar[:, 0:TBc], in_=rvar[:, 0:TBc]), reads=["r_var"], writes=["r_var"])
                S.dve(lambda e, p=p: e.tensor_tensor(out=yall[:, p, 0:TBc], in0=yall[:, p, 0:TBc], in1=rmean[:, 0:TBc], op=ALU.subtract), reads=["r_yall", "r_mean"], writes=["r_yall"])
                S.dve(lambda e, p=p: e.tensor_tensor(out=yall[:, p, 0:TBc], in0=yall[:, p, 0:TBc], in1=rvar[:, 0:TBc], op=ALU.mult), reads=["r_yall", "r_var"], writes=["r_yall"])
                S.dve(lambda e, p=p: e.tensor_scalar(out=yall[:, p, 0:TBc], in0=yall[:, p, 0:TBc], scalar1=P("lg%d" % p), scalar2=P("lb%d" % p), op0=ALU.mult, op1=ALU.add), reads=["r_yall", "par"], writes=["r_yall"])
                S.dve(lambda e, p=p: e.tensor_tensor(out=yall[:, p, 0:TBc], in0=yall[:, p, 0:TBc], in1=bon[:, p, 0:TBc], op=ALU.add), reads=["r_yall", ("r_bon", p)], writes=["r_yall"])
                S.dve(lambda e, p=p: e.tensor_tensor(out=ryo[:, p, 0:TBc], in0=yall[:, p, 0:TBc], in1=rsz[:, p, 0:TBc], op=ALU.mult), reads=["r_yall", ("r_sz", p)], writes=[("r_yo", p)])
                yd_, ydk_ = y_dst(512 + p * 128, t0, TBc)
                outs.append(S.dma("pool", lambda e, p=p, yd_=yd_: e.dma_start(out=yd_, in_=ryo[:, p, 0:TBc]), reads=[("r_yo", p)], writes=[ydk_]))

    outs = []
    macro(0, 0, NMETA, NMETA)
    for ms in range(NMS):
        macro(ms + 1, NMETA + ms * TB, TB, LCH)
    return outs


def build_B(nc, S, es, TOK, dram, proj=True, last=False, pref="b", skip_meta=False):
    def sb(name, shape, dt=F32):
        return es.enter_context(nc.sbuf_tensor("s_" + pref + name, shape, dt))

    def ps(name, shape, dt=F32):
        return es.enter_context(nc.psum_tensor("p_" + pref + name, shape, dt))

    NT = min(512, TOK - NMETA)
    tiles = [(0, NMETA)] + [(NMETA + i * NT, NT) for i in range((TOK - NMETA) // NT)]
    if skip_meta:
        tiles = tiles[1:]
    if "h_src" not in dram:
        hsrc = dram["hT"].rearrange("(kc p) t -> p kc t", p=128)
        dram = dict(dram)
        dram["h_src"] = lambda t0, N: (hsrc[:, :, t0:t0 + N], ("d_hin", t0))
        if proj:
            hdst = dram["ho"].rearrange("(kc p) t -> p kc t", p=128)
            xsrc = dram["xnT"].rearrange("(kc p) t -> p kc t", p=128)
            ysrc = dram["yT"].rearrange("(kc p) t -> p kc t", p=128)
            dram["h_dst"] = lambda t0, N: (hdst[:, :, t0:t0 + N], ("d_ho", t0))
            dram["xn_srcB"] = lambda t0, N: [(xsrc[:, :, t0:t0 + N], "d_xinB", 0, N)]

            def y_load(y, cand, t0, N):
                S.dma("sp", lambda e: e.dma_start(out=y[:, :, 0:N], in_=ysrc[:, :, t0:t0 + N]), writes=["y"])
            dram["y_load"] = y_load
        xdst = dram["xo"].rearrange("(kc p) t -> p kc t", p=128)
        dram["x_dst"] = lambda t0, N: [(xdst[:, :, t0:t0 + N], ("d_xo", t0), 0, N)]
        if "xof" in dram:
            xfd = dram["xof"].rearrange("(kc p) t -> p kc t", p=128)
            dram["out_dst"] = lambda ob, t0, N: (xfd[:, ob, t0:t0 + N], ("d_xof", ob, t0))
    Bk = [ps("B%d" % i, [128, 512]) for i in range(8)]
    onesf = sb("onesf", [128, 128])
    S.pool(lambda e: e.memset(onesf[:], 1.0), writes=["onesf"])
    gn = sb("gn", [128, 16])
    S.dma("sp", lambda e: e.dma_start(out=gn[:], in_=dram["gnext"]), writes=["gn"])
    epsb = sb("epsb", [128, 1])
    S.pool(lambda e: e.memset(epsb[:], 1e-6), writes=["epsb"])
    h = sb("h", [128, 16, NT]); sq = sb("sq", [128, NT]); rstd = sb("rstd", [128, NT])
    xo = sb("xo", [128, 16, NT], BF16)
    want_f32 = "out_dst" in dram
    if want_f32:
        otmp = [sb("otmp%d" % i, [128, NT]) for i in range(2)]
    outs = []
    if proj:
        xn = sb("xn", [128, 16, NT], BF16); y = sb("y", [128, 24, NT], BF16); mg = sb("mg", [128, 16, NT], BF16)
        cand = sb("cand", [128, 24, NT], BF16) if dram.get("need_cand") else None
        wgs = dram["wg"].rearrange("(kc p) n -> p kc n", p=128)
        wbs = dram["wbr"].rearrange("(b kc p) n -> p b kc n", p=128, b=3)
        wos = dram["wo"].rearrange("(kc p) n -> p kc n", p=128)
        GW = 128
        wgt = [sb("wg%d" % i, [128, 3, 16, GW], BF16) for i in range(2)]
        wbt = [sb("wb%d" % i, [128, 3, 8, GW], BF16) for i in range(2)]
        wot = [sb("wo%d" % i, [128, 16, GW], BF16) for i in range(2)]
        sg = sb("sg", [128, 3, NT]); tt = sb("tt", [128, 3, NT]); msum = sb("msum", [128, NT])
    wcount = [0]
    yidx = dram.get("yidx", lambda br, kc: br * 8 + kc)
    wcache = dram.get("wcache")
    for ti_, (t0, N) in enumerate(tiles):
        first_pass = ti_ == 0
        hin, hk = dram["h_src"](t0, N)
        S.dma("sp", lambda e, hin=hin, N=N: e.dma_start(out=h[:, :, 0:N], in_=hin), reads=[hk], writes=["h"])
        if proj:
            for (xin, xk, lo, hi) in dram["xn_srcB"](t0, N):
                S.dma("sp", lambda e, xin=xin, lo=lo, hi=hi: e.dma_start(out=xn[:, :, lo:hi], in_=xin), reads=[xk], writes=["xn"])
            dram["y_load"](y, cand, t0, N)
            for g in range(2048 // GW):
                wi = wcount[0] % 2
                wcount[0] += 1
                if wcache is not None and not first_pass:
                    S.dma("sp", lambda e, g=g, wi=wi: e.dma_start(out=wgt[wi][:].rearrange("p a b c -> p (a b c)"), in_=wcache[0][g]), reads=[("d_wgc", g)], writes=["wg%d" % wi])
                    S.dma("sp", lambda e, g=g, wi=wi: e.dma_start(out=wbt[wi][:].rearrange("p a b c -> p (a b c)"), in_=wcache[1][g]), reads=[("d_wbc", g)], writes=["wb%d" % wi])
                else:
                    for br in range(3):
                        S.dma("pool", lambda e, g=g, br=br, wi=wi: e.dma_start(out=wgt[wi][:, br], in_=wgs[:, :, br * 2048 + g * GW:br * 2048 + (g + 1) * GW]), writes=["wg%d" % wi])
                        S.dma("pool", lambda e, g=g, br=br, wi=wi: e.dma_start(out=wbt[wi][:, br], in_=wbs[:, br, :, g * GW:(g + 1) * GW]), writes=["wb%d" % wi])
                    if wcache is not None:
                        S.dma("sp", lambda e, g=g, wi=wi: e.dma_start(out=wcache[0][g], in_=wgt[wi][:].rearrange("p a b c -> p (a b c)")), reads=["wg%d" % wi], writes=[("d_wgc", g)])
                        S.dma("sp", lambda e, g=g, wi=wi: e.dma_start(out=wcache[1][g], in_=wbt[wi][:].rearrange("p a b c -> p (a b c)")), reads=["wb%d" % wi], writes=[("d_wbc", g)])
                for d in range(GW // 128):
                    db = g * (GW // 128) + d
                    for br in range(3):
                        gp = Bk[br][:, 0:N]
                        for kc in range(16):
                            S.pe(lambda e, br=br, kc=kc, wi=wi, d=d, gp=gp, N=N: e.matmul(gp, wgt[wi][:, br, kc, d * 128:(d + 1) * 128], xn[:, kc, 0:N], start=(kc == 0), stop=(kc == 15)),
                                 reads=["wg%d" % wi, "xn"], writes=["B%d" % br])
                        S.act(lambda e, br=br, gp=gp, N=N: e.activation(out=sg[:, br, 0:N], in_=gp, func=AF.Sigmoid), reads=["B%d" % br], writes=[("sg", br)])
                        pp = Bk[3 + br][:, 0:N]
                        for kc in range(8):
                            S.pe(lambda e, br=br, kc=kc, wi=wi, d=d, pp=pp, N=N: e.matmul(pp, wbt[wi][:, br, kc, d * 128:(d + 1) * 128], y[:, yidx(br, kc), 0:N], start=(kc == 0), stop=(kc == 7)),
                                 reads=["wb%d" % wi, "y"], writes=["B%d" % (3 + br)])
                        S.dve(lambda e, br=br, pp=pp, N=N: e.tensor_tensor(out=tt[:, br, 0:N], in0=sg[:, br, 0:N], in1=pp, op=ALU.mult), reads=[("sg", br), "B%d" % (3 + br)], writes=[("tt", br)])
                    S.pool(lambda e, N=N: e.tensor_tensor(out=msum[:, 0:N], in0=tt[:, 0, 0:N], in1=tt[:, 1, 0:N], op=ALU.add), reads=[("tt", 0), ("tt", 1)], writes=["msum"])
                    S.pool(lambda e, N=N, db=db: e.tensor_tensor(out=mg[:, db, 0:N], in0=msum[:, 0:N], in1=tt[:, 2, 0:N], op=ALU.add), reads=["msum", ("tt", 2)], writes=[("mg", db)])
            for g in range(2048 // GW):
                wi = g % 2
                if wcache is not None and not first_pass:
                    S.dma("sp", lambda e, g=g, wi=wi: e.dma_start(out=wot[wi][:].rearrange("p b c -> p (b c)"), in_=wcache[2][g]), reads=[("d_woc", g)], writes=["wo%d" % wi])
                else:
                    S.dma("pool", lambda e, g=g, wi=wi: e.dma_start(out=wot[wi][:], in_=wos[:, :, g * GW:(g + 1) * GW]), writes=["wo%d" % wi])
                    if wcache is not None:
                        S.dma("sp", lambda e, g=g, wi=wi: e.dma_start(out=wcache[2][g], in_=wot[wi][:].rearrange("p b c -> p (b c)")), reads=["wo%d" % wi], writes=[("d_woc", g)])
                for d in range(GW // 128):
                    ob = g * (GW // 128) + d
                    op_ = Bk[6][:, 0:N]
                    for kc in range(16):
                        S.pe(lambda e, kc=kc, wi=wi, d=d, op_=op_, N=N: e.matmul(op_, wot[wi][:, kc, d * 128:(d + 1) * 128], mg[:, kc, 0:N], start=(kc == 0), stop=(kc == 15)),
                             reads=["wo%d" % wi] + [("mg", k) for k in range(16)] if kc == 0 else ["wo%d" % wi], writes=["B6"])
                    S.dve(lambda e, ob=ob, op_=op_, N=N: e.tensor_tensor(out=h[:, ob, 0:N], in0=h[:, ob, 0:N], in1=op_, op=ALU.add), reads=["h", "B6"], writes=["h"])
            if "h_dst" in dram:
                hd, hdk = dram["h_dst"](t0, N)
                outs.append(S.dma("sp", lambda e, hd=hd, N=N: e.dma_start(out=hd, in_=h[:, :, 0:N]), reads=["h"], writes=[hdk]))
        ssp = Bk[7][:, 0:N]
        for ob in range(16):
            S.act(lambda e, ob=ob, N=N: e.activation(out=sq[:, 0:N], in_=h[:, ob, 0:N], func=AF.Square), reads=["h"], writes=["sq"])
            S.pe(lambda e, ob=ob, ssp=ssp, N=N: e.matmul(ssp, onesf[:], sq[:, 0:N], start=(ob == 0), stop=(ob == 15)), reads=["sq", "onesf"], writes=["B7"])
        S.act(lambda e, ssp=ssp, N=N: e.activation(out=rstd[:, 0:N], in_=ssp, func=AF.Sqrt, bias=epsb[:], scale=1.0 / 2048), reads=["B7", "epsb"], writes=["rstd"])
        S.dve(lambda e, N=N: e.reciprocal(out=rstd[:, 0:N], in_=rstd[:, 0:N]), reads=["rstd"], writes=["rstd"])
        for ob in range(16):
            if want_f32:
                ot = otmp[ob % 2]
                otk = "otmp%d" % (ob % 2)
                S.dve(lambda e, ob=ob, N=N, ot=ot: e.scalar_tensor_tensor(out=ot[:, 0:N], in0=h[:, ob, 0:N], scalar=gn[:, ob:ob + 1], in1=rstd[:, 0:N], op0=ALU.mult, op1=ALU.mult),
                      reads=["h", "gn", "rstd"], writes=[otk])
                if "x_dst" in dram:
                    S.act(lambda e, ob=ob, N=N, ot=ot: e.copy(out=xo[:, ob, 0:N], in_=ot[:, 0:N]), reads=[otk], writes=["xo"])
                od, odk = dram["out_dst"](ob, t0, N)
                outs.append(S.dma("sp", lambda e, od=od, ot=ot, N=N: e.dma_start(out=od, in_=ot[:, 0:N]), reads=[otk], writes=[odk]))
            else:
                S.dve(lambda e, ob=ob, N=N: e.scalar_tensor_tensor(out=xo[:, ob, 0:N], in0=h[:, ob, 0:N], scalar=gn[:, ob:ob + 1], in1=rstd[:, 0:N], op0=ALU.mult, op1=ALU.mult),
                      reads=["h", "gn", "rstd"], writes=["xo"])
        if "x_dst" in dram:
            for (xd, xdk, lo, hi) in dram["x_dst"](t0, N):
                outs.append(S.dma("sp", lambda e, xd=xd, lo=lo, hi=hi: e.dma_start(out=xd, in_=xo[:, :, lo:hi]), reads=["xo"], writes=[xdk]))
    return outs


def prog_fused(NMS, TB):
    SEQr = NMS * TB
    Q = SEQr // 4
    TOKC_ = NMETA + Q
    CHX = min(256, Q)
    CHY = min(512, Q)
    NCX = Q // CHX
    NCY = SEQr // CHY
    nc = bass.Bass("TRN2", target_bir_lowering=False)
    ext = {}
    ext["hT"] = nc.dram_tensor("hT", [2048, TOKC_], F32, kind="ExternalInput").ap()
    ext["sel"] = nc.dram_tensor("sel", [128, 4], F32, kind="ExternalInput").ap()
    for nm, shp in CONST_SHAPES.items():
        ext[nm] = nc.dram_tensor(nm, shp, F32, kind="ExternalInput").ap()
    for l in range(5):
        ext["gn%d" % l] = nc.dram_tensor("gn%d" % l, [128, 16], F32, kind="ExternalInput").ap()
    for l in range(4):
        ext["wA%d" % l] = nc.dram_tensor("wA%d" % l, [2048, NCOLA], F32, kind="ExternalInput").ap()
        ext["parA%d" % l] = nc.dram_tensor("parA%d" % l, [128, NPA], F32, kind="ExternalInput").ap()
        ext["wud%d" % l] = nc.dram_tensor("wud%d" % l, [128, 256], F32, kind="ExternalInput").ap()
        ext["wg%d" % l] = nc.dram_tensor("wg%d" % l, [2048, 6144], F32, kind="ExternalInput").ap()
        ext["wbr%d" % l] = nc.dram_tensor("wbr%d" % l, [3072, 2048], F32, kind="ExternalInput").ap()
        ext["wo%d" % l] = nc.dram_tensor("wo%d" % l, [2048, 2048], F32, kind="ExternalInput").ap()
    out = nc.dram_tensor("out", [2048, Q], F32, kind="ExternalOutput").ap()
    hloc = nc.dram_tensor("hloc", [2048, TOKC_], F32).ap()
    xm = nc.dram_tensor("xm", [2048, NMETA], BF16).ap()
    xr_t = [nc.dram_tensor("xr%d" % k, [2048, CHX], BF16) for k in range(NCX)]
    xg_t = [nc.dram_tensor("xg%d" % k, [4 * 2048, CHX], BF16) for k in range(NCX)]
    ylm_t = nc.dram_tensor("ylm", [768, NMETA], BF16)
    ygm_t = nc.dram_tensor("ygm", [4 * 768, NMETA], BF16)
    yl_t = [nc.dram_tensor("yl%d" % k, [768, CHY], BF16) for k in range(NCY)]
    yg_t = [nc.dram_tensor("yg%d" % k, [4 * 768, CHY], BF16) for k in range(NCY)]
    wgc = nc.dram_tensor("wgc", [16, 128, 3 * 16 * 128], BF16).ap()
    wbc = nc.dram_tensor("wbc", [16, 128, 3 * 8 * 128], BF16).ap()
    woc = nc.dram_tensor("woc", [16, 128, 16 * 128], BF16).ap()
    groups = [[0, 1, 2, 3], [4, 5, 6, 7]]
    S = Sched(nc)
    hT3 = ext["hT"].rearrange("(kc p) t -> p kc t", p=128)
    hl3 = hloc.rearrange("(kc p) t -> p kc t", p=128)
    xm3 = xm.rearrange("(kc p) t -> p kc t", p=128)
    xr3 = [t.ap().rearrange("(kc p) t -> p kc t", p=128) for t in xr_t]
    xg4 = [t.ap().rearrange("(q kc p) t -> p q kc t", q=4, p=128) for t in xg_t]
    ygm4 = ygm_t.ap().rearrange("(jj c p) t -> p jj c t", jj=4, c=6, p=128)
    yg4 = [t.ap().rearrange("(jj c p) t -> p jj c t", jj=4, c=6, p=128) for t in yg_t]
    out3 = out.rearrange("(kc p) t -> p kc t", p=128)

    def xloc(t0, N):
        if t0 == 0:
            return [(xm3[:, :, 0:N], "d_xm", 0, N)]
        r0 = t0 - NMETA
        res = []
        for k in range(r0 // CHX, (r0 + N) // CHX):
            res.append((xr3[k][:, :, :], ("d_xr", k), k * CHX - r0, (k + 1) * CHX - r0))
        return res

    with ExitStack() as top:
        selt = top.enter_context(nc.sbuf_tensor("s_sel", [128, 4], F32))
        S.dma("sp", lambda e: e.dma_start(out=selt[:], in_=ext["sel"]), writes=["sel"])

        def y_load(y, cand, t0, N):
            if t0 == 0:
                for jj in range(4):
                    S.dma("sp", lambda e, jj=jj: e.dma_start(out=y[:, jj * 6:(jj + 1) * 6, 0:N], in_=ygm4[:, jj, :, 0:N]), reads=["d_ygm"], writes=["y"])
                return
            for q in range(4):
                kk = (q * Q + (t0 - NMETA)) // CHY
                for jj in range(4):
                    S.dma("sp", lambda e, jj=jj, kk=kk: e.dma_start(out=cand[:, jj * 6:(jj + 1) * 6, 0:N], in_=yg4[kk][:, jj, :, 0:N]), reads=[("d_yg", kk)], writes=["cand"])
                if q == 0:
                    S.dve(lambda e: e.tensor_scalar_mul(out=y[:, :, 0:N], in0=cand[:, :, 0:N], scalar1=selt[:, 0:1]), reads=["cand", "sel"], writes=["y"])
                else:
                    S.dve(lambda e, q=q: e.scalar_tensor_tensor(out=y[:, :, 0:N], in0=cand[:, :, 0:N], scalar=selt[:, q:q + 1], in1=y[:, :, 0:N], op0=ALU.mult, op1=ALU.add),
                          reads=["cand", "sel", "y"], writes=["y"])

        def gather_all(pairs, outs):
            last = None
            for (src_t, dst_t, wk) in pairs:
                last = S.cc(lambda e, src_t=src_t, dst_t=dst_t: e.collective_compute("AllGather", ALU.bypass, replica_groups=groups, ins=[src_t.ap().opt()], outs=[dst_t.ap().opt()]),
                            writes=[wk], extra=outs)
            S.barrier()
            return last

        xpairs = [(xr_t[k], xg_t[k], ("d_xg", k)) for k in range(NCX)]
        ypairs = [(ylm_t, ygm_t, "d_ygm")] + [(yl_t[k], yg_t[k], ("d_yg", k)) for k in range(NCY)]
        dN = dict(gnext=ext["gn0"], h_src=lambda t0, N: (hT3[:, :, t0:t0 + N], ("d_h", t0)), x_dst=xloc)
        with ExitStack() as es:
            outs = build_B(nc, S, es, TOKC_, dN, proj=False, pref="n0")
        g0 = gather_all(xpairs, outs)
        finals = None
        FSTOP = int(os.environ.get("FSTOP", "99"))
        if FSTOP == 0:
            S.emit(final_wait_ops=[g0])
            return nc, S
        for l in range(4):
            dA = dict(wA=ext["wA%d" % l], parA=ext["parA%d" % l], wud=ext["wud%d" % l])
            for nm in CONST_SHAPES:
                dA[nm] = ext[nm]

            def xn_src(t0, TBc):
                if t0 == 0:
                    return xm3[:, :, 0:TBc], "d_xm"
                q, off = divmod(t0 - NMETA, Q)
                k, c = divmod(off, CHX)
                return xg4[k][:, q, :, c:c + TBc], ("d_xg", k)

            def y_dst(r0, t0, TBc):
                if t0 == 0:
                    return ylm_t.ap()[r0:r0 + 128, 0:TBc], ("d_yl", r0, t0)
                k, c = divmod(t0 - NMETA, CHY)
                return yl_t[k].ap()[r0:r0 + 128, c:c + TBc], ("d_yl", r0, t0)

            dA["xn_src"] = xn_src
            dA["y_dst"] = y_dst
            with ExitStack() as es:
                outs = build_A(nc, S, es, NMS, TB, dA, pref="a%d" % l)
            g1 = gather_all(ypairs, outs)
            if FSTOP == 1:
                S.emit(final_wait_ops=[g1])
                return nc, S
            last = l == 3
            hsrc3 = hT3 if l == 0 else hl3
            dB = dict(gnext=ext["gn%d" % (l + 1)], wg=ext["wg%d" % l], wbr=ext["wbr%d" % l], wo=ext["wo%d" % l],
                      h_src=lambda t0, N, hsrc3=hsrc3: (hsrc3[:, :, t0:t0 + N], ("d_h", t0)),
                      xn_srcB=xloc, y_load=y_load, need_cand=True, wcache=(wgc, wbc, woc), yidx=lambda br, kc: (kc // 2) * 6 + br * 2 + (kc % 2))
            if not last:
                dB["h_dst"] = lambda t0, N: (hl3[:, :, t0:t0 + N], ("d_h", t0))
                dB["x_dst"] = xloc
            else:
                dB["out_dst"] = lambda ob, t0, N: (out3[:, ob, t0 - NMETA:t0 - NMETA + N], ("d_out", ob, t0))
            with ExitStack() as es:
                outs = build_B(nc, S, es, TOKC_, dB, proj=True, last=last, skip_meta=last, pref="b%d" % l)
            if not last:
                gather_all(xpairs, outs)
            else:
                finals = outs
        S.emit(final_wait_ops=finals)
    return nc, S


SPL = [1024, 1024, 1024, 1024, 512, 512, 1024, 1024, 4, 4, 1024, 1024, 1024, 1024, 64, 64, 1024, 2048, 2048, 2048]
NAMES = ["a_q", "a_f", "a_i", "a_z", "b_q", "b_k", "b_v", "b_o", "b_ig", "b_fg", "b_z", "c_r", "c_k", "c_v", "c_wd", "c_ad", "c_z", "g_a", "g_b", "g_c"]
OFF = {}
_o = 0
for n_, w_ in zip(NAMES, SPL):
    OFF[n_] = _o
    _o += w_
NIN = _o


def colsA(j):
    cols = np.zeros(NCOLA, np.int64)

    def put(tile, start):
        cols[CT[tile] * 128:(CT[tile] + 1) * 128] = np.arange(start, start + 128)

    for hh in range(2):
        h = 2 * j + hh
        put("hq%d" % hh, OFF["a_q"] + h * 128)
        put("hf%d" % hh, OFF["a_f"] + h * 128)
        put("hi%d" % hh, OFF["a_i"] + h * 128)
        put("hz%d" % hh, OFF["a_z"] + h * 128)
    put("mq", OFF["b_q"] + j * 128)
    put("mk", OFF["b_k"] + j * 128)
    for i in range(2):
        put("mv%d" % i, OFF["b_v"] + j * 256 + i * 128)
        put("mo%d" % i, OFF["b_o"] + j * 256 + i * 128)
        put("mz%d" % i, OFF["b_z"] + j * 256 + i * 128)
    for p in range(2):
        c0 = j * 256 + p * 128
        put("rr%d" % p, OFF["c_r"] + c0)
        put("rk%d" % p, OFF["c_k"] + c0)
        put("rv%d" % p, OFF["c_v"] + c0)
        put("rz%d" % p, OFF["c_z"] + c0)
    cols[CT["rwa"] * 128:CT["rwa"] * 128 + 64] = np.arange(OFF["c_wd"], OFF["c_wd"] + 64)
    cols[CT["rwa"] * 128 + 64:CT["rwa"] * 128 + 128] = np.arange(OFF["c_ad"], OFF["c_ad"] + 64)
    cols[25 * 128] = OFF["b_ig"] + j
    cols[25 * 128 + 1] = OFF["b_fg"] + j
    return cols


def pack_A(inp, l, j):
    f = np.float32
    wA = np.ascontiguousarray(np.asarray(inp["w_in"][l])[:, colsA(j)], dtype=f)
    par = np.zeros((128, NPA), f)

    def put(nm, v):
        o, w = PA[nm]
        v = np.asarray(v, f)
        if v.ndim == 0:
            par[:, o:o + w] = v
        else:
            par[:, o:o + w] = v.reshape(128, w)

    lbl = np.asarray(inp["hgrn_lb_logits"])
    par[:, PA["lbsel"][0]:PA["lbsel"][0] + 4] = np.array([1.0 if 1 <= i <= l else 0.0 for i in range(4)], f)[None, :]
    for hh in range(2):
        ch = slice((2 * j + hh) * 128, (2 * j + hh + 1) * 128)
        put("lbl%d" % hh, lbl[:, ch].T)
        put("hg%d" % hh, np.asarray(inp["hgrn_norm_g"][l])[ch])
    cw = np.asarray(inp["mlstm_conv"][l])
    put("cwq", cw[:, j * 128:(j + 1) * 128].T)
    put("cwk", cw[:, 512 + j * 128:512 + (j + 1) * 128].T)
    for i in range(2):
        put("mg%d" % i, np.asarray(inp["mlstm_norm_g"][l])[j * 256 + i * 128:j * 256 + (i + 1) * 128])
    put("igb", float(np.asarray(inp["mlstm_ig_b"])[l, j]))
    put("fgb", float(np.asarray(inp["mlstm_fg_b"])[l, j]))
    put("eps", 1e-6)
    put("lneps", 64e-5)
    put("zero", 0.0)
    mu = np.asarray(inp["rwkv_mu"][l])
    for p in range(2):
        c = slice(j * 256 + p * 128, j * 256 + (p + 1) * 128)
        put("mur%d" % p, mu[0:1024][c])
        put("muk%d" % p, mu[1024:2048][c])
        put("muv%d" % p, mu[2048:3072][c])
        put("w0%d" % p, np.asarray(inp["rwkv_w0"][l])[c])
        put("a0%d" % p, np.asarray(inp["rwkv_a0"][l])[c])
        put("kk%d" % p, np.asarray(inp["rwkv_k_k"][l])[c])
        put("ka%d" % p, np.asarray(inp["rwkv_k_a"][l])[c])
        put("rk%d" % p, np.asarray(inp["rwkv_r_k"][l])[c])
        put("lg%d" % p, np.asarray(inp["rwkv_ln_g"][l])[c])
        put("lb%d" % p, np.asarray(inp["rwkv_ln_b"][l])[c])
    put("muwa", mu[3072:3200])
    wud = np.zeros((128, 256), f)
    wud[0:64] = np.asarray(inp["rwkv_w_up"][l])[:, j * 256:(j + 1) * 256]
    wud[64:128] = np.asarray(inp["rwkv_a_up"][l])[:, j * 256:(j + 1) * 256]
    return dict(wA=wA, parA=par, wud=wud)


TB_A = 256
SEQ = 8192
NMS_A = SEQ // TB_A


def kernel(**inp):
    f = np.float32
    x = np.asarray(inp["x"], f)
    meta = np.asarray(inp["meta_tokens"], f)
    Q = SEQ // 4
    nc, _S = prog_fused(NMS_A, TB_A)
    cs = host_consts()

    def gn(g):
        return np.ascontiguousarray(np.asarray(g, f).reshape(16, 128).T)

    gns = [gn(inp["norm_g"][l]) for l in range(4)] + [gn(inp["final_norm_g"])]
    shared = {}
    for l in range(4):
        w_in_l = np.asarray(inp["w_in"][l])
        shared["wg%d" % l] = np.ascontiguousarray(w_in_l[:, OFF["g_a"]:OFF["g_a"] + 6144], dtype=f)
        shared["wbr%d" % l] = np.ascontiguousarray(np.asarray(inp["w_br"][l], f).reshape(3072, 2048))
        shared["wo%d" % l] = np.ascontiguousarray(np.asarray(inp["w_out"][l], f))
    packs = {}
    for j in range(4):
        for l in range(4):
            packs[(l, j)] = pack_A(inp, l, j)
    maps = []
    for c in range(8):
        b, j = divmod(c, 4)
        m = {"hT": np.ascontiguousarray(np.concatenate([meta, x[b, j * Q:(j + 1) * Q]], axis=0).T)}
        sel = np.zeros((128, 4), f)
        sel[:, j] = 1.0
        m["sel"] = sel
        m.update(cs)
        for l in range(5):
            m["gn%d" % l] = gns[l]
        for l in range(4):
            pa = packs[(l, j)]
            m["wA%d" % l] = pa["wA"]
            m["parA%d" % l] = pa["parA"]
            m["wud%d" % l] = pa["wud"]
        m.update(shared)
        maps.append(m)
    res = run_bass_kernel_spmd(nc, maps, core_ids=list(range(8)))
    out = np.empty((2, SEQ, 2048), f)
    for c in range(8):
        b, j = divmod(c, 4)
        out[b, j * Q:(j + 1) * Q] = np.asarray(res.results[c]["out"], f).T
    return out
```

```python
import ml_dtypes
from concourse.bass_utils import run_bass_kernel_spmd
import numpy as np
import concourse.bass as bass
import concourse.mybir as mybir
from contextlib import ExitStack

F32 = mybir.dt.float32
BF16 = mybir.dt.bfloat16
AF = mybir.ActivationFunctionType
ALU = mybir.AluOpType

COMPUTE = ("pe", "act", "dve", "pool")
DMAQ = ("sp", "poolq")
NDMASEM = 8
import os as _os
FUSEWAIT = _os.environ.get('FUSEWAIT', '1') == '1'


class Sched:
    def __init__(self, nc, strict_same=True):
        self.nc = nc
        self.ops = []
        self.last_w = {}
        self.readers = {}
        self.strict_same = strict_same
        self.pe_last = {}
        self.keymap = {}
        self.bar = set()
        self.epoch = 0
        self.last_eng = {}
        self.dma_since = []

    def op(self, eng, fn, reads=(), writes=(), dma=False, rowgrp=None, extra=(), cc=False):
        i = len(self.ops)
        if self.keymap:
            km = self.keymap

            def _mk(k):
                if isinstance(k, tuple) and k and k[0] in km:
                    return (km[k[0]],) + tuple(k[1:])
                if isinstance(k, str) and k in km:
                    return km[k]
                return k
            reads = [_mk(k) for k in reads]
            writes = [_mk(k) for k in writes]
        ex = [k for k in reads if isinstance(k, str) and k[0] == "B" and len(k) <= 2]
        if ex:
            reads = [k for k in reads if k not in ex]
            writes = list(writes) + [k for k in ex if k not in writes]
        deps = set()
        for k in reads:
            w = self.last_w.get(k)
            if w is not None:
                deps.add(w)
        for k in writes:
            w = self.last_w.get(k)
            if w is not None:
                deps.add(w)
            for r in self.readers.get(k, {}).values():
                for x in r:
                    deps.add(x)
        deps.discard(i)
        forced = set()
        if eng == "pe":
            for k in writes:
                if isinstance(k, str) and k[0] == "B" and len(k) <= 2:
                    pl = self.pe_last.get(k)
                    if pl is not None and pl[1] != rowgrp:
                        forced.add(pl[0])
                    self.pe_last[k] = (i, rowgrp)
        deps |= forced
        deps |= set(extra)
        deps |= self.bar
        self.ops.append(dict(eng=eng, fn=fn, deps=deps, dma=dma, forced=forced, cc=cc, epoch=self.epoch))
        self.last_eng[eng] = i
        if dma or cc:
            self.dma_since.append(i)
        for k in writes:
            self.last_w[k] = i
            self.readers[k] = {}
        for k in reads:
            d = self.readers.setdefault(k, {})
            if dma:
                d.setdefault(eng + "_dma", []).append(i)
            else:
                d[eng] = [i]
        return i

    def barrier(self):
        self.bar = set(self.last_eng.values()) | set(self.dma_since)
        self.dma_since = []
        self.epoch += 1
        self.pe_last = {}

    def cc(self, fn, reads=(), writes=(), extra=()):
        return self.op("pool", fn, reads, writes, cc=True, extra=extra)

    def pe(self, fn, reads=(), writes=(), rowgrp=None):
        return self.op("pe", fn, reads, writes, rowgrp=rowgrp)

    def act(self, fn, reads=(), writes=()):
        return self.op("act", fn, reads, writes)

    def dve(self, fn, reads=(), writes=()):
        return self.op("dve", fn, reads, writes)

    def pool(self, fn, reads=(), writes=()):
        return self.op("pool", fn, reads, writes)

    def dma(self, q, fn, reads=(), writes=()):
        return self.op(q, fn, reads, writes, dma=True)

    def emit(self, final_wait_ops=()):
        nc = self.nc
        ops = self.ops
        streams = {"pe": [], "act": [], "dve": [], "pool": [], "sp": []}
        for i, o in enumerate(ops):
            streams[o["eng"]].append(i)
        need = [False] * len(ops)
        for i, o in enumerate(ops):
            for d in o["deps"]:
                od = ops[d]
                if od["dma"] or od["cc"]:
                    need[d] = True
                elif od["eng"] != o["eng"]:
                    need[d] = True
                elif o["eng"] != "pe" and self.strict_same:
                    need[d] = True
                elif d in o["forced"]:
                    need[d] = True
        for d in final_wait_ops:
            need[d] = True
        with ExitStack() as es:
            nep = self.epoch + 1
            csem = {(e, ep): es.enter_context(nc.semaphore("c_%s%d" % (e, ep))) for e in COMPUTE for ep in range(nep)}
            ccsem = es.enter_context(nc.semaphore("ccsem"))
            cccnt = 0
            dsem = {
                q: [es.enter_context(nc.semaphore("d_%s%d" % (q, k))) for k in range(NDMASEM)]
                for q in ("sp", "pool")
            }
            cnt = {(e, ep): 0 for e in COMPUTE for ep in range(nep)}
            dcnt = {"sp": 0, "pool": 0}
            sig = [None] * len(ops)
            prev_same_sem = [None] * len(ops)
            for i, o in enumerate(ops):
                if o["dma"]:
                    q = o["eng"]
                    n = dcnt[q]
                    dcnt[q] += 1
                    s = dsem[q][n % NDMASEM]
                    sig[i] = (("d", q, n % NDMASEM), s, 16 * (n // NDMASEM + 1))
                    if n >= NDMASEM:
                        prev_same_sem[i] = (("d", q, n % NDMASEM), s, 16 * (n // NDMASEM))
                elif o["cc"]:
                    cccnt += 1
                    sig[i] = (("cc",), ccsem, cccnt)
                elif need[i]:
                    e = (o["eng"], o["epoch"])
                    cnt[e] += 1
                    sig[i] = (("c", e), csem[e], cnt[e])
            self.stats = dict(n_ops=len(ops), cnt={str(k): v for k, v in cnt.items()}, dcnt=dict(dcnt),
                              per_eng={e: len(v) for e, v in streams.items()})
            blk = es.enter_context(nc.Block())

            def run_stream(ename, eobj):
                known = {}
                nwait = 0
                for i in streams[ename]:
                    o = ops[i]
                    waits = {}
                    if prev_same_sem[i] is not None:
                        k, s, v = prev_same_sem[i]
                        if known.get(k, 0) < v:
                            waits[k] = (s, v)
                    for d in o["deps"]:
                        od = ops[d]
                        if not od["dma"] and not od["cc"] and od["eng"] == ename and (ename == "pe" or not self.strict_same) and d not in o["forced"]:
                            continue
                        if sig[d] is None:
                            continue
                        k, s, v = sig[d]
                        if known.get(k, 0) < v and (k not in waits or waits[k][1] < v):
                            waits[k] = (s, v)
                    wl = list(waits.items())
                    fuse = None
                    if FUSEWAIT and wl and not o["cc"]:
                        fuse = wl.pop()
                    for k, (s, v) in wl:
                        eobj.wait_ge(s, v)
                        known[k] = v
                        nwait += 1
                    ins = o["fn"](eobj)
                    if fuse is not None:
                        k, (s, v) = fuse
                        ins._wait_ge(s, v)
                        known[k] = v
                    if sig[i] is not None:
                        ins.then_inc(sig[i][1], 16 if o["dma"] else 1)
                if ename == "sp":
                    for d in final_wait_ops:
                        k, s, v = sig[d]
                        eobj.wait_ge(s, v)
                self.stats["waits_" + ename] = nwait

            blk.sync(lambda e: run_stream("sp", e))
            blk.tensor(lambda e: run_stream("pe", e))
            blk.scalar(lambda e: run_stream("act", e))
            blk.vector(lambda e: run_stream("dve", e))
            blk.gpsimd(lambda e: run_stream("pool", e))


import math, os
RSTOP = int(os.environ.get('RSTOP', '9'))

D = 2048
KC = 16
NMETA = 16
LCH = 64
C0 = math.exp(-0.5)
CT = dict(hq0=0, hq1=1, hf0=2, hf1=3, hi0=4, hi1=5, hz0=6, hz1=7,
          mq=8, mk=9, mv0=10, mv1=11, mo0=12, mo1=13, mz0=14, mz1=15,
          rr0=16, rr1=17, rk0=18, rk1=19, rv0=20, rv1=21, rz0=22, rz1=23, rwa=24)
NCOLA = 25 * 128 + 2
PA = {}
_n = 0
for nm, w in [("lbsel", 4), ("lbl0", 4), ("lbl1", 4), ("hg0", 1), ("hg1", 1), ("cwq", 4), ("cwk", 4), ("mg0", 1), ("mg1", 1),
              ("igb", 1), ("fgb", 1), ("eps", 1), ("lneps", 1), ("zero", 1),
              ("mur0", 1), ("mur1", 1), ("muk0", 1), ("muk1", 1), ("muv0", 1), ("muv1", 1), ("muwa", 1),
              ("w00", 1), ("w01", 1), ("a00", 1), ("a01", 1), ("kk0", 1), ("kk1", 1), ("ka0", 1), ("ka1", 1),
              ("rk0", 1), ("rk1", 1), ("lg0", 1), ("lg1", 1), ("lb0", 1), ("lb1", 1)]:
    PA[nm] = (_n, w)
    _n += w
NPA = _n


def host_consts():
    c = {}
    c["ident"] = np.eye(128, dtype=np.float32)
    c["ones"] = np.ones((128, 128), np.float32)
    bd = np.zeros((128, 128), np.float32)
    bd[:64, :64] = 1
    bd[64:, 64:] = 1
    c["bd"] = bd
    s = np.arange(64)[:, None]
    t = np.arange(64)[None, :]
    mI = (s <= t).astype(np.float32)
    mS = (s < t).astype(np.float32)
    c["maskI"] = mI
    c["mask2"] = np.stack([mS, mI], axis=1)
    c["maskL"] = (t < s).astype(np.float32)
    pm = np.zeros((64, 2, 128), np.float32)
    pm[:, 0, :64] = 1
    pm[:, 1, 64:] = 1
    c["padmask"] = pm
    return c


CONST_SHAPES = dict(ident=[128, 128], ones=[128, 128], bd=[128, 128], maskI=[64, 64], mask2=[64, 2, 64],
                    maskL=[64, 64], padmask=[64, 2, 128])


class Ctx:
    pass


def build_A(nc, S, es, NMS, TB, dram, mixers=("h", "m", "r"), layer=0, first=True, pref=""):
    def sb(name, shape, dt=F32):
        return es.enter_context(nc.sbuf_tensor("s_" + pref + name, shape, dt))

    def ps(name, shape, dt=F32):
        return es.enter_context(nc.psum_tensor("p_" + pref + name, shape, dt))

    TBM = TB
    NCH = TB // LCH
    cst = {}
    for nm, shp in CONST_SHAPES.items():
        dt = F32 if nm in ("maskI", "mask2", "maskL", "padmask") else BF16
        cst[nm] = sb("c_" + nm, shp, dt)
        S.dma("pool", lambda e, nm=nm: e.dma_start(out=cst[nm][:], in_=dram[nm]), writes=["c_" + nm])
    ident64f = sb("ident64f", [64, 64])
    S.dma("sp", lambda e: e.dma_start(out=ident64f[:], in_=dram["ident"][0:64, 0:64]), writes=["ident64f"])
    bdmaskf = sb("bdmaskf", [128, 128])
    S.dma("sp", lambda e: e.dma_start(out=bdmaskf[:], in_=dram["bd"]), writes=["bdmaskf"])
    onesf = sb("onesf", [128, TBM])
    S.pool(lambda e: e.memset(onesf[:], 1.0), writes=["onesf"])
    par = sb("par", [128, NPA])
    S.dma("sp", lambda e: e.dma_start(out=par[:], in_=dram["parA"]), writes=["par"])

    def P(nm, i=0):
        o, w = PA[nm]
        return par[:, o + i:o + i + 1]

    wA = sb("wA", [128, KC, NCOLA], BF16)
    wsrc = dram["wA"].rearrange("(kc p) n -> p kc n", p=128)
    for kc in range(KC):
        S.dma("pool", lambda e, kc=kc: e.dma_start(out=wA[:, kc, :], in_=wsrc[:, kc, :]), writes=[("wA", kc)])
    WAK = [("wA", kc) for kc in range(KC)]
    wud = sb("wud", [128, 256], BF16)
    S.dma("pool", lambda e: e.dma_start(out=wud[:], in_=dram["wud"]), writes=["wud"])

    B0 = ps("B0", [128, 512]); B1 = ps("B1", [128, 512]); Bt = ps("Bt", [128, 1024], BF16)
    B2 = ps("B2", [128, 512]); B3 = ps("B3", [128, 512]); B4 = ps("B4", [128, 512])
    B5 = ps("B5", [128, 512]); B6 = ps("B6", [128, 512])
    ppbuf = [(B0[:, 0:256], "B0"), (B1[:, 0:256], "B1")]
    ppi = [0]

    NXB = 1 if TB >= 256 else 2
    xn = [sb("xn%d" % i, [128, KC, TBM], BF16) for i in range(NXB)]
    if "xn_src" in dram:
        xn_src = dram["xn_src"]
        y_dst = dram["y_dst"]
    else:
        xsrc = dram["xnT"].rearrange("(kc p) t -> p kc t", p=128)
        ydst = dram["yT"]

        def xn_src(t0, TBc):
            return xsrc[:, :, t0:t0 + TBc], "d_xin"

        def y_dst(r0, t0, TBc):
            return ydst[r0:r0 + 128, t0:t0 + TBc], ("d_yl", r0, t0)

    st = Ctx()
    if "h" in mixers:
        st.hS = sb("hS", [128, 2, 128]); st.hSb = sb("hSb", [128, 2, 128], BF16)
        S.pool(lambda e: e.memset(st.hS[:], 0.0), writes=["hS"])
        S.pool(lambda e: e.memset(st.hSb[:], 0.0), writes=["hSb"])
        st.lb = sb("lb", [128, 2]); st.oml = sb("oml", [128, 2]); st.lbm1 = sb("lbm1", [128, 2])
        lbe = sb("lbe", [128, 2, 4]); lbs = sb("lbs", [128, 2]); lbr = sb("lbr", [128, 2])
        o0, _ = PA["lbl0"]
        S.act(lambda e: e.activation(out=lbe[:].rearrange("p a b -> p (a b)"), in_=par[:, o0:o0 + 8], func=AF.Exp), reads=["par"], writes=["lbe"])
        S.dve(lambda e: e.tensor_reduce(out=lbs[:], in_=lbe[:], axis=mybir.AxisListType.X, op=ALU.add), reads=["lbe"], writes=["lbs"])
        S.dve(lambda e: e.reciprocal(out=lbr[:], in_=lbs[:]), reads=["lbs"], writes=["lbr"])
        lbt = sb("lbt", [128, 2]); lbm = sb("lbm", [128, 2, 4])
        osel, _ = PA["lbsel"]
        S.dve(lambda e: e.tensor_tensor(out=lbm[:], in0=lbe[:], in1=par[:, osel:osel + 4].unsqueeze(1).to_broadcast([128, 2, 4]), op=ALU.mult), reads=["lbe", "par"], writes=["lbm"])
        S.dve(lambda e: e.tensor_reduce(out=lbt[:], in_=lbm[:], axis=mybir.AxisListType.X, op=ALU.add), reads=["lbm"], writes=["lbt"])
        S.dve(lambda e: e.tensor_tensor(out=st.lb[:], in0=lbt[:], in1=lbr[:], op=ALU.mult), reads=["lbt", "lbr"], writes=["lb"])
        S.dve(lambda e: e.tensor_scalar(out=st.oml[:], in0=st.lb[:], scalar1=-1.0, scalar2=1.0, op0=ALU.mult, op1=ALU.add), reads=["lb"], writes=["oml"])
        S.dve(lambda e: e.tensor_scalar_add(out=st.lbm1[:], in0=st.lb[:], scalar1=-1.0), reads=["lb"], writes=["lbm1"])
    if "m" in mixers:
        st.mC = sb("mC", [128, 257]); st.mCb = sb("mCb", [128, 257], BF16)
        S.pool(lambda e: e.memset(st.mC[:], 0.0), writes=["mC"])
        st.mxq = sb("mxq", [128, 3 + TBM]); st.mxk = sb("mxk", [128, 3 + TBM])
        S.pool(lambda e: e.memset(st.mxq[:, 0:3], 0.0), writes=["mqx"])
        S.pool(lambda e: e.memset(st.mxk[:, 0:3], 0.0), writes=["mkx"])
        st.mmin = sb("mmin", [1, 1])
        S.pool(lambda e: e.memset(st.mmin[:], 0.0), writes=["mmin"])
        st.mTT = sb("mTT", [64, 3 * 128 + 1], BF16)
        S.pool(lambda e: e.memset(st.mTT[:], 1.0), writes=["mTT"])
        st.onesrow = sb("onesrow", [1, 128])
        S.pool(lambda e: e.memset(st.onesrow[:], 1.0), writes=["onesrow"])
    if "r" in mixers:
        st.rS = sb("rS", [128, 2, 128]); st.rSb = sb("rSb", [128, 2, 128], BF16)
        S.pool(lambda e: e.memset(st.rS[:], 0.0), writes=["rS"])
        S.pool(lambda e: e.memset(st.rSb[:], 0.0), writes=["rSb"])
        st.rraw = {}
        for nm in ("rr0", "rr1", "rk0", "rk1", "rv0", "rv1", "rwa"):
            st.rraw[nm] = sb("raw_" + nm, [128, 1 + TBM])
            S.pool(lambda e, nm=nm: e.memset(st.rraw[nm][:, 0:1], 0.0), writes=["raw_" + nm])

    W = {}

    ALIAS = {}
    if TB >= 256:
        AL = [("h_d1", "h_ft", "h_ft"), ("h_rs", "h_e1", "h_e1"), ("h_sq", "h_Qi", "h_Qi"),
              ("r_sgw", "h_qa", "h_qa"), ("r_a", "h_sig", "h_sig"), ("r_kkn", "h_sz", "h_sz"), ("r_kf", "h_k", "h_k"),
              ("r_bv", "h_ft", "h_ft"), ("r_cs", "h_cg", "h_cg"), ("r_tmp", "h_e1", "h_e1"), ("r_ecw", "h_eg", "h_eg"),
              ("r_yall", "h_oall", "h_oall"),
              ("r_tmpb", "h_vb", "h_vb"), ("r_Bd", "h_Qi", "h_Qi"), ("r_Kd", "h_Ki", "h_Ki"), ("r_Btl", "h_Qx", "h_Qx"),
              ("r_Ktl", "h_Kt", "h_Kt"), ("r_vb", "h_yo", "h_yo"),
              ("r_yb", "h_Qi", "h_Qi"), ("r_sq", "h_Ki", "h_Ki"), ("r_yo", "h_Qx", "h_Qx"),
              ("m_mean", "m_qs", "m_mqs"), ("m_var", "m_ks", "m_mks"), ("r_mean", "r_m_rr0", "r_m_rr0"), ("r_var", "r_m_rr1", "r_m_rr1")]
        for a, t_, k_ in AL:
            ALIAS[a] = t_
            S.keymap[a] = k_

    def wt(name, shape, dt=F32):
        if name in ALIAS:
            return W[ALIAS[name]]
        if name not in W:
            W[name] = sb("w_" + name, shape, dt)
        return W[name]

    rr = [0]

    def ev(out, in_, reads, writes):
        rr[0] ^= 1
        if rr[0]:
            return S.dve(lambda e: e.tensor_copy(out=out, in_=in_), reads, writes)
        return S.act(lambda e: e.copy(out=out, in_=in_), reads, writes)

    def inproj(xt, xkey, col0, ncols, TBc):
        buf, key = ppbuf[ppi[0] % 2]
        ppi[0] += 1
        out = buf[0:ncols, 0:TBc]
        for kc in range(KC):
            S.pe(lambda e, kc=kc: e.matmul(out, wA[:, kc, col0:col0 + ncols], xt[:, kc, 0:TBc], start=(kc == 0), stop=(kc == KC - 1)),
                 reads=[xkey, ("wA", kc)], writes=[key])
        return out, key

    def macro(ms, t0, TBc, L):
        nch = TBc // L
        xt = xn[ms % NXB]
        xkey = "xn%d" % (ms % NXB)
        xin_, xink_ = xn_src(t0, TBc)
        S.dma("sp", lambda e: e.dma_start(out=xt[:, :, 0:TBc], in_=xin_), reads=[xink_], writes=[xkey])

        def c3(ap):
            return ap.rearrange("p (c j) -> p c j", j=L)

        def gen_h():
            qa = wt("h_qa", [128, 2, TBM]); sig = wt("h_sig", [128, 2, TBM]); vb = wt("h_vb", [128, 2, TBM], BF16)
            sz = wt("h_sz", [128, 2, TBM])
            for hh in range(2):
                p_, k_ = inproj(xt, xkey, CT["hq%d" % hh] * 128, 128, TBc)
                S.act(lambda e, p_=p_, hh=hh: e.activation(out=qa[:, hh, 0:TBc], in_=p_, func=AF.Silu), reads=[k_], writes=[("h_qa", hh)])
                p_, k_ = inproj(xt, xkey, CT["hf%d" % hh] * 128, 128, TBc)
                S.act(lambda e, p_=p_, hh=hh: e.activation(out=sig[:, hh, 0:TBc], in_=p_, func=AF.Sigmoid), reads=[k_], writes=[("h_sig", hh)])
                p_, k_ = inproj(xt, xkey, CT["hi%d" % hh] * 128, 128, TBc)
                S.dve(lambda e, p_=p_, hh=hh: e.tensor_copy(out=vb[:, hh, 0:TBc], in_=p_), reads=[k_], writes=[("h_vb", hh)])
                p_, k_ = inproj(xt, xkey, CT["hz%d" % hh] * 128, 128, TBc)
                S.act(lambda e, p_=p_, hh=hh: e.activation(out=sz[:, hh, 0:TBc], in_=p_, func=AF.Silu), reads=[k_], writes=[("h_sz", hh)])
            yield
            kk = wt("h_k", [128, 2, TBM]); ft = wt("h_ft", [128, 2, TBM]); cg = wt("h_cg", [128, 2, TBM])
            d1 = wt("h_d1", [128, 2, TBM]); e1 = wt("h_e1", [128, 2, TBM]); eg = wt("h_eg", [128, 2, TBM])
            Qi = wt("h_Qi", [128, 2, TBM], BF16); Ki = wt("h_Ki", [128, 2, TBM], BF16)
            Qx = wt("h_Qx", [128, 2, TBM], BF16); Kt = wt("h_Kt", [128, 2, TBM], BF16)
            mid = L // 2
            for hh in range(2):
                S.dve(lambda e, hh=hh: e.tensor_scalar(out=kk[:, hh, 0:TBc], in0=sig[:, hh, 0:TBc], scalar1=-1.0, scalar2=st.lbm1[:, hh:hh + 1], op0=ALU.add, op1=ALU.mult),
                      reads=[("h_sig", hh), "lbm1"], writes=[("h_k", hh)])
                S.dve(lambda e, hh=hh: e.tensor_scalar(out=ft[:, hh, 0:TBc], in0=sig[:, hh, 0:TBc], scalar1=st.oml[:, hh:hh + 1], scalar2=st.lb[:, hh:hh + 1], op0=ALU.mult, op1=ALU.add),
                      reads=[("h_sig", hh), "oml", "lb"], writes=[("h_ft", hh)])
                S.dve(lambda e, hh=hh: e.tensor_scalar_max(out=ft[:, hh, 0:TBc], in0=ft[:, hh, 0:TBc], scalar1=1e-12), reads=[("h_ft", hh)], writes=[("h_ft", hh)])
                S.act(lambda e, hh=hh: e.activation(out=ft[:, hh, 0:TBc], in_=ft[:, hh, 0:TBc], func=AF.Ln), reads=[("h_ft", hh)], writes=[("h_ft", hh)])
                for c in range(nch):
                    S.dve(lambda e, hh=hh, c=c: e.tensor_tensor_scan(out=cg[:, hh, c * L:(c + 1) * L], data0=onesf[:, 0:L], data1=ft[:, hh, c * L:(c + 1) * L], initial=0.0, op0=ALU.mult, op1=ALU.add),
                          reads=[("h_ft", hh), "onesf"], writes=[("h_cg", hh)])
                cg3 = c3(cg[:, hh, 0:TBc])
                S.dve(lambda e, hh=hh, cg3=cg3: e.tensor_tensor(out=c3(d1[:, hh, 0:TBc]), in0=cg3, in1=cg3[:, :, mid:mid + 1].to_broadcast([128, nch, L]), op=ALU.subtract),
                      reads=[("h_cg", hh)], writes=[("h_d1", hh)])
                S.act(lambda e, hh=hh: e.activation(out=e1[:, hh, 0:TBc], in_=d1[:, hh, 0:TBc], func=AF.Exp), reads=[("h_d1", hh)], writes=[("h_e1", hh)])
                S.dve(lambda e, hh=hh: e.scalar_tensor_tensor(out=Qi[:, hh, 0:TBc], in0=qa[:, hh, 0:TBc], scalar=128.0 ** -0.5, in1=e1[:, hh, 0:TBc], op0=ALU.mult, op1=ALU.mult),
                      reads=[("h_qa", hh), ("h_e1", hh)], writes=[("h_Qi", hh)])
                S.act(lambda e, hh=hh: e.activation(out=e1[:, hh, 0:TBc], in_=d1[:, hh, 0:TBc], func=AF.Exp, scale=-1.0), reads=[("h_d1", hh), ("h_e1", hh)], writes=[("h_e1", hh)])
                S.dve(lambda e, hh=hh: e.tensor_tensor(out=Ki[:, hh, 0:TBc], in0=kk[:, hh, 0:TBc], in1=e1[:, hh, 0:TBc], op=ALU.mult),
                      reads=[("h_k", hh), ("h_e1", hh)], writes=[("h_Ki", hh)])
                S.act(lambda e, hh=hh: e.activation(out=eg[:, hh, 0:TBc], in_=cg[:, hh, 0:TBc], func=AF.Exp), reads=[("h_cg", hh)], writes=[("h_eg", hh)])
                S.dve(lambda e, hh=hh: e.scalar_tensor_tensor(out=Qx[:, hh, 0:TBc], in0=qa[:, hh, 0:TBc], scalar=128.0 ** -0.5, in1=eg[:, hh, 0:TBc], op0=ALU.mult, op1=ALU.mult),
                      reads=[("h_qa", hh), ("h_eg", hh)], writes=[("h_Qx", hh)])
                S.dve(lambda e, hh=hh, cg3=cg3: e.tensor_tensor(out=c3(d1[:, hh, 0:TBc]), in0=cg3[:, :, L - 1:L].to_broadcast([128, nch, L]), in1=cg3, op=ALU.subtract),
                      reads=[("h_cg", hh), ("h_d1", hh)], writes=[("h_d1", hh)])
                S.act(lambda e, hh=hh: e.activation(out=e1[:, hh, 0:TBc], in_=d1[:, hh, 0:TBc], func=AF.Exp), reads=[("h_d1", hh), ("h_e1", hh)], writes=[("h_e1", hh)])
                S.dve(lambda e, hh=hh: e.tensor_tensor(out=Kt[:, hh, 0:TBc], in0=kk[:, hh, 0:TBc], in1=e1[:, hh, 0:TBc], op=ALU.mult),
                      reads=[("h_k", hh), ("h_e1", hh)], writes=[("h_Kt", hh)])
            yield
            oall = wt("h_oall", [128, 2, TBM])
            hT = wt("h_T", [64, 4, 128], BF16); hatt = wt("h_att", [64, 2, 64], BF16)
            for c in range(nch):
                sl = slice(c * L, (c + 1) * L)
                h_trp = Bt[0:L, 0:512].rearrange("p (a b) -> p a b", b=128)
                for hh in range(2):
                    S.pe(lambda e, hh=hh, sl=sl: e.transpose(h_trp[:, hh, :], vb[:, hh, sl], cst["ident"][:]), reads=[("h_vb", hh), "c_ident"], writes=["Bt"])
                    S.pe(lambda e, hh=hh, sl=sl: e.transpose(h_trp[:, 2 + hh, :], Kt[:, hh, sl], cst["ident"][:]), reads=[("h_Kt", hh), "c_ident"], writes=["Bt"])
                ev(hT[0:L], h_trp, reads=["Bt"], writes=["h_T"])
                h_scp = B2[0:L, 0:128].rearrange("p (a b) -> p a b", b=64)
                for hh in range(2):
                    S.pe(lambda e, hh=hh, sl=sl: e.matmul(h_scp[:, hh, 0:L], Ki[:, hh, sl], Qi[:, hh, sl], start=True, stop=True), reads=[("h_Ki", hh), ("h_Qi", hh)], writes=["B2"])
                S.dve(lambda e: e.tensor_tensor(out=hatt[0:L, :, 0:L], in0=h_scp[:, :, 0:L], in1=cst["maskI"][0:L, 0:L].unsqueeze(1).to_broadcast([L, 2, L]), op=ALU.mult),
                      reads=["B2", "c_maskI"], writes=["h_att"])
                h_op = B5[:, 0:128].rearrange("p (a b) -> p a b", b=64)
                for hh in range(2):
                    S.pe(lambda e, hh=hh: e.matmul(h_op[:, hh, 0:L], hT[0:L, hh, :], hatt[0:L, hh, 0:L], start=True, stop=False), reads=["h_T", "h_att"], writes=["B5"])
                    S.pe(lambda e, hh=hh, sl=sl: e.matmul(h_op[:, hh, 0:L], st.hSb[:, hh, :], Qx[:, hh, sl], start=False, stop=True), reads=["hSb", ("h_Qx", hh)], writes=["B5"])
                ev(oall[:, :, sl], h_op[:, :, 0:L], reads=["B5"], writes=["h_oall"])
                h_up = B1[:, 0:256].rearrange("p (a b) -> p a b", b=128)
                for hh in range(2):
                    S.pe(lambda e, hh=hh: e.matmul(h_up[:, hh, :], hT[0:L, 2 + hh, :], hT[0:L, hh, :], start=True, stop=True), reads=["h_T"], writes=["B1"])
                for hh in range(2):
                    S.dve(lambda e, hh=hh, c=c: e.scalar_tensor_tensor(out=st.hS[:, hh, :], in0=st.hS[:, hh, :], scalar=eg[:, hh, c * L + L - 1:c * L + L], in1=h_up[:, hh, :], op0=ALU.mult, op1=ALU.add),
                          reads=["hS", ("h_eg", hh), "B1"], writes=["hS"])
                S.act(lambda e: e.copy(out=st.hSb[:], in_=st.hS[:]), reads=["hS"], writes=["hSb"])
            sq = wt("h_sq", [128, 2, TBM], BF16); rs = wt("h_rs", [128, 2, TBM]); yo = wt("h_yo", [128, 2, TBM], BF16)
            for hh in range(2):
                S.act(lambda e, hh=hh: e.activation(out=sq[:, hh, 0:TBc], in_=oall[:, hh, 0:TBc], func=AF.Square), reads=["h_oall"], writes=[("h_sq", hh)])
                ssp = B3[:, hh * 256:hh * 256 + TBc]
                S.pe(lambda e, hh=hh, ssp=ssp: e.matmul(ssp, cst["ones"][:], sq[:, hh, 0:TBc], start=True, stop=True), reads=[("h_sq", hh), "c_ones"], writes=["B3"])
                S.act(lambda e, hh=hh, ssp=ssp: e.activation(out=rs[:, hh, 0:TBc], in_=ssp, func=AF.Sqrt, bias=P("eps"), scale=1.0 / 128), reads=["B3", "par"], writes=[("h_rs", hh)])
                S.dve(lambda e, hh=hh: e.reciprocal(out=rs[:, hh, 0:TBc], in_=rs[:, hh, 0:TBc]), reads=[("h_rs", hh)], writes=[("h_rs", hh)])
                S.dve(lambda e, hh=hh: e.tensor_tensor(out=rs[:, hh, 0:TBc], in0=rs[:, hh, 0:TBc], in1=oall[:, hh, 0:TBc], op=ALU.mult), reads=[("h_rs", hh), "h_oall"], writes=[("h_rs", hh)])
                S.dve(lambda e, hh=hh: e.scalar_tensor_tensor(out=yo[:, hh, 0:TBc], in0=rs[:, hh, 0:TBc], scalar=P("hg%d" % hh), in1=sz[:, hh, 0:TBc], op0=ALU.mult, op1=ALU.mult),
                      reads=[("h_rs", hh), "par", ("h_sz", hh)], writes=[("h_yo", hh)])
                yd_, ydk_ = y_dst(hh * 128, t0, TBc)
                outs.append(S.dma("pool", lambda e, hh=hh, yd_=yd_: e.dma_start(out=yd_, in_=yo[:, hh, 0:TBc]), reads=[("h_yo", hh)], writes=[ydk_]))

        def gen_m():
            for nm, buf in (("mq", st.mxq), ("mk", st.mxk)):
                p_, k_ = inproj(xt, xkey, CT[nm] * 128, 128, TBc)
                ev(buf[:, 3:3 + TBc], p_, reads=[k_], writes=[nm + "x"])
            mvb = wt("m_vb", [128, 2, TBM], BF16); mso = wt("m_so", [128, 2, TBM]); msz = wt("m_sz", [128, 2, TBM])
            for i in range(2):
                p_, k_ = inproj(xt, xkey, CT["mv%d" % i] * 128, 128, TBc)
                S.dve(lambda e, p_=p_, i=i: e.tensor_copy(out=mvb[:, i, 0:TBc], in_=p_), reads=[k_], writes=[("m_vb", i)])
                p_, k_ = inproj(xt, xkey, CT["mo%d" % i] * 128, 128, TBc)
                S.act(lambda e, p_=p_, i=i: e.activation(out=mso[:, i, 0:TBc], in_=p_, func=AF.Sigmoid), reads=[k_], writes=[("m_so", i)])
                p_, k_ = inproj(xt, xkey, CT["mz%d" % i] * 128, 128, TBc)
                S.act(lambda e, p_=p_, i=i: e.activation(out=msz[:, i, 0:TBc], in_=p_, func=AF.Silu), reads=[k_], writes=[("m_sz", i)])
            rows = wt("m_rows", [1, 8, TBM])
            p_, k_ = inproj(xt, xkey, 25 * 128, 1, TBc)
            S.act(lambda e, p_=p_: e.activation(out=rows[:, 0, 0:TBc], in_=p_, func=AF.Identity, bias=par[0:1, PA["igb"][0]:PA["igb"][0] + 1]), reads=[k_, "par"], writes=[("m_rows", 0)])
            p_, k_ = inproj(xt, xkey, 25 * 128 + 1, 1, TBc)
            S.act(lambda e, p_=p_: e.activation(out=rows[:, 1, 0:TBc], in_=p_, func=AF.Sigmoid, bias=par[0:1, PA["fgb"][0]:PA["fgb"][0] + 1]), reads=[k_, "par"], writes=[("m_rows", 1)])
            yield
            S.act(lambda e: e.activation(out=rows[:, 1, 0:TBc], in_=rows[:, 1, 0:TBc], func=AF.Ln), reads=[("m_rows", 1)], writes=[("m_rows", 1)])
            S.dve(lambda e: e.tensor_tensor_scan(out=rows[:, 2, 0:TBc], data0=onesf[0:1, 0:TBc], data1=rows[:, 1, 0:TBc], initial=0.0, op0=ALU.mult, op1=ALU.add),
                  reads=[("m_rows", 1), "onesf"], writes=[("m_rows", 2)])
            S.dve(lambda e: e.tensor_tensor(out=rows[:, 3, 0:TBc], in0=rows[:, 0, 0:TBc], in1=rows[:, 2, 0:TBc], op=ALU.subtract), reads=[("m_rows", 0), ("m_rows", 2)], writes=[("m_rows", 3)])
            al0 = wt("m_al0", [1, 8])
            S.dve(lambda e: e.tensor_copy(out=al0[:, 0:1], in_=st.mmin[:]), reads=["mmin"], writes=["m_al0"])
            S.dve(lambda e: e.tensor_tensor_scan(out=rows[:, 4, 0:TBc], data0=onesf[0:1, 0:TBc], data1=rows[:, 3, 0:TBc], initial=st.mmin[:], op0=ALU.mult, op1=ALU.max),
                  reads=[("m_rows", 3), "onesf", "mmin"], writes=[("m_rows", 4)])
            Al3 = rows[:, 4, 0:TBc].rearrange("p (c j) -> p c j", j=L)
            al3 = rows[:, 3, 0:TBc].rearrange("p (c j) -> p c j", j=L)
            if nch > 1:
                S.dve(lambda e: e.tensor_copy(out=al0[:, 1:nch], in_=Al3[:, 0:nch - 1, L - 1]), reads=[("m_rows", 4), "m_al0"], writes=["m_al0"])
            cn_b = Al3[:, :, L - 1:L].to_broadcast([1, nch, L])
            S.dve(lambda e: e.tensor_tensor(out=rows[:, 5, 0:TBc].rearrange("p (c j) -> p c j", j=L), in0=al3, in1=cn_b, op=ALU.subtract), reads=[("m_rows", 3), ("m_rows", 4)], writes=[("m_rows", 5)])
            S.dve(lambda e: e.tensor_tensor(out=rows[:, 6, 0:TBc].rearrange("p (c j) -> p c j", j=L), in0=cn_b, in1=Al3, op=ALU.subtract), reads=[("m_rows", 4)], writes=[("m_rows", 6)])
            car = wt("m_car", [1, 8])
            S.dve(lambda e: e.tensor_tensor(out=car[:, 0:nch], in0=al0[:, 0:nch], in1=Al3[:, :, L - 1], op=ALU.subtract), reads=["m_al0", ("m_rows", 4)], writes=["m_car"])
            S.act(lambda e: e.activation(out=rows[:, 5:7, 0:TBc], in_=rows[:, 5:7, 0:TBc], func=AF.Exp), reads=[("m_rows", 5), ("m_rows", 6)], writes=[("m_rows", 5), ("m_rows", 6)])
            S.act(lambda e: e.activation(out=car[:, 0:nch], in_=car[:, 0:nch], func=AF.Exp), reads=["m_car"], writes=["m_car"])
            S.dve(lambda e: e.tensor_tensor(out=rows[:, 7, 0:TBc], in0=rows[:, 2, 0:TBc], in1=rows[:, 4, 0:TBc], op=ALU.add), reads=[("m_rows", 2), ("m_rows", 4)], writes=[("m_rows", 7)])
            S.dve(lambda e: e.tensor_copy(out=st.mmin[:], in_=rows[:, 7, TBc - 1:TBc]), reads=[("m_rows", 7), "mmin"], writes=["mmin"])
            S.act(lambda e: e.activation(out=rows[:, 7, 0:TBc], in_=rows[:, 7, 0:TBc], func=AF.Exp, scale=-1.0), reads=[("m_rows", 7)], writes=[("m_rows", 7)])
            bws = B3[:, 0:TBc]; bwt = B3[:, 256:256 + TBc]; bcar = B4[:, 0:nch]
            S.pe(lambda e: e.matmul(bws, st.onesrow[:], rows[:, 5, 0:TBc], start=True, stop=True), reads=["onesrow", ("m_rows", 5)], writes=["B3"])
            S.pe(lambda e: e.matmul(bwt, st.onesrow[:], rows[:, 6, 0:TBc], start=True, stop=True), reads=["onesrow", ("m_rows", 6)], writes=["B3"])
            S.pe(lambda e: e.matmul(bcar, st.onesrow[:], car[:, 0:nch], start=True, stop=True), reads=["onesrow", "m_car"], writes=["B4"])
            carb = wt("m_carb", [128, 8])
            S.act(lambda e: e.copy(out=carb[:, 0:nch], in_=bcar), reads=["B4"], writes=["m_carb"])
            qs = wt("m_qs", [128, TBM]); ks = wt("m_ks", [128, TBM]); kp = wt("m_kp", [128, TBM], BF16); qpp = wt("m_qpp", [128, TBM], BF16)
            for nm, buf, dst, cw in (("mq", st.mxq, qs, "cwq"), ("mk", st.mxk, ks, "cwk")):
                S.dve(lambda e, buf=buf, dst=dst, cw=cw: e.tensor_scalar_mul(out=dst[:, 0:TBc], in0=buf[:, 0:TBc], scalar1=P(cw, 0)), reads=[nm + "x", "par"], writes=["m_" + nm + "s"])
                for i in range(1, 4):
                    S.dve(lambda e, buf=buf, dst=dst, cw=cw, i=i: e.scalar_tensor_tensor(out=dst[:, 0:TBc], in0=buf[:, i:i + TBc], scalar=P(cw, i), in1=dst[:, 0:TBc], op0=ALU.mult, op1=ALU.add),
                          reads=[nm + "x", "par", "m_" + nm + "s"], writes=["m_" + nm + "s"])
                S.act(lambda e, dst=dst: e.activation(out=dst[:, 0:TBc], in_=dst[:, 0:TBc], func=AF.Silu), reads=["m_" + nm + "s"], writes=["m_" + nm + "s"])
                S.pool(lambda e, buf=buf: e.tensor_copy(out=buf[:, 0:3], in_=buf[:, TBc:TBc + 3]), reads=[nm + "x"], writes=[nm + "x"])
            S.dve(lambda e: e.tensor_tensor(out=kp[:, 0:TBc], in0=ks[:, 0:TBc], in1=bws, op=ALU.mult), reads=["m_mks", "B3"], writes=["m_kp"])
            S.dve(lambda e: e.scalar_tensor_tensor(out=qpp[:, 0:TBc], in0=qs[:, 0:TBc], scalar=128.0 ** -0.5, in1=bwt, op0=ALU.mult, op1=ALU.mult), reads=["m_mqs", "B3"], writes=["m_qpp"])
            yield
            numall = wt("m_num", [128, 2, TBM]); denr = wt("m_den", [1, TBM]); matt = wt("m_att", [64, 64], BF16)
            for c in range(nch):
                sl = slice(c * L, (c + 1) * L)
                m_trp = Bt[0:L, 0:384].rearrange("p (a b) -> p a b", b=128)
                S.pe(lambda e, sl=sl: e.transpose(m_trp[:, 0, :], kp[:, sl], cst["ident"][:]), reads=["m_kp", "c_ident"], writes=["Bt"])
                for i in range(2):
                    S.pe(lambda e, sl=sl, i=i: e.transpose(m_trp[:, 1 + i, :], mvb[:, i, sl], cst["ident"][:]), reads=[("m_vb", i), "c_ident"], writes=["Bt"])
                ev(st.mTT[0:L, 0:384], Bt[0:L, 0:384], reads=["Bt"], writes=["mTT"])
                S.dve(lambda e, c=c: e.tensor_scalar_mul(out=st.mC[:], in0=st.mC[:], scalar1=carb[:, c:c + 1]), reads=["mC", "m_carb"], writes=["mC"])
                S.act(lambda e: e.copy(out=st.mCb[:], in_=st.mC[:]), reads=["mC"], writes=["mCb"])
                m_scp = B2[0:L, 128:128 + L]
                S.pe(lambda e, sl=sl: e.matmul(m_scp, kp[:, sl], qpp[:, sl], start=True, stop=True), reads=["m_kp", "m_qpp"], writes=["B2"])
                S.dve(lambda e: e.tensor_tensor(out=matt[0:L, 0:L], in0=m_scp, in1=cst["maskI"][0:L, 0:L], op=ALU.mult), reads=["B2", "c_maskI"], writes=["m_att"])
                m_np = B5[:, 128:256].rearrange("p (a b) -> p a b", b=64)
                for i in range(2):
                    S.pe(lambda e, i=i: e.matmul(m_np[:, i, 0:L], st.mTT[0:L, 128 + i * 128:256 + i * 128], matt[0:L, 0:L], start=True, stop=False), reads=["mTT", "m_att"], writes=["B5"])
                    S.pe(lambda e, i=i, sl=sl: e.matmul(m_np[:, i, 0:L], st.mCb[:, i * 128:(i + 1) * 128], qpp[:, sl], start=False, stop=True), reads=["mCb", "m_qpp"], writes=["B5"])
                m_dp = B5[0:1, 256:256 + L]
                S.pe(lambda e: e.matmul(m_dp, st.mTT[0:L, 384:385], matt[0:L, 0:L], start=True, stop=False), reads=["mTT", "m_att"], writes=["B5"])
                S.pe(lambda e, sl=sl: e.matmul(m_dp, st.mCb[:, 256:257], qpp[:, sl], start=False, stop=True), reads=["mCb", "m_qpp"], writes=["B5"])
                ev(numall[:, :, sl], m_np[:, :, 0:L], reads=["B5"], writes=["m_num"])
                ev(denr[:, sl], m_dp, reads=["B5"], writes=["m_den"])
                cup = B1[:, 256:512]
                nup = B5[:, 448:449]
                S.pe(lambda e: e.matmul(cup, st.mTT[0:L, 0:128], st.mTT[0:L, 128:384], start=True, stop=True), reads=["mTT"], writes=["B1"])
                S.pe(lambda e: e.matmul(nup, st.mTT[0:L, 0:128], st.mTT[0:L, 384:385], start=True, stop=True), reads=["mTT"], writes=["B5"])
                S.dve(lambda e: e.tensor_tensor(out=st.mC[:, 0:256], in0=st.mC[:, 0:256], in1=cup, op=ALU.add), reads=["mC", "B1"], writes=["mC"])
                S.dve(lambda e: e.tensor_tensor(out=st.mC[:, 256:257], in0=st.mC[:, 256:257], in1=nup, op=ALU.add), reads=["mC", "B5"], writes=["mC"])
            S.act(lambda e: e.activation(out=denr[:, 0:TBc], in_=denr[:, 0:TBc], func=AF.Abs), reads=["m_den"], writes=["m_den"])
            S.dve(lambda e: e.tensor_tensor(out=denr[:, 0:TBc], in0=denr[:, 0:TBc], in1=rows[:, 7, 0:TBc], op=ALU.max), reads=["m_den", ("m_rows", 7)], writes=["m_den"])
            S.dve(lambda e: e.reciprocal(out=denr[:, 0:TBc], in_=denr[:, 0:TBc]), reads=["m_den"], writes=["m_den"])
            bdd = B4[:, 256:256 + TBc]
            S.pe(lambda e: e.matmul(bdd, st.onesrow[:], denr[:, 0:TBc], start=True, stop=True), reads=["onesrow", "m_den"], writes=["B4"])
            mh = wt("m_h", [128, 2, TBM]); mhb = wt("m_hb", [128, 2, TBM], BF16); msq = wt("m_sq", [128, 2, TBM], BF16)
            S.dve(lambda e: e.tensor_tensor(out=mh[:, :, 0:TBc], in0=numall[:, :, 0:TBc], in1=bdd.unsqueeze(1).to_broadcast([128, 2, TBc]), op=ALU.mult), reads=["m_num", "B4"], writes=["m_h"])
            S.act(lambda e: e.copy(out=mhb[:, :, 0:TBc], in_=mh[:, :, 0:TBc]), reads=["m_h"], writes=["m_hb"])
            S.act(lambda e: e.activation(out=msq[:, :, 0:TBc], in_=mh[:, :, 0:TBc], func=AF.Square), reads=["m_h"], writes=["m_sq"])
            sm = B3[:, 0:TBc]; sm2 = B3[:, 256:256 + TBc]
            for i in range(2):
                S.pe(lambda e, i=i: e.matmul(sm, cst["ones"][:], mhb[:, i, 0:TBc], start=(i == 0), stop=(i == 1)), reads=["m_hb", "c_ones"], writes=["B3"])
            for i in range(2):
                S.pe(lambda e, i=i: e.matmul(sm2, cst["ones"][:], msq[:, i, 0:TBc], start=(i == 0), stop=(i == 1)), reads=["m_sq", "c_ones"], writes=["B3"])
            mean = wt("m_mean", [128, TBM]); var = wt("m_var", [128, TBM]); myo = wt("m_yo", [128, 2, TBM], BF16)
            S.act(lambda e: e.mul(out=mean[:, 0:TBc], in_=sm, mul=1.0 / 256), reads=["B3"], writes=["m_mean"])
            S.dve(lambda e: e.tensor_tensor(out=var[:, 0:TBc], in0=mean[:, 0:TBc], in1=mean[:, 0:TBc], op=ALU.mult), reads=["m_mean"], writes=["m_var"])
            S.dve(lambda e: e.scalar_tensor_tensor(out=var[:, 0:TBc], in0=sm2, scalar=1.0 / 256, in1=var[:, 0:TBc], op0=ALU.mult, op1=ALU.subtract), reads=["B3", "m_var"], writes=["m_var"])
            S.act(lambda e: e.activation(out=var[:, 0:TBc], in_=var[:, 0:TBc], func=AF.Sqrt, bias=P("eps")), reads=["m_var", "par"], writes=["m_var"])
            S.dve(lambda e: e.reciprocal(out=var[:, 0:TBc], in_=var[:, 0:TBc]), reads=["m_var"], writes=["m_var"])
            S.dve(lambda e: e.tensor_tensor(out=mh[:, :, 0:TBc], in0=mh[:, :, 0:TBc], in1=mean[:, 0:TBc].unsqueeze(1).to_broadcast([128, 2, TBc]), op=ALU.subtract), reads=["m_h", "m_mean"], writes=["m_h"])
            S.dve(lambda e: e.tensor_tensor(out=mh[:, :, 0:TBc], in0=mh[:, :, 0:TBc], in1=var[:, 0:TBc].unsqueeze(1).to_broadcast([128, 2, TBc]), op=ALU.mult), reads=["m_h", "m_var"], writes=["m_h"])
            for i in range(2):
                S.dve(lambda e, i=i: e.scalar_tensor_tensor(out=mh[:, i, 0:TBc], in0=mh[:, i, 0:TBc], scalar=P("mg%d" % i), in1=mso[:, i, 0:TBc], op0=ALU.mult, op1=ALU.mult), reads=["m_h", "par", ("m_so", i)], writes=["m_h"])
            S.dve(lambda e: e.tensor_tensor(out=myo[:, :, 0:TBc], in0=mh[:, :, 0:TBc], in1=msz[:, :, 0:TBc], op=ALU.mult), reads=["m_h", ("m_sz", 0), ("m_sz", 1)], writes=["m_yo"])
            for i in range(2):
                yd_, ydk_ = y_dst(256 + i * 128, t0, TBc)
                outs.append(S.dma("pool", lambda e, i=i, yd_=yd_: e.dma_start(out=yd_, in_=myo[:, i, 0:TBc]), reads=["m_yo"], writes=[ydk_]))

        def gen_r():
            for nm in ("rr0", "rr1", "rk0", "rk1", "rv0", "rv1", "rwa"):
                p_, k_ = inproj(xt, xkey, CT[nm] * 128, 128, TBc)
                ev(st.rraw[nm][:, 1:1 + TBc], p_, reads=[k_], writes=["raw_" + nm])
            rsz = wt("r_sz", [128, 2, TBM])
            for p in range(2):
                p_, k_ = inproj(xt, xkey, CT["rz%d" % p] * 128, 128, TBc)
                S.act(lambda e, p_=p_, p=p: e.activation(out=rsz[:, p, 0:TBc], in_=p_, func=AF.Silu), reads=[k_], writes=[("r_sz", p)])
            yield
            lm = {}
            for nm, mu in (("rr0", "mur0"), ("rr1", "mur1"), ("rk0", "muk0"), ("rk1", "muk1"), ("rv0", "muv0"), ("rv1", "muv1"), ("rwa", "muwa")):
                raw = st.rraw[nm]
                m = wt("r_m_" + nm, [128, TBM])
                lm[nm] = m
                S.dve(lambda e, raw=raw, m=m: e.tensor_tensor(out=m[:, 0:TBc], in0=raw[:, 0:TBc], in1=raw[:, 1:1 + TBc], op=ALU.subtract), reads=["raw_" + nm], writes=["r_m_" + nm])
                S.dve(lambda e, raw=raw, m=m, mu=mu: e.scalar_tensor_tensor(out=m[:, 0:TBc], in0=m[:, 0:TBc], scalar=P(mu), in1=raw[:, 1:1 + TBc], op0=ALU.mult, op1=ALU.add),
                      reads=["raw_" + nm, "r_m_" + nm, "par"], writes=["r_m_" + nm])
                S.pool(lambda e, raw=raw: e.tensor_copy(out=raw[:, 0:1], in_=raw[:, TBc:TBc + 1]), reads=["raw_" + nm], writes=["raw_" + nm])
            wab = wt("r_wab", [128, TBM], BF16)
            S.act(lambda e: e.activation(out=wab[0:64, 0:TBc], in_=lm["rwa"][0:64, 0:TBc], func=AF.Tanh), reads=["r_m_rwa"], writes=["r_wab0"])
            S.act(lambda e: e.copy(out=wab[64:128, 0:TBc], in_=lm["rwa"][64:128, 0:TBc]), reads=["r_m_rwa"], writes=["r_wab1"])
            if RSTOP <= -2:
                return
            sgw = wt("r_sgw", [128, 2, TBM]); av = wt("r_a", [128, 2, TBM]); kkn = wt("r_kkn", [128, 2, TBM]); kf = wt("r_kf", [128, 2, TBM])
            bv = wt("r_bv", [128, 2, TBM]); cs = wt("r_cs", [128, 2, TBM]); tmp = wt("r_tmp", [128, 2, TBM]); tmpb = wt("r_tmpb", [128, 2, TBM], BF16)
            ecw = wt("r_ecw", [128, 2, TBM]); ex = wt("r_ex", [128, 2, TBM])
            AR = wt("r_AR", [128, 2, max(NCH, 1), 2, 64], BF16)
            Bd = wt("r_Bd", [128, 2, TBM], BF16); Kd = wt("r_Kd", [128, 2, TBM], BF16)
            Btl = wt("r_Btl", [128, 2, TBM], BF16); Ktl = wt("r_Ktl", [128, 2, TBM], BF16)
            rvb = wt("r_vb", [128, 2, TBM], BF16); bon = wt("r_bon", [128, 2, TBM])
            for p in range(2):
                rm = lm["rr%d" % p]; km = lm["rk%d" % p]; vm = lm["rv%d" % p]
                RK = ["r_m_rr%d" % p, "r_m_rk%d" % p, "r_m_rv%d" % p]
                wp = B3[:, 0:TBc]; ap_ = B3[:, 256:256 + TBc]
                S.pe(lambda e, p=p, wp=wp: e.matmul(wp, wud[0:64, p * 128:(p + 1) * 128], wab[0:64, 0:TBc], start=True, stop=True), reads=["wud", "r_wab0"], writes=["B3"])
                S.pe(lambda e, p=p, ap_=ap_: e.matmul(ap_, wud[64:128, p * 128:(p + 1) * 128], wab[64:128, 0:TBc], start=True, stop=True), reads=["wud", "r_wab1"], writes=["B3"], rowgrp=1)
                S.act(lambda e, p=p, wp=wp: e.activation(out=sgw[:, p, 0:TBc], in_=wp, func=AF.Sigmoid, bias=P("w0%d" % p)), reads=["B3", "par"], writes=[("r_sgw", p)])
                S.act(lambda e, p=p, ap_=ap_: e.activation(out=av[:, p, 0:TBc], in_=ap_, func=AF.Sigmoid, bias=P("a0%d" % p)), reads=["B3", "par"], writes=[("r_a", p)])
                S.dve(lambda e, p=p, km=km: e.tensor_scalar_mul(out=kkn[:, p, 0:TBc], in0=km[:, 0:TBc], scalar1=P("kk%d" % p)), reads=[RK[1], "par"], writes=[("r_kkn", p)])
                S.act(lambda e, p=p: e.activation(out=tmpb[:, p, 0:TBc], in_=kkn[:, p, 0:TBc], func=AF.Square), reads=[("r_kkn", p)], writes=[("r_tmpb", p)])
                ssk = B4[:, 0:TBc]
                S.pe(lambda e, p=p, ssk=ssk: e.matmul(ssk, cst["bd"][:], tmpb[:, p, 0:TBc], start=True, stop=True), reads=[("r_tmpb", p), "c_bd"], writes=["B4"])
                S.act(lambda e, p=p, ssk=ssk: e.activation(out=tmp[:, p, 0:TBc], in_=ssk, func=AF.Sqrt), reads=["B4"], writes=[("r_tmp", p)])
                S.dve(lambda e, p=p: e.tensor_scalar_max(out=tmp[:, p, 0:TBc], in0=tmp[:, p, 0:TBc], scalar1=1e-12), reads=[("r_tmp", p)], writes=[("r_tmp", p)])
                S.dve(lambda e, p=p: e.reciprocal(out=tmp[:, p, 0:TBc], in_=tmp[:, p, 0:TBc]), reads=[("r_tmp", p)], writes=[("r_tmp", p)])
                S.dve(lambda e, p=p: e.tensor_tensor(out=kkn[:, p, 0:TBc], in0=kkn[:, p, 0:TBc], in1=tmp[:, p, 0:TBc], op=ALU.mult), reads=[("r_kkn", p), ("r_tmp", p)], writes=[("r_kkn", p)])
                S.dve(lambda e, p=p: e.tensor_scalar(out=kf[:, p, 0:TBc], in0=av[:, p, 0:TBc], scalar1=-1.0, scalar2=P("ka%d" % p), op0=ALU.add, op1=ALU.mult), reads=[("r_a", p), "par"], writes=[("r_kf", p)])
                S.dve(lambda e, p=p, km=km: e.scalar_tensor_tensor(out=kf[:, p, 0:TBc], in0=kf[:, p, 0:TBc], scalar=1.0, in1=km[:, 0:TBc], op0=ALU.add, op1=ALU.mult), reads=[("r_kf", p), RK[1]], writes=[("r_kf", p)])
                S.dve(lambda e, p=p: e.tensor_tensor(out=bv[:, p, 0:TBc], in0=kkn[:, p, 0:TBc], in1=av[:, p, 0:TBc], op=ALU.mult), reads=[("r_kkn", p), ("r_a", p)], writes=[("r_bv", p)])
                for c in range(nch):
                    S.dve(lambda e, p=p, c=c: e.tensor_tensor_scan(out=cs[:, p, c * L:(c + 1) * L], data0=onesf[:, 0:L], data1=sgw[:, p, c * L:(c + 1) * L], initial=0.0, op0=ALU.mult, op1=ALU.add),
                          reads=[("r_sgw", p), "onesf"], writes=[("r_cs", p)])
                cs3 = c3(cs[:, p, 0:TBc])
                ARp = AR[:, p, 0:nch, :, 0:L]
                S.act(lambda e, p=p: e.activation(out=ecw[:, p, 0:TBc], in_=cs[:, p, 0:TBc], func=AF.Exp, scale=-C0), reads=[("r_cs", p)], writes=[("r_ecw", p)])
                S.dve(lambda e, p=p, rm=rm, ARp=ARp: e.tensor_tensor(out=ARp[:, :, 1, :], in0=c3(rm[:, 0:TBc]), in1=c3(ecw[:, p, 0:TBc]), op=ALU.mult), reads=[RK[0], ("r_ecw", p)], writes=[("r_AR", p)])
                S.act(lambda e, p=p: e.activation(out=ex[:, p, 0:TBc], in_=cs[:, p, 0:TBc], func=AF.Exp, scale=C0), reads=[("r_cs", p)], writes=[("r_ex", p)])
                S.dve(lambda e, p=p: e.tensor_tensor(out=Kd[:, p, 0:TBc], in0=kf[:, p, 0:TBc], in1=ex[:, p, 0:TBc], op=ALU.mult), reads=[("r_kf", p), ("r_ex", p)], writes=[("r_Kd", p)])
                S.dve(lambda e, p=p: e.tensor_tensor(out=Bd[:, p, 0:TBc], in0=bv[:, p, 0:TBc], in1=ex[:, p, 0:TBc], op=ALU.mult), reads=[("r_bv", p), ("r_ex", p)], writes=[("r_Bd", p)])
                S.dve(lambda e, p=p: e.tensor_tensor(out=tmp[:, p, 0:TBc], in0=cs[:, p, 0:TBc], in1=sgw[:, p, 0:TBc], op=ALU.subtract), reads=[("r_cs", p), ("r_sgw", p), ("r_tmp", p)], writes=[("r_tmp", p)])
                S.act(lambda e, p=p: e.activation(out=ex[:, p, 0:TBc], in_=tmp[:, p, 0:TBc], func=AF.Exp, scale=-C0), reads=[("r_tmp", p), ("r_ex", p)], writes=[("r_ex", p)])
                S.dve(lambda e, p=p, ARp=ARp: e.scalar_tensor_tensor(out=ARp[:, :, 0, :], in0=c3(kkn[:, p, 0:TBc]), scalar=-1.0, in1=c3(ex[:, p, 0:TBc]), op0=ALU.mult, op1=ALU.mult), reads=[("r_kkn", p), ("r_ex", p)], writes=[("r_AR", p)])
                S.dve(lambda e, p=p, cs3=cs3: e.tensor_tensor(out=c3(tmp[:, p, 0:TBc]), in0=cs3[:, :, L - 1:L].to_broadcast([128, nch, L]), in1=cs3, op=ALU.subtract), reads=[("r_cs", p), ("r_tmp", p)], writes=[("r_tmp", p)])
                S.act(lambda e, p=p: e.activation(out=ex[:, p, 0:TBc], in_=tmp[:, p, 0:TBc], func=AF.Exp, scale=-C0), reads=[("r_tmp", p), ("r_ex", p)], writes=[("r_ex", p)])
                S.dve(lambda e, p=p: e.tensor_tensor(out=Btl[:, p, 0:TBc], in0=bv[:, p, 0:TBc], in1=ex[:, p, 0:TBc], op=ALU.mult), reads=[("r_bv", p), ("r_ex", p)], writes=[("r_Btl", p)])
                S.dve(lambda e, p=p: e.tensor_tensor(out=Ktl[:, p, 0:TBc], in0=kf[:, p, 0:TBc], in1=ex[:, p, 0:TBc], op=ALU.mult), reads=[("r_kf", p), ("r_ex", p)], writes=[("r_Ktl", p)])
                S.act(lambda e, p=p, vm=vm: e.copy(out=rvb[:, p, 0:TBc], in_=vm[:, 0:TBc]), reads=[RK[2]], writes=[("r_vb", p)])
                S.dve(lambda e, p=p, rm=rm: e.scalar_tensor_tensor(out=tmpb[:, p, 0:TBc], in0=rm[:, 0:TBc], scalar=P("rk%d" % p), in1=kf[:, p, 0:TBc], op0=ALU.mult, op1=ALU.mult), reads=[RK[0], "par", ("r_kf", p), ("r_tmpb", p)], writes=[("r_tmpb", p)])
                bsp = B4[:, 256:256 + TBc]
                S.pe(lambda e, p=p, bsp=bsp: e.matmul(bsp, cst["bd"][:], tmpb[:, p, 0:TBc], start=True, stop=True), reads=[("r_tmpb", p), "c_bd"], writes=["B4"])
                S.dve(lambda e, p=p, vm=vm, bsp=bsp: e.tensor_tensor(out=bon[:, p, 0:TBc], in0=vm[:, 0:TBc], in1=bsp, op=ALU.mult), reads=[RK[2], "B4"], writes=[("r_bon", p)])
            yall = wt("r_yall", [128, 2, TBM])
            T3 = wt("r_T3", [64, 2, 3, 128], BF16)
            scBm = wt("r_scBm", [64, 4, 2, 64], BF16); scKm = wt("r_scKm", [64, 4, 2, 64], BF16); labm = wt("r_labm", [64, 4, 64], BF16)
            TTf = wt("r_TTf", [64, 4, 64]); TTb = wt("r_TTb", [64, 4, 64], BF16)
            X = [wt("r_X%d" % i, [64, 4, 2, 64], BF16) for i in range(2)]
            Q = [wt("r_Q%d" % i, [64, 4, 64], BF16) for i in range(2)]
            rhsb = wt("r_rhsb", [64, 4, 64], BF16); ub = wt("r_ub", [64, 4, 64], BF16)
            upad = wt("r_upad", [64, 2, 2, 128], BF16); vpad = wt("r_vpad", [64, 2, 2, 128], BF16)
            tmpS = wt("r_tmpS", [128, 2, 128])
            nlev = int(round(math.log2(L)))
            for c in range(nch if RSTOP >= 1 else 0):
                sl = slice(c * L, (c + 1) * L)
                trp = Bt[0:L, 0:768].rearrange("p (a b c) -> p a b c", a=2, b=3)
                for p in range(2):
                    S.pe(lambda e, p=p, sl=sl: e.transpose(trp[:, p, 0, :], rvb[:, p, sl], cst["ident"][:]), reads=[("r_vb", p), "c_ident"], writes=["Bt"])
                    S.pe(lambda e, p=p, sl=sl: e.transpose(trp[:, p, 1, :], Btl[:, p, sl], cst["ident"][:]), reads=[("r_Btl", p), "c_ident"], writes=["Bt"])
                    S.pe(lambda e, p=p, sl=sl: e.transpose(trp[:, p, 2, :], Ktl[:, p, sl], cst["ident"][:]), reads=[("r_Ktl", p), "c_ident"], writes=["Bt"])
                ev(T3[0:L], trp, reads=["Bt"], writes=["r_T3"])
                S.dve(lambda e: e.tensor_tensor(out=vpad[0:L], in0=T3[0:L, :, 0, :].unsqueeze(2).to_broadcast([L, 2, 2, 128]), in1=cst["padmask"][0:L].unsqueeze(1).to_broadcast([L, 2, 2, 128]), op=ALU.mult),
                      reads=["r_T3", "c_padmask"], writes=["r_vpad"])
                scB = B3[0:L, :].rearrange("p (h a j) -> p h a j", h=4, a=2)
                scK = B4[0:L, :].rearrange("p (h a j) -> p h a j", h=4, a=2)
                lab = B2[0:L, 256:512].rearrange("p (h j) -> p h j", h=4)
                for h in (0, 2, 1, 3):
                    p, hh = divmod(h, 2)
                    b0 = hh * 64
                    rg = 1 if hh == 1 else None
                    arh = AR[b0:b0 + 64, p, c, :, 0:L]
                    S.pe(lambda e, h=h, p=p, b0=b0, arh=arh, sl=sl: e.matmul(scB[:, h, :, 0:L], Bd[b0:b0 + 64, p, sl], arh, start=True, stop=True), reads=[("r_Bd", p), ("r_AR", p)], writes=["B3"], rowgrp=rg)
                    S.pe(lambda e, h=h, p=p, b0=b0, arh=arh, sl=sl: e.matmul(scK[:, h, :, 0:L], Kd[b0:b0 + 64, p, sl], arh, start=True, stop=True), reads=[("r_Kd", p), ("r_AR", p)], writes=["B4"], rowgrp=rg)
                    S.pe(lambda e, h=h, p=p, b0=b0, arh=arh, sl=sl: e.matmul(lab[:, h, 0:L], arh[:, 0, :], Bd[b0:b0 + 64, p, sl], start=True, stop=True), reads=[("r_Bd", p), ("r_AR", p)], writes=["B2"], rowgrp=rg)
                m2 = cst["mask2"][0:L, :, 0:L].unsqueeze(1).to_broadcast([L, 4, 2, L])
                S.dve(lambda e: e.tensor_tensor(out=scBm[0:L, :, :, 0:L], in0=scB[:, :, :, 0:L], in1=m2, op=ALU.mult), reads=["B3", "c_mask2"], writes=["r_scBm"])
                S.dve(lambda e: e.tensor_tensor(out=scKm[0:L, :, :, 0:L], in0=scK[:, :, :, 0:L], in1=m2, op=ALU.mult), reads=["B4", "c_mask2"], writes=["r_scKm"])
                S.dve(lambda e: e.tensor_tensor(out=labm[0:L, :, 0:L], in0=lab[:, :, 0:L], in1=cst["maskL"][0:L, 0:L].unsqueeze(1).to_broadcast([L, 4, L]), op=ALU.mult), reads=["B2", "c_maskL"], writes=["r_labm"])
                if RSTOP < 2:
                    continue
                S.dve(lambda e: e.tensor_tensor(out=TTf[0:L, :, 0:L], in0=scBm[0:L, :, 0, 0:L], in1=ident64f[0:L, 0:L].unsqueeze(1).to_broadcast([L, 4, L]), op=ALU.add), reads=["r_scBm", "ident64f"], writes=["r_TTf"])
                S.act(lambda e: e.copy(out=X[0][0:L, :, 0, 0:L], in_=TTf[0:L, :, 0:L]), reads=["r_TTf"], writes=["r_X0a"])
                PPp = B6[0:L, :].rearrange("p (h a j) -> p h a j", h=4, a=2)
                QQp = B1[0:L, 0:256].rearrange("p (h j) -> p h j", h=4)
                for h in range(4):
                    S.pe(lambda e, h=h: e.matmul(PPp[:, h, 1, 0:L], labm[0:L, h, 0:L], scBm[0:L, h, 0, 0:L], start=True, stop=True), reads=["r_labm", "r_scBm"], writes=["B6"])
                    S.pe(lambda e, h=h: e.matmul(QQp[:, h, 0:L], scBm[0:L, h, 0, 0:L], labm[0:L, h, 0:L], start=True, stop=True), reads=["r_labm", "r_scBm"], writes=["B1"])
                S.dve(lambda e: e.tensor_copy(out=X[0][0:L, :, 1, 0:L], in_=PPp[:, :, 1, 0:L]), reads=["B6"], writes=["r_X0b"])
                S.act(lambda e: e.copy(out=Q[0][0:L, :, 0:L], in_=QQp[:, :, 0:L]), reads=["B1"], writes=["r_Q0"])
                cur = 0
                for lev in range(1, nlev):
                    last = lev == nlev - 1
                    Xc, Qc = X[cur], Q[cur]
                    xk = ["r_X%da" % cur, "r_X%db" % cur]
                    qk = "r_Q%d" % cur
                    nxt = cur ^ 1
                    for h in range(4):
                        if last:
                            S.pe(lambda e, h=h, Xc=Xc, Qc=Qc: e.matmul(PPp[:, h, 0, 0:L], Qc[0:L, h, 0:L], Xc[0:L, h, 0, 0:L], start=True, stop=True), reads=[qk] + xk, writes=["B6"])
                        else:
                            S.pe(lambda e, h=h, Xc=Xc, Qc=Qc: e.matmul(PPp[:, h, :, 0:L], Qc[0:L, h, 0:L], Xc[0:L, h, :, 0:L], start=True, stop=True), reads=[qk] + xk, writes=["B6"])
                            S.pe(lambda e, h=h, Xc=Xc, Qc=Qc: e.matmul(QQp[:, h, 0:L], Xc[0:L, h, 1, 0:L], Qc[0:L, h, 0:L], start=True, stop=True), reads=[qk] + xk, writes=["B1"])
                    S.dve(lambda e: e.tensor_tensor(out=TTf[0:L, :, 0:L], in0=TTf[0:L, :, 0:L], in1=PPp[:, :, 0, 0:L], op=ALU.add), reads=["r_TTf", "B6"], writes=["r_TTf"])
                    if last:
                        S.act(lambda e: e.copy(out=TTb[0:L, :, 0:L], in_=TTf[0:L, :, 0:L]), reads=["r_TTf"], writes=["r_TTb"])
                    else:
                        S.act(lambda e, nxt=nxt: e.copy(out=X[nxt][0:L, :, 0, 0:L], in_=TTf[0:L, :, 0:L]), reads=["r_TTf"], writes=["r_X%da" % nxt])
                        S.dve(lambda e, nxt=nxt: e.tensor_copy(out=X[nxt][0:L, :, 1, 0:L], in_=PPp[:, :, 1, 0:L]), reads=["B6"], writes=["r_X%db" % nxt])
                        S.act(lambda e, nxt=nxt: e.copy(out=Q[nxt][0:L, :, 0:L], in_=QQp[:, :, 0:L]), reads=["B1"], writes=["r_Q%d" % nxt])
                    cur = nxt
                if RSTOP < 3:
                    continue
                rhp = B6[0:L, 0:256].rearrange("p (h j) -> p h j", h=4)
                for h in range(4):
                    p, hh = divmod(h, 2)
                    S.pe(lambda e, h=h, p=p, hh=hh, c=c: e.matmul(rhp[:, h, :], AR[:, p, c, 0, 0:L], st.rSb[:, p, hh * 64:(hh + 1) * 64], start=True, stop=False), reads=[("r_AR", p), "rSb"], writes=["B6"])
                    S.pe(lambda e, h=h, p=p, hh=hh: e.matmul(rhp[:, h, :], scKm[0:L, h, 0, 0:L], T3[0:L, p, 0, hh * 64:(hh + 1) * 64], start=False, stop=True), reads=["r_scKm", "r_T3"], writes=["B6"])
                S.act(lambda e: e.copy(out=rhsb[0:L], in_=rhp), reads=["B6"], writes=["r_rhsb"])
                up_ = B6[0:L, 256:512].rearrange("p (h j) -> p h j", h=4)
                for h in range(4):
                    S.pe(lambda e, h=h: e.matmul(up_[:, h, :], TTb[0:L, h, 0:L], rhsb[0:L, h, :], start=True, stop=True), reads=["r_TTb", "r_rhsb"], writes=["B6"])
                S.act(lambda e: e.copy(out=ub[0:L], in_=up_), reads=["B6"], writes=["r_ub"])
                S.dve(lambda e: e.tensor_tensor(out=upad[0:L], in0=B6[0:L, 256:512].rearrange("p (a k) -> p a k", a=2).unsqueeze(2).to_broadcast([L, 2, 2, 128]), in1=cst["padmask"][0:L].unsqueeze(1).to_broadcast([L, 2, 2, 128]), op=ALU.mult),
                      reads=["B6", "c_padmask"], writes=["r_upad"])
                yp = B5[:, 320:448].rearrange("p (a j) -> p a j", a=2)
                for p in range(2):
                    S.pe(lambda e, p=p, c=c: e.matmul(yp[:, p, 0:L], st.rSb[:, p, :], AR[:, p, c, 1, 0:L], start=True, stop=False), reads=["rSb", ("r_AR", p)], writes=["B5"])
                    for hh in range(2):
                        h = p * 2 + hh
                        S.pe(lambda e, p=p, hh=hh, h=h: e.matmul(yp[:, p, 0:L], upad[0:L, p, hh, :], scBm[0:L, h, 1, 0:L], start=False, stop=False), reads=["r_upad", "r_scBm"], writes=["B5"])
                        S.pe(lambda e, p=p, hh=hh, h=h: e.matmul(yp[:, p, 0:L], vpad[0:L, p, hh, :], scKm[0:L, h, 1, 0:L], start=False, stop=(hh == 1)), reads=["r_vpad", "r_scKm"], writes=["B5"])
                ev(yall[:, :, sl], yp[:, :, 0:L], reads=["B5"], writes=["r_yall"])
                sup = B2[:, 0:256].rearrange("p (a j) -> p a j", a=2)
                for p in range(2):
                    S.pe(lambda e, p=p: e.matmul(sup[:, p, :], T3[0:L, p, 1, :], ub[0:L, 2 * p:2 * p + 2, :], start=True, stop=False), reads=["r_T3", "r_ub"], writes=["B2"])
                    S.pe(lambda e, p=p: e.matmul(sup[:, p, :], T3[0:L, p, 2, :], T3[0:L, p, 0, :], start=False, stop=True), reads=["r_T3"], writes=["B2"])
                S.dve(lambda e: e.tensor_tensor(out=tmpS[:], in0=sup, in1=bdmaskf[:].unsqueeze(1).to_broadcast([128, 2, 128]), op=ALU.mult), reads=["B2", "bdmaskf"], writes=["r_tmpS"])
                for p in range(2):
                    S.dve(lambda e, p=p, c=c: e.scalar_tensor_tensor(out=st.rS[:, p, :], in0=st.rS[:, p, :], scalar=ecw[:, p, c * L + L - 1:c * L + L], in1=tmpS[:, p, :], op0=ALU.mult, op1=ALU.add),
                          reads=["rS", ("r_ecw", p), "r_tmpS"], writes=["rS"])
                S.act(lambda e: e.copy(out=st.rSb[:], in_=st.rS[:]), reads=["rS"], writes=["rSb"])
            if RSTOP <= -1:
                return
            ryb = wt("r_yb", [128, 2, TBM], BF16); rsq = wt("r_sq", [128, 2, TBM], BF16); ryo = wt("r_yo", [128, 2, TBM], BF16)
            rmean = wt("r_mean", [128, TBM]); rvar = wt("r_var", [128, TBM])
            for p in range(2):
                S.act(lambda e, p=p: e.copy(out=ryb[:, p, 0:TBc], in_=yall[:, p, 0:TBc]), reads=["r_yall"], writes=[("r_yb", p)])
                S.act(lambda e, p=p: e.activation(out=rsq[:, p, 0:TBc], in_=yall[:, p, 0:TBc], func=AF.Square), reads=["r_yall"], writes=[("r_sq", p)])
                sm = B3[:, 0:TBc]; sm2 = B3[:, 256:256 + TBc]
                S.pe(lambda e, p=p, sm=sm: e.matmul(sm, cst["bd"][:], ryb[:, p, 0:TBc], start=True, stop=True), reads=[("r_yb", p), "c_bd"], writes=["B3"])
                S.pe(lambda e, p=p, sm2=sm2: e.matmul(sm2, cst["bd"][:], rsq[:, p, 0:TBc], start=True, stop=True), reads=[("r_sq", p), "c_bd"], writes=["B3"])
                S.act(lambda e, sm=sm: e.mul(out=rmean[:, 0:TBc], in_=sm, mul=1.0 / 64), reads=["B3"], writes=["r_mean"])
                S.dve(lambda e: e.tensor_tensor(out=rvar[:, 0:TBc], in0=rmean[:, 0:TBc], in1=rmean[:, 0:TBc], op=ALU.mult), reads=["r_mean"], writes=["r_var"])
                S.dve(lambda e, sm2=sm2: e.scalar_tensor_tensor(out=rvar[:, 0:TBc], in0=sm2, scalar=1.0 / 64, in1=rvar[:, 0:TBc], op0=ALU.mult, op1=ALU.subtract), reads=["B3", "r_var"], writes=["r_var"])
                S.act(lambda e: e.activation(out=rvar[:, 0:TBc], in_=rvar[:, 0:TBc], func=AF.Sqrt, bias=P("lneps")), reads=["r_var", "par"], writes=["r_var"])
                S.dve(lambda e: e.reciprocal(out=rvar[:, 0:TBc], in_=rvar[:, 0:TBc]), reads=["r_var"], writes=["r_var"])
                S.dve(lambda e, p=p: e.tensor_tensor(out=yall[:, p, 0:TBc], in0=yall[:, p, 0:TBc], in1=rmean[:, 0:TBc], op=ALU.subtract), reads=["r_yall", "r_mean"], writes=["r_yall"])
                S.dve(lambda e, p=p: e.tensor_tensor(out=yall[:, p, 0:TBc], in0=yall[:, p, 0:TBc], in1=rvar[:, 0:TBc], op=ALU.mult), reads=["r_yall", "r_var"], writes=["r_yall"])
                S.dve(lambda e, p=p: e.tensor_scalar(out=yall[:, p, 0:TBc], in0=yall[:, p, 0:TBc], scalar1=P("lg%d" % p), scalar2=P("lb%d" % p), op0=ALU.mult, op1=ALU.add), reads=["r_yall", "par"], writes=["r_yall"])
                S.dve(lambda e, p=p: e.tensor_tensor(out=yall[:, p, 0:TBc], in0=yall[:, p, 0:TBc], in1=bon[:, p, 0:TBc], op=ALU.add), reads=["r_yall", ("r_bon", p)], writes=["r_yall"])
                S.dve(lambda e, p=p: e.tensor_tensor(out=ryo[:, p, 0:TBc], in0=yall[:, p, 0:TBc], in1=rsz[:, p, 0:TBc], op=ALU.mult), reads=["r_yall", ("r_sz", p)], writes=[("r_yo", p)])
                yd_, ydk_ = y_dst(512 + p * 128, t0, TBc)
                outs.append(S.dma("pool", lambda e, p=p, yd_=yd_: e.dma_start(out=yd_, in_=ryo[:, p, 0:TBc]), reads=[("r_yo", p)], writes=[ydk_]))

        gens = []
        if "h" in mixers:
            gens.append(("h", gen_h()))
        if "m" in mixers:
            gens.append(("m", gen_m()))
        if "r" in mixers:
            gens.append(("r", gen_r()))
        for _, g in gens:
            next(g, None)
        for nm_, g in gens:
            if nm_ in ("h", "m"):
                next(g, None)
        for _, g in gens:
            for _x in g:
                pass

    outs = []
    after_ms = dram.get("after_ms")
    macro(0, 0, NMETA, NMETA)
    if after_ms:
        after_ms(0, 0, NMETA, outs)
    for ms in range(NMS):
        macro(ms + 1, NMETA + ms * TB, TB, LCH)
        if after_ms:
            after_ms(ms + 1, NMETA + ms * TB, TB, outs)
    return outs


def build_B(nc, S, es, TOK, dram, proj=True, last=False, pref="b", skip_meta=False):
    def sb(name, shape, dt=F32):
        return es.enter_context(nc.sbuf_tensor("s_" + pref + name, shape, dt))

    def ps(name, shape, dt=F32):
        return es.enter_context(nc.psum_tensor("p_" + pref + name, shape, dt))

    NT = min(512, TOK - NMETA)
    tiles = [(0, NMETA)] + [(NMETA + i * NT, NT) for i in range((TOK - NMETA) // NT)]
    if skip_meta:
        tiles = tiles[1:]
    if "h_src" not in dram:
        hsrc = dram["hT"].rearrange("(kc p) t -> p kc t", p=128)
        dram = dict(dram)
        dram["h_src"] = lambda t0, N: (hsrc[:, :, t0:t0 + N], ("d_hin", t0))
        if proj:
            hdst = dram["ho"].rearrange("(kc p) t -> p kc t", p=128)
            xsrc = dram["xnT"].rearrange("(kc p) t -> p kc t", p=128)
            ysrc = dram["yT"].rearrange("(kc p) t -> p kc t", p=128)
            dram["h_dst"] = lambda t0, N: (hdst[:, :, t0:t0 + N], ("d_ho", t0))
            dram["xn_srcB"] = lambda t0, N: [(xsrc[:, :, t0:t0 + N], "d_xinB", 0, N)]

            def y_load(y, cand, t0, N):
                S.dma("sp", lambda e: e.dma_start(out=y[:, :, 0:N], in_=ysrc[:, :, t0:t0 + N]), writes=["y"])
            dram["y_load"] = y_load
        xdst = dram["xo"].rearrange("(kc p) t -> p kc t", p=128)
        dram["x_dst"] = lambda t0, N: [(xdst[:, :, t0:t0 + N], ("d_xo", t0), 0, N)]
        if "xof" in dram:
            xfd = dram["xof"].rearrange("(kc p) t -> p kc t", p=128)
            dram["out_dst"] = lambda ob, t0, N: (xfd[:, ob, t0:t0 + N], ("d_xof", ob, t0))
    Bk = [ps("B%d" % i, [128, 512]) for i in range(8)]
    onesf = sb("onesf", [128, 128])
    S.pool(lambda e: e.memset(onesf[:], 1.0), writes=["onesf"])
    gn = sb("gn", [128, 16])
    S.dma("sp", lambda e: e.dma_start(out=gn[:], in_=dram["gnext"]), writes=["gn"])
    epsb = sb("epsb", [128, 1])
    S.pool(lambda e: e.memset(epsb[:], 1e-6), writes=["epsb"])
    h = sb("h", [128, 16, NT]); sq = sb("sq", [128, NT]); rstd = sb("rstd", [128, NT])
    xo = sb("xo", [128, 16, NT], BF16)
    want_f32 = "out_dst" in dram
    if want_f32:
        otmp = [sb("otmp%d" % i, [128, NT]) for i in range(2)]
    outs = []
    if proj:
        xn = sb("xn", [128, 16, NT], BF16); y = sb("y", [128, 24, NT], BF16); mg = sb("mg", [128, 16, NT], BF16)
        cand = sb("cand", [128, 24, NT], BF16) if dram.get("need_cand") else None
        wgs = dram["wg"].rearrange("(kc p) n -> p kc n", p=128)
        wbs = dram["wbr"].rearrange("(b kc p) n -> p b kc n", p=128, b=3)
        wos = dram["wo"].rearrange("(kc p) n -> p kc n", p=128)
        GW = 128
        wgt = [sb("wg%d" % i, [128, 3, 16, GW], BF16) for i in range(2)]
        wbt = [sb("wb%d" % i, [128, 3, 8, GW], BF16) for i in range(2)]
        wot = [sb("wo%d" % i, [128, 16, GW], BF16) for i in range(2)]
        sg = sb("sg", [128, 3, NT]); tt = sb("tt", [128, 3, NT]); msum = sb("msum", [128, NT])
    wcount = [0]
    yidx = dram.get("yidx", lambda br, kc: br * 8 + kc)
    wcache = dram.get("wcache")
    for ti_, (t0, N) in enumerate(tiles):
        first_pass = ti_ == 0
        hin, hk = dram["h_src"](t0, N)
        S.dma("sp", lambda e, hin=hin, N=N: e.dma_start(out=h[:, :, 0:N], in_=hin), reads=[hk], writes=["h"])
        if proj:
            for (xin, xk, lo, hi) in dram["xn_srcB"](t0, N):
                S.dma("sp", lambda e, xin=xin, lo=lo, hi=hi: e.dma_start(out=xn[:, :, lo:hi], in_=xin), reads=[xk], writes=["xn"])
            dram["y_load"](y, cand, t0, N)
            for g in range(2048 // GW):
                wi = wcount[0] % 2
                wcount[0] += 1
                if wcache is not None and not first_pass:
                    S.dma("sp", lambda e, g=g, wi=wi: e.dma_start(out=wgt[wi][:].rearrange("p a b c -> p (a b c)"), in_=wcache[0][g]), reads=[("d_wgc", g)], writes=["wg%d" % wi])
                    S.dma("sp", lambda e, g=g, wi=wi: e.dma_start(out=wbt[wi][:].rearrange("p a b c -> p (a b c)"), in_=wcache[1][g]), reads=[("d_wbc", g)], writes=["wb%d" % wi])
                else:
                    for br in range(3):
                        S.dma("pool", lambda e, g=g, br=br, wi=wi: e.dma_start(out=wgt[wi][:, br], in_=wgs[:, :, br * 2048 + g * GW:br * 2048 + (g + 1) * GW]), writes=["wg%d" % wi])
                        S.dma("pool", lambda e, g=g, br=br, wi=wi: e.dma_start(out=wbt[wi][:, br], in_=wbs[:, br, :, g * GW:(g + 1) * GW]), writes=["wb%d" % wi])
                    if wcache is not None:
                        S.dma("sp", lambda e, g=g, wi=wi: e.dma_start(out=wcache[0][g], in_=wgt[wi][:].rearrange("p a b c -> p (a b c)")), reads=["wg%d" % wi], writes=[("d_wgc", g)])
                        S.dma("sp", lambda e, g=g, wi=wi: e.dma_start(out=wcache[1][g], in_=wbt[wi][:].rearrange("p a b c -> p (a b c)")), reads=["wb%d" % wi], writes=[("d_wbc", g)])
                for d in range(GW // 128):
                    db = g * (GW // 128) + d
                    for br in range(3):
                        gp = Bk[br][:, 0:N]
                        for kc in range(16):
                            S.pe(lambda e, br=br, kc=kc, wi=wi, d=d, gp=gp, N=N: e.matmul(gp, wgt[wi][:, br, kc, d * 128:(d + 1) * 128], xn[:, kc, 0:N], start=(kc == 0), stop=(kc == 15)),
                                 reads=["wg%d" % wi, "xn"], writes=["B%d" % br])
                        S.act(lambda e, br=br, gp=gp, N=N: e.activation(out=sg[:, br, 0:N], in_=gp, func=AF.Sigmoid), reads=["B%d" % br], writes=[("sg", br)])
                        pp = Bk[3 + br][:, 0:N]
                        for kc in range(8):
                            S.pe(lambda e, br=br, kc=kc, wi=wi, d=d, pp=pp, N=N: e.matmul(pp, wbt[wi][:, br, kc, d * 128:(d + 1) * 128], y[:, yidx(br, kc), 0:N], start=(kc == 0), stop=(kc == 7)),
                                 reads=["wb%d" % wi, "y"], writes=["B%d" % (3 + br)])
                        S.dve(lambda e, br=br, pp=pp, N=N: e.tensor_tensor(out=tt[:, br, 0:N], in0=sg[:, br, 0:N], in1=pp, op=ALU.mult), reads=[("sg", br), "B%d" % (3 + br)], writes=[("tt", br)])
                    S.pool(lambda e, N=N: e.tensor_tensor(out=msum[:, 0:N], in0=tt[:, 0, 0:N], in1=tt[:, 1, 0:N], op=ALU.add), reads=[("tt", 0), ("tt", 1)], writes=["msum"])
                    S.pool(lambda e, N=N, db=db: e.tensor_tensor(out=mg[:, db, 0:N], in0=msum[:, 0:N], in1=tt[:, 2, 0:N], op=ALU.add), reads=["msum", ("tt", 2)], writes=[("mg", db)])
            for g in range(2048 // GW):
                wi = g % 2
                if wcache is not None and not first_pass:
                    S.dma("sp", lambda e, g=g, wi=wi: e.dma_start(out=wot[wi][:].rearrange("p b c -> p (b c)"), in_=wcache[2][g]), reads=[("d_woc", g)], writes=["wo%d" % wi])
                else:
                    S.dma("pool", lambda e, g=g, wi=wi: e.dma_start(out=wot[wi][:], in_=wos[:, :, g * GW:(g + 1) * GW]), writes=["wo%d" % wi])
                    if wcache is not None:
                        S.dma("sp", lambda e, g=g, wi=wi: e.dma_start(out=wcache[2][g], in_=wot[wi][:].rearrange("p b c -> p (b c)")), reads=["wo%d" % wi], writes=[("d_woc", g)])
                for d in range(GW // 128):
                    ob = g * (GW // 128) + d
                    op_ = Bk[6][:, 0:N]
                    for kc in range(16):
                        S.pe(lambda e, kc=kc, wi=wi, d=d, op_=op_, N=N: e.matmul(op_, wot[wi][:, kc, d * 128:(d + 1) * 128], mg[:, kc, 0:N], start=(kc == 0), stop=(kc == 15)),
                             reads=["wo%d" % wi] + [("mg", k) for k in range(16)] if kc == 0 else ["wo%d" % wi], writes=["B6"])
                    S.dve(lambda e, ob=ob, op_=op_, N=N: e.tensor_tensor(out=h[:, ob, 0:N], in0=h[:, ob, 0:N], in1=op_, op=ALU.add), reads=["h", "B6"], writes=["h"])
            if "h_dst" in dram:
                hd, hdk = dram["h_dst"](t0, N)
                outs.append(S.dma("sp", lambda e, hd=hd, N=N: e.dma_start(out=hd, in_=h[:, :, 0:N]), reads=["h"], writes=[hdk]))
        ssp = Bk[7][:, 0:N]
        for ob in range(16):
            S.act(lambda e, ob=ob, N=N: e.activation(out=sq[:, 0:N], in_=h[:, ob, 0:N], func=AF.Square), reads=["h"], writes=["sq"])
            S.pe(lambda e, ob=ob, ssp=ssp, N=N: e.matmul(ssp, onesf[:], sq[:, 0:N], start=(ob == 0), stop=(ob == 15)), reads=["sq", "onesf"], writes=["B7"])
        S.act(lambda e, ssp=ssp, N=N: e.activation(out=rstd[:, 0:N], in_=ssp, func=AF.Sqrt, bias=epsb[:], scale=1.0 / 2048), reads=["B7", "epsb"], writes=["rstd"])
        S.dve(lambda e, N=N: e.reciprocal(out=rstd[:, 0:N], in_=rstd[:, 0:N]), reads=["rstd"], writes=["rstd"])
        for ob in range(16):
            if want_f32:
                ot = otmp[ob % 2]
                otk = "otmp%d" % (ob % 2)
                S.dve(lambda e, ob=ob, N=N, ot=ot: e.scalar_tensor_tensor(out=ot[:, 0:N], in0=h[:, ob, 0:N], scalar=gn[:, ob:ob + 1], in1=rstd[:, 0:N], op0=ALU.mult, op1=ALU.mult),
                      reads=["h", "gn", "rstd"], writes=[otk])
                if "x_dst" in dram:
                    S.act(lambda e, ob=ob, N=N, ot=ot: e.copy(out=xo[:, ob, 0:N], in_=ot[:, 0:N]), reads=[otk], writes=["xo"])
                od, odk = dram["out_dst"](ob, t0, N)
                outs.append(S.dma("sp", lambda e, od=od, ot=ot, N=N: e.dma_start(out=od, in_=ot[:, 0:N]), reads=[otk], writes=[odk]))
            else:
                S.dve(lambda e, ob=ob, N=N: e.scalar_tensor_tensor(out=xo[:, ob, 0:N], in0=h[:, ob, 0:N], scalar=gn[:, ob:ob + 1], in1=rstd[:, 0:N], op0=ALU.mult, op1=ALU.mult),
                      reads=["h", "gn", "rstd"], writes=["xo"])
        if "x_dst" in dram:
            for (xd, xdk, lo, hi) in dram["x_dst"](t0, N):
                outs.append(S.dma("sp", lambda e, xd=xd, lo=lo, hi=hi: e.dma_start(out=xd, in_=xo[:, :, lo:hi]), reads=["xo"], writes=[xdk]))
            if dram.get("after_tile"):
                dram["after_tile"](t0, N, outs)
    return outs


def prog_fused(NMS, TB):
    SEQr = NMS * TB
    Q = SEQr // 4
    TOKC_ = NMETA + Q
    CHX = min(256, Q)
    CHY = min(512, Q)
    NCX = Q // CHX
    NCY = SEQr // CHY
    nc = bass.Bass("TRN2", target_bir_lowering=False)
    ext = {}
    ext["hT"] = nc.dram_tensor("hT", [2048, TOKC_], F32, kind="ExternalInput").ap()
    ext["sel"] = nc.dram_tensor("sel", [128, 4], F32, kind="ExternalInput").ap()
    for nm, shp in CONST_SHAPES.items():
        ext[nm] = nc.dram_tensor(nm, shp, F32, kind="ExternalInput").ap()
    for l in range(5):
        ext["gn%d" % l] = nc.dram_tensor("gn%d" % l, [128, 16], F32, kind="ExternalInput").ap()
    for l in range(4):
        ext["wA%d" % l] = nc.dram_tensor("wA%d" % l, [2048, NCOLA], F32, kind="ExternalInput").ap()
        ext["parA%d" % l] = nc.dram_tensor("parA%d" % l, [128, NPA], F32, kind="ExternalInput").ap()
        ext["wud%d" % l] = nc.dram_tensor("wud%d" % l, [128, 256], F32, kind="ExternalInput").ap()
        ext["wg%d" % l] = nc.dram_tensor("wg%d" % l, [2048, 6144], F32, kind="ExternalInput").ap()
        ext["wbr%d" % l] = nc.dram_tensor("wbr%d" % l, [3072, 2048], F32, kind="ExternalInput").ap()
        ext["wo%d" % l] = nc.dram_tensor("wo%d" % l, [2048, 2048], F32, kind="ExternalInput").ap()
    out = nc.dram_tensor("out", [2048, Q], F32, kind="ExternalOutput").ap()
    hloc = nc.dram_tensor("hloc", [2048, TOKC_], F32).ap()
    xm = nc.dram_tensor("xm", [2048, NMETA], BF16).ap()
    xr_t = [nc.dram_tensor("xr%d" % k, [2048, CHX], BF16) for k in range(NCX)]
    xg_t = [nc.dram_tensor("xg%d" % k, [4 * 2048, CHX], BF16) for k in range(NCX)]
    ylm_t = nc.dram_tensor("ylm", [768, NMETA], BF16)
    ygm_t = nc.dram_tensor("ygm", [4 * 768, NMETA], BF16)
    yl_t = [nc.dram_tensor("yl%d" % k, [768, CHY], BF16) for k in range(NCY)]
    yg_t = [nc.dram_tensor("yg%d" % k, [4 * 768, CHY], BF16) for k in range(NCY)]
    wgc = nc.dram_tensor("wgc", [16, 128, 3 * 16 * 128], BF16).ap()
    wbc = nc.dram_tensor("wbc", [16, 128, 3 * 8 * 128], BF16).ap()
    woc = nc.dram_tensor("woc", [16, 128, 16 * 128], BF16).ap()
    groups = [[0, 1, 2, 3], [4, 5, 6, 7]]
    S = Sched(nc)
    hT3 = ext["hT"].rearrange("(kc p) t -> p kc t", p=128)
    hl3 = hloc.rearrange("(kc p) t -> p kc t", p=128)
    xm3 = xm.rearrange("(kc p) t -> p kc t", p=128)
    xr3 = [t.ap().rearrange("(kc p) t -> p kc t", p=128) for t in xr_t]
    xg4 = [t.ap().rearrange("(q kc p) t -> p q kc t", q=4, p=128) for t in xg_t]
    ygm4 = ygm_t.ap().rearrange("(jj c p) t -> p jj c t", jj=4, c=6, p=128)
    yg4 = [t.ap().rearrange("(jj c p) t -> p jj c t", jj=4, c=6, p=128) for t in yg_t]
    out3 = out.rearrange("(kc p) t -> p kc t", p=128)

    def xloc(t0, N):
        if t0 == 0:
            return [(xm3[:, :, 0:N], "d_xm", 0, N)]
        r0 = t0 - NMETA
        res = []
        for k in range(r0 // CHX, (r0 + N) // CHX):
            res.append((xr3[k][:, :, :], ("d_xr", k), k * CHX - r0, (k + 1) * CHX - r0))
        return res

    with ExitStack() as top:
        selt = top.enter_context(nc.sbuf_tensor("s_sel", [128, 4], F32))
        S.dma("sp", lambda e: e.dma_start(out=selt[:], in_=ext["sel"]), writes=["sel"])

        def y_load(y, cand, t0, N):
            if t0 == 0:
                for jj in range(4):
                    S.dma("sp", lambda e, jj=jj: e.dma_start(out=y[:, jj * 6:(jj + 1) * 6, 0:N], in_=ygm4[:, jj, :, 0:N]), reads=["d_ygm"], writes=["y"])
                return
            for q in range(4):
                kk = (q * Q + (t0 - NMETA)) // CHY
                for jj in range(4):
                    S.dma("sp", lambda e, jj=jj, kk=kk: e.dma_start(out=cand[:, jj * 6:(jj + 1) * 6, 0:N], in_=yg4[kk][:, jj, :, 0:N]), reads=[("d_yg", kk)], writes=["cand"])
                if q == 0:
                    S.dve(lambda e: e.tensor_scalar_mul(out=y[:, :, 0:N], in0=cand[:, :, 0:N], scalar1=selt[:, 0:1]), reads=["cand", "sel"], writes=["y"])
                else:
                    S.dve(lambda e, q=q: e.scalar_tensor_tensor(out=y[:, :, 0:N], in0=cand[:, :, 0:N], scalar=selt[:, q:q + 1], in1=y[:, :, 0:N], op0=ALU.mult, op1=ALU.add),
                          reads=["cand", "sel", "y"], writes=["y"])

        def cc_gather(src_t, dst_t, wk, extra):
            return S.cc(lambda e, src_t=src_t, dst_t=dst_t: e.collective_compute("AllGather", ALU.bypass, replica_groups=groups, ins=[src_t.ap().opt()], outs=[dst_t.ap().opt()]),
                        writes=[wk], extra=list(extra))

        def make_after_tile():
            st_ = dict(n=0)

            def after_tile(t0, N, outs):
                if t0 == 0:
                    return
                r0 = t0 - NMETA
                for k in range(r0 // CHX, (r0 + N) // CHX):
                    cc_gather(xr_t[k], xg_t[k], ("d_xg", k), outs[st_["n"]:])
                st_["n"] = len(outs)
            return after_tile

        def make_after_ms():
            st_ = dict(n=0, k=0)

            def after_ms(ms, t0, TBc, outs):
                if ms == 0:
                    cc_gather(ylm_t, ygm_t, "d_ygm", outs[st_["n"]:])
                    st_["n"] = len(outs)
                    return
                done = (t0 - NMETA + TBc) // CHY
                if done > st_["k"]:
                    for k in range(st_["k"], done):
                        cc_gather(yl_t[k], yg_t[k], ("d_yg", k), outs[st_["n"]:])
                    st_["k"] = done
                    st_["n"] = len(outs)
            return after_ms

        dN = dict(gnext=ext["gn0"], h_src=lambda t0, N: (hT3[:, :, t0:t0 + N], ("d_h", t0)), x_dst=xloc, after_tile=make_after_tile())
        with ExitStack() as es:
            outs = build_B(nc, S, es, TOKC_, dN, proj=False, pref="n0")
        S.barrier()
        g0 = outs[-1]
        finals = None
        FSTOP = int(os.environ.get("FSTOP", "99"))
        if FSTOP == 0:
            S.emit(final_wait_ops=[g0])
            return nc, S
        for l in range(4):
            dA = dict(wA=ext["wA%d" % l], parA=ext["parA%d" % l], wud=ext["wud%d" % l])
            for nm in CONST_SHAPES:
                dA[nm] = ext[nm]

            def xn_src(t0, TBc):
                if t0 == 0:
                    return xm3[:, :, 0:TBc], "d_xm"
                q, off = divmod(t0 - NMETA, Q)
                k, c = divmod(off, CHX)
                return xg4[k][:, q, :, c:c + TBc], ("d_xg", k)

            def y_dst(r0, t0, TBc):
                if t0 == 0:
                    return ylm_t.ap()[r0:r0 + 128, 0:TBc], ("d_yl", r0, t0)
                k, c = divmod(t0 - NMETA, CHY)
                return yl_t[k].ap()[r0:r0 + 128, c:c + TBc], ("d_yl", r0, t0)

            dA["xn_src"] = xn_src
            dA["y_dst"] = y_dst
            dA["after_ms"] = make_after_ms()
            with ExitStack() as es:
                outs = build_A(nc, S, es, NMS, TB, dA, pref="a%d" % l)
            S.barrier()
            g1 = outs[-1]
            if FSTOP == 1:
                S.emit(final_wait_ops=[g1])
                return nc, S
            last = l == 3
            hsrc3 = hT3 if l == 0 else hl3
            dB = dict(gnext=ext["gn%d" % (l + 1)], wg=ext["wg%d" % l], wbr=ext["wbr%d" % l], wo=ext["wo%d" % l],
                      h_src=lambda t0, N, hsrc3=hsrc3: (hsrc3[:, :, t0:t0 + N], ("d_h", t0)),
                      xn_srcB=xloc, y_load=y_load, need_cand=True, wcache=(wgc, wbc, woc), yidx=lambda br, kc: (kc // 2) * 6 + br * 2 + (kc % 2))
            if not last:
                dB["h_dst"] = lambda t0, N: (hl3[:, :, t0:t0 + N], ("d_h", t0))
                dB["x_dst"] = xloc
                dB["after_tile"] = make_after_tile()
            else:
                dB["out_dst"] = lambda ob, t0, N: (out3[:, ob, t0 - NMETA:t0 - NMETA + N], ("d_out", ob, t0))
            with ExitStack() as es:
                outs = build_B(nc, S, es, TOKC_, dB, proj=True, last=last, skip_meta=last, pref="b%d" % l)
            if not last:
                S.barrier()
            else:
                finals = outs
        S.emit(final_wait_ops=finals)
    return nc, S


SPL = [1024, 1024, 1024, 1024, 512, 512, 1024, 1024, 4, 4, 1024, 1024, 1024, 1024, 64, 64, 1024, 2048, 2048, 2048]
NAMES = ["a_q", "a_f", "a_i", "a_z", "b_q", "b_k", "b_v", "b_o", "b_ig", "b_fg", "b_z", "c_r", "c_k", "c_v", "c_wd", "c_ad", "c_z", "g_a", "g_b", "g_c"]
OFF = {}
_o = 0
for n_, w_ in zip(NAMES, SPL):
    OFF[n_] = _o
    _o += w_
NIN = _o


def colsA(j):
    cols = np.zeros(NCOLA, np.int64)

    def put(tile, start):
        cols[CT[tile] * 128:(CT[tile] + 1) * 128] = np.arange(start, start + 128)

    for hh in range(2):
        h = 2 * j + hh
        put("hq%d" % hh, OFF["a_q"] + h * 128)
        put("hf%d" % hh, OFF["a_f"] + h * 128)
        put("hi%d" % hh, OFF["a_i"] + h * 128)
        put("hz%d" % hh, OFF["a_z"] + h * 128)
    put("mq", OFF["b_q"] + j * 128)
    put("mk", OFF["b_k"] + j * 128)
    for i in range(2):
        put("mv%d" % i, OFF["b_v"] + j * 256 + i * 128)
        put("mo%d" % i, OFF["b_o"] + j * 256 + i * 128)
        put("mz%d" % i, OFF["b_z"] + j * 256 + i * 128)
    for p in range(2):
        c0 = j * 256 + p * 128
        put("rr%d" % p, OFF["c_r"] + c0)
        put("rk%d" % p, OFF["c_k"] + c0)
        put("rv%d" % p, OFF["c_v"] + c0)
        put("rz%d" % p, OFF["c_z"] + c0)
    cols[CT["rwa"] * 128:CT["rwa"] * 128 + 64] = np.arange(OFF["c_wd"], OFF["c_wd"] + 64)
    cols[CT["rwa"] * 128 + 64:CT["rwa"] * 128 + 128] = np.arange(OFF["c_ad"], OFF["c_ad"] + 64)
    cols[25 * 128] = OFF["b_ig"] + j
    cols[25 * 128 + 1] = OFF["b_fg"] + j
    return cols


def pack_A(inp, l, j):
    f = np.float32
    wA = np.ascontiguousarray(np.asarray(inp["w_in"][l])[:, colsA(j)], dtype=f)
    par = np.zeros((128, NPA), f)

    def put(nm, v):
        o, w = PA[nm]
        v = np.asarray(v, f)
        if v.ndim == 0:
            par[:, o:o + w] = v
        else:
            par[:, o:o + w] = v.reshape(128, w)

    lbl = np.asarray(inp["hgrn_lb_logits"])
    par[:, PA["lbsel"][0]:PA["lbsel"][0] + 4] = np.array([1.0 if 1 <= i <= l else 0.0 for i in range(4)], f)[None, :]
    for hh in range(2):
        ch = slice((2 * j + hh) * 128, (2 * j + hh + 1) * 128)
        put("lbl%d" % hh, lbl[:, ch].T)
        put("hg%d" % hh, np.asarray(inp["hgrn_norm_g"][l])[ch])
    cw = np.asarray(inp["mlstm_conv"][l])
    put("cwq", cw[:, j * 128:(j + 1) * 128].T)
    put("cwk", cw[:, 512 + j * 128:512 + (j + 1) * 128].T)
    for i in range(2):
        put("mg%d" % i, np.asarray(inp["mlstm_norm_g"][l])[j * 256 + i * 128:j * 256 + (i + 1) * 128])
    put("igb", float(np.asarray(inp["mlstm_ig_b"])[l, j]))
    put("fgb", float(np.asarray(inp["mlstm_fg_b"])[l, j]))
    put("eps", 1e-6)
    put("lneps", 64e-5)
    put("zero", 0.0)
    mu = np.asarray(inp["rwkv_mu"][l])
    for p in range(2):
        c = slice(j * 256 + p * 128, j * 256 + (p + 1) * 128)
        put("mur%d" % p, mu[0:1024][c])
        put("muk%d" % p, mu[1024:2048][c])
        put("muv%d" % p, mu[2048:3072][c])
        put("w0%d" % p, np.asarray(inp["rwkv_w0"][l])[c])
        put("a0%d" % p, np.asarray(inp["rwkv_a0"][l])[c])
        put("kk%d" % p, np.asarray(inp["rwkv_k_k"][l])[c])
        put("ka%d" % p, np.asarray(inp["rwkv_k_a"][l])[c])
        put("rk%d" % p, np.asarray(inp["rwkv_r_k"][l])[c])
        put("lg%d" % p, np.asarray(inp["rwkv_ln_g"][l])[c])
        put("lb%d" % p, np.asarray(inp["rwkv_ln_b"][l])[c])
    put("muwa", mu[3072:3200])
    wud = np.zeros((128, 256), f)
    wud[0:64] = np.asarray(inp["rwkv_w_up"][l])[:, j * 256:(j + 1) * 256]
    wud[64:128] = np.asarray(inp["rwkv_a_up"][l])[:, j * 256:(j + 1) * 256]
    return dict(wA=wA, parA=par, wud=wud)


TB_A = 256
SEQ = 8192
NMS_A = SEQ // TB_A


def kernel(**inp):
    f = np.float32
    x = np.asarray(inp["x"], f)
    meta = np.asarray(inp["meta_tokens"], f)
    Q = SEQ // 4
    nc, _S = prog_fused(NMS_A, TB_A)
    cs = host_consts()

    def gn(g):
        return np.ascontiguousarray(np.asarray(g, f).reshape(16, 128).T)

    gns = [gn(inp["norm_g"][l]) for l in range(4)] + [gn(inp["final_norm_g"])]
    shared = {}
    for l in range(4):
        w_in_l = np.asarray(inp["w_in"][l])
        shared["wg%d" % l] = np.ascontiguousarray(w_in_l[:, OFF["g_a"]:OFF["g_a"] + 6144], dtype=f)
        shared["wbr%d" % l] = np.ascontiguousarray(np.asarray(inp["w_br"][l], f).reshape(3072, 2048))
        shared["wo%d" % l] = np.ascontiguousarray(np.asarray(inp["w_out"][l], f))
    packs = {}
    for j in range(4):
        for l in range(4):
            packs[(l, j)] = pack_A(inp, l, j)
    maps = []
    for c in range(8):
        b, j = divmod(c, 4)
        m = {"hT": np.ascontiguousarray(np.concatenate([meta, x[b, j * Q:(j + 1) * Q]], axis=0).T)}
        sel = np.zeros((128, 4), f)
        sel[:, j] = 1.0
        m["sel"] = sel
        m.update(cs)
        for l in range(5):
            m["gn%d" % l] = gns[l]
        for l in range(4):
            pa = packs[(l, j)]
            m["wA%d" % l] = pa["wA"]
            m["parA%d" % l] = pa["parA"]
            m["wud%d" % l] = pa["wud"]
        m.update(shared)
        maps.append(m)
    res = run_bass_kernel_spmd(nc, maps, core_ids=list(range(8)))
    out = np.empty((2, SEQ, 2048), f)
    for c in range(8):
        b, j = divmod(c, 4)
        out[b, j * Q:(j + 1) * Q] = np.asarray(res.results[c]["out"], f).T
    return out
```

```python
import ml_dtypes
from concourse.bass_utils import run_bass_kernel_spmd
import numpy as np
import concourse.bass as bass
import concourse.mybir as mybir
from contextlib import ExitStack

F32 = mybir.dt.float32
BF16 = mybir.dt.bfloat16
AF = mybir.ActivationFunctionType
ALU = mybir.AluOpType

COMPUTE = ("pe", "act", "dve", "pool")
DMAQ = ("sp", "poolq")
NDMASEM = 8
import os as _os
FUSEWAIT = _os.environ.get('FUSEWAIT', '1') == '1'


class Sched:
    def __init__(self, nc, strict_same=True):
        self.nc = nc
        self.ops = []
        self.last_w = {}
        self.readers = {}
        self.strict_same = strict_same
        self.pe_last = {}
        self.keymap = {}
        self.bar = set()
        self.epoch = 0
        self.last_eng = {}
        self.dma_since = []

    def op(self, eng, fn, reads=(), writes=(), dma=False, rowgrp=None, extra=(), cc=False):
        i = len(self.ops)
        if self.keymap:
            km = self.keymap

            def _mk(k):
                if isinstance(k, tuple) and k and k[0] in km:
                    return (km[k[0]],) + tuple(k[1:])
                if isinstance(k, str) and k in km:
                    return km[k]
                return k
            reads = [_mk(k) for k in reads]
            writes = [_mk(k) for k in writes]
        ex = [k for k in reads if isinstance(k, str) and k[0] == "B" and len(k) <= 2]
        if ex:
            reads = [k for k in reads if k not in ex]
            writes = list(writes) + [k for k in ex if k not in writes]
        deps = set()
        for k in reads:
            w = self.last_w.get(k)
            if w is not None:
                deps.add(w)
        for k in writes:
            w = self.last_w.get(k)
            if w is not None:
                deps.add(w)
            for r in self.readers.get(k, {}).values():
                for x in r:
                    deps.add(x)
        deps.discard(i)
        forced = set()
        if eng == "pe":
            for k in writes:
                if isinstance(k, str) and k[0] == "B" and len(k) <= 2:
                    pl = self.pe_last.get(k)
                    if pl is not None and pl[1] != rowgrp:
                        forced.add(pl[0])
                    self.pe_last[k] = (i, rowgrp)
        deps |= forced
        deps |= set(extra)
        deps |= self.bar
        self.ops.append(dict(eng=eng, fn=fn, deps=deps, dma=dma, forced=forced, cc=cc, epoch=self.epoch))
        self.last_eng[eng] = i
        if dma or cc:
            self.dma_since.append(i)
        for k in writes:
            self.last_w[k] = i
            self.readers[k] = {}
        for k in reads:
            d = self.readers.setdefault(k, {})
            if dma:
                d.setdefault(eng + "_dma", []).append(i)
            else:
                d[eng] = [i]
        return i

    def barrier(self):
        self.bar = set(self.last_eng.values()) | set(self.dma_since)
        self.dma_since = []
        self.epoch += 1
        self.pe_last = {}

    def cc(self, fn, reads=(), writes=(), extra=()):
        return self.op("pool", fn, reads, writes, cc=True, extra=extra)

    def pe(self, fn, reads=(), writes=(), rowgrp=None):
        return self.op("pe", fn, reads, writes, rowgrp=rowgrp)

    def act(self, fn, reads=(), writes=()):
        return self.op("act", fn, reads, writes)

    def dve(self, fn, reads=(), writes=()):
        return self.op("dve", fn, reads, writes)

    def pool(self, fn, reads=(), writes=()):
        return self.op("pool", fn, reads, writes)

    def dma(self, q, fn, reads=(), writes=()):
        return self.op(q, fn, reads, writes, dma=True)

    def emit(self, final_wait_ops=()):
        nc = self.nc
        ops = self.ops
        streams = {"pe": [], "act": [], "dve": [], "pool": [], "sp": []}
        for i, o in enumerate(ops):
            streams[o["eng"]].append(i)
        need = [False] * len(ops)
        for i, o in enumerate(ops):
            for d in o["deps"]:
                od = ops[d]
                if od["dma"] or od["cc"]:
                    need[d] = True
                elif od["eng"] != o["eng"]:
                    need[d] = True
                elif o["eng"] != "pe" and self.strict_same:
                    need[d] = True
                elif d in o["forced"]:
                    need[d] = True
        for d in final_wait_ops:
            need[d] = True
        with ExitStack() as es:
            nep = self.epoch + 1
            csem = {(e, ep): es.enter_context(nc.semaphore("c_%s%d" % (e, ep))) for e in COMPUTE for ep in range(nep)}
            ccsem = es.enter_context(nc.semaphore("ccsem"))
            cccnt = 0
            dsem = {
                q: [es.enter_context(nc.semaphore("d_%s%d" % (q, k))) for k in range(NDMASEM)]
                for q in ("sp", "pool")
            }
            cnt = {(e, ep): 0 for e in COMPUTE for ep in range(nep)}
            dcnt = {"sp": 0, "pool": 0}
            sig = [None] * len(ops)
            prev_same_sem = [None] * len(ops)
            for i, o in enumerate(ops):
                if o["dma"]:
                    q = o["eng"]
                    n = dcnt[q]
                    dcnt[q] += 1
                    s = dsem[q][n % NDMASEM]
                    sig[i] = (("d", q, n % NDMASEM), s, 16 * (n // NDMASEM + 1))
                    if n >= NDMASEM:
                        prev_same_sem[i] = (("d", q, n % NDMASEM), s, 16 * (n // NDMASEM))
                elif o["cc"]:
                    cccnt += 1
                    sig[i] = (("cc",), ccsem, cccnt)
                elif need[i]:
                    e = (o["eng"], o["epoch"])
                    cnt[e] += 1
                    sig[i] = (("c", e), csem[e], cnt[e])
            self.stats = dict(n_ops=len(ops), cnt={str(k): v for k, v in cnt.items()}, dcnt=dict(dcnt),
                              per_eng={e: len(v) for e, v in streams.items()})
            blk = es.enter_context(nc.Block())

            def run_stream(ename, eobj):
                known = {}
                nwait = 0
                for i in streams[ename]:
                    o = ops[i]
                    waits = {}
                    if prev_same_sem[i] is not None:
                        k, s, v = prev_same_sem[i]
                        if known.get(k, 0) < v:
                            waits[k] = (s, v)
                    for d in o["deps"]:
                        od = ops[d]
                        if not od["dma"] and not od["cc"] and od["eng"] == ename and (ename == "pe" or not self.strict_same) and d not in o["forced"]:
                            continue
                        if sig[d] is None:
                            continue
                        k, s, v = sig[d]
                        if known.get(k, 0) < v and (k not in waits or waits[k][1] < v):
                            waits[k] = (s, v)
                    wl = list(waits.items())
                    fuse = None
                    if FUSEWAIT and wl and not o["cc"]:
                        fuse = wl.pop()
                    for k, (s, v) in wl:
                        eobj.wait_ge(s, v)
                        known[k] = v
                        nwait += 1
                    ins = o["fn"](eobj)
                    if fuse is not None:
                        k, (s, v) = fuse
                        ins._wait_ge(s, v)
                        known[k] = v
                    if sig[i] is not None:
                        ins.then_inc(sig[i][1], 16 if o["dma"] else 1)
                if ename == "sp":
                    for d in final_wait_ops:
                        k, s, v = sig[d]
                        eobj.wait_ge(s, v)
                self.stats["waits_" + ename] = nwait

            blk.sync(lambda e: run_stream("sp", e))
            blk.tensor(lambda e: run_stream("pe", e))
            blk.scalar(lambda e: run_stream("act", e))
            blk.vector(lambda e: run_stream("dve", e))
            blk.gpsimd(lambda e: run_stream("pool", e))


import math, os
RSTOP = int(os.environ.get('RSTOP', '9'))

D = 2048
KC = 16
NMETA = 16
LCH = 64
C0 = math.exp(-0.5)
CT = dict(hq0=0, hq1=1, hf0=2, hf1=3, hi0=4, hi1=5, hz0=6, hz1=7,
          mq=8, mk=9, mv0=10, mv1=11, mo0=12, mo1=13, mz0=14, mz1=15,
          rr0=16, rr1=17, rk0=18, rk1=19, rv0=20, rv1=21, rz0=22, rz1=23, rwa=24)
NCOLA = 25 * 128 + 2
PA = {}
_n = 0
for nm, w in [("lbsel", 4), ("lbl0", 4), ("lbl1", 4), ("hg0", 1), ("hg1", 1), ("cwq", 4), ("cwk", 4), ("mg0", 1), ("mg1", 1),
              ("igb", 1), ("fgb", 1), ("eps", 1), ("lneps", 1), ("zero", 1),
              ("mur0", 1), ("mur1", 1), ("muk0", 1), ("muk1", 1), ("muv0", 1), ("muv1", 1), ("muwa", 1),
              ("w00", 1), ("w01", 1), ("a00", 1), ("a01", 1), ("kk0", 1), ("kk1", 1), ("ka0", 1), ("ka1", 1),
              ("rk0", 1), ("rk1", 1), ("lg0", 1), ("lg1", 1), ("lb0", 1), ("lb1", 1)]:
    PA[nm] = (_n, w)
    _n += w
NPA = _n


def host_consts():
    c = {}
    c["ident"] = np.eye(128, dtype=np.float32)
    c["ones"] = np.ones((128, 128), np.float32)
    bd = np.zeros((128, 128), np.float32)
    bd[:64, :64] = 1
    bd[64:, 64:] = 1
    c["bd"] = bd
    s = np.arange(64)[:, None]
    t = np.arange(64)[None, :]
    mI = (s <= t).astype(np.float32)
    mS = (s < t).astype(np.float32)
    c["maskI"] = mI
    c["mask2"] = np.stack([mS, mI], axis=1)
    c["maskL"] = (t < s).astype(np.float32)
    pm = np.zeros((64, 2, 128), np.float32)
    pm[:, 0, :64] = 1
    pm[:, 1, 64:] = 1
    c["padmask"] = pm
    return c


CONST_SHAPES = dict(ident=[128, 128], ones=[128, 128], bd=[128, 128], maskI=[64, 64], mask2=[64, 2, 64],
                    maskL=[64, 64], padmask=[64, 2, 128])


class Ctx:
    pass


def build_A(nc, S, es, NMS, TB, dram, mixers=("h", "m", "r"), layer=0, first=True, pref=""):
    def sb(name, shape, dt=F32):
        return es.enter_context(nc.sbuf_tensor("s_" + pref + name, shape, dt))

    def ps(name, shape, dt=F32):
        return es.enter_context(nc.psum_tensor("p_" + pref + name, shape, dt))

    TBM = TB
    NCH = TB // LCH
    cst = {}
    for nm, shp in CONST_SHAPES.items():
        dt = F32 if nm in ("maskI", "mask2", "maskL", "padmask") else BF16
        cst[nm] = sb("c_" + nm, shp, dt)
        S.dma("pool", lambda e, nm=nm: e.dma_start(out=cst[nm][:], in_=dram[nm]), writes=["c_" + nm])
    ident64f = sb("ident64f", [64, 64])
    S.dma("sp", lambda e: e.dma_start(out=ident64f[:], in_=dram["ident"][0:64, 0:64]), writes=["ident64f"])
    bdmaskf = sb("bdmaskf", [128, 128])
    S.dma("sp", lambda e: e.dma_start(out=bdmaskf[:], in_=dram["bd"]), writes=["bdmaskf"])
    onesf = sb("onesf", [128, TBM])
    S.pool(lambda e: e.memset(onesf[:], 1.0), writes=["onesf"])
    par = sb("par", [128, NPA])
    S.dma("sp", lambda e: e.dma_start(out=par[:], in_=dram["parA"]), writes=["par"])

    def P(nm, i=0):
        o, w = PA[nm]
        return par[:, o + i:o + i + 1]

    wA = sb("wA", [128, KC, NCOLA], BF16)
    wsrc = dram["wA"].rearrange("(kc p) n -> p kc n", p=128)
    for kc in range(KC):
        S.dma("pool", lambda e, kc=kc: e.dma_start(out=wA[:, kc, :], in_=wsrc[:, kc, :]), writes=[("wA", kc)])
    WAK = [("wA", kc) for kc in range(KC)]
    wud = sb("wud", [128, 256], BF16)
    S.dma("pool", lambda e: e.dma_start(out=wud[:], in_=dram["wud"]), writes=["wud"])

    B0 = ps("B0", [128, 512]); B1 = ps("B1", [128, 512]); Bt = ps("Bt", [128, 1024], BF16)
    B2 = ps("B2", [128, 512]); B3 = ps("B3", [128, 512]); B4 = ps("B4", [128, 512])
    B5 = ps("B5", [128, 512]); B6 = ps("B6", [128, 512])
    ppbuf = [(B0[:, 0:256], "B0"), (B1[:, 0:256], "B1")]
    ppi = [0]

    NXB = 1 if TB >= 256 else 2
    xn = [sb("xn%d" % i, [128, KC, TBM], BF16) for i in range(NXB)]
    if "xn_src" in dram:
        xn_src = dram["xn_src"]
        y_dst = dram["y_dst"]
    else:
        xsrc = dram["xnT"].rearrange("(kc p) t -> p kc t", p=128)
        ydst = dram["yT"]

        def xn_src(t0, TBc):
            return xsrc[:, :, t0:t0 + TBc], "d_xin"

        def y_dst(r0, t0, TBc):
            return ydst[r0:r0 + 128, t0:t0 + TBc], ("d_yl", r0, t0)

    st = Ctx()
    if "h" in mixers:
        st.hS = sb("hS", [128, 2, 128]); st.hSb = sb("hSb", [128, 2, 128], BF16)
        S.pool(lambda e: e.memset(st.hS[:], 0.0), writes=["hS"])
        S.pool(lambda e: e.memset(st.hSb[:], 0.0), writes=["hSb"])
        st.lb = sb("lb", [128, 2]); st.oml = sb("oml", [128, 2]); st.lbm1 = sb("lbm1", [128, 2])
        lbe = sb("lbe", [128, 2, 4]); lbs = sb("lbs", [128, 2]); lbr = sb("lbr", [128, 2])
        o0, _ = PA["lbl0"]
        S.act(lambda e: e.activation(out=lbe[:].rearrange("p a b -> p (a b)"), in_=par[:, o0:o0 + 8], func=AF.Exp), reads=["par"], writes=["lbe"])
        S.dve(lambda e: e.tensor_reduce(out=lbs[:], in_=lbe[:], axis=mybir.AxisListType.X, op=ALU.add), reads=["lbe"], writes=["lbs"])
        S.dve(lambda e: e.reciprocal(out=lbr[:], in_=lbs[:]), reads=["lbs"], writes=["lbr"])
        lbt = sb("lbt", [128, 2]); lbm = sb("lbm", [128, 2, 4])
        osel, _ = PA["lbsel"]
        S.dve(lambda e: e.tensor_tensor(out=lbm[:], in0=lbe[:], in1=par[:, osel:osel + 4].unsqueeze(1).to_broadcast([128, 2, 4]), op=ALU.mult), reads=["lbe", "par"], writes=["lbm"])
        S.dve(lambda e: e.tensor_reduce(out=lbt[:], in_=lbm[:], axis=mybir.AxisListType.X, op=ALU.add), reads=["lbm"], writes=["lbt"])
        S.dve(lambda e: e.tensor_tensor(out=st.lb[:], in0=lbt[:], in1=lbr[:], op=ALU.mult), reads=["lbt", "lbr"], writes=["lb"])
        S.dve(lambda e: e.tensor_scalar(out=st.oml[:], in0=st.lb[:], scalar1=-1.0, scalar2=1.0, op0=ALU.mult, op1=ALU.add), reads=["lb"], writes=["oml"])
        S.dve(lambda e: e.tensor_scalar_add(out=st.lbm1[:], in0=st.lb[:], scalar1=-1.0), reads=["lb"], writes=["lbm1"])
    if "m" in mixers:
        st.mC = sb("mC", [128, 257]); st.mCb = sb("mCb", [128, 257], BF16)
        S.pool(lambda e: e.memset(st.mC[:], 0.0), writes=["mC"])
        st.mxq = sb("mxq", [128, 3 + TBM]); st.mxk = sb("mxk", [128, 3 + TBM])
        S.pool(lambda e: e.memset(st.mxq[:, 0:3], 0.0), writes=["mqx"])
        S.pool(lambda e: e.memset(st.mxk[:, 0:3], 0.0), writes=["mkx"])
        st.mmin = sb("mmin", [1, 1])
        S.pool(lambda e: e.memset(st.mmin[:], 0.0), writes=["mmin"])
        st.mTT = sb("mTT", [64, 3 * 128 + 1], BF16)
        S.pool(lambda e: e.memset(st.mTT[:], 1.0), writes=["mTT"])
        st.onesrow = sb("onesrow", [1, 128])
        S.pool(lambda e: e.memset(st.onesrow[:], 1.0), writes=["onesrow"])
    if "r" in mixers:
        st.rS = sb("rS", [128, 2, 128]); st.rSb = sb("rSb", [128, 2, 128], BF16)
        S.pool(lambda e: e.memset(st.rS[:], 0.0), writes=["rS"])
        S.pool(lambda e: e.memset(st.rSb[:], 0.0), writes=["rSb"])
        st.rraw = {}
        for nm in ("rr0", "rr1", "rk0", "rk1", "rv0", "rv1", "rwa"):
            st.rraw[nm] = sb("raw_" + nm, [128, 1 + TBM])
            S.pool(lambda e, nm=nm: e.memset(st.rraw[nm][:, 0:1], 0.0), writes=["raw_" + nm])

    W = {}

    ALIAS = {}
    if TB >= 256:
        AL = [("h_d1", "h_ft", "h_ft"), ("h_rs", "h_e1", "h_e1"), ("h_sq", "h_Qi", "h_Qi"),
              ("r_sgw", "h_qa", "h_qa"), ("r_a", "h_sig", "h_sig"), ("r_kkn", "h_sz", "h_sz"), ("r_kf", "h_k", "h_k"),
              ("r_bv", "h_ft", "h_ft"), ("r_cs", "h_cg", "h_cg"), ("r_tmp", "h_e1", "h_e1"), ("r_ecw", "h_eg", "h_eg"),
              ("r_yall", "h_oall", "h_oall"),
              ("r_tmpb", "h_vb", "h_vb"), ("r_Bd", "h_Qi", "h_Qi"), ("r_Kd", "h_Ki", "h_Ki"), ("r_Btl", "h_Qx", "h_Qx"),
              ("r_Ktl", "h_Kt", "h_Kt"), ("r_vb", "h_yo", "h_yo"),
              ("r_yb", "h_Qi", "h_Qi"), ("r_sq", "h_Ki", "h_Ki"), ("r_yo", "h_Qx", "h_Qx"),
              ("m_mean", "m_qs", "m_mqs"), ("m_var", "m_ks", "m_mks"), ("r_mean", "r_m_rr0", "r_m_rr0"), ("r_var", "r_m_rr1", "r_m_rr1")]
        for a, t_, k_ in AL:
            ALIAS[a] = t_
            S.keymap[a] = k_

    def wt(name, shape, dt=F32):
        if name in ALIAS:
            return W[ALIAS[name]]
        if name not in W:
            W[name] = sb("w_" + name, shape, dt)
        return W[name]

    rr = [0]

    def ev(out, in_, reads, writes):
        rr[0] ^= 1
        if rr[0]:
            return S.dve(lambda e: e.tensor_copy(out=out, in_=in_), reads, writes)
        return S.act(lambda e: e.copy(out=out, in_=in_), reads, writes)

    def inproj(xt, xkey, col0, ncols, TBc):
        buf, key = ppbuf[ppi[0] % 2]
        ppi[0] += 1
        out = buf[0:ncols, 0:TBc]
        for kc in range(KC):
            S.pe(lambda e, kc=kc: e.matmul(out, wA[:, kc, col0:col0 + ncols], xt[:, kc, 0:TBc], start=(kc == 0), stop=(kc == KC - 1)),
                 reads=[xkey, ("wA", kc)], writes=[key])
        return out, key

    def macro(ms, t0, TBc, L):
        nch = TBc // L
        xt = xn[ms % NXB]
        xkey = "xn%d" % (ms % NXB)
        xin_, xink_ = xn_src(t0, TBc)
        S.dma("sp", lambda e: e.dma_start(out=xt[:, :, 0:TBc], in_=xin_), reads=[xink_], writes=[xkey])

        def c3(ap):
            return ap.rearrange("p (c j) -> p c j", j=L)

        def gen_h():
            qa = wt("h_qa", [128, 2, TBM]); sig = wt("h_sig", [128, 2, TBM]); vb = wt("h_vb", [128, 2, TBM], BF16)
            sz = wt("h_sz", [128, 2, TBM])
            for hh in range(2):
                p_, k_ = inproj(xt, xkey, CT["hq%d" % hh] * 128, 128, TBc)
                S.act(lambda e, p_=p_, hh=hh: e.activation(out=qa[:, hh, 0:TBc], in_=p_, func=AF.Silu), reads=[k_], writes=[("h_qa", hh)])
            for hh in range(2):
                p_, k_ = inproj(xt, xkey, CT["hz%d" % hh] * 128, 128, TBc)
                S.act(lambda e, p_=p_, hh=hh: e.activation(out=sz[:, hh, 0:TBc], in_=p_, func=AF.Silu), reads=[k_], writes=[("h_sz", hh)])
            for hh in range(2):
                p_, k_ = inproj(xt, xkey, CT["hi%d" % hh] * 128, 128, TBc)
                S.dve(lambda e, p_=p_, hh=hh: e.tensor_copy(out=vb[:, hh, 0:TBc], in_=p_), reads=[k_], writes=[("h_vb", hh)])
            for hh in range(2):
                p_, k_ = inproj(xt, xkey, CT["hf%d" % hh] * 128, 128, TBc)
                S.act(lambda e, p_=p_, hh=hh: e.activation(out=sig[:, hh, 0:TBc], in_=p_, func=AF.Sigmoid), reads=[k_], writes=[("h_sig", hh)])
            kk = wt("h_k", [128, 2, TBM]); ft = wt("h_ft", [128, 2, TBM]); cg = wt("h_cg", [128, 2, TBM])
            d1 = wt("h_d1", [128, 2, TBM]); e1 = wt("h_e1", [128, 2, TBM]); eg = wt("h_eg", [128, 2, TBM])
            Qi = wt("h_Qi", [128, 2, TBM], BF16); Ki = wt("h_Ki", [128, 2, TBM], BF16)
            Qx = wt("h_Qx", [128, 2, TBM], BF16); Kt = wt("h_Kt", [128, 2, TBM], BF16)
            mid = L // 2
            for hh in range(2):
                S.dve(lambda e, hh=hh: e.tensor_scalar(out=kk[:, hh, 0:TBc], in0=sig[:, hh, 0:TBc], scalar1=-1.0, scalar2=st.lbm1[:, hh:hh + 1], op0=ALU.add, op1=ALU.mult),
                      reads=[("h_sig", hh), "lbm1"], writes=[("h_k", hh)])
                S.dve(lambda e, hh=hh: e.tensor_scalar(out=ft[:, hh, 0:TBc], in0=sig[:, hh, 0:TBc], scalar1=st.oml[:, hh:hh + 1], scalar2=st.lb[:, hh:hh + 1], op0=ALU.mult, op1=ALU.add),
                      reads=[("h_sig", hh), "oml", "lb"], writes=[("h_ft", hh)])
                S.dve(lambda e, hh=hh: e.tensor_scalar_max(out=ft[:, hh, 0:TBc], in0=ft[:, hh, 0:TBc], scalar1=1e-12), reads=[("h_ft", hh)], writes=[("h_ft", hh)])
                S.act(lambda e, hh=hh: e.activation(out=ft[:, hh, 0:TBc], in_=ft[:, hh, 0:TBc], func=AF.Ln), reads=[("h_ft", hh)], writes=[("h_ft", hh)])
                for c in range(nch):
                    S.dve(lambda e, hh=hh, c=c: e.tensor_tensor_scan(out=cg[:, hh, c * L:(c + 1) * L], data0=onesf[:, 0:L], data1=ft[:, hh, c * L:(c + 1) * L], initial=0.0, op0=ALU.mult, op1=ALU.add),
                          reads=[("h_ft", hh), "onesf"], writes=[("h_cg", hh)])
                cg3 = c3(cg[:, hh, 0:TBc])
                S.dve(lambda e, hh=hh, cg3=cg3: e.tensor_tensor(out=c3(d1[:, hh, 0:TBc]), in0=cg3, in1=cg3[:, :, mid:mid + 1].to_broadcast([128, nch, L]), op=ALU.subtract),
                      reads=[("h_cg", hh)], writes=[("h_d1", hh)])
                S.act(lambda e, hh=hh: e.activation(out=e1[:, hh, 0:TBc], in_=d1[:, hh, 0:TBc], func=AF.Exp), reads=[("h_d1", hh)], writes=[("h_e1", hh)])
                S.dve(lambda e, hh=hh: e.scalar_tensor_tensor(out=Qi[:, hh, 0:TBc], in0=qa[:, hh, 0:TBc], scalar=128.0 ** -0.5, in1=e1[:, hh, 0:TBc], op0=ALU.mult, op1=ALU.mult),
                      reads=[("h_qa", hh), ("h_e1", hh)], writes=[("h_Qi", hh)])
                S.act(lambda e, hh=hh: e.activation(out=e1[:, hh, 0:TBc], in_=d1[:, hh, 0:TBc], func=AF.Exp, scale=-1.0), reads=[("h_d1", hh), ("h_e1", hh)], writes=[("h_e1", hh)])
                S.dve(lambda e, hh=hh: e.tensor_tensor(out=Ki[:, hh, 0:TBc], in0=kk[:, hh, 0:TBc], in1=e1[:, hh, 0:TBc], op=ALU.mult),
                      reads=[("h_k", hh), ("h_e1", hh)], writes=[("h_Ki", hh)])
                S.act(lambda e, hh=hh: e.activation(out=eg[:, hh, 0:TBc], in_=cg[:, hh, 0:TBc], func=AF.Exp), reads=[("h_cg", hh)], writes=[("h_eg", hh)])
                S.dve(lambda e, hh=hh: e.scalar_tensor_tensor(out=Qx[:, hh, 0:TBc], in0=qa[:, hh, 0:TBc], scalar=128.0 ** -0.5, in1=eg[:, hh, 0:TBc], op0=ALU.mult, op1=ALU.mult),
                      reads=[("h_qa", hh), ("h_eg", hh)], writes=[("h_Qx", hh)])
                S.dve(lambda e, hh=hh, cg3=cg3: e.tensor_tensor(out=c3(d1[:, hh, 0:TBc]), in0=cg3[:, :, L - 1:L].to_broadcast([128, nch, L]), in1=cg3, op=ALU.subtract),
                      reads=[("h_cg", hh), ("h_d1", hh)], writes=[("h_d1", hh)])
                S.act(lambda e, hh=hh: e.activation(out=e1[:, hh, 0:TBc], in_=d1[:, hh, 0:TBc], func=AF.Exp), reads=[("h_d1", hh), ("h_e1", hh)], writes=[("h_e1", hh)])
                S.dve(lambda e, hh=hh: e.tensor_tensor(out=Kt[:, hh, 0:TBc], in0=kk[:, hh, 0:TBc], in1=e1[:, hh, 0:TBc], op=ALU.mult),
                      reads=[("h_k", hh), ("h_e1", hh)], writes=[("h_Kt", hh)])
            yield
            oall = wt("h_oall", [128, 2, TBM])
            hT = wt("h_T", [64, 4, 128], BF16); hatt = wt("h_att", [64, 2, 64], BF16)
            for c in range(nch):
                sl = slice(c * L, (c + 1) * L)
                h_trp = Bt[0:L, 0:512].rearrange("p (a b) -> p a b", b=128)
                for hh in range(2):
                    S.pe(lambda e, hh=hh, sl=sl: e.transpose(h_trp[:, hh, :], vb[:, hh, sl], cst["ident"][:]), reads=[("h_vb", hh), "c_ident"], writes=["Bt"])
                    S.pe(lambda e, hh=hh, sl=sl: e.transpose(h_trp[:, 2 + hh, :], Kt[:, hh, sl], cst["ident"][:]), reads=[("h_Kt", hh), "c_ident"], writes=["Bt"])
                ev(hT[0:L], h_trp, reads=["Bt"], writes=["h_T"])
                h_scp = B2[0:L, 0:128].rearrange("p (a b) -> p a b", b=64)
                for hh in range(2):
                    S.pe(lambda e, hh=hh, sl=sl: e.matmul(h_scp[:, hh, 0:L], Ki[:, hh, sl], Qi[:, hh, sl], start=True, stop=True), reads=[("h_Ki", hh), ("h_Qi", hh)], writes=["B2"])
                S.dve(lambda e: e.tensor_tensor(out=hatt[0:L, :, 0:L], in0=h_scp[:, :, 0:L], in1=cst["maskI"][0:L, 0:L].unsqueeze(1).to_broadcast([L, 2, L]), op=ALU.mult),
                      reads=["B2", "c_maskI"], writes=["h_att"])
                h_op = B5[:, 0:128].rearrange("p (a b) -> p a b", b=64)
                for hh in range(2):
                    S.pe(lambda e, hh=hh: e.matmul(h_op[:, hh, 0:L], hT[0:L, hh, :], hatt[0:L, hh, 0:L], start=True, stop=False), reads=["h_T", "h_att"], writes=["B5"])
                    S.pe(lambda e, hh=hh, sl=sl: e.matmul(h_op[:, hh, 0:L], st.hSb[:, hh, :], Qx[:, hh, sl], start=False, stop=True), reads=["hSb", ("h_Qx", hh)], writes=["B5"])
                ev(oall[:, :, sl], h_op[:, :, 0:L], reads=["B5"], writes=["h_oall"])
                h_up = B1[:, 0:256].rearrange("p (a b) -> p a b", b=128)
                for hh in range(2):
                    S.pe(lambda e, hh=hh: e.matmul(h_up[:, hh, :], hT[0:L, 2 + hh, :], hT[0:L, hh, :], start=True, stop=True), reads=["h_T"], writes=["B1"])
                for hh in range(2):
                    S.dve(lambda e, hh=hh, c=c: e.scalar_tensor_tensor(out=st.hS[:, hh, :], in0=st.hS[:, hh, :], scalar=eg[:, hh, c * L + L - 1:c * L + L], in1=h_up[:, hh, :], op0=ALU.mult, op1=ALU.add),
                          reads=["hS", ("h_eg", hh), "B1"], writes=["hS"])
                S.act(lambda e: e.copy(out=st.hSb[:], in_=st.hS[:]), reads=["hS"], writes=["hSb"])
            sq = wt("h_sq", [128, 2, TBM], BF16); rs = wt("h_rs", [128, 2, TBM]); yo = wt("h_yo", [128, 2, TBM], BF16)
            for hh in range(2):
                S.act(lambda e, hh=hh: e.activation(out=sq[:, hh, 0:TBc], in_=oall[:, hh, 0:TBc], func=AF.Square), reads=["h_oall"], writes=[("h_sq", hh)])
                ssp = B3[:, hh * 256:hh * 256 + TBc]
                S.pe(lambda e, hh=hh, ssp=ssp: e.matmul(ssp, cst["ones"][:], sq[:, hh, 0:TBc], start=True, stop=True), reads=[("h_sq", hh), "c_ones"], writes=["B3"])
                S.act(lambda e, hh=hh, ssp=ssp: e.activation(out=rs[:, hh, 0:TBc], in_=ssp, func=AF.Ln, bias=P("eps"), scale=1.0 / 128), reads=["B3", "par"], writes=[("h_rs", hh)])
                S.act(lambda e, hh=hh: e.activation(out=rs[:, hh, 0:TBc], in_=rs[:, hh, 0:TBc], func=AF.Exp, scale=-0.5), reads=[("h_rs", hh)], writes=[("h_rs", hh)])
                S.dve(lambda e, hh=hh: e.tensor_tensor(out=rs[:, hh, 0:TBc], in0=rs[:, hh, 0:TBc], in1=oall[:, hh, 0:TBc], op=ALU.mult), reads=[("h_rs", hh), "h_oall"], writes=[("h_rs", hh)])
                S.dve(lambda e, hh=hh: e.scalar_tensor_tensor(out=yo[:, hh, 0:TBc], in0=rs[:, hh, 0:TBc], scalar=P("hg%d" % hh), in1=sz[:, hh, 0:TBc], op0=ALU.mult, op1=ALU.mult),
                      reads=[("h_rs", hh), "par", ("h_sz", hh)], writes=[("h_yo", hh)])
                yd_, ydk_ = y_dst(hh * 128, t0, TBc)
                outs.append(S.dma("pool", lambda e, hh=hh, yd_=yd_: e.dma_start(out=yd_, in_=yo[:, hh, 0:TBc]), reads=[("h_yo", hh)], writes=[ydk_]))

        def gen_m():
            for nm, buf in (("mq", st.mxq), ("mk", st.mxk)):
                p_, k_ = inproj(xt, xkey, CT[nm] * 128, 128, TBc)
                ev(buf[:, 3:3 + TBc], p_, reads=[k_], writes=[nm + "x"])
            mvb = wt("m_vb", [128, 2, TBM], BF16); mso = wt("m_so", [128, 2, TBM]); msz = wt("m_sz", [128, 2, TBM])
            for i in range(2):
                p_, k_ = inproj(xt, xkey, CT["mv%d" % i] * 128, 128, TBc)
                S.dve(lambda e, p_=p_, i=i: e.tensor_copy(out=mvb[:, i, 0:TBc], in_=p_), reads=[k_], writes=[("m_vb", i)])
            for i in range(2):
                p_, k_ = inproj(xt, xkey, CT["mz%d" % i] * 128, 128, TBc)
                S.act(lambda e, p_=p_, i=i: e.activation(out=msz[:, i, 0:TBc], in_=p_, func=AF.Silu), reads=[k_], writes=[("m_sz", i)])
            for i in range(2):
                p_, k_ = inproj(xt, xkey, CT["mo%d" % i] * 128, 128, TBc)
                S.act(lambda e, p_=p_, i=i: e.activation(out=mso[:, i, 0:TBc], in_=p_, func=AF.Sigmoid), reads=[k_], writes=[("m_so", i)])
            rows = wt("m_rows", [1, 8, TBM])
            p_, k_ = inproj(xt, xkey, 25 * 128, 1, TBc)
            S.act(lambda e, p_=p_: e.activation(out=rows[:, 0, 0:TBc], in_=p_, func=AF.Identity, bias=par[0:1, PA["igb"][0]:PA["igb"][0] + 1]), reads=[k_, "par"], writes=[("m_rows", 0)])
            p_, k_ = inproj(xt, xkey, 25 * 128 + 1, 1, TBc)
            S.act(lambda e, p_=p_: e.activation(out=rows[:, 1, 0:TBc], in_=p_, func=AF.Sigmoid, bias=par[0:1, PA["fgb"][0]:PA["fgb"][0] + 1]), reads=[k_, "par"], writes=[("m_rows", 1)])
            yield
            S.act(lambda e: e.activation(out=rows[:, 1, 0:TBc], in_=rows[:, 1, 0:TBc], func=AF.Ln), reads=[("m_rows", 1)], writes=[("m_rows", 1)])
            S.dve(lambda e: e.tensor_tensor_scan(out=rows[:, 2, 0:TBc], data0=onesf[0:1, 0:TBc], data1=rows[:, 1, 0:TBc], initial=0.0, op0=ALU.mult, op1=ALU.add),
                  reads=[("m_rows", 1), "onesf"], writes=[("m_rows", 2)])
            S.dve(lambda e: e.tensor_tensor(out=rows[:, 3, 0:TBc], in0=rows[:, 0, 0:TBc], in1=rows[:, 2, 0:TBc], op=ALU.subtract), reads=[("m_rows", 0), ("m_rows", 2)], writes=[("m_rows", 3)])
            al0 = wt("m_al0", [1, 8])
            S.dve(lambda e: e.tensor_copy(out=al0[:, 0:1], in_=st.mmin[:]), reads=["mmin"], writes=["m_al0"])
            S.dve(lambda e: e.tensor_tensor_scan(out=rows[:, 4, 0:TBc], data0=onesf[0:1, 0:TBc], data1=rows[:, 3, 0:TBc], initial=st.mmin[:], op0=ALU.mult, op1=ALU.max),
                  reads=[("m_rows", 3), "onesf", "mmin"], writes=[("m_rows", 4)])
            Al3 = rows[:, 4, 0:TBc].rearrange("p (c j) -> p c j", j=L)
            al3 = rows[:, 3, 0:TBc].rearrange("p (c j) -> p c j", j=L)
            if nch > 1:
                S.dve(lambda e: e.tensor_copy(out=al0[:, 1:nch], in_=Al3[:, 0:nch - 1, L - 1]), reads=[("m_rows", 4), "m_al0"], writes=["m_al0"])
            cn_b = Al3[:, :, L - 1:L].to_broadcast([1, nch, L])
            S.dve(lambda e: e.tensor_tensor(out=rows[:, 5, 0:TBc].rearrange("p (c j) -> p c j", j=L), in0=al3, in1=cn_b, op=ALU.subtract), reads=[("m_rows", 3), ("m_rows", 4)], writes=[("m_rows", 5)])
            S.dve(lambda e: e.tensor_tensor(out=rows[:, 6, 0:TBc].rearrange("p (c j) -> p c j", j=L), in0=cn_b, in1=Al3, op=ALU.subtract), reads=[("m_rows", 4)], writes=[("m_rows", 6)])
            car = wt("m_car", [1, 8])
            S.dve(lambda e: e.tensor_tensor(out=car[:, 0:nch], in0=al0[:, 0:nch], in1=Al3[:, :, L - 1], op=ALU.subtract), reads=["m_al0", ("m_rows", 4)], writes=["m_car"])
            S.act(lambda e: e.activation(out=rows[:, 5:7, 0:TBc], in_=rows[:, 5:7, 0:TBc], func=AF.Exp), reads=[("m_rows", 5), ("m_rows", 6)], writes=[("m_rows", 5), ("m_rows", 6)])
            S.act(lambda e: e.activation(out=car[:, 0:nch], in_=car[:, 0:nch], func=AF.Exp), reads=["m_car"], writes=["m_car"])
            S.dve(lambda e: e.tensor_tensor(out=rows[:, 7, 0:TBc], in0=rows[:, 2, 0:TBc], in1=rows[:, 4, 0:TBc], op=ALU.add), reads=[("m_rows", 2), ("m_rows", 4)], writes=[("m_rows", 7)])
            S.dve(lambda e: e.tensor_copy(out=st.mmin[:], in_=rows[:, 7, TBc - 1:TBc]), reads=[("m_rows", 7), "mmin"], writes=["mmin"])
            S.act(lambda e: e.activation(out=rows[:, 7, 0:TBc], in_=rows[:, 7, 0:TBc], func=AF.Exp, scale=-1.0), reads=[("m_rows", 7)], writes=[("m_rows", 7)])
            bws = B3[:, 0:TBc]; bwt = B3[:, 256:256 + TBc]; bcar = B4[:, 0:nch]
            S.pe(lambda e: e.matmul(bws, st.onesrow[:], rows[:, 5, 0:TBc], start=True, stop=True), reads=["onesrow", ("m_rows", 5)], writes=["B3"])
            S.pe(lambda e: e.matmul(bwt, st.onesrow[:], rows[:, 6, 0:TBc], start=True, stop=True), reads=["onesrow", ("m_rows", 6)], writes=["B3"])
            S.pe(lambda e: e.matmul(bcar, st.onesrow[:], car[:, 0:nch], start=True, stop=True), reads=["onesrow", "m_car"], writes=["B4"])
            carb = wt("m_carb", [128, 8])
            S.act(lambda e: e.copy(out=carb[:, 0:nch], in_=bcar), reads=["B4"], writes=["m_carb"])
            qs = wt("m_qs", [128, TBM]); ks = wt("m_ks", [128, TBM]); kp = wt("m_kp", [128, TBM], BF16); qpp = wt("m_qpp", [128, TBM], BF16)
            for nm, buf, dst, cw in (("mq", st.mxq, qs, "cwq"), ("mk", st.mxk, ks, "cwk")):
                S.dve(lambda e, buf=buf, dst=dst, cw=cw: e.tensor_scalar_mul(out=dst[:, 0:TBc], in0=buf[:, 0:TBc], scalar1=P(cw, 0)), reads=[nm + "x", "par"], writes=["m_" + nm + "s"])
                for i in range(1, 4):
                    S.dve(lambda e, buf=buf, dst=dst, cw=cw, i=i: e.scalar_tensor_tensor(out=dst[:, 0:TBc], in0=buf[:, i:i + TBc], scalar=P(cw, i), in1=dst[:, 0:TBc], op0=ALU.mult, op1=ALU.add),
                          reads=[nm + "x", "par", "m_" + nm + "s"], writes=["m_" + nm + "s"])
                S.act(lambda e, dst=dst: e.activation(out=dst[:, 0:TBc], in_=dst[:, 0:TBc], func=AF.Silu), reads=["m_" + nm + "s"], writes=["m_" + nm + "s"])
                S.pool(lambda e, buf=buf: e.tensor_copy(out=buf[:, 0:3], in_=buf[:, TBc:TBc + 3]), reads=[nm + "x"], writes=[nm + "x"])
            S.dve(lambda e: e.tensor_tensor(out=kp[:, 0:TBc], in0=ks[:, 0:TBc], in1=bws, op=ALU.mult), reads=["m_mks", "B3"], writes=["m_kp"])
            S.dve(lambda e: e.scalar_tensor_tensor(out=qpp[:, 0:TBc], in0=qs[:, 0:TBc], scalar=128.0 ** -0.5, in1=bwt, op0=ALU.mult, op1=ALU.mult), reads=["m_mqs", "B3"], writes=["m_qpp"])
            yield
            numall = wt("m_num", [128, 2, TBM]); denr = wt("m_den", [1, TBM]); matt = wt("m_att", [64, 64], BF16)
            for c in range(nch):
                sl = slice(c * L, (c + 1) * L)
                m_trp = Bt[0:L, 0:384].rearrange("p (a b) -> p a b", b=128)
                S.pe(lambda e, sl=sl: e.transpose(m_trp[:, 0, :], kp[:, sl], cst["ident"][:]), reads=["m_kp", "c_ident"], writes=["Bt"])
                for i in range(2):
                    S.pe(lambda e, sl=sl, i=i: e.transpose(m_trp[:, 1 + i, :], mvb[:, i, sl], cst["ident"][:]), reads=[("m_vb", i), "c_ident"], writes=["Bt"])
                ev(st.mTT[0:L, 0:384], Bt[0:L, 0:384], reads=["Bt"], writes=["mTT"])
                S.dve(lambda e, c=c: e.tensor_scalar_mul(out=st.mC[:], in0=st.mC[:], scalar1=carb[:, c:c + 1]), reads=["mC", "m_carb"], writes=["mC"])
                S.act(lambda e: e.copy(out=st.mCb[:], in_=st.mC[:]), reads=["mC"], writes=["mCb"])
                m_scp = B2[0:L, 128:128 + L]
                S.pe(lambda e, sl=sl: e.matmul(m_scp, kp[:, sl], qpp[:, sl], start=True, stop=True), reads=["m_kp", "m_qpp"], writes=["B2"])
                S.dve(lambda e: e.tensor_tensor(out=matt[0:L, 0:L], in0=m_scp, in1=cst["maskI"][0:L, 0:L], op=ALU.mult), reads=["B2", "c_maskI"], writes=["m_att"])
                m_np = B5[:, 128:256].rearrange("p (a b) -> p a b", b=64)
                for i in range(2):
                    S.pe(lambda e, i=i: e.matmul(m_np[:, i, 0:L], st.mTT[0:L, 128 + i * 128:256 + i * 128], matt[0:L, 0:L], start=True, stop=False), reads=["mTT", "m_att"], writes=["B5"])
                    S.pe(lambda e, i=i, sl=sl: e.matmul(m_np[:, i, 0:L], st.mCb[:, i * 128:(i + 1) * 128], qpp[:, sl], start=False, stop=True), reads=["mCb", "m_qpp"], writes=["B5"])
                m_dp = B5[0:1, 256:256 + L]
                S.pe(lambda e: e.matmul(m_dp, st.mTT[0:L, 384:385], matt[0:L, 0:L], start=True, stop=False), reads=["mTT", "m_att"], writes=["B5"])
                S.pe(lambda e, sl=sl: e.matmul(m_dp, st.mCb[:, 256:257], qpp[:, sl], start=False, stop=True), reads=["mCb", "m_qpp"], writes=["B5"])
                ev(numall[:, :, sl], m_np[:, :, 0:L], reads=["B5"], writes=["m_num"])
                ev(denr[:, sl], m_dp, reads=["B5"], writes=["m_den"])
                cup = B1[:, 256:512]
                nup = B5[:, 448:449]
                S.pe(lambda e: e.matmul(cup, st.mTT[0:L, 0:128], st.mTT[0:L, 128:384], start=True, stop=True), reads=["mTT"], writes=["B1"])
                S.pe(lambda e: e.matmul(nup, st.mTT[0:L, 0:128], st.mTT[0:L, 384:385], start=True, stop=True), reads=["mTT"], writes=["B5"])
                S.dve(lambda e: e.tensor_tensor(out=st.mC[:, 0:256], in0=st.mC[:, 0:256], in1=cup, op=ALU.add), reads=["mC", "B1"], writes=["mC"])
                S.dve(lambda e: e.tensor_tensor(out=st.mC[:, 256:257], in0=st.mC[:, 256:257], in1=nup, op=ALU.add), reads=["mC", "B5"], writes=["mC"])
            S.act(lambda e: e.activation(out=denr[:, 0:TBc], in_=denr[:, 0:TBc], func=AF.Abs), reads=["m_den"], writes=["m_den"])
            S.dve(lambda e: e.tensor_tensor(out=denr[:, 0:TBc], in0=denr[:, 0:TBc], in1=rows[:, 7, 0:TBc], op=ALU.max), reads=["m_den", ("m_rows", 7)], writes=["m_den"])
            S.dve(lambda e: e.reciprocal(out=denr[:, 0:TBc], in_=denr[:, 0:TBc]), reads=["m_den"], writes=["m_den"])
            bdd = B4[:, 256:256 + TBc]
            S.pe(lambda e: e.matmul(bdd, st.onesrow[:], denr[:, 0:TBc], start=True, stop=True), reads=["onesrow", "m_den"], writes=["B4"])
            mh = wt("m_h", [128, 2, TBM]); mhb = wt("m_hb", [128, 2, TBM], BF16); msq = wt("m_sq", [128, 2, TBM], BF16)
            S.dve(lambda e: e.tensor_tensor(out=mh[:, :, 0:TBc], in0=numall[:, :, 0:TBc], in1=bdd.unsqueeze(1).to_broadcast([128, 2, TBc]), op=ALU.mult), reads=["m_num", "B4"], writes=["m_h"])
            S.act(lambda e: e.copy(out=mhb[:, :, 0:TBc], in_=mh[:, :, 0:TBc]), reads=["m_h"], writes=["m_hb"])
            S.act(lambda e: e.activation(out=msq[:, :, 0:TBc], in_=mh[:, :, 0:TBc], func=AF.Square), reads=["m_h"], writes=["m_sq"])
            sm = B3[:, 0:TBc]; sm2 = B3[:, 256:256 + TBc]
            for i in range(2):
                S.pe(lambda e, i=i: e.matmul(sm, cst["ones"][:], mhb[:, i, 0:TBc], start=(i == 0), stop=(i == 1)), reads=["m_hb", "c_ones"], writes=["B3"])
            for i in range(2):
                S.pe(lambda e, i=i: e.matmul(sm2, cst["ones"][:], msq[:, i, 0:TBc], start=(i == 0), stop=(i == 1)), reads=["m_sq", "c_ones"], writes=["B3"])
            mean = wt("m_mean", [128, TBM]); var = wt("m_var", [128, TBM]); myo = wt("m_yo", [128, 2, TBM], BF16)
            S.act(lambda e: e.mul(out=mean[:, 0:TBc], in_=sm, mul=1.0 / 256), reads=["B3"], writes=["m_mean"])
            S.dve(lambda e: e.tensor_tensor(out=var[:, 0:TBc], in0=mean[:, 0:TBc], in1=mean[:, 0:TBc], op=ALU.mult), reads=["m_mean"], writes=["m_var"])
            S.dve(lambda e: e.scalar_tensor_tensor(out=var[:, 0:TBc], in0=sm2, scalar=1.0 / 256, in1=var[:, 0:TBc], op0=ALU.mult, op1=ALU.subtract), reads=["B3", "m_var"], writes=["m_var"])
            S.act(lambda e: e.activation(out=var[:, 0:TBc], in_=var[:, 0:TBc], func=AF.Ln, bias=P("eps")), reads=["m_var", "par"], writes=["m_var"])
            S.act(lambda e: e.activation(out=var[:, 0:TBc], in_=var[:, 0:TBc], func=AF.Exp, scale=-0.5), reads=["m_var"], writes=["m_var"])
            S.dve(lambda e: e.tensor_tensor(out=mh[:, :, 0:TBc], in0=mh[:, :, 0:TBc], in1=mean[:, 0:TBc].unsqueeze(1).to_broadcast([128, 2, TBc]), op=ALU.subtract), reads=["m_h", "m_mean"], writes=["m_h"])
            S.dve(lambda e: e.tensor_tensor(out=mh[:, :, 0:TBc], in0=mh[:, :, 0:TBc], in1=var[:, 0:TBc].unsqueeze(1).to_broadcast([128, 2, TBc]), op=ALU.mult), reads=["m_h", "m_var"], writes=["m_h"])
            for i in range(2):
                S.dve(lambda e, i=i: e.scalar_tensor_tensor(out=mh[:, i, 0:TBc], in0=mh[:, i, 0:TBc], scalar=P("mg%d" % i), in1=mso[:, i, 0:TBc], op0=ALU.mult, op1=ALU.mult), reads=["m_h", "par", ("m_so", i)], writes=["m_h"])
            S.dve(lambda e: e.tensor_tensor(out=myo[:, :, 0:TBc], in0=mh[:, :, 0:TBc], in1=msz[:, :, 0:TBc], op=ALU.mult), reads=["m_h", ("m_sz", 0), ("m_sz", 1)], writes=["m_yo"])
            for i in range(2):
                yd_, ydk_ = y_dst(256 + i * 128, t0, TBc)
                outs.append(S.dma("pool", lambda e, i=i, yd_=yd_: e.dma_start(out=yd_, in_=myo[:, i, 0:TBc]), reads=["m_yo"], writes=[ydk_]))

        def gen_r():
            for nm in ("rr0", "rr1", "rk0", "rk1", "rv0", "rv1", "rwa"):
                p_, k_ = inproj(xt, xkey, CT[nm] * 128, 128, TBc)
                ev(st.rraw[nm][:, 1:1 + TBc], p_, reads=[k_], writes=["raw_" + nm])
            rsz = wt("r_sz", [128, 2, TBM])
            for p in range(2):
                p_, k_ = inproj(xt, xkey, CT["rz%d" % p] * 128, 128, TBc)
                S.act(lambda e, p_=p_, p=p: e.activation(out=rsz[:, p, 0:TBc], in_=p_, func=AF.Silu), reads=[k_], writes=[("r_sz", p)])
            yield
            lm = {}
            for nm, mu in (("rr0", "mur0"), ("rr1", "mur1"), ("rk0", "muk0"), ("rk1", "muk1"), ("rv0", "muv0"), ("rv1", "muv1"), ("rwa", "muwa")):
                raw = st.rraw[nm]
                m = wt("r_m_" + nm, [128, TBM])
                lm[nm] = m
                S.dve(lambda e, raw=raw, m=m: e.tensor_tensor(out=m[:, 0:TBc], in0=raw[:, 0:TBc], in1=raw[:, 1:1 + TBc], op=ALU.subtract), reads=["raw_" + nm], writes=["r_m_" + nm])
                S.dve(lambda e, raw=raw, m=m, mu=mu: e.scalar_tensor_tensor(out=m[:, 0:TBc], in0=m[:, 0:TBc], scalar=P(mu), in1=raw[:, 1:1 + TBc], op0=ALU.mult, op1=ALU.add),
                      reads=["raw_" + nm, "r_m_" + nm, "par"], writes=["r_m_" + nm])
                S.pool(lambda e, raw=raw: e.tensor_copy(out=raw[:, 0:1], in_=raw[:, TBc:TBc + 1]), reads=["raw_" + nm], writes=["raw_" + nm])
            wab = wt("r_wab", [128, TBM], BF16)
            S.act(lambda e: e.activation(out=wab[0:64, 0:TBc], in_=lm["rwa"][0:64, 0:TBc], func=AF.Tanh), reads=["r_m_rwa"], writes=["r_wab0"])
            S.act(lambda e: e.copy(out=wab[64:128, 0:TBc], in_=lm["rwa"][64:128, 0:TBc]), reads=["r_m_rwa"], writes=["r_wab1"])
            if RSTOP <= -2:
                return
            sgw = wt("r_sgw", [128, 2, TBM]); av = wt("r_a", [128, 2, TBM]); kkn = wt("r_kkn", [128, 2, TBM]); kf = wt("r_kf", [128, 2, TBM])
            bv = wt("r_bv", [128, 2, TBM]); cs = wt("r_cs", [128, 2, TBM]); tmp = wt("r_tmp", [128, 2, TBM]); tmpb = wt("r_tmpb", [128, 2, TBM], BF16)
            ecw = wt("r_ecw", [128, 2, TBM]); ex = wt("r_ex", [128, 2, TBM])
            AR = wt("r_AR", [128, 2, max(NCH, 1), 2, 64], BF16)
            Bd = wt("r_Bd", [128, 2, TBM], BF16); Kd = wt("r_Kd", [128, 2, TBM], BF16)
            Btl = wt("r_Btl", [128, 2, TBM], BF16); Ktl = wt("r_Ktl", [128, 2, TBM], BF16)
            rvb = wt("r_vb", [128, 2, TBM], BF16); bon = wt("r_bon", [128, 2, TBM])
            for p in range(2):
                rm = lm["rr%d" % p]; km = lm["rk%d" % p]; vm = lm["rv%d" % p]
                RK = ["r_m_rr%d" % p, "r_m_rk%d" % p, "r_m_rv%d" % p]
                wp = B3[:, 0:TBc]; ap_ = B3[:, 256:256 + TBc]
                S.pe(lambda e, p=p, wp=wp: e.matmul(wp, wud[0:64, p * 128:(p + 1) * 128], wab[0:64, 0:TBc], start=True, stop=True), reads=["wud", "r_wab0"], writes=["B3"])
                S.pe(lambda e, p=p, ap_=ap_: e.matmul(ap_, wud[64:128, p * 128:(p + 1) * 128], wab[64:128, 0:TBc], start=True, stop=True), reads=["wud", "r_wab1"], writes=["B3"], rowgrp=1)
                S.act(lambda e, p=p, wp=wp: e.activation(out=sgw[:, p, 0:TBc], in_=wp, func=AF.Sigmoid, bias=P("w0%d" % p)), reads=["B3", "par"], writes=[("r_sgw", p)])
                S.act(lambda e, p=p, ap_=ap_: e.activation(out=av[:, p, 0:TBc], in_=ap_, func=AF.Sigmoid, bias=P("a0%d" % p)), reads=["B3", "par"], writes=[("r_a", p)])
                S.dve(lambda e, p=p, km=km: e.tensor_scalar_mul(out=kkn[:, p, 0:TBc], in0=km[:, 0:TBc], scalar1=P("kk%d" % p)), reads=[RK[1], "par"], writes=[("r_kkn", p)])
                S.act(lambda e, p=p: e.activation(out=tmpb[:, p, 0:TBc], in_=kkn[:, p, 0:TBc], func=AF.Square), reads=[("r_kkn", p)], writes=[("r_tmpb", p)])
                ssk = B4[:, 0:TBc]
                S.pe(lambda e, p=p, ssk=ssk: e.matmul(ssk, cst["bd"][:], tmpb[:, p, 0:TBc], start=True, stop=True), reads=[("r_tmpb", p), "c_bd"], writes=["B4"])
                S.dve(lambda e, p=p, ssk=ssk: e.tensor_scalar_max(out=tmp[:, p, 0:TBc], in0=ssk, scalar1=1e-24), reads=["B4"], writes=[("r_tmp", p)])
                S.act(lambda e, p=p: e.activation(out=tmp[:, p, 0:TBc], in_=tmp[:, p, 0:TBc], func=AF.Ln), reads=[("r_tmp", p)], writes=[("r_tmp", p)])
                S.act(lambda e, p=p: e.activation(out=tmp[:, p, 0:TBc], in_=tmp[:, p, 0:TBc], func=AF.Exp, scale=-0.5), reads=[("r_tmp", p)], writes=[("r_tmp", p)])
                S.dve(lambda e, p=p: e.tensor_tensor(out=kkn[:, p, 0:TBc], in0=kkn[:, p, 0:TBc], in1=tmp[:, p, 0:TBc], op=ALU.mult), reads=[("r_kkn", p), ("r_tmp", p)], writes=[("r_kkn", p)])
                S.dve(lambda e, p=p: e.tensor_scalar(out=kf[:, p, 0:TBc], in0=av[:, p, 0:TBc], scalar1=-1.0, scalar2=P("ka%d" % p), op0=ALU.add, op1=ALU.mult), reads=[("r_a", p), "par"], writes=[("r_kf", p)])
                S.dve(lambda e, p=p, km=km: e.scalar_tensor_tensor(out=kf[:, p, 0:TBc], in0=kf[:, p, 0:TBc], scalar=1.0, in1=km[:, 0:TBc], op0=ALU.add, op1=ALU.mult), reads=[("r_kf", p), RK[1]], writes=[("r_kf", p)])
                S.dve(lambda e, p=p: e.tensor_tensor(out=bv[:, p, 0:TBc], in0=kkn[:, p, 0:TBc], in1=av[:, p, 0:TBc], op=ALU.mult), reads=[("r_kkn", p), ("r_a", p)], writes=[("r_bv", p)])
                for c in range(nch):
                    S.dve(lambda e, p=p, c=c: e.tensor_tensor_scan(out=cs[:, p, c * L:(c + 1) * L], data0=onesf[:, 0:L], data1=sgw[:, p, c * L:(c + 1) * L], initial=0.0, op0=ALU.mult, op1=ALU.add),
                          reads=[("r_sgw", p), "onesf"], writes=[("r_cs", p)])
                cs3 = c3(cs[:, p, 0:TBc])
                ARp = AR[:, p, 0:nch, :, 0:L]
                S.act(lambda e, p=p: e.activation(out=ecw[:, p, 0:TBc], in_=cs[:, p, 0:TBc], func=AF.Exp, scale=-C0), reads=[("r_cs", p)], writes=[("r_ecw", p)])
                S.dve(lambda e, p=p, rm=rm, ARp=ARp: e.tensor_tensor(out=ARp[:, :, 1, :], in0=c3(rm[:, 0:TBc]), in1=c3(ecw[:, p, 0:TBc]), op=ALU.mult), reads=[RK[0], ("r_ecw", p)], writes=[("r_AR", p)])
                S.act(lambda e, p=p: e.activation(out=ex[:, p, 0:TBc], in_=cs[:, p, 0:TBc], func=AF.Exp, scale=C0), reads=[("r_cs", p)], writes=[("r_ex", p)])
                S.dve(lambda e, p=p: e.tensor_tensor(out=Kd[:, p, 0:TBc], in0=kf[:, p, 0:TBc], in1=ex[:, p, 0:TBc], op=ALU.mult), reads=[("r_kf", p), ("r_ex", p)], writes=[("r_Kd", p)])
                S.dve(lambda e, p=p: e.tensor_tensor(out=Bd[:, p, 0:TBc], in0=bv[:, p, 0:TBc], in1=ex[:, p, 0:TBc], op=ALU.mult), reads=[("r_bv", p), ("r_ex", p)], writes=[("r_Bd", p)])
                S.dve(lambda e, p=p: e.tensor_tensor(out=tmp[:, p, 0:TBc], in0=cs[:, p, 0:TBc], in1=sgw[:, p, 0:TBc], op=ALU.subtract), reads=[("r_cs", p), ("r_sgw", p), ("r_tmp", p)], writes=[("r_tmp", p)])
                S.act(lambda e, p=p: e.activation(out=ex[:, p, 0:TBc], in_=tmp[:, p, 0:TBc], func=AF.Exp, scale=-C0), reads=[("r_tmp", p), ("r_ex", p)], writes=[("r_ex", p)])
                S.dve(lambda e, p=p, ARp=ARp: e.scalar_tensor_tensor(out=ARp[:, :, 0, :], in0=c3(kkn[:, p, 0:TBc]), scalar=-1.0, in1=c3(ex[:, p, 0:TBc]), op0=ALU.mult, op1=ALU.mult), reads=[("r_kkn", p), ("r_ex", p)], writes=[("r_AR", p)])
                S.dve(lambda e, p=p, cs3=cs3: e.tensor_tensor(out=c3(tmp[:, p, 0:TBc]), in0=cs3[:, :, L - 1:L].to_broadcast([128, nch, L]), in1=cs3, op=ALU.subtract), reads=[("r_cs", p), ("r_tmp", p)], writes=[("r_tmp", p)])
                S.act(lambda e, p=p: e.activation(out=ex[:, p, 0:TBc], in_=tmp[:, p, 0:TBc], func=AF.Exp, scale=-C0), reads=[("r_tmp", p), ("r_ex", p)], writes=[("r_ex", p)])
                S.dve(lambda e, p=p: e.tensor_tensor(out=Btl[:, p, 0:TBc], in0=bv[:, p, 0:TBc], in1=ex[:, p, 0:TBc], op=ALU.mult), reads=[("r_bv", p), ("r_ex", p)], writes=[("r_Btl", p)])
                S.dve(lambda e, p=p: e.tensor_tensor(out=Ktl[:, p, 0:TBc], in0=kf[:, p, 0:TBc], in1=ex[:, p, 0:TBc], op=ALU.mult), reads=[("r_kf", p), ("r_ex", p)], writes=[("r_Ktl", p)])
                S.act(lambda e, p=p, vm=vm: e.copy(out=rvb[:, p, 0:TBc], in_=vm[:, 0:TBc]), reads=[RK[2]], writes=[("r_vb", p)])
                S.dve(lambda e, p=p, rm=rm: e.scalar_tensor_tensor(out=tmpb[:, p, 0:TBc], in0=rm[:, 0:TBc], scalar=P("rk%d" % p), in1=kf[:, p, 0:TBc], op0=ALU.mult, op1=ALU.mult), reads=[RK[0], "par", ("r_kf", p), ("r_tmpb", p)], writes=[("r_tmpb", p)])
                bsp = B4[:, 256:256 + TBc]
                S.pe(lambda e, p=p, bsp=bsp: e.matmul(bsp, cst["bd"][:], tmpb[:, p, 0:TBc], start=True, stop=True), reads=[("r_tmpb", p), "c_bd"], writes=["B4"])
                S.dve(lambda e, p=p, vm=vm, bsp=bsp: e.tensor_tensor(out=bon[:, p, 0:TBc], in0=vm[:, 0:TBc], in1=bsp, op=ALU.mult), reads=[RK[2], "B4"], writes=[("r_bon", p)])
            yall = wt("r_yall", [128, 2, TBM])
            T3 = wt("r_T3", [64, 2, 3, 128], BF16)
            scBm = wt("r_scBm", [64, 4, 2, 64], BF16); scKm = wt("r_scKm", [64, 4, 2, 64], BF16); labm = wt("r_labm", [64, 4, 64], BF16)
            TTf = wt("r_TTf", [64, 4, 64]); TTb = wt("r_TTb", [64, 4, 64], BF16)
            X = [wt("r_X%d" % i, [64, 4, 2, 64], BF16) for i in range(2)]
            Q = [wt("r_Q%d" % i, [64, 4, 64], BF16) for i in range(2)]
            rhsb = wt("r_rhsb", [64, 4, 64], BF16); ub = wt("r_ub", [64, 4, 64], BF16)
            upad = wt("r_upad", [64, 2, 2, 128], BF16); vpad = wt("r_vpad", [64, 2, 2, 128], BF16)
            tmpS = wt("r_tmpS", [128, 2, 128])
            nlev = int(round(math.log2(L)))
            for c in range(nch if RSTOP >= 1 else 0):
                sl = slice(c * L, (c + 1) * L)
                trp = Bt[0:L, 0:768].rearrange("p (a b c) -> p a b c", a=2, b=3)
                for p in range(2):
                    S.pe(lambda e, p=p, sl=sl: e.transpose(trp[:, p, 0, :], rvb[:, p, sl], cst["ident"][:]), reads=[("r_vb", p), "c_ident"], writes=["Bt"])
                    S.pe(lambda e, p=p, sl=sl: e.transpose(trp[:, p, 1, :], Btl[:, p, sl], cst["ident"][:]), reads=[("r_Btl", p), "c_ident"], writes=["Bt"])
                    S.pe(lambda e, p=p, sl=sl: e.transpose(trp[:, p, 2, :], Ktl[:, p, sl], cst["ident"][:]), reads=[("r_Ktl", p), "c_ident"], writes=["Bt"])
                ev(T3[0:L], trp, reads=["Bt"], writes=["r_T3"])
                S.dve(lambda e: e.tensor_tensor(out=vpad[0:L], in0=T3[0:L, :, 0, :].unsqueeze(2).to_broadcast([L, 2, 2, 128]), in1=cst["padmask"][0:L].unsqueeze(1).to_broadcast([L, 2, 2, 128]), op=ALU.mult),
                      reads=["r_T3", "c_padmask"], writes=["r_vpad"])
                scB = B3[0:L, :].rearrange("p (h a j) -> p h a j", h=4, a=2)
                scK = B4[0:L, :].rearrange("p (h a j) -> p h a j", h=4, a=2)
                lab = B2[0:L, 256:512].rearrange("p (h j) -> p h j", h=4)
                for h in (0, 2, 1, 3):
                    p, hh = divmod(h, 2)
                    b0 = hh * 64
                    rg = 1 if hh == 1 else None
                    arh = AR[b0:b0 + 64, p, c, :, 0:L]
                    S.pe(lambda e, h=h, p=p, b0=b0, arh=arh, sl=sl: e.matmul(scB[:, h, :, 0:L], Bd[b0:b0 + 64, p, sl], arh, start=True, stop=True), reads=[("r_Bd", p), ("r_AR", p)], writes=["B3"], rowgrp=rg)
                    S.pe(lambda e, h=h, p=p, b0=b0, arh=arh, sl=sl: e.matmul(scK[:, h, :, 0:L], Kd[b0:b0 + 64, p, sl], arh, start=True, stop=True), reads=[("r_Kd", p), ("r_AR", p)], writes=["B4"], rowgrp=rg)
                    S.pe(lambda e, h=h, p=p, b0=b0, arh=arh, sl=sl: e.matmul(lab[:, h, 0:L], arh[:, 0, :], Bd[b0:b0 + 64, p, sl], start=True, stop=True), reads=[("r_Bd", p), ("r_AR", p)], writes=["B2"], rowgrp=rg)
                m2 = cst["mask2"][0:L, :, 0:L].unsqueeze(1).to_broadcast([L, 4, 2, L])
                S.dve(lambda e: e.tensor_tensor(out=scBm[0:L, :, :, 0:L], in0=scB[:, :, :, 0:L], in1=m2, op=ALU.mult), reads=["B3", "c_mask2"], writes=["r_scBm"])
                S.dve(lambda e: e.tensor_tensor(out=scKm[0:L, :, :, 0:L], in0=scK[:, :, :, 0:L], in1=m2, op=ALU.mult), reads=["B4", "c_mask2"], writes=["r_scKm"])
                S.dve(lambda e: e.tensor_tensor(out=labm[0:L, :, 0:L], in0=lab[:, :, 0:L], in1=cst["maskL"][0:L, 0:L].unsqueeze(1).to_broadcast([L, 4, L]), op=ALU.mult), reads=["B2", "c_maskL"], writes=["r_labm"])
                if RSTOP < 2:
                    continue
                S.dve(lambda e: e.tensor_tensor(out=TTf[0:L, :, 0:L], in0=scBm[0:L, :, 0, 0:L], in1=ident64f[0:L, 0:L].unsqueeze(1).to_broadcast([L, 4, L]), op=ALU.add), reads=["r_scBm", "ident64f"], writes=["r_TTf"])
                S.act(lambda e: e.copy(out=X[0][0:L, :, 0, 0:L], in_=TTf[0:L, :, 0:L]), reads=["r_TTf"], writes=["r_X0a"])
                PPp = B6[0:L, :].rearrange("p (h a j) -> p h a j", h=4, a=2)
                QQp = B1[0:L, 0:256].rearrange("p (h j) -> p h j", h=4)
                for h in range(4):
                    S.pe(lambda e, h=h: e.matmul(PPp[:, h, 1, 0:L], labm[0:L, h, 0:L], scBm[0:L, h, 0, 0:L], start=True, stop=True), reads=["r_labm", "r_scBm"], writes=["B6"])
                    S.pe(lambda e, h=h: e.matmul(QQp[:, h, 0:L], scBm[0:L, h, 0, 0:L], labm[0:L, h, 0:L], start=True, stop=True), reads=["r_labm", "r_scBm"], writes=["B1"])
                S.dve(lambda e: e.tensor_copy(out=X[0][0:L, :, 1, 0:L], in_=PPp[:, :, 1, 0:L]), reads=["B6"], writes=["r_X0b"])
                S.act(lambda e: e.copy(out=Q[0][0:L, :, 0:L], in_=QQp[:, :, 0:L]), reads=["B1"], writes=["r_Q0"])
                cur = 0
                for lev in range(1, nlev):
                    last = lev == nlev - 1
                    Xc, Qc = X[cur], Q[cur]
                    xk = ["r_X%da" % cur, "r_X%db" % cur]
                    qk = "r_Q%d" % cur
                    nxt = cur ^ 1
                    for h in range(4):
                        if last:
                            S.pe(lambda e, h=h, Xc=Xc, Qc=Qc: e.matmul(PPp[:, h, 0, 0:L], Qc[0:L, h, 0:L], Xc[0:L, h, 0, 0:L], start=True, stop=True), reads=[qk] + xk, writes=["B6"])
                        else:
                            S.pe(lambda e, h=h, Xc=Xc, Qc=Qc: e.matmul(PPp[:, h, :, 0:L], Qc[0:L, h, 0:L], Xc[0:L, h, :, 0:L], start=True, stop=True), reads=[qk] + xk, writes=["B6"])
                            S.pe(lambda e, h=h, Xc=Xc, Qc=Qc: e.matmul(QQp[:, h, 0:L], Xc[0:L, h, 1, 0:L], Qc[0:L, h, 0:L], start=True, stop=True), reads=[qk] + xk, writes=["B1"])
                    S.dve(lambda e: e.tensor_tensor(out=TTf[0:L, :, 0:L], in0=TTf[0:L, :, 0:L], in1=PPp[:, :, 0, 0:L], op=ALU.add), reads=["r_TTf", "B6"], writes=["r_TTf"])
                    if last:
                        S.act(lambda e: e.copy(out=TTb[0:L, :, 0:L], in_=TTf[0:L, :, 0:L]), reads=["r_TTf"], writes=["r_TTb"])
                    else:
                        S.act(lambda e, nxt=nxt: e.copy(out=X[nxt][0:L, :, 0, 0:L], in_=TTf[0:L, :, 0:L]), reads=["r_TTf"], writes=["r_X%da" % nxt])
                        S.dve(lambda e, nxt=nxt: e.tensor_copy(out=X[nxt][0:L, :, 1, 0:L], in_=PPp[:, :, 1, 0:L]), reads=["B6"], writes=["r_X%db" % nxt])
                        S.act(lambda e, nxt=nxt: e.copy(out=Q[nxt][0:L, :, 0:L], in_=QQp[:, :, 0:L]), reads=["B1"], writes=["r_Q%d" % nxt])
                    cur = nxt
                if RSTOP < 3:
                    continue
                rhp = B6[0:L, 0:256].rearrange("p (h j) -> p h j", h=4)
                for h in range(4):
                    p, hh = divmod(h, 2)
                    S.pe(lambda e, h=h, p=p, hh=hh, c=c: e.matmul(rhp[:, h, :], AR[:, p, c, 0, 0:L], st.rSb[:, p, hh * 64:(hh + 1) * 64], start=True, stop=False), reads=[("r_AR", p), "rSb"], writes=["B6"])
                    S.pe(lambda e, h=h, p=p, hh=hh: e.matmul(rhp[:, h, :], scKm[0:L, h, 0, 0:L], T3[0:L, p, 0, hh * 64:(hh + 1) * 64], start=False, stop=True), reads=["r_scKm", "r_T3"], writes=["B6"])
                S.act(lambda e: e.copy(out=rhsb[0:L], in_=rhp), reads=["B6"], writes=["r_rhsb"])
                up_ = B6[0:L, 256:512].rearrange("p (h j) -> p h j", h=4)
                for h in range(4):
                    S.pe(lambda e, h=h: e.matmul(up_[:, h, :], TTb[0:L, h, 0:L], rhsb[0:L, h, :], start=True, stop=True), reads=["r_TTb", "r_rhsb"], writes=["B6"])
                S.act(lambda e: e.copy(out=ub[0:L], in_=up_), reads=["B6"], writes=["r_ub"])
                S.dve(lambda e: e.tensor_tensor(out=upad[0:L], in0=B6[0:L, 256:512].rearrange("p (a k) -> p a k", a=2).unsqueeze(2).to_broadcast([L, 2, 2, 128]), in1=cst["padmask"][0:L].unsqueeze(1).to_broadcast([L, 2, 2, 128]), op=ALU.mult),
                      reads=["B6", "c_padmask"], writes=["r_upad"])
                yp = B5[:, 320:448].rearrange("p (a j) -> p a j", a=2)
                for p in range(2):
                    S.pe(lambda e, p=p, c=c: e.matmul(yp[:, p, 0:L], st.rSb[:, p, :], AR[:, p, c, 1, 0:L], start=True, stop=False), reads=["rSb", ("r_AR", p)], writes=["B5"])
                    for hh in range(2):
                        h = p * 2 + hh
                        S.pe(lambda e, p=p, hh=hh, h=h: e.matmul(yp[:, p, 0:L], upad[0:L, p, hh, :], scBm[0:L, h, 1, 0:L], start=False, stop=False), reads=["r_upad", "r_scBm"], writes=["B5"])
                        S.pe(lambda e, p=p, hh=hh, h=h: e.matmul(yp[:, p, 0:L], vpad[0:L, p, hh, :], scKm[0:L, h, 1, 0:L], start=False, stop=(hh == 1)), reads=["r_vpad", "r_scKm"], writes=["B5"])
                ev(yall[:, :, sl], yp[:, :, 0:L], reads=["B5"], writes=["r_yall"])
                sup = B2[:, 0:256].rearrange("p (a j) -> p a j", a=2)
                for p in range(2):
                    S.pe(lambda e, p=p: e.matmul(sup[:, p, :], T3[0:L, p, 1, :], ub[0:L, 2 * p:2 * p + 2, :], start=True, stop=False), reads=["r_T3", "r_ub"], writes=["B2"])
                    S.pe(lambda e, p=p: e.matmul(sup[:, p, :], T3[0:L, p, 2, :], T3[0:L, p, 0, :], start=False, stop=True), reads=["r_T3"], writes=["B2"])
                S.dve(lambda e: e.tensor_tensor(out=tmpS[:], in0=sup, in1=bdmaskf[:].unsqueeze(1).to_broadcast([128, 2, 128]), op=ALU.mult), reads=["B2", "bdmaskf"], writes=["r_tmpS"])
                for p in range(2):
                    S.dve(lambda e, p=p, c=c: e.scalar_tensor_tensor(out=st.rS[:, p, :], in0=st.rS[:, p, :], scalar=ecw[:, p, c * L + L - 1:c * L + L], in1=tmpS[:, p, :], op0=ALU.mult, op1=ALU.add),
                          reads=["rS", ("r_ecw", p), "r_tmpS"], writes=["rS"])
                S.act(lambda e: e.copy(out=st.rSb[:], in_=st.rS[:]), reads=["rS"], writes=["rSb"])
            if RSTOP <= -1:
                return
            ryb = wt("r_yb", [128, 2, TBM], BF16); rsq = wt("r_sq", [128, 2, TBM], BF16); ryo = wt("r_yo", [128, 2, TBM], BF16)
            rmean = wt("r_mean", [128, TBM]); rvar = wt("r_var", [128, TBM])
            for p in range(2):
                S.act(lambda e, p=p: e.copy(out=ryb[:, p, 0:TBc], in_=yall[:, p, 0:TBc]), reads=["r_yall"], writes=[("r_yb", p)])
                S.act(lambda e, p=p: e.activation(out=rsq[:, p, 0:TBc], in_=yall[:, p, 0:TBc], func=AF.Square), reads=["r_yall"], writes=[("r_sq", p)])
                sm = B3[:, 0:TBc]; sm2 = B3[:, 256:256 + TBc]
                S.pe(lambda e, p=p, sm=sm: e.matmul(sm, cst["bd"][:], ryb[:, p, 0:TBc], start=True, stop=True), reads=[("r_yb", p), "c_bd"], writes=["B3"])
                S.pe(lambda e, p=p, sm2=sm2: e.matmul(sm2, cst["bd"][:], rsq[:, p, 0:TBc], start=True, stop=True), reads=[("r_sq", p), "c_bd"], writes=["B3"])
                S.act(lambda e, sm=sm: e.mul(out=rmean[:, 0:TBc], in_=sm, mul=1.0 / 64), reads=["B3"], writes=["r_mean"])
                S.dve(lambda e: e.tensor_tensor(out=rvar[:, 0:TBc], in0=rmean[:, 0:TBc], in1=rmean[:, 0:TBc], op=ALU.mult), reads=["r_mean"], writes=["r_var"])
                S.dve(lambda e, sm2=sm2: e.scalar_tensor_tensor(out=rvar[:, 0:TBc], in0=sm2, scalar=1.0 / 64, in1=rvar[:, 0:TBc], op0=ALU.mult, op1=ALU.subtract), reads=["B3", "r_var"], writes=["r_var"])
                S.act(lambda e: e.activation(out=rvar[:, 0:TBc], in_=rvar[:, 0:TBc], func=AF.Ln, bias=P("lneps")), reads=["r_var", "par"], writes=["r_var"])
                S.act(lambda e: e.activation(out=rvar[:, 0:TBc], in_=rvar[:, 0:TBc], func=AF.Exp, scale=-0.5), reads=["r_var"], writes=["r_var"])
                S.dve(lambda e, p=p: e.tensor_tensor(out=yall[:, p, 0:TBc], in0=yall[:, p, 0:TBc], in1=rmean[:, 0:TBc], op=ALU.subtract), reads=["r_yall", "r_mean"], writes=["r_yall"])
                S.dve(lambda e, p=p: e.tensor_tensor(out=yall[:, p, 0:TBc], in0=yall[:, p, 0:TBc], in1=rvar[:, 0:TBc], op=ALU.mult), reads=["r_yall", "r_var"], writes=["r_yall"])
                S.dve(lambda e, p=p: e.tensor_scalar(out=yall[:, p, 0:TBc], in0=yall[:, p, 0:TBc], scalar1=P("lg%d" % p), scalar2=P("lb%d" % p), op0=ALU.mult, op1=ALU.add), reads=["r_yall", "par"], writes=["r_yall"])
                S.dve(lambda e, p=p: e.tensor_tensor(out=yall[:, p, 0:TBc], in0=yall[:, p, 0:TBc], in1=bon[:, p, 0:TBc], op=ALU.add), reads=["r_yall", ("r_bon", p)], writes=["r_yall"])
                S.dve(lambda e, p=p: e.tensor_tensor(out=ryo[:, p, 0:TBc], in0=yall[:, p, 0:TBc], in1=rsz[:, p, 0:TBc], op=ALU.mult), reads=["r_yall", ("r_sz", p)], writes=[("r_yo", p)])
                yd_, ydk_ = y_dst(512 + p * 128, t0, TBc)
                outs.append(S.dma("pool", lambda e, p=p, yd_=yd_: e.dma_start(out=yd_, in_=ryo[:, p, 0:TBc]), reads=[("r_yo", p)], writes=[ydk_]))

        gens = []
        if "h" in mixers:
            gens.append(("h", gen_h()))
        if "m" in mixers:
            gens.append(("m", gen_m()))
        if "r" in mixers:
            gens.append(("r", gen_r()))
        for _, g in gens:
            next(g, None)
        for nm_, g in gens:
            if nm_ in ("h", "m"):
                next(g, None)
        for _, g in gens:
            for _x in g:
                pass

    outs = []
    after_ms = dram.get("after_ms")
    macro(0, 0, NMETA, NMETA)
    if after_ms:
        after_ms(0, 0, NMETA, outs)
    for ms in range(NMS):
        macro(ms + 1, NMETA + ms * TB, TB, LCH)
        if after_ms:
            after_ms(ms + 1, NMETA + ms * TB, TB, outs)
    return outs


def build_B(nc, S, es, TOK, dram, proj=True, last=False, pref="b", skip_meta=False):
    def sb(name, shape, dt=F32):
        return es.enter_context(nc.sbuf_tensor("s_" + pref + name, shape, dt))

    def ps(name, shape, dt=F32):
        return es.enter_context(nc.psum_tensor("p_" + pref + name, shape, dt))

    NT = min(512, TOK - NMETA)
    tiles = [(0, NMETA)] + [(NMETA + i * NT, NT) for i in range((TOK - NMETA) // NT)]
    if skip_meta:
        tiles = tiles[1:]
    if "h_src" not in dram:
        hsrc = dram["hT"].rearrange("(kc p) t -> p kc t", p=128)
        dram = dict(dram)
        dram["h_src"] = lambda t0, N: (hsrc[:, :, t0:t0 + N], ("d_hin", t0))
        if proj:
            hdst = dram["ho"].rearrange("(kc p) t -> p kc t", p=128)
            xsrc = dram["xnT"].rearrange("(kc p) t -> p kc t", p=128)
            ysrc = dram["yT"].rearrange("(kc p) t -> p kc t", p=128)
            dram["h_dst"] = lambda t0, N: (hdst[:, :, t0:t0 + N], ("d_ho", t0))
            dram["xn_srcB"] = lambda t0, N: [(xsrc[:, :, t0:t0 + N], "d_xinB", 0, N)]

            def y_load(y, cand, t0, N):
                S.dma("sp", lambda e: e.dma_start(out=y[:, :, 0:N], in_=ysrc[:, :, t0:t0 + N]), writes=["y"])
            dram["y_load"] = y_load
        xdst = dram["xo"].rearrange("(kc p) t -> p kc t", p=128)
        dram["x_dst"] = lambda t0, N: [(xdst[:, :, t0:t0 + N], ("d_xo", t0), 0, N)]
        if "xof" in dram:
            xfd = dram["xof"].rearrange("(kc p) t -> p kc t", p=128)
            dram["out_dst"] = lambda ob, t0, N: (xfd[:, ob, t0:t0 + N], ("d_xof", ob, t0))
    Bk = [ps("B%d" % i, [128, 512]) for i in range(8)]
    onesf = sb("onesf", [128, 128])
    S.pool(lambda e: e.memset(onesf[:], 1.0), writes=["onesf"])
    gn = sb("gn", [128, 16])
    S.dma("sp", lambda e: e.dma_start(out=gn[:], in_=dram["gnext"]), writes=["gn"])
    epsb = sb("epsb", [128, 1])
    S.pool(lambda e: e.memset(epsb[:], 1e-6), writes=["epsb"])
    h = sb("h", [128, 16, NT]); sq = sb("sq", [128, NT]); rstd = sb("rstd", [128, NT])
    xo = sb("xo", [128, 16, NT], BF16)
    want_f32 = "out_dst" in dram
    if want_f32:
        otmp = [sb("otmp%d" % i, [128, NT]) for i in range(2)]
    outs = []
    if proj:
        xn = sb("xn", [128, 16, NT], BF16); y = sb("y", [128, 24, NT], BF16); mg = sb("mg", [128, 16, NT], BF16)
        cand = sb("cand", [128, 24, NT], BF16) if dram.get("need_cand") else None
        wgs = dram["wg"].rearrange("(kc p) n -> p kc n", p=128)
        wbs = dram["wbr"].rearrange("(b kc p) n -> p b kc n", p=128, b=3)
        wos = dram["wo"].rearrange("(kc p) n -> p kc n", p=128)
        GW = 128
        wgt = [sb("wg%d" % i, [128, 3, 16, GW], BF16) for i in range(2)]
        wbt = [sb("wb%d" % i, [128, 3, 8, GW], BF16) for i in range(2)]
        wot = [sb("wo%d" % i, [128, 16, GW], BF16) for i in range(2)]
        sg = sb("sg", [128, 3, NT]); tt = sb("tt", [128, 3, NT]); msum = sb("msum", [128, NT])
    wcount = [0]
    yidx = dram.get("yidx", lambda br, kc: br * 8 + kc)
    wcache = dram.get("wcache")
    for ti_, (t0, N) in enumerate(tiles):
        first_pass = ti_ == 0
        hin, hk = dram["h_src"](t0, N)
        S.dma("sp", lambda e, hin=hin, N=N: e.dma_start(out=h[:, :, 0:N], in_=hin), reads=[hk], writes=["h"])
        if proj:
            for (xin, xk, lo, hi) in dram["xn_srcB"](t0, N):
                S.dma("sp", lambda e, xin=xin, lo=lo, hi=hi: e.dma_start(out=xn[:, :, lo:hi], in_=xin), reads=[xk], writes=["xn"])
            dram["y_load"](y, cand, t0, N)
            for g in range(2048 // GW):
                wi = wcount[0] % 2
                wcount[0] += 1
                if wcache is not None and not first_pass:
                    S.dma("sp", lambda e, g=g, wi=wi: e.dma_start(out=wgt[wi][:].rearrange("p a b c -> p (a b c)"), in_=wcache[0][g]), reads=[("d_wgc", g)], writes=["wg%d" % wi])
                    S.dma("sp", lambda e, g=g, wi=wi: e.dma_start(out=wbt[wi][:].rearrange("p a b c -> p (a b c)"), in_=wcache[1][g]), reads=[("d_wbc", g)], writes=["wb%d" % wi])
                else:
                    for br in range(3):
                        S.dma("pool", lambda e, g=g, br=br, wi=wi: e.dma_start(out=wgt[wi][:, br], in_=wgs[:, :, br * 2048 + g * GW:br * 2048 + (g + 1) * GW]), writes=["wg%d" % wi])
                        S.dma("pool", lambda e, g=g, br=br, wi=wi: e.dma_start(out=wbt[wi][:, br], in_=wbs[:, br, :, g * GW:(g + 1) * GW]), writes=["wb%d" % wi])
                    if wcache is not None:
                        S.dma("sp", lambda e, g=g, wi=wi: e.dma_start(out=wcache[0][g], in_=wgt[wi][:].rearrange("p a b c -> p (a b c)")), reads=["wg%d" % wi], writes=[("d_wgc", g)])
                        S.dma("sp", lambda e, g=g, wi=wi: e.dma_start(out=wcache[1][g], in_=wbt[wi][:].rearrange("p a b c -> p (a b c)")), reads=["wb%d" % wi], writes=[("d_wbc", g)])
                for d in range(GW // 128):
                    db = g * (GW // 128) + d
                    for br in range(3):
                        gp = Bk[br][:, 0:N]
                        for kc in range(16):
                            S.pe(lambda e, br=br, kc=kc, wi=wi, d=d, gp=gp, N=N: e.matmul(gp, wgt[wi][:, br, kc, d * 128:(d + 1) * 128], xn[:, kc, 0:N], start=(kc == 0), stop=(kc == 15)),
                                 reads=["wg%d" % wi, "xn"], writes=["B%d" % br])
                        S.act(lambda e, br=br, gp=gp, N=N: e.activation(out=sg[:, br, 0:N], in_=gp, func=AF.Sigmoid), reads=["B%d" % br], writes=[("sg", br)])
                        pp = Bk[3 + br][:, 0:N]
                        for kc in range(8):
                            S.pe(lambda e, br=br, kc=kc, wi=wi, d=d, pp=pp, N=N: e.matmul(pp, wbt[wi][:, br, kc, d * 128:(d + 1) * 128], y[:, yidx(br, kc), 0:N], start=(kc == 0), stop=(kc == 7)),
                                 reads=["wb%d" % wi, "y"], writes=["B%d" % (3 + br)])
                        S.dve(lambda e, br=br, pp=pp, N=N: e.tensor_tensor(out=tt[:, br, 0:N], in0=sg[:, br, 0:N], in1=pp, op=ALU.mult), reads=[("sg", br), "B%d" % (3 + br)], writes=[("tt", br)])
                    S.pool(lambda e, N=N: e.tensor_tensor(out=msum[:, 0:N], in0=tt[:, 0, 0:N], in1=tt[:, 1, 0:N], op=ALU.add), reads=[("tt", 0), ("tt", 1)], writes=["msum"])
                    S.pool(lambda e, N=N, db=db: e.tensor_tensor(out=mg[:, db, 0:N], in0=msum[:, 0:N], in1=tt[:, 2, 0:N], op=ALU.add), reads=["msum", ("tt", 2)], writes=[("mg", db)])
            for g in range(2048 // GW):
                wi = g % 2
                if wcache is not None and not first_pass:
                    S.dma("sp", lambda e, g=g, wi=wi: e.dma_start(out=wot[wi][:].rearrange("p b c -> p (b c)"), in_=wcache[2][g]), reads=[("d_woc", g)], writes=["wo%d" % wi])
                else:
                    S.dma("pool", lambda e, g=g, wi=wi: e.dma_start(out=wot[wi][:], in_=wos[:, :, g * GW:(g + 1) * GW]), writes=["wo%d" % wi])
                    if wcache is not None:
                        S.dma("sp", lambda e, g=g, wi=wi: e.dma_start(out=wcache[2][g], in_=wot[wi][:].rearrange("p b c -> p (b c)")), reads=["wo%d" % wi], writes=[("d_woc", g)])
                for d in range(GW // 128):
                    ob = g * (GW // 128) + d
                    op_ = Bk[6][:, 0:N]
                    for kc in range(16):
                        S.pe(lambda e, kc=kc, wi=wi, d=d, op_=op_, N=N: e.matmul(op_, wot[wi][:, kc, d * 128:(d + 1) * 128], mg[:, kc, 0:N], start=(kc == 0), stop=(kc == 15)),
                             reads=["wo%d" % wi] + [("mg", k) for k in range(16)] if kc == 0 else ["wo%d" % wi], writes=["B6"])
                    S.dve(lambda e, ob=ob, op_=op_, N=N: e.tensor_tensor(out=h[:, ob, 0:N], in0=h[:, ob, 0:N], in1=op_, op=ALU.add), reads=["h", "B6"], writes=["h"])
            if "h_dst" in dram:
                hd, hdk = dram["h_dst"](t0, N)
                outs.append(S.dma("sp", lambda e, hd=hd, N=N: e.dma_start(out=hd, in_=h[:, :, 0:N]), reads=["h"], writes=[hdk]))
        ssp = Bk[7][:, 0:N]
        for ob in range(16):
            S.act(lambda e, ob=ob, N=N: e.activation(out=sq[:, 0:N], in_=h[:, ob, 0:N], func=AF.Square), reads=["h"], writes=["sq"])
            S.pe(lambda e, ob=ob, ssp=ssp, N=N: e.matmul(ssp, onesf[:], sq[:, 0:N], start=(ob == 0), stop=(ob == 15)), reads=["sq", "onesf"], writes=["B7"])
        S.act(lambda e, ssp=ssp, N=N: e.activation(out=rstd[:, 0:N], in_=ssp, func=AF.Sqrt, bias=epsb[:], scale=1.0 / 2048), reads=["B7", "epsb"], writes=["rstd"])
        S.dve(lambda e, N=N: e.reciprocal(out=rstd[:, 0:N], in_=rstd[:, 0:N]), reads=["rstd"], writes=["rstd"])
        for ob in range(16):
            if want_f32:
                ot = otmp[ob % 2]
                otk = "otmp%d" % (ob % 2)
                S.dve(lambda e, ob=ob, N=N, ot=ot: e.scalar_tensor_tensor(out=ot[:, 0:N], in0=h[:, ob, 0:N], scalar=gn[:, ob:ob + 1], in1=rstd[:, 0:N], op0=ALU.mult, op1=ALU.mult),
                      reads=["h", "gn", "rstd"], writes=[otk])
                if "x_dst" in dram:
                    S.act(lambda e, ob=ob, N=N, ot=ot: e.copy(out=xo[:, ob, 0:N], in_=ot[:, 0:N]), reads=[otk], writes=["xo"])
                od, odk = dram["out_dst"](ob, t0, N)
                outs.append(S.dma("sp", lambda e, od=od, ot=ot, N=N: e.dma_start(out=od, in_=ot[:, 0:N]), reads=[otk], writes=[odk]))
            else:
                S.dve(lambda e, ob=ob, N=N: e.scalar_tensor_tensor(out=xo[:, ob, 0:N], in0=h[:, ob, 0:N], scalar=gn[:, ob:ob + 1], in1=rstd[:, 0:N], op0=ALU.mult, op1=ALU.mult),
                      reads=["h", "gn", "rstd"], writes=["xo"])
        if "x_dst" in dram:
            for (xd, xdk, lo, hi) in dram["x_dst"](t0, N):
                outs.append(S.dma("sp", lambda e, xd=xd, lo=lo, hi=hi: e.dma_start(out=xd, in_=xo[:, :, lo:hi]), reads=["xo"], writes=[xdk]))
            if dram.get("after_tile"):
                dram["after_tile"](t0, N, outs)
    return outs


def prog_fused(NMS, TB):
    SEQr = NMS * TB
    Q = SEQr // 4
    TOKC_ = NMETA + Q
    CHX = min(256, Q)
    CHY = min(512, Q)
    NCX = Q // CHX
    NCY = SEQr // CHY
    nc = bass.Bass("TRN2", target_bir_lowering=False)
    ext = {}
    ext["hT"] = nc.dram_tensor("hT", [2048, TOKC_], F32, kind="ExternalInput").ap()
    ext["sel"] = nc.dram_tensor("sel", [128, 4], F32, kind="ExternalInput").ap()
    for nm, shp in CONST_SHAPES.items():
        ext[nm] = nc.dram_tensor(nm, shp, F32, kind="ExternalInput").ap()
    for l in range(5):
        ext["gn%d" % l] = nc.dram_tensor("gn%d" % l, [128, 16], F32, kind="ExternalInput").ap()
    for l in range(4):
        ext["wA%d" % l] = nc.dram_tensor("wA%d" % l, [2048, NCOLA], F32, kind="ExternalInput").ap()
        ext["parA%d" % l] = nc.dram_tensor("parA%d" % l, [128, NPA], F32, kind="ExternalInput").ap()
        ext["wud%d" % l] = nc.dram_tensor("wud%d" % l, [128, 256], F32, kind="ExternalInput").ap()
        ext["wg%d" % l] = nc.dram_tensor("wg%d" % l, [2048, 6144], F32, kind="ExternalInput").ap()
        ext["wbr%d" % l] = nc.dram_tensor("wbr%d" % l, [3072, 2048], F32, kind="ExternalInput").ap()
        ext["wo%d" % l] = nc.dram_tensor("wo%d" % l, [2048, 2048], F32, kind="ExternalInput").ap()
    out = nc.dram_tensor("out", [2048, Q], F32, kind="ExternalOutput").ap()
    hloc = nc.dram_tensor("hloc", [2048, TOKC_], F32).ap()
    xm = nc.dram_tensor("xm", [2048, NMETA], BF16).ap()
    xr_t = [nc.dram_tensor("xr%d" % k, [2048, CHX], BF16) for k in range(NCX)]
    xg_t = [nc.dram_tensor("xg%d" % k, [4 * 2048, CHX], BF16) for k in range(NCX)]
    ylm_t = nc.dram_tensor("ylm", [768, NMETA], BF16)
    ygm_t = nc.dram_tensor("ygm", [4 * 768, NMETA], BF16)
    yl_t = [nc.dram_tensor("yl%d" % k, [768, CHY], BF16) for k in range(NCY)]
    yg_t = [nc.dram_tensor("yg%d" % k, [4 * 768, CHY], BF16) for k in range(NCY)]
    wgc = nc.dram_tensor("wgc", [16, 128, 3 * 16 * 128], BF16).ap()
    wbc = nc.dram_tensor("wbc", [16, 128, 3 * 8 * 128], BF16).ap()
    woc = nc.dram_tensor("woc", [16, 128, 16 * 128], BF16).ap()
    groups = [[0, 1, 2, 3], [4, 5, 6, 7]]
    S = Sched(nc)
    hT3 = ext["hT"].rearrange("(kc p) t -> p kc t", p=128)
    hl3 = hloc.rearrange("(kc p) t -> p kc t", p=128)
    xm3 = xm.rearrange("(kc p) t -> p kc t", p=128)
    xr3 = [t.ap().rearrange("(kc p) t -> p kc t", p=128) for t in xr_t]
    xg4 = [t.ap().rearrange("(q kc p) t -> p q kc t", q=4, p=128) for t in xg_t]
    ygm4 = ygm_t.ap().rearrange("(jj c p) t -> p jj c t", jj=4, c=6, p=128)
    yg4 = [t.ap().rearrange("(jj c p) t -> p jj c t", jj=4, c=6, p=128) for t in yg_t]
    out3 = out.rearrange("(kc p) t -> p kc t", p=128)

    def xloc(t0, N):
        if t0 == 0:
            return [(xm3[:, :, 0:N], "d_xm", 0, N)]
        r0 = t0 - NMETA
        res = []
        for k in range(r0 // CHX, (r0 + N) // CHX):
            res.append((xr3[k][:, :, :], ("d_xr", k), k * CHX - r0, (k + 1) * CHX - r0))
        return res

    with ExitStack() as top:
        selt = top.enter_context(nc.sbuf_tensor("s_sel", [128, 4], F32))
        S.dma("sp", lambda e: e.dma_start(out=selt[:], in_=ext["sel"]), writes=["sel"])

        def y_load(y, cand, t0, N):
            if t0 == 0:
                for jj in range(4):
                    S.dma("sp", lambda e, jj=jj: e.dma_start(out=y[:, jj * 6:(jj + 1) * 6, 0:N], in_=ygm4[:, jj, :, 0:N]), reads=["d_ygm"], writes=["y"])
                return
            for q in range(4):
                kk = (q * Q + (t0 - NMETA)) // CHY
                for jj in range(4):
                    S.dma("sp", lambda e, jj=jj, kk=kk: e.dma_start(out=cand[:, jj * 6:(jj + 1) * 6, 0:N], in_=yg4[kk][:, jj, :, 0:N]), reads=[("d_yg", kk)], writes=["cand"])
                if q == 0:
                    S.dve(lambda e: e.tensor_scalar_mul(out=y[:, :, 0:N], in0=cand[:, :, 0:N], scalar1=selt[:, 0:1]), reads=["cand", "sel"], writes=["y"])
                else:
                    S.dve(lambda e, q=q: e.scalar_tensor_tensor(out=y[:, :, 0:N], in0=cand[:, :, 0:N], scalar=selt[:, q:q + 1], in1=y[:, :, 0:N], op0=ALU.mult, op1=ALU.add),
                          reads=["cand", "sel", "y"], writes=["y"])

        def cc_gather(src_t, dst_t, wk, extra):
            return S.cc(lambda e, src_t=src_t, dst_t=dst_t: e.collective_compute("AllGather", ALU.bypass, replica_groups=groups, ins=[src_t.ap().opt()], outs=[dst_t.ap().opt()]),
                        writes=[wk], extra=list(extra))

        def make_after_tile():
            st_ = dict(n=0)

            def after_tile(t0, N, outs):
                if t0 == 0:
                    return
                r0 = t0 - NMETA
                for k in range(r0 // CHX, (r0 + N) // CHX):
                    cc_gather(xr_t[k], xg_t[k], ("d_xg", k), outs[st_["n"]:])
                st_["n"] = len(outs)
            return after_tile

        def make_after_ms():
            st_ = dict(n=0, k=0)

            def after_ms(ms, t0, TBc, outs):
                if ms == 0:
                    cc_gather(ylm_t, ygm_t, "d_ygm", outs[st_["n"]:])
                    st_["n"] = len(outs)
                    return
                done = (t0 - NMETA + TBc) // CHY
                if done > st_["k"]:
                    for k in range(st_["k"], done):
                        cc_gather(yl_t[k], yg_t[k], ("d_yg", k), outs[st_["n"]:])
                    st_["k"] = done
                    st_["n"] = len(outs)
            return after_ms

        dN = dict(gnext=ext["gn0"], h_src=lambda t0, N: (hT3[:, :, t0:t0 + N], ("d_h", t0)), x_dst=xloc, after_tile=make_after_tile())
        with ExitStack() as es:
            outs = build_B(nc, S, es, TOKC_, dN, proj=False, pref="n0")
        S.barrier()
        g0 = outs[-1]
        finals = None
        FSTOP = int(os.environ.get("FSTOP", "99"))
        if FSTOP == 0:
            S.emit(final_wait_ops=[g0])
            return nc, S
        for l in range(4):
            dA = dict(wA=ext["wA%d" % l], parA=ext["parA%d" % l], wud=ext["wud%d" % l])
            for nm in CONST_SHAPES:
                dA[nm] = ext[nm]

            def xn_src(t0, TBc):
                if t0 == 0:
                    return xm3[:, :, 0:TBc], "d_xm"
                q, off = divmod(t0 - NMETA, Q)
                k, c = divmod(off, CHX)
                return xg4[k][:, q, :, c:c + TBc], ("d_xg", k)

            def y_dst(r0, t0, TBc):
                if t0 == 0:
                    return ylm_t.ap()[r0:r0 + 128, 0:TBc], ("d_yl", r0, t0)
                k, c = divmod(t0 - NMETA, CHY)
                return yl_t[k].ap()[r0:r0 + 128, c:c + TBc], ("d_yl", r0, t0)

            dA["xn_src"] = xn_src
            dA["y_dst"] = y_dst
            dA["after_ms"] = make_after_ms()
            with ExitStack() as es:
                outs = build_A(nc, S, es, NMS, TB, dA, pref="a%d" % l)
            S.barrier()
            g1 = outs[-1]
            if FSTOP == 1:
                S.emit(final_wait_ops=[g1])
                return nc, S
            last = l == 3
            hsrc3 = hT3 if l == 0 else hl3
            dB = dict(gnext=ext["gn%d" % (l + 1)], wg=ext["wg%d" % l], wbr=ext["wbr%d" % l], wo=ext["wo%d" % l],
                      h_src=lambda t0, N, hsrc3=hsrc3: (hsrc3[:, :, t0:t0 + N], ("d_h", t0)),
                      xn_srcB=xloc, y_load=y_load, need_cand=True, wcache=(wgc, wbc, woc), yidx=lambda br, kc: (kc // 2) * 6 + br * 2 + (kc % 2))
            if not last:
                dB["h_dst"] = lambda t0, N: (hl3[:, :, t0:t0 + N], ("d_h", t0))
                dB["x_dst"] = xloc
                dB["after_tile"] = make_after_tile()
            else:
                dB["out_dst"] = lambda ob, t0, N: (out3[:, ob, t0 - NMETA:t0 - NMETA + N], ("d_out", ob, t0))
            with ExitStack() as es:
                outs = build_B(nc, S, es, TOKC_, dB, proj=True, last=last, skip_meta=last, pref="b%d" % l)
            if not last:
                S.barrier()
            else:
                finals = outs
        S.emit(final_wait_ops=finals)
    return nc, S


SPL = [1024, 1024, 1024, 1024, 512, 512, 1024, 1024, 4, 4, 1024, 1024, 1024, 1024, 64, 64, 1024, 2048, 2048, 2048]
NAMES = ["a_q", "a_f", "a_i", "a_z", "b_q", "b_k", "b_v", "b_o", "b_ig", "b_fg", "b_z", "c_r", "c_k", "c_v", "c_wd", "c_ad", "c_z", "g_a", "g_b", "g_c"]
OFF = {}
_o = 0
for n_, w_ in zip(NAMES, SPL):
    OFF[n_] = _o
    _o += w_
NIN = _o


def colsA(j):
    cols = np.zeros(NCOLA, np.int64)

    def put(tile, start):
        cols[CT[tile] * 128:(CT[tile] + 1) * 128] = np.arange(start, start + 128)

    for hh in range(2):
        h = 2 * j + hh
        put("hq%d" % hh, OFF["a_q"] + h * 128)
        put("hf%d" % hh, OFF["a_f"] + h * 128)
        put("hi%d" % hh, OFF["a_i"] + h * 128)
        put("hz%d" % hh, OFF["a_z"] + h * 128)
    put("mq", OFF["b_q"] + j * 128)
    put("mk", OFF["b_k"] + j * 128)
    for i in range(2):
        put("mv%d" % i, OFF["b_v"] + j * 256 + i * 128)
        put("mo%d" % i, OFF["b_o"] + j * 256 + i * 128)
        put("mz%d" % i, OFF["b_z"] + j * 256 + i * 128)
    for p in range(2):
        c0 = j * 256 + p * 128
        put("rr%d" % p, OFF["c_r"] + c0)
        put("rk%d" % p, OFF["c_k"] + c0)
        put("rv%d" % p, OFF["c_v"] + c0)
        put("rz%d" % p, OFF["c_z"] + c0)
    cols[CT["rwa"] * 128:CT["rwa"] * 128 + 64] = np.arange(OFF["c_wd"], OFF["c_wd"] + 64)
    cols[CT["rwa"] * 128 + 64:CT["rwa"] * 128 + 128] = np.arange(OFF["c_ad"], OFF["c_ad"] + 64)
    cols[25 * 128] = OFF["b_ig"] + j
    cols[25 * 128 + 1] = OFF["b_fg"] + j
    return cols


def pack_A(inp, l, j):
    f = np.float32
    wA = np.ascontiguousarray(np.asarray(inp["w_in"][l])[:, colsA(j)], dtype=f)
    par = np.zeros((128, NPA), f)

    def put(nm, v):
        o, w = PA[nm]
        v = np.asarray(v, f)
        if v.ndim == 0:
            par[:, o:o + w] = v
        else:
            par[:, o:o + w] = v.reshape(128, w)

    lbl = np.asarray(inp["hgrn_lb_logits"])
    par[:, PA["lbsel"][0]:PA["lbsel"][0] + 4] = np.array([1.0 if 1 <= i <= l else 0.0 for i in range(4)], f)[None, :]
    for hh in range(2):
        ch = slice((2 * j + hh) * 128, (2 * j + hh + 1) * 128)
        put("lbl%d" % hh, lbl[:, ch].T)
        put("hg%d" % hh, np.asarray(inp["hgrn_norm_g"][l])[ch])
    cw = np.asarray(inp["mlstm_conv"][l])
    put("cwq", cw[:, j * 128:(j + 1) * 128].T)
    put("cwk", cw[:, 512 + j * 128:512 + (j + 1) * 128].T)
    for i in range(2):
        put("mg%d" % i, np.asarray(inp["mlstm_norm_g"][l])[j * 256 + i * 128:j * 256 + (i + 1) * 128])
    put("igb", float(np.asarray(inp["mlstm_ig_b"])[l, j]))
    put("fgb", float(np.asarray(inp["mlstm_fg_b"])[l, j]))
    put("eps", 1e-6)
    put("lneps", 64e-5)
    put("zero", 0.0)
    mu = np.asarray(inp["rwkv_mu"][l])
    for p in range(2):
        c = slice(j * 256 + p * 128, j * 256 + (p + 1) * 128)
        put("mur%d" % p, mu[0:1024][c])
        put("muk%d" % p, mu[1024:2048][c])
        put("muv%d" % p, mu[2048:3072][c])
        put("w0%d" % p, np.asarray(inp["rwkv_w0"][l])[c])
        put("a0%d" % p, np.asarray(inp["rwkv_a0"][l])[c])
        put("kk%d" % p, np.asarray(inp["rwkv_k_k"][l])[c])
        put("ka%d" % p, np.asarray(inp["rwkv_k_a"][l])[c])
        put("rk%d" % p, np.asarray(inp["rwkv_r_k"][l])[c])
        put("lg%d" % p, np.asarray(inp["rwkv_ln_g"][l])[c])
        put("lb%d" % p, np.asarray(inp["rwkv_ln_b"][l])[c])
    put("muwa", mu[3072:3200])
    wud = np.zeros((128, 256), f)
    wud[0:64] = np.asarray(inp["rwkv_w_up"][l])[:, j * 256:(j + 1) * 256]
    wud[64:128] = np.asarray(inp["rwkv_a_up"][l])[:, j * 256:(j + 1) * 256]
    return dict(wA=wA, parA=par, wud=wud)


TB_A = 256
SEQ = 8192
NMS_A = SEQ // TB_A


def kernel(**inp):
    f = np.float32
    x = np.asarray(inp["x"], f)
    meta = np.asarray(inp["meta_tokens"], f)
    Q = SEQ // 4
    nc, _S = prog_fused(NMS_A, TB_A)
    cs = host_consts()

    def gn(g):
        return np.ascontiguousarray(np.asarray(g, f).reshape(16, 128).T)

    gns = [gn(inp["norm_g"][l]) for l in range(4)] + [gn(inp["final_norm_g"])]
    shared = {}
    for l in range(4):
        w_in_l = np.asarray(inp["w_in"][l])
        shared["wg%d" % l] = np.ascontiguousarray(w_in_l[:, OFF["g_a"]:OFF["g_a"] + 6144], dtype=f)
        shared["wbr%d" % l] = np.ascontiguousarray(np.asarray(inp["w_br"][l], f).reshape(3072, 2048))
        shared["wo%d" % l] = np.ascontiguousarray(np.asarray(inp["w_out"][l], f))
    packs = {}
    for j in range(4):
        for l in range(4):
            packs[(l, j)] = pack_A(inp, l, j)
    maps = []
    for c in range(8):
        b, j = divmod(c, 4)
        m = {"hT": np.ascontiguousarray(np.concatenate([meta, x[b, j * Q:(j + 1) * Q]], axis=0).T)}
        sel = np.zeros((128, 4), f)
        sel[:, j] = 1.0
        m["sel"] = sel
        m.update(cs)
        for l in range(5):
            m["gn%d" % l] = gns[l]
        for l in range(4):
            pa = packs[(l, j)]
            m["wA%d" % l] = pa["wA"]
            m["parA%d" % l] = pa["parA"]
            m["wud%d" % l] = pa["wud"]
        m.update(shared)
        maps.append(m)
    res = run_bass_kernel_spmd(nc, maps, core_ids=list(range(8)))
    out = np.empty((2, SEQ, 2048), f)
    for c in range(8):
        b, j = divmod(c, 4)
        out[b, j * Q:(j + 1) * Q] = np.asarray(res.results[c]["out"], f).T
    return out
```
